# Optimizing a Trainium2 kernel written in Bass

```python
import math
import jax, jax.numpy as jnp
from jax import lax
import numpy as np

D_MODEL = 1024
BATCH = 8
SEQ = 2048
DEPTH = 1
DEC_BATCH = 128
DEC_SEQ = 4
PAST_LEN = 16384
PAGE_SIZE = 128

D_A = D_MODEL
HEAD_A = 64
N_HEADS_A = D_A // HEAD_A
DECAY_LORA = 64
ICLR_LORA = 64
GATE_LORA = 128
RWKV_COLS = 3 * D_A + DECAY_LORA + ICLR_LORA + GATE_LORA
GN_EPS = HEAD_A * 1e-5
D_LRU = D_MODEL
LRU_BLOCKS = 16
LRU_BS = D_LRU // LRU_BLOCKS
CONV_W = 4
LRU_C = 8.0
N_IN = RWKV_COLS + 2 * D_LRU + 2 * D_MODEL
D_FF = ((8 * D_MODEL // 3 + 255) // 256) * 256
ALPHA = (2.0 * DEPTH) ** 0.25
BETA = (8.0 * DEPTH) ** -0.25
LN_EPS = 1e-5

kernel_name = "rwkv7_rglru_gated_hybrid_step"


def _layer_norm(x, g, b):
    xf = x.astype(jnp.float32)
    mu = xf.mean(-1, keepdims=True)
    var = jnp.square(xf - mu).mean(-1, keepdims=True)
    return ((xf - mu) * lax.rsqrt(var + LN_EPS) * g + b).astype(x.dtype)


def _wkv7_scan(r, decay, k, v, kk, a, S0):
    def step(S, inp):
        r_t, w_t, k_t, v_t, kk_t, a_t = inp
        sa = jnp.einsum('bhij,bhj->bhi', S, -kk_t)
        S = (S * w_t[:, :, None, :]
             + sa[..., None] * (kk_t * a_t)[:, :, None, :]
             + v_t[..., None] * k_t[:, :, None, :])
        return S, jnp.einsum('bhij,bhj->bhi', S, r_t)
    xs = tuple(jnp.swapaxes(t.astype(jnp.float32), 0, 1) for t in (r, decay, k, v, kk, a))
    S, y = lax.scan(step, S0.astype(jnp.float32), xs)
    return jnp.swapaxes(y, 0, 1), S


def _rwkv7_branch(z_rw, z_prev_first, S0, mu, w0, w2, a0, a2, g2, k_k, k_a, r_k, lnx_g, lnx_b):
    B, T, _ = z_rw.shape
    z_prev = jnp.concatenate([z_prev_first[:, None].astype(z_rw.dtype), z_rw[:, :-1]], axis=1)
    zm = z_rw + mu * (z_prev - z_rw)
    o = 0
    r = zm[..., o:o + D_A]; o += D_A
    k = zm[..., o:o + D_A]; o += D_A
    v = zm[..., o:o + D_A]; o += D_A
    w_lo = zm[..., o:o + DECAY_LORA]; o += DECAY_LORA
    a_lo = zm[..., o:o + ICLR_LORA]; o += ICLR_LORA
    g_lo = zm[..., o:o + GATE_LORA]
    w = -jax.nn.softplus(-(w0 + jnp.tanh(w_lo) @ w2).astype(jnp.float32)) - 0.5
    decay = jnp.exp(-jnp.exp(w))
    a = jax.nn.sigmoid(a0 + a_lo @ a2)
    g = jax.nn.sigmoid(g_lo) @ g2
    heads = lambda t: t.reshape(B, T, N_HEADS_A, HEAD_A)
    kk = heads(k * k_k).astype(jnp.float32)
    kk = kk / jnp.maximum(jnp.sqrt(jnp.sum(kk * kk, -1, keepdims=True)), 1e-12)
    k = k * (1.0 + (a - 1.0) * k_a)
    rh, kh, vh = heads(r), heads(k), heads(v)
    y, S = _wkv7_scan(rh, heads(decay), kh, vh, kk, heads(a), S0)
    mu_y = y.mean(-1, keepdims=True)
    var_y = jnp.square(y - mu_y).mean(-1, keepdims=True)
    y = ((y - mu_y) * lax.rsqrt(var_y + GN_EPS)).reshape(B, T, D_A) * lnx_g + lnx_b
    bonus = (jnp.sum((rh * kh * r_k).astype(jnp.float32), -1, keepdims=True) * vh).reshape(B, T, D_A)
    return (y + bonus) * g, S


def _combine(c1, c2):
    a1, b1 = c1
    a2, b2 = c2
    return a1 * a2, a2 * b1 + b2


def _rglru_branch(xb, gate_in, conv_buf, h0, conv_w, conv_b, wa, ba, wi, bi, lam, reset_first):
    B, T, _ = xb.shape
    xpad = jnp.concatenate([conv_buf.astype(xb.dtype), xb], axis=1)
    u = conv_b + sum(conv_w[j] * xpad[:, j:j + T] for j in range(CONV_W))
    new_conv = xpad[:, T:]
    ub = u.reshape(B, T, LRU_BLOCKS, LRU_BS)
    r = jax.nn.sigmoid(jnp.einsum('btnc,ncd->btnd', ub, wa) + ba).reshape(B, T, D_LRU)
    i = jax.nn.sigmoid(jnp.einsum('btnc,ncd->btnd', ub, wi) + bi).reshape(B, T, D_LRU)
    log_a = -LRU_C * r.astype(jnp.float32) * jax.nn.softplus(-lam.astype(jnp.float32))
    a = jnp.exp(log_a)
    mult = jnp.sqrt(-jnp.expm1(2.0 * log_a))
    if reset_first:
        mult = mult.at[:, 0].set(1.0)
    bx = mult * i.astype(jnp.float32) * u.astype(jnp.float32)
    bx = bx.at[:, 0].add(a[:, 0] * h0.astype(jnp.float32))
    _, h = lax.associative_scan(_combine, (a, bx), axis=1)
    out = h * jax.nn.gelu(gate_in.astype(jnp.float32))
    return out, new_conv, h[:, -1]


def _layer(x, shift_buf, wkv0, conv_buf, h0, reset_first,
           w_in, tmix_mu, w0, w2_decay, a0, a2_iclr, g2_gate, k_k, k_a, r_k, lnx_g, lnx_b,
           conv_w, conv_b, lru_wa, lru_ba, lru_wi, lru_bi, lru_lambda, w_o,
           ln1_g, ln1_b, w_ffn_gate, w_ffn_up, w_ffn_down, ln2_g, ln2_b):
    z = jnp.einsum('btd,de->bte', x, w_in)
    z_prev_first = shift_buf.astype(x.dtype) @ w_in[:, :RWKV_COLS]
    yA, S = _rwkv7_branch(z[..., :RWKV_COLS], z_prev_first, wkv0, tmix_mu, w0, w2_decay, a0,
                          a2_iclr, g2_gate, k_k, k_a, r_k, lnx_g, lnx_b)
    o = RWKV_COLS
    xb = z[..., o:o + D_LRU]
    gb = z[..., o + D_LRU:o + 2 * D_LRU]
    o2 = o + 2 * D_LRU
    gate_a = jax.nn.sigmoid(z[..., o2:o2 + D_MODEL].astype(jnp.float32))
    gate_b = jax.nn.sigmoid(z[..., o2 + D_MODEL:].astype(jnp.float32))
    yB, new_conv, h_last = _rglru_branch(xb, gb, conv_buf, h0, conv_w, conv_b, lru_wa, lru_ba,
                                         lru_wi, lru_bi, lru_lambda, reset_first)
    merged = (gate_a * yA + gate_b * yB).astype(x.dtype)
    mix = jnp.einsum('bte,ed->btd', merged, w_o)
    h1 = _layer_norm(ALPHA * x + mix, ln1_g, ln1_b)
    ffn = (jax.nn.silu(h1 @ w_ffn_gate) * (h1 @ w_ffn_up)) @ w_ffn_down
    y = _layer_norm(ALPHA * h1 + ffn, ln2_g, ln2_b)
    return y, x[:, -1], S, new_conv, h_last


def setup_inputs(seed: int = 0) -> dict:
    key = jax.random.key(seed)
    ks = jax.random.split(key, 40)
    nrm = lambda k, shape, s: jax.random.normal(k, shape, jnp.float32) * s
    L = DEPTH
    u_lam = jax.random.uniform(ks[20], (L, D_LRU), jnp.float32, 0.9, 0.999) ** (1.0 / LRU_C)
    return {
        "x_prompt": nrm(ks[0], (BATCH, SEQ, D_MODEL), 1.0),
        "x_sample": nrm(ks[1], (DEC_BATCH, DEC_SEQ, D_MODEL), 1.0),
        "state_shift": nrm(ks[2], (L, DEC_BATCH, D_MODEL), 1.0),
        "state_wkv": nrm(ks[3], (L, DEC_BATCH, N_HEADS_A, HEAD_A, HEAD_A), 0.5),
        "state_conv": nrm(ks[4], (L, DEC_BATCH, CONV_W - 1, D_LRU), 1.0),
        "state_lru": nrm(ks[5], (L, DEC_BATCH, D_LRU), 0.5),
        "w_in": nrm(ks[6], (L, D_MODEL, N_IN), D_MODEL ** -0.5),
        "tmix_mu": jax.random.uniform(ks[7], (L, RWKV_COLS), jnp.float32, 0.0, 1.0),
        "w0": jax.random.uniform(ks[8], (L, D_A), jnp.float32, -4.0, 1.0),
        "w2_decay": nrm(ks[9], (L, DECAY_LORA, D_A), 0.1 * DECAY_LORA ** -0.5),
        "a0": nrm(ks[10], (L, D_A), 0.1),
        "a2_iclr": nrm(ks[11], (L, ICLR_LORA, D_A), 0.5 * ICLR_LORA ** -0.5),
        "g2_gate": nrm(ks[12], (L, GATE_LORA, D_A), GATE_LORA ** -0.5),
        "k_k": 0.85 + nrm(ks[13], (L, D_A), 0.02),
        "k_a": 1.0 + nrm(ks[14], (L, D_A), 0.02),
        "r_k": nrm(ks[15], (L, N_HEADS_A, HEAD_A), 0.1),
        "lnx_g": 1.0 + nrm(ks[16], (L, D_A), 0.02),
        "lnx_b": nrm(ks[17], (L, D_A), 0.02),
        "conv_w": nrm(ks[18], (L, CONV_W, D_LRU), CONV_W ** -0.5),
        "conv_b": nrm(ks[19], (L, D_LRU), 0.02),
        "lru_wa": nrm(ks[21], (L, LRU_BLOCKS, LRU_BS, LRU_BS), LRU_BS ** -0.5),
        "lru_ba": nrm(ks[22], (L, LRU_BLOCKS, LRU_BS), 0.02),
        "lru_wi": nrm(ks[23], (L, LRU_BLOCKS, LRU_BS, LRU_BS), LRU_BS ** -0.5),
        "lru_bi": nrm(ks[24], (L, LRU_BLOCKS, LRU_BS), 0.02),
        "lru_lambda": jnp.log(u_lam / (1.0 - u_lam)),
        "w_o": nrm(ks[25], (L, D_MODEL, D_MODEL), BETA * D_MODEL ** -0.5),
        "ln1_g": 1.0 + nrm(ks[26], (L, D_MODEL), 0.02),
        "ln1_b": nrm(ks[27], (L, D_MODEL), 0.02),
        "w_ffn_gate": nrm(ks[28], (L, D_MODEL, D_FF), D_MODEL ** -0.5),
        "w_ffn_up": nrm(ks[29], (L, D_MODEL, D_FF), D_MODEL ** -0.5),
        "w_ffn_down": nrm(ks[30], (L, D_FF, D_MODEL), BETA * D_FF ** -0.5),
        "ln2_g": 1.0 + nrm(ks[31], (L, D_MODEL), 0.02),
        "ln2_b": nrm(ks[32], (L, D_MODEL), 0.02),
    }


def reference(x_prompt, x_sample, state_shift, state_wkv, state_conv, state_lru,
              w_in, tmix_mu, w0, w2_decay, a0, a2_iclr, g2_gate, k_k, k_a, r_k, lnx_g, lnx_b,
              conv_w, conv_b, lru_wa, lru_ba, lru_wi, lru_bi, lru_lambda, w_o,
              ln1_g, ln1_b, w_ffn_gate, w_ffn_up, w_ffn_down, ln2_g, ln2_b):
    params = (w_in, tmix_mu, w0, w2_decay, a0, a2_iclr, g2_gate, k_k, k_a, r_k, lnx_g, lnx_b,
              conv_w, conv_b, lru_wa, lru_ba, lru_wi, lru_bi, lru_lambda, w_o,
              ln1_g, ln1_b, w_ffn_gate, w_ffn_up, w_ffn_down, ln2_g, ln2_b)
    Bp = x_prompt.shape[0]
    yp, ys = x_prompt, x_sample
    sp_shift, sp_wkv, sp_conv, sp_lru = [], [], [], []
    ss_shift, ss_wkv, ss_conv, ss_lru = [], [], [], []
    for l in range(DEPTH):
        p = [w[l] for w in params]
        yp, s1, s2, s3, s4 = _layer(
            yp, jnp.zeros((Bp, D_MODEL), yp.dtype),
            jnp.zeros((Bp, N_HEADS_A, HEAD_A, HEAD_A), jnp.float32),
            jnp.zeros((Bp, CONV_W - 1, D_LRU), yp.dtype),
            jnp.zeros((Bp, D_LRU), jnp.float32), True, *p)
        sp_shift.append(s1); sp_wkv.append(s2); sp_conv.append(s3); sp_lru.append(s4)
        ys, t1, t2, t3, t4 = _layer(ys, state_shift[l], state_wkv[l], state_conv[l], state_lru[l],
                                    False, *p)
        ss_shift.append(t1); ss_wkv.append(t2); ss_conv.append(t3); ss_lru.append(t4)
    return (yp, ys,
            jnp.stack(sp_shift), jnp.stack(sp_wkv), jnp.stack(sp_conv), jnp.stack(sp_lru),
            jnp.stack(ss_shift), jnp.stack(ss_wkv), jnp.stack(ss_conv), jnp.stack(ss_lru))
```

```python
import numpy as np
import ml_dtypes
from contextlib import ExitStack
import concourse.bass as bass
import concourse.mybir as mybir
from concourse.bass_utils import run_bass_kernel_spmd

F32 = mybir.dt.float32
BF16 = mybir.dt.bfloat16
AF = mybir.ActivationFunctionType
ALU = mybir.AluOpType
AX = mybir.AxisListType
ml_bf16 = ml_dtypes.bfloat16

D = 1024
NCORES = 8
SEQ = 2048
NSB = 16
DEC = 4
NCH = 8
D_FF = 2816
NFC = D_FF // 128
KAPPA = float(np.exp(-0.5))
ALPHA = 2.0 ** 0.25
LN_EPS = 1e-5
GN_EPS = 64e-5
CH = 64
PIPE = True
RSCALE = 1.0
TAILPIPE = True
TAILSEL = lambda kind, idx: True
QB = 4
SEGP = QB * CH
NSEGP = SEQ // SEGP
SB = 16
NSEGS = NSB // SB
XS0 = SEQ
XCOLS = SEQ + NSB * 5
MCOLS = SEQ + NSB * DEC
NFIN = 68
(V_MUR, V_MUK, V_MUV, V_W0, V_A0, V_KK, V_KA, V_RK, V_LNXG, V_LNXB, V_CW0, V_CW1, V_CW2, V_CW3,
 V_CB, V_BA, V_BI, V_LAM, V_MUL) = range(19)
NV = 19


class Sched:
    COMPUTE = ("pe", "act", "dve", "pool")

    def __init__(self, nc, es, ndma=14):
        self.nc = nc
        self.eng = {"pe": nc.tensor, "act": nc.scalar, "dve": nc.vector, "pool": nc.gpsimd, "sp": nc.sync}
        self.sem = {}
        self.cnt = {}
        for e in self.COMPUTE:
            self.sem[e] = es.enter_context(nc.semaphore("sem_" + e))
            self.cnt[e] = 0
        self.ndma = ndma
        for i in range(ndma):
            d = "d%d" % i
            self.sem[d] = es.enter_context(nc.semaphore("sem_" + d))
            self.cnt[d] = 0
        self.rr = 0
        self.waited = {e: {} for e in list(self.COMPUTE) + ["sp"]}
        self.W = {}
        self.R = {}
        self.excl = set()
        self.nops = {e: 0 for e in list(self.COMPUTE) + ["sp"]}

    def _collect(self, e, reads, writes):
        need = {}

        def add(d, c, raw):
            if d == e and (e == "pe" or not raw):
                return
            if c > need.get(d, 0):
                need[d] = c

        reads = [getattr(r, "base", r) for r in reads]
        writes = [getattr(w, "base", w) for w in writes]
        for r in reads:
            for d, c in self.W.get(r, {}).items():
                add(d, c, True)
            if r in self.excl:
                for d, c in self.R.get(r, {}).items():
                    add(d, c, False)
        for w in writes:
            for d, c in self.W.get(w, {}).items():
                add(d, c, False)
            for d, c in self.R.get(w, {}).items():
                add(d, c, False)
        return need

    LOG = None

    def _emit_waits(self, e, need):
        wd = self.waited[e]
        for d, c in need.items():
            if wd.get(d, 0) >= c:
                continue
            self.eng[e].wait_ge(self.sem[d], c)
            wd[d] = c
            if self.LOG is not None:
                self.LOG.append((e, d, c, dict(self.cnt)))

    def _record(self, dom, val, reads, writes):
        reads = [getattr(r, "base", r) for r in reads]
        writes = [getattr(w, "base", w) for w in writes]
        for r in reads:
            rr = self.R.setdefault(r, {})
            if val > rr.get(dom, 0):
                rr[dom] = val
        for w in writes:
            if self.R.get(w):
                self.W[w] = {dom: val}
                self.R[w] = {}
            else:
                self.W.setdefault(w, {})[dom] = val

    def op(self, e, fn, reads=(), writes=(), inc=True):
        need = self._collect(e, reads, writes)
        att = None
        if e != "pe":
            wd = self.waited[e]
            pend = [(d, c) for d, c in need.items() if wd.get(d, 0) < c]
            if pend:
                att = pend[-1]
                need = dict(pend[:-1])
        self._emit_waits(e, need)
        ins = fn(self.eng[e])
        if att is not None:
            ins._wait_ge(self.sem[att[0]], att[1])
            self.waited[e][att[0]] = att[1]
        self.nops[e] += 1
        if inc:
            self.cnt[e] += 1
            ins.then_inc(self.sem[e], 1)
            val = self.cnt[e]
        else:
            val = self.cnt[e] + 1
        self._record(e, val, reads, writes)

    def dma(self, out, in_, reads=(), writes=(), q="sp"):
        d = "d%d" % self.rr
        self.rr = (self.rr + 1) % self.ndma
        need = self._collect(q, reads, writes)
        if self.cnt[d] > 0:
            need[d] = max(need.get(d, 0), self.cnt[d])
        self._emit_waits(q, need)
        ins = self.eng[q].dma_start(out=out, in_=in_)
        self.nops[q] += 1
        self.cnt[d] += 16
        ins.then_inc(self.sem[d], 16)
        self._record(d, self.cnt[d], reads, writes)

    def barrier(self):
        for e in list(self.COMPUTE) + ["sp"]:
            need = {d: c for d, c in self.cnt.items() if c > 0 and d != e}
            self._emit_waits(e, need)

    def finish(self):
        for i in range(self.ndma):
            d = "d%d" % i
            if self.cnt[d] > 0:
                self.eng["sp"].wait_ge(self.sem[d], self.cnt[d])
        for e in self.COMPUTE:
            if self.cnt[e] > 0:
                self.eng["sp"].wait_ge(self.sem[e], self.cnt[e])


class StopBuild(Exception):
    pass


class T:
    def __init__(self, h, name):
        self.h = h
        self.name = name

    def __getitem__(self, k):
        return self.h[k]

    def __repr__(self):
        return "T(%s)" % self.name


class Off:
    def __init__(self, base, off, width):
        self.base = base
        self.off = off
        self.width = width

    def __getitem__(self, k):
        if not isinstance(k, tuple):
            return self.base[:, self.off:self.off + self.width]
        p_, c_ = k
        lo = 0 if c_.start is None else c_.start
        hi = self.width if c_.stop is None else c_.stop
        return self.base[p_, self.off + lo:self.off + hi]


class SubSeg:
    kind = "s"

    def __init__(self, b):
        self.idx = b
        self.ncol = QB * 5
        self.nb = QB
        self.T = DEC
        self.tv = DEC
        self.nq = QB
        self.m0 = SEQ + b * QB * DEC
        self.mcols = QB * DEC
        self.nlev = 2

    def tokv(self, ap):
        return ap.rearrange("p (b s) -> p b s", s=5)[:, :, 1:5]

    chv = tokv


class Pool:
    def __init__(self, tiles):
        self.free = list(tiles)
        self.all = list(tiles)

    def get(self):
        return self.free.pop(0)

    def put(self, *ts):
        for t in ts:
            assert t in self.all and t not in self.free
            self.free.append(t)


class Seg:
    def __init__(self, kind, idx):
        self.kind = kind
        self.idx = idx
        if kind == "p":
            self.x0 = idx * SEGP
            self.ncol = SEGP
            self.nb = 1
            self.T = SEGP
            self.tv = CH
            self.m0 = idx * SEGP
            self.mcols = SEGP
            self.nlev = 6
        else:
            self.x0 = XS0 + idx * SB * 5
            self.ncol = SB * 5
            self.nb = SB
            self.T = DEC
            self.tv = DEC
            self.m0 = SEQ + idx * SB * DEC
            self.mcols = SB * DEC
            self.nlev = 2
        self.nq = QB if kind == "p" else SB

    def tokv(self, ap):
        if self.kind == "p":
            return ap.rearrange("p (b t) -> p b t", b=1)
        return ap.rearrange("p (b s) -> p b s", s=5)[:, :, 1:5]

    def chv(self, ap):
        if self.kind == "p":
            return ap.rearrange("p (q t) -> p q t", t=CH)
        return ap.rearrange("p (b s) -> p b s", s=5)[:, :, 1:5]


def build_program(debug=None, stop_after=None):
    nc = bass.Bass("TRN2", target_bir_lowering=False)
    es = ExitStack()
    S = Sched(nc, es)

    def din(name, shape):
        return nc.dram_tensor(name, list(shape), F32, kind="ExternalInput").ap()

    def dout(name, shape):
        return nc.dram_tensor(name, list(shape), F32, kind="ExternalOutput").ap()

    xp = din("xp", [SEQ, D])
    xs = din("xs", [NSB * 5, D])
    xst = din("xst", [NSB * DEC, D])
    st = din("st", [64, D])
    s0 = din("s0", [NSB, 16, 64, 64])
    w1 = din("w1", [NCH, 128, 7, 8, 128])
    wl = din("wl", [128, 2, 8, 128])
    wo = din("wo", [128, 8, D])
    wgu = din("wgu", [NFC, 128, 2, 8, 128])
    wd = din("wd", [NFC, 128, D])
    vec = din("vec", [128, NV, 8])
    w2d = din("w2d", [64, D])
    a2d = din("a2d", [64, D])
    g2d = din("g2d", [128, D])
    wab = din("wab", [128, 2, 8, 128])
    lnv = din("lnv", [128, 4, D])

    yp = dout("yp", [SEQ, D])
    ys = dout("ys", [NSB * DEC, D])
    o_shift = dout("o_shift", [1 + NSB, D])
    o_wkv_p = dout("o_wkv_p", [16, 64, 64])
    o_wkv_s = dout("o_wkv_s", [NSB, 16, 64, 64])
    o_fin = dout("o_fin", [NFIN, D])
    dbg_out = {}
    if debug:
        for name, (shape, dt_) in debug.items():
            dbg_out[name] = nc.dram_tensor("dbg_" + name, list(shape), dt_, kind="ExternalOutput").ap()

    def sb(name, shape, dt=F32, stack=None):
        h = (stack or es).enter_context(nc.sbuf_tensor(name, list(shape), dt))
        return T(h, name)

    psl = [T(es.enter_context(nc.psum_tensor("ps%d" % i, [128, 512], F32)), "ps%d" % i) for i in range(8)]
    ps_state = {"i": 0}
    S.excl.update(psl)

    ps_banks = {None: list(range(8)), "s1": [6, 7], "s1a": [6], "s1b": [7], "s1c": [5], "s2A": [0, 1], "s2B": [2, 3], "tailA": [4], "tailB": [4]}
    ps_ctr = {k: 0 for k in (None, "s1", "s1a", "s1b", "s1c", "s2A", "s2B", "tailA", "tailB")}
    cur_alloc = {"who": None}

    def getps():
        who = cur_alloc["who"]
        lst = ps_banks[who]
        t = psl[lst[ps_ctr[who] % len(lst)]]
        ps_ctr[who] += 1
        return t

    cur = {"segi": -1}

    def ck(label):
        if stop_after == label or stop_after == "%s@%d" % (label, cur["segi"]):
            raise StopBuild()

    ident_f = sb("ident_f", [128, 128])
    ident_b = sb("ident_b", [128, 128], BF16)
    II_f = sb("II_f", [128, 64])
    II_b = sb("II_b", [128, 64], BF16)
    ones_bd = sb("ones_bd", [128, 128])
    mUs = sb("mUs", [128, QB, 128], BF16)
    mUi = sb("mUi", [128, QB, 128], BF16)
    mLs = sb("mLs", [128, QB, 128], BF16)
    rmask_p = sb("rmask_p", [128, SEGP])
    rmask_s = sb("rmask_s", [128, SB * 5])
    vecs = sb("vecs", [128, NV, 8])
    der = sb("der", [128, 12, 8])
    w2b = sb("w2b", [128, D], BF16)
    g2b = sb("g2b", [128, D], BF16)
    wabb = sb("wabb", [128, 2, 8, 128], BF16)
    fin = sb("fin", [128, NCH, NFIN])
    stT = sb("stT", [128, NCH, 64])

    S.op("pool", lambda E: E.memset(ident_f[:], 1.0), writes=[ident_f])
    S.op("pool", lambda E: E.affine_select(out=ident_f[:], in_=ident_f[:], pattern=[[-1, 128]],
                                           compare_op=ALU.is_equal, fill=0.0, base=0, channel_multiplier=1),
         reads=[ident_f], writes=[ident_f])
    S.op("dve", lambda E: E.tensor_copy(out=ident_b[:], in_=ident_f[:]), reads=[ident_f], writes=[ident_b])
    S.op("dve", lambda E: E.tensor_tensor(out=II_f[:], in0=ident_f[:, 0:64], in1=ident_f[:, 64:128], op=ALU.add),
         reads=[ident_f], writes=[II_f])
    S.op("dve", lambda E: E.tensor_copy(out=II_b[:], in_=II_f[:]), reads=[II_f], writes=[II_b])
    S.op("pool", lambda E: E.memset(ones_bd[:], 0.0), writes=[ones_bd])
    S.op("pool", lambda E: E.memset(ones_bd[0:64, 0:64], 1.0), reads=[ones_bd], writes=[ones_bd])
    S.op("pool", lambda E: E.memset(ones_bd[64:128, 64:128], 1.0), reads=[ones_bd], writes=[ones_bd])
    for m, patt, cm, cmp in ((mUs, 1, -1, ALU.is_gt), (mUi, 1, -1, ALU.is_ge), (mLs, -1, 1, ALU.is_gt)):
        S.op("pool", lambda E, m=m: E.memset(m[:], 1.0), writes=[m])
        S.op("pool", lambda E, m=m, patt=patt, cm=cm, cmp=cmp: E.affine_select(
            out=m[:], in_=m[:], pattern=[[0, QB], [patt, 128]], compare_op=cmp, fill=0.0, base=0,
            channel_multiplier=cm), reads=[m], writes=[m])
    S.op("pool", lambda E: E.memset(rmask_p[:], 1.0), writes=[rmask_p])
    S.op("pool", lambda E: E.memset(rmask_p[:].rearrange("p (q t) -> p q t", t=CH)[:, :, 0:1], 0.0),
         reads=[rmask_p], writes=[rmask_p])
    S.op("pool", lambda E: E.memset(rmask_s[:], 1.0), writes=[rmask_s])
    S.op("pool", lambda E: E.memset(rmask_s[:].rearrange("p (b s) -> p b s", s=5)[:, :, 0:2], 0.0),
         reads=[rmask_s], writes=[rmask_s])
    S.op("pool", lambda E: E.memset(fin[:], 0.0), writes=[(fin, c) for c in range(NCH)])

    S.dma(vecs[:], vec[:, :, :], writes=[vecs])

    def vcol(i, c):
        return vecs[:, i, c:c + 1]

    S.op("dve", lambda E: E.tensor_scalar(out=der[:, 0, :], in0=vecs[:, V_KA, :], scalar1=-1.0, scalar2=1.0,
                                          op0=ALU.mult, op1=ALU.add), reads=[vecs], writes=[(der, 0)])
    S.op("act", lambda E: E.activation(out=der[:, 3, :], in_=vecs[:, V_LAM, :], func=AF.Exp, scale=-1.0),
         reads=[vecs], writes=[(der, 3)])
    S.op("act", lambda E: E.activation(out=der[:, 3, :], in_=der[:, 3, :], func=AF.Ln, bias=1.0, scale=1.0),
         reads=[(der, 3)], writes=[(der, 3)])
    S.op("dve", lambda E: E.tensor_scalar(out=der[:, 1, :], in0=der[:, 3, :], scalar1=-8.0, scalar2=None,
                                          op0=ALU.mult), reads=[(der, 3)], writes=[(der, 1)])
    S.op("dve", lambda E: E.tensor_scalar(out=der[:, 2, :], in0=der[:, 3, :], scalar1=-16.0, scalar2=None,
                                          op0=ALU.mult), reads=[(der, 3)], writes=[(der, 2)])
    for di, vi in ((4, V_W0), (5, V_A0), (6, V_BA), (7, V_BI), (8, V_KA)):
        S.op("dve", lambda E, di=di, vi=vi: E.tensor_scalar(out=der[:, di, :], in0=vecs[:, vi, :], scalar1=0.5,
                                                            scalar2=None, op0=ALU.mult), reads=[vecs], writes=[(der, di)])
    S.op("dve", lambda E: E.tensor_scalar(out=der[:, 9, :], in0=vecs[:, V_KA, :], scalar1=-0.5, scalar2=1.0,
                                          op0=ALU.mult, op1=ALU.add), reads=[vecs], writes=[(der, 9)])
    S.op("dve", lambda E: E.tensor_scalar(out=der[:, 10, :], in0=der[:, 3, :], scalar1=-4.0, scalar2=None,
                                          op0=ALU.mult), reads=[(der, 3)], writes=[(der, 10)])
    S.op("dve", lambda E: E.tensor_scalar(out=der[:, 11, :], in0=der[:, 3, :], scalar1=-8.0, scalar2=None,
                                          op0=ALU.mult), reads=[(der, 3)], writes=[(der, 11)])
    negh = sb("negh", [128, QB])
    S.op("pool", lambda E: E.memset(negh[:], -0.5), writes=[negh])

    p1 = ExitStack()
    merged = sb("merged", [128, 8, MCOLS], BF16)
    xsc = nc.dram_tensor("xsc", [128, 8, XCOLS], BF16).ap()
    xtt = [sb("xtt%d" % i, [128, 8, 128], BF16, p1) for i in range(1)]
    xs_pool = Pool([sb("xseg%d" % i, [128, 8, SEGP], BF16, p1) for i in range(2)])
    if stop_after is not None:
        for c_ in range(8):
            S.op("pool", lambda E, c_=c_: E.memset(merged[:, c_, :], 0.0), writes=[(merged, c_)])
    lora0 = sb("lora0", [128, XCOLS], BF16, p1)
    lora1 = sb("lora1", [128, XCOLS], BF16, p1)
    stage = [sb("stage%d" % i, [128, 8, 128], F32, p1) for i in range(3)]
    wcb = sb("wcb", [128, 7, 8, 128], BF16, p1)
    NSCR = 37
    scr = Pool([sb("scr%d" % i, [128, SEGP + 4], F32, p1) for i in range(NSCR)])
    NWK = 27
    wk = Pool([sb("wk%d" % i, [128, QB, 128], BF16, p1) for i in range(NWK)])
    bdn = ("aT", "bT", "kT", "rT", "bgT", "kgT", "vT")
    bd = {}
    for kind in ("p", "s"):
        bd[kind] = []
        for par_ in range(2):
            st_ = {n: sb("bd_%s%d_%s" % (kind, par_, n), [128, QB, 2, 64], BF16, p1) for n in bdn}
            bd[kind].append(st_)
            for n in bdn:
                S.op("pool", lambda E, t=st_[n]: E.memset(t[:], 0.0), writes=[(st_[n], 0), (st_[n], 1)])
    ybd = sb("ybd", [128, QB, 2, 64], F32, p1)
    S.op("pool", lambda E: E.memset(ybd[:], 0.0), writes=[(ybd, 0), (ybd, 1)])
    hbd = sb("hbd", [128, 2, 64], F32, p1)
    S.op("pool", lambda E: E.memset(hbd[:], 0.0), writes=[(hbd, 0), (hbd, 1)])
    zbuf = [sb("zbuf%d" % i, [128, 1 + SEGP], F32, p1) for i in range(3)]
    xbuf = sb("xbuf", [128, SB * (3 + SEGP // 1) if False else max(3 + SEGP, SB * 7)], F32, p1)
    carry3 = sb("carry3", [128, 3], F32, p1)
    hcar = sb("hcar", [128, 1], F32, p1)
    H32 = [sb("H32_%d" % i, [128, QB + 1, 64], F32, p1) for i in range(2)]
    Hbf = [sb("Hbf_%d" % i, [128, QB + 1, 64], BF16, p1) for i in range(2)]
    H32s = [[sb("H32s_%d%d" % (i, j), [128, QB, 64], F32, p1) for j in range(2)] for i in range(2)]
    Hbfs = [sb("Hbfs_%d" % i, [128, QB, 64], BF16, p1) for i in range(2)]
    gamC = sb("gamC", [128, QB], F32, p1)
    stat = sb("stat", [128, 8, QB], F32, p1)
    bst4 = sb("bst4", [128, QB, 6], F32, p1)
    mvq = sb("mvq", [128, QB, 2], F32, p1)

    S.dma(stage[0][0:64, :, :].rearrange("p a b -> p (a b)"), w2d[:, :], writes=[stage[0]])
    S.dma(stage[1][64:128, :, :].rearrange("p a b -> p (a b)"), a2d[:, :], writes=[stage[1]])
    S.op("pool", lambda E: E.tensor_copy(out=w2b[0:64, :], in_=stage[0][0:64, :, :].rearrange("p a b -> p (a b)")),
         reads=[stage[0]], writes=[(w2b, 0)])
    S.op("pool", lambda E: E.tensor_copy(out=w2b[64:128, :], in_=stage[1][64:128, :, :].rearrange("p a b -> p (a b)")),
         reads=[stage[1]], writes=[(w2b, 1)])
    S.dma(stage[2][:, :, :].rearrange("p a b -> p (a b)"), g2d[:, :], writes=[stage[2]])
    S.op("pool", lambda E: E.tensor_copy(out=g2b[:], in_=stage[2][:, :, :].rearrange("p a b -> p (a b)")),
         reads=[stage[2]], writes=[g2b])
    for j in range(2):
        S.dma(stage[j][:], wab[:, j, :, :], writes=[stage[j]])
        S.op("pool", lambda E, j=j: E.tensor_copy(out=wabb[:, j, :, :], in_=stage[j][:]),
             reads=[stage[j]], writes=[(wabb, j)])
    for j in range(2):
        S.dma(stage[j][:], wl[:, j, :, :], writes=[stage[j]])
        S.op("pool", lambda E, j=j: E.tensor_copy(out=wcb[:, j, :, :], in_=stage[j][:]),
             reads=[stage[j]], writes=[(wcb, j)])

    rr = {"ev": 0, "ew": 0}

    def ev_eng():
        return "act"

    def ew_eng():
        rr["ew"] ^= 1
        return "dve" if rr["ew"] else "pool"

    def copy_op(e, out, in_, reads, writes):
        if e == "act":
            S.op("act", lambda E: E.copy(out=out, in_=in_), reads, writes)
        else:
            S.op(e, lambda E: E.tensor_copy(out=out, in_=in_), reads, writes)

    ntile = SEQ // 128
    for t in range(ntile + 2):
        xi = stage[t % 3]
        xiv = xi[:].rearrange("p a b -> p (a b)")
        if t < ntile:
            rows = 128
            S.dma(xiv, xp[t * 128:(t + 1) * 128, :], writes=[xi])
        elif t == ntile:
            rows = NSB * 5
            S.dma(xiv[0:rows, :], xs[:, :], writes=[xi])
        else:
            rows = 64
            S.dma(xiv[0:rows, :], st[:, :], writes=[xi])
        for half in range(2):
            ps = getps()
            for k4 in range(4):
                kc = half * 4 + k4
                S.op("pe", lambda E, ps=ps, k4=k4, kc=kc, xi=xi, rows=rows: E.transpose(
                    out=ps[:, k4 * 128:k4 * 128 + rows], in_=xi[0:rows, kc, :],
                    identity=ident_f[0:rows, 0:rows]), reads=[xi, ident_f], writes=[ps], inc=(k4 == 3))
            src = ps[:].rearrange("p (k t) -> p k t", t=128)[:, :, 0:rows]
            if t <= ntile:
                xo = xtt[0]
                copy_op(ev_eng(), xo[:, half * 4:(half + 1) * 4, 0:rows], src, [ps], [(xo, half)])
                if half == 1:
                    S.dma(xsc[:, :, t * 128:t * 128 + rows], xo[:, :, 0:rows], reads=[(xo, 0), (xo, 1)],
                          writes=[("xsc", t)])
            else:
                dst = stT[:, half * 4:(half + 1) * 4, :]
                copy_op(ev_eng(), dst, src, [ps], [stT])
    xstate = {"tile": {}, "order": [], "pos": 0}

    def _issue_x(seg):
        t_ = xs_pool.get()
        t0_, t1_ = seg.x0 // 128, (seg.x0 + seg.ncol - 1) // 128
        S.dma(t_[:, :, 0:seg.ncol], xsc[:, :, seg.x0:seg.x0 + seg.ncol],
              reads=[("xsc", k) for k in range(t0_, t1_ + 1)], writes=[t_])
        return t_

    def get_x(seg):
        order, pos = xstate["order"], xstate["pos"]
        assert order[pos] is seg
        t_ = xstate["tile"].pop(pos, None)
        if t_ is None:
            t_ = _issue_x(seg)
        if pos + 1 < len(order):
            xstate["tile"][pos + 1] = _issue_x(order[pos + 1])
        xstate["pos"] = pos + 1
        return t_

    def dump(name, ap, reads):
        if name in dbg_out:
            S.dma(dbg_out[name], ap, reads=reads)


    def proj_ps(wsel, seg, xseg):
        ps = getps()
        ncol = seg.ncol
        for kc in range(8):
            lhsT, wres = wsel(kc)
            S.op("pe", lambda E, ps=ps, lhsT=lhsT, kc=kc: E.matmul(
                ps[:, 0:ncol], lhsT=lhsT, rhs=xseg[:, kc, 0:ncol], start=(kc == 0), stop=(kc == 7)),
                reads=[wres, xseg], writes=[ps], inc=(kc == 7))
        return ps

    def shifted(wsel, mu_ap, seg, zb, first, xseg):
        ncol = seg.ncol
        if seg.kind == "p":
            if first:
                S.op("pool", lambda E: E.memset(zb[:, 0:1], 0.0), writes=[zb])
            else:
                S.op("pool", lambda E: E.tensor_copy(out=zb[:, 0:1], in_=zb[:, SEGP:SEGP + 1]), reads=[zb], writes=[zb])
        ps = proj_ps(wsel, seg, xseg)
        S.op("dve", lambda E: E.tensor_copy(out=zb[:, 1:1 + ncol], in_=ps[:, 0:ncol]), reads=[ps], writes=[zb])
        dt_ = scr.get()
        zm = scr.get()
        S.op("pool", lambda E: E.tensor_tensor(out=dt_[:, 0:ncol], in0=zb[:, 0:ncol], in1=zb[:, 1:1 + ncol],
                                               op=ALU.subtract), reads=[zb], writes=[dt_])
        S.op("dve", lambda E: E.scalar_tensor_tensor(out=zm[:, 0:ncol], in0=dt_[:, 0:ncol], scalar=mu_ap,
                                                     in1=zb[:, 1:1 + ncol], op0=ALU.mult, op1=ALU.add),
             reads=[dt_, zb, vecs], writes=[zm])
        scr.put(dt_)
        return zm

    segs = [Seg("p", i) for i in range(NSEGP)] + [Seg("s", i) for i in range(NSEGS)]

    xstate["order"] = segs * 2 + segs * NCH
    for L in range(2):
        for si, seg in enumerate(segs):
            xseg = get_x(seg)
            zm = shifted(lambda kc, L=L: (wcb[:, L, kc, :], (wcb, L)), vcol(V_MUL, L), seg, zbuf[0],
                         first=(seg.kind == "p" and seg.idx == 0), xseg=xseg)
            xs_pool.put(xseg)
            nco = seg.ncol
            if L == 0:
                S.op("act", lambda E: E.activation(out=lora0[0:64, seg.x0:seg.x0 + nco], in_=zm[0:64, 0:nco],
                                                   func=AF.Tanh), reads=[zm], writes=[(lora0, 0)])
                S.op("act", lambda E: E.copy(out=lora0[64:128, seg.x0:seg.x0 + nco], in_=zm[64:128, 0:nco]),
                     reads=[zm], writes=[(lora0, 1)])
            else:
                S.op("act", lambda E: E.activation(out=lora1[:, seg.x0:seg.x0 + nco], in_=zm[:, 0:nco],
                                                   func=AF.Sigmoid), reads=[zm], writes=[lora1])
            scr.put(zm)
    dump("lora0", lora0[:, :], [(lora0, 0), (lora0, 1)])
    dump("lora1", lora1[:, :], [lora1])

    if stop_after == "lora":
        S.finish()
        p1.close()
        es.close()
        return nc

    gam_pool = Pool([sb("gamC%d" % i, [128, SB], F32, p1) for i in range(5)])
    ubf_pool = Pool([sb("ubf%d" % i, [128, SEGP], BF16, p1) for i in range(2)])
    sbd = sb("sbd", [128, 2, 64], F32, p1)
    S.op("pool", lambda E: E.memset(sbd[:], 0.0), writes=[(sbd, 0), (sbd, 1)])

    print("sbuf remaining in phase 1:", nc.sbuf_bytes_remaining)

    def wsel(g):
        return lambda kc: (wcb[:, g, kc, :], (wcb, g))

    def mm(ps_ap, lhsT, rhs, reads, ps, start=True, stop=True, inc=True):
        S.op("pe", lambda E: E.matmul(ps_ap, lhsT=lhsT, rhs=rhs, start=start, stop=stop),
             reads=reads, writes=[ps], inc=inc)

    def tt(e, out, a, b, op, reads, writes):
        S.op(e, lambda E: E.tensor_tensor(out=out, in0=a, in1=b, op=op), reads, writes)

    def act(out, in_, func, reads, writes, bias=None, scale=None):
        kw = {}
        if bias is not None:
            kw["bias"] = bias
        if scale is not None:
            kw["scale"] = scale
        S.op("act", lambda E: E.activation(out=out, in_=in_, func=func, **kw), reads, writes)

    def bd_write(seg, B, K, col0):
        tv, nco = seg.tv, seg.ncol
        for h in range(2):
            P = slice(h * 64, (h + 1) * 64)

            def dst(n):
                return B[n][P, :, h, 0:tv]

            def cv(t_):
                return seg.chv(t_[P, col0:col0 + nco])

            kk, E1, E2, E3, E4, tb, kf, zr, zv = (K[n] for n in ("kk", "E1", "E2", "E3", "E4", "tb", "kf", "zr", "zv"))
            S.op("dve", lambda E: E.scalar_tensor_tensor(out=dst("aT"), in0=cv(kk), scalar=-1.0, in1=cv(E3),
                                                         op0=ALU.mult, op1=ALU.mult), [kk, E3], [(B["aT"], h)])
            yield
            tt(ew_eng(), dst("bT"), cv(tb), cv(E2), ALU.mult, [tb, E2], [(B["bT"], h)])
            yield
            tt(ew_eng(), dst("bgT"), cv(tb), cv(E4), ALU.mult, [tb, E4], [(B["bgT"], h)])
            yield
            tt(ew_eng(), dst("kT"), cv(kf), cv(E2), ALU.mult, [kf, E2], [(B["kT"], h)])
            yield
            tt(ew_eng(), dst("kgT"), cv(kf), cv(E4), ALU.mult, [kf, E4], [(B["kgT"], h)])
            yield
            tt(ew_eng(), dst("rT"), cv(zr), cv(E1), ALU.mult, [zr, E1], [(B["rT"], h)])
            yield
            S.op("pool", lambda E: E.tensor_copy(out=dst("vT"), in_=cv(zv)), [zv], [(B["vT"], h)])
            yield

    def stage1c(c, seg, shared):
        nco, kind, tv, nq, x0 = seg.ncol, seg.kind, seg.tv, seg.nq, seg.x0
        first = (kind == "p" and seg.idx == 0)
        B = bd[kind][seg.idx % 2]
        cc = slice(c * 128, (c + 1) * 128)
        ps = getps()
        mm(ps[:, 0:nco], w2b[0:64, cc], lora0[0:64, x0:x0 + nco], [(w2b, 0), (lora0, 0)], ps)
        yield
        sg = scr.get()
        act(sg[:, 0:nco], ps[:, 0:nco], AF.Tanh, [ps, (der, 4)], [sg], bias=der[:, 4, c:c + 1], scale=0.5)
        S.op("pool", lambda E: E.tensor_scalar(out=sg[:, 0:nco], in0=sg[:, 0:nco], scalar1=0.5, scalar2=0.5,
                                               op0=ALU.mult, op1=ALU.add), [sg], [sg])
        yield
        cs = scr.get()
        rmask = rmask_p if kind == "p" else rmask_s
        S.op("dve", lambda E: E.tensor_tensor_scan(out=cs[:, 0:nco], data0=rmask[:, 0:nco], data1=sg[:, 0:nco],
                                                   initial=0.0, op0=ALU.mult, op1=ALU.add),
             reads=[rmask, sg], writes=[cs])
        yield
        E1 = scr.get(); E2 = scr.get(); E3 = scr.get(); E4 = scr.get(); t0 = scr.get()
        act(E1[:, 0:nco], cs[:, 0:nco], AF.Exp, [cs], [E1], scale=-KAPPA)
        yield
        act(E2[:, 0:nco], cs[:, 0:nco], AF.Exp, [cs], [E2], scale=KAPPA)
        yield
        tt("pool", t0[:, 0:nco], cs[:, 0:nco], sg[:, 0:nco], ALU.subtract, [cs, sg], [t0])
        yield
        act(E3[:, 0:nco], t0[:, 0:nco], AF.Exp, [t0], [E3], scale=-KAPPA)
        yield
        csv = seg.chv(cs[:, 0:nco])
        t0v = seg.chv(t0[:, 0:nco])
        tt("dve", t0v, csv[:, :, tv - 1:tv].to_broadcast([128, nq, tv]), csv, ALU.subtract, [cs], [t0])
        yield
        act(seg.chv(E4[:, 0:nco]), t0v, AF.Exp, [t0], [E4], scale=-KAPPA)
        yield
        gam = gam_pool.get()
        act(gam[:, 0:nq].rearrange("p (q o) -> p q o", o=1), csv[:, :, tv - 1:tv], AF.Exp, [cs], [gam], scale=-KAPPA)
        yield
        scr.put(sg, cs, t0)
        ps = getps()
        mm(ps[:, 0:nco], w2b[64:128, cc], lora0[64:128, x0:x0 + nco], [(w2b, 1), (lora0, 1)], ps)
        yield
        a_ = scr.get()
        act(a_[:, 0:nco], ps[:, 0:nco], AF.Tanh, [ps, (der, 5)], [a_], bias=der[:, 5, c:c + 1], scale=0.5)
        S.op("pool", lambda E: E.tensor_scalar(out=a_[:, 0:nco], in0=a_[:, 0:nco], scalar1=0.5, scalar2=0.5,
                                               op0=ALU.mult, op1=ALU.add), [a_], [a_])
        yield
        ps = getps()
        mm(ps[:, 0:nco], g2b[:, cc], lora1[:, x0:x0 + nco], [g2b, lora1], ps)
        yield
        g_ = scr.get()
        S.op("act", lambda E: E.mul(out=g_[:, 0:nco], in_=ps[:, 0:nco], mul=0.5), [ps], [g_])
        yield
        shared.update(E1=E1, E2=E2, E3=E3, E4=E4, a_=a_, g_=g_, gam=gam)

    def stage1a(c, seg, xseg, shared):
        nco, kind, tv, nq, x0 = seg.ncol, seg.kind, seg.tv, seg.nq, seg.x0
        first = (kind == "p" and seg.idx == 0)
        B = bd[kind][seg.idx % 2]
        cc = slice(c * 128, (c + 1) * 128)
        zr = shifted(wsel(0), vcol(V_MUR, c), seg, zbuf[0], first, xseg)
        yield
        zk = shifted(wsel(1), vcol(V_MUK, c), seg, zbuf[1], first, xseg)
        yield
        zv = shifted(wsel(2), vcol(V_MUV, c), seg, zbuf[2], first, xseg)
        yield
        kkr = scr.get(); sq = scr.get(); kk = scr.get()
        S.op("dve", lambda E: E.tensor_scalar(out=kkr[:, 0:nco], in0=zk[:, 0:nco], scalar1=vcol(V_KK, c), scalar2=None,
                                              op0=ALU.mult), [zk, vecs], [kkr])
        yield
        tt("pool", sq[:, 0:nco], kkr[:, 0:nco], kkr[:, 0:nco], ALU.mult, [kkr], [sq])
        yield
        ps = getps()
        mm(ps[:, 0:nco], ones_bd[:], sq[:, 0:nco], [ones_bd, sq], ps)
        yield
        act(sq[:, 0:nco], ps[:, 0:nco], AF.Sqrt, [ps], [sq])
        yield
        S.op("dve", lambda E: E.tensor_scalar(out=sq[:, 0:nco], in0=sq[:, 0:nco], scalar1=1e-12, scalar2=None,
                                              op0=ALU.max), [sq], [sq])
        yield
        S.op("dve", lambda E: E.reciprocal(out=sq[:, 0:nco], in_=sq[:, 0:nco]), [sq], [sq])
        yield
        tt("pool", kk[:, 0:nco], kkr[:, 0:nco], sq[:, 0:nco], ALU.mult, [kkr, sq], [kk])
        yield
        yield "NEEDC"
        E1, E2, E3, E4, a_, g_, gam = (shared[k_] for k_ in ("E1", "E2", "E3", "E4", "a_", "g_", "gam"))
        t1 = scr.get(); kf = scr.get(); bonus = scr.get()
        S.op("dve", lambda E: E.tensor_scalar(out=t1[:, 0:nco], in0=a_[:, 0:nco], scalar1=vcol(V_KA, c),
                                              scalar2=der[:, 0, c:c + 1], op0=ALU.mult, op1=ALU.add),
             [a_, vecs, (der, 0)], [t1])
        yield
        tt("pool", kf[:, 0:nco], zk[:, 0:nco], t1[:, 0:nco], ALU.mult, [zk, t1], [kf])
        yield
        S.op("dve", lambda E: E.scalar_tensor_tensor(out=t1[:, 0:nco], in0=zr[:, 0:nco], scalar=vcol(V_RK, c),
                                                     in1=kf[:, 0:nco], op0=ALU.mult, op1=ALU.mult),
             [zr, kf, vecs, t1], [t1])
        yield
        ps = getps()
        mm(ps[:, 0:nco], ones_bd[:], t1[:, 0:nco], [ones_bd, t1], ps)
        yield
        tt("dve", bonus[:, 0:nco], ps[:, 0:nco], zv[:, 0:nco], ALU.mult, [ps, zv], [bonus])
        yield
        tb = kkr
        tt("pool", tb[:, 0:nco], kk[:, 0:nco], a_[:, 0:nco], ALU.mult, [kk, a_, kkr], [tb])
        yield
        keep = dict(kk=kk, E3=E3, tb=tb, E2=E2, E4=E4, kf=kf, zr=zr, E1=E1, zv=zv)
        if kind == "p":
            yield from bd_write(seg, B, keep, 0)
            scr.put(zr, zk, zv, E1, E2, E3, E4, a_, kkr, sq, kk, t1, kf)
        else:
            scr.put(zk, a_, sq, t1)
        return dict(bonus=bonus, g=g_, gam=gam, keep=keep)

    def stage1b(c, seg, xseg):
        nco, kind, tv, nq, x0 = seg.ncol, seg.kind, seg.tv, seg.nq, seg.x0
        first = (kind == "p" and seg.idx == 0)
        B = bd[kind][seg.idx % 2]
        cc = slice(c * 128, (c + 1) * 128)
        T_, nb = seg.T, seg.nb
        nt = nb * T_
        xbv = xbuf[:, 0:nb * (3 + T_)].rearrange("p (b t) -> p b t", t=3 + T_)

        def dv(t_):
            return t_[:, 0:nt].rearrange("p (b t) -> p b t", t=T_)

        ps = proj_ps(wsel(3), seg, xseg)
        yield
        if kind == "p":
            if seg.idx == 0:
                S.op("pool", lambda E: E.memset(xbv[:, :, 0:3], 0.0), writes=[xbuf])
                yield
            else:
                S.op("pool", lambda E: E.tensor_copy(out=xbv[:, 0, 0:3], in_=carry3[:, :]), [carry3], [xbuf])
                yield
        else:
            b0 = seg.idx * SB
            S.op("pool", lambda E: E.tensor_copy(
                out=xbv[:, :, 0:3], in_=stT[:, c, 0:48].rearrange("p (b j) -> p b j", j=3)[:, b0:b0 + SB, :]),
                [stT], [xbuf])
            yield
        S.op("act", lambda E: E.copy(out=xbv[:, :, 3:3 + T_], in_=seg.tokv(ps[:, 0:nco])), [ps, xbuf], [xbuf])
        yield
        if kind == "p":
            S.op("pool", lambda E: E.tensor_copy(out=carry3[:, :], in_=xbv[:, 0, T_:T_ + 3]), [xbuf], [carry3])
            yield
            if seg.idx == NSEGP - 1:
                S.op("pool", lambda E: E.tensor_copy(out=fin[:, c, 0:3], in_=xbv[:, 0, T_:T_ + 3]), [xbuf], [(fin, c)])
                yield
        else:
            S.op("pool", lambda E: E.tensor_copy(
                out=fin[:, c, 4 + 3 * b0:4 + 3 * (b0 + SB)].rearrange("p (b j) -> p b j", j=3),
                in_=xbv[:, :, T_:T_ + 3]), [xbuf], [(fin, c)])
            yield
        u = scr.get()
        uv = dv(u)
        S.op("dve", lambda E: E.tensor_scalar(out=uv, in0=xbv[:, :, 0:T_], scalar1=vcol(V_CW0, c),
                                              scalar2=vcol(V_CB, c), op0=ALU.mult, op1=ALU.add),
             [xbuf, vecs], [u])
        yield
        for j in range(1, 4):
            S.op("dve", lambda E, j=j: E.scalar_tensor_tensor(out=uv, in0=xbv[:, :, j:j + T_],
                                                              scalar=vcol(V_CW0 + j, c), in1=uv,
                                                              op0=ALU.mult, op1=ALU.add), [xbuf, vecs, u], [u])
            yield
        ubf = ubf_pool.get()
        S.op("pool", lambda E: E.tensor_copy(out=ubf[:, 0:nt], in_=u[:, 0:nt]), [u], [ubf])
        yield
        rg = scr.get(); ig = scr.get(); al = scr.get(); e2 = scr.get(); hh = scr.get()
        ps = getps()
        mm(ps[:, 0:nt], wabb[:, 0, c, :], ubf[:, 0:nt], [(wabb, 0), ubf], ps)
        yield
        act(rg[:, 0:nt], ps[:, 0:nt], AF.Tanh, [ps, (der, 6)], [rg], bias=der[:, 6, c:c + 1], scale=0.5)
        yield
        ps = getps()
        mm(ps[:, 0:nt], wabb[:, 1, c, :], ubf[:, 0:nt], [(wabb, 1), ubf], ps)
        yield
        act(ig[:, 0:nt], ps[:, 0:nt], AF.Tanh, [ps, (der, 7)], [ig], bias=der[:, 7, c:c + 1], scale=0.5)
        yield
        ubf_pool.put(ubf)
        act(al[:, 0:nt], rg[:, 0:nt], AF.Exp, [rg, (der, 10)], [al], scale=der[:, 10, c:c + 1], bias=der[:, 10, c:c + 1])
        yield
        act(e2[:, 0:nt], rg[:, 0:nt], AF.Exp, [rg, (der, 11)], [e2], scale=der[:, 11, c:c + 1], bias=der[:, 11, c:c + 1])
        yield
        S.op("pool", lambda E: E.tensor_scalar(out=e2[:, 0:nt], in0=e2[:, 0:nt], scalar1=-0.25, scalar2=0.25,
                                               op0=ALU.mult, op1=ALU.add), [e2], [e2])
        yield
        act(e2[:, 0:nt], e2[:, 0:nt], AF.Sqrt, [e2], [e2])
        yield
        if first:
            S.op("pool", lambda E: E.memset(e2[:, 0:1], 0.5), [e2], [e2])
            yield
        S.op("dve", lambda E: E.scalar_tensor_tensor(out=ig[:, 0:nt], in0=ig[:, 0:nt], scalar=1.0, in1=e2[:, 0:nt],
                                                     op0=ALU.add, op1=ALU.mult), [ig, e2], [ig])
        yield
        tt("dve", ig[:, 0:nt], ig[:, 0:nt], u[:, 0:nt], ALU.mult, [ig, u], [ig])
        yield
        if kind == "p":
            init = 0.0 if seg.idx == 0 else hcar[:, 0:1]
            S.op("dve", lambda E: E.tensor_tensor_scan(out=hh[:, 0:nt], data0=al[:, 0:nt], data1=ig[:, 0:nt],
                                                       initial=init, op0=ALU.mult, op1=ALU.add),
                 [al, ig, hcar], [hh])
            yield
            S.op("pool", lambda E: E.tensor_copy(out=hcar[:, 0:1], in_=hh[:, nt - 1:nt]), [hh], [hcar])
            yield
            if seg.idx == NSEGP - 1:
                S.op("pool", lambda E: E.tensor_copy(out=fin[:, c, 3:4], in_=hh[:, nt - 1:nt]), [hh], [(fin, c)])
                yield
        else:
            for b in range(nb):
                S.op("dve", lambda E, b=b: E.tensor_tensor_scan(
                    out=hh[:, b * T_:(b + 1) * T_], data0=al[:, b * T_:(b + 1) * T_], data1=ig[:, b * T_:(b + 1) * T_],
                    initial=stT[:, c, 48 + b0 + b:48 + b0 + b + 1], op0=ALU.mult, op1=ALU.add),
                    [al, ig, stT, hh], [hh])
                yield
            S.op("pool", lambda E: E.tensor_copy(out=fin[:, c, 52 + b0:52 + b0 + SB].rearrange("p (b o) -> p b o", o=1),
                                                 in_=dv(hh)[:, :, T_ - 1:T_]), [hh], [(fin, c)])
            yield
        scr.put(rg, al, e2, u, ig)
        ps = proj_ps(wsel(4), seg, xseg)
        yield
        gbs = scr.get(); p_ = scr.get()
        S.op("act", lambda E: E.copy(out=dv(gbs), in_=seg.tokv(ps[:, 0:nco])), [ps], [gbs])
        yield
        tt("pool", p_[:, 0:nt], gbs[:, 0:nt], gbs[:, 0:nt], ALU.mult, [gbs], [p_])
        yield
        S.op("pool", lambda E: E.tensor_scalar(out=p_[:, 0:nt], in0=p_[:, 0:nt], scalar1=0.044715, scalar2=1.0,
                                               op0=ALU.mult, op1=ALU.add), [p_], [p_])
        yield
        tt("pool", p_[:, 0:nt], p_[:, 0:nt], gbs[:, 0:nt], ALU.mult, [p_, gbs], [p_])
        yield
        act(p_[:, 0:nt], p_[:, 0:nt], AF.Tanh, [p_], [p_], scale=0.7978845608028654)
        yield
        S.op("dve", lambda E: E.scalar_tensor_tensor(out=gbs[:, 0:nt], in0=p_[:, 0:nt], scalar=1.0, in1=gbs[:, 0:nt],
                                                     op0=ALU.add, op1=ALU.mult), [gbs, p_], [gbs])
        yield
        tt("dve", hh[:, 0:nt], hh[:, 0:nt], gbs[:, 0:nt], ALU.mult, [hh, gbs], [hh])
        yield
        scr.put(gbs)
        ps = proj_ps(wsel(6), seg, xseg)
        yield
        act(dv(p_), seg.tokv(ps[:, 0:nco]), AF.Tanh, [ps], [p_], scale=0.5)
        yield
        S.op("dve", lambda E: E.scalar_tensor_tensor(out=hh[:, 0:nt], in0=p_[:, 0:nt], scalar=1.0, in1=hh[:, 0:nt],
                                                     op0=ALU.add, op1=ALU.mult), [hh, p_], [hh])
        yield
        scr.put(p_)
        ps = proj_ps(wsel(5), seg, xseg)
        yield
        gA = scr.get()
        act(dv(gA), seg.tokv(ps[:, 0:nco]), AF.Tanh, [ps], [gA], scale=0.5)
        yield
        return dict(gA=gA, m2=hh)


    def stage1(c, seg):
        xseg = get_x(seg)
        shared = {}
        gens = [stage1a(c, seg, xseg, shared), stage1b(c, seg, xseg), stage1c(c, seg, shared)]
        who = ["s1a", "s1b", "s1c"]
        res = [None, None, None]
        done = [False, False, False]
        hold_a = False
        while not all(done):
            for i_ in range(3):
                if done[i_]:
                    continue
                if i_ == 0 and hold_a:
                    if not done[2]:
                        continue
                    hold_a = False
                cur_alloc["who"] = who[i_]
                try:
                    v_ = next(gens[i_])
                    if v_ == "NEEDC":
                        hold_a = True
                except StopIteration as e_:
                    res[i_] = e_.value
                    done[i_] = True
                yield
        xs_pool.put(xseg)
        res[0].update(res[1])
        return res[0]

    def stage2(c, seg, s1, chain):
        kind, tv, nq, nlev, nco = seg.kind, seg.tv, seg.nq, seg.nlev, seg.ncol
        B = bd[kind][seg.idx % 2]
        gam = s1["gam"]

        def bdv(n):
            return B[n][:].rearrange("p q h t -> p q (h t)")

        def br(n):
            return [(B[n], 0), (B[n], 1)]

        def q128(ps):
            return ps[:].rearrange("p (q t) -> p q t", t=128)

        def q64(ps):
            return ps[:, 0:QB * 64].rearrange("p (q t) -> p q t", t=64)

        def prod(l, r, mask):
            ps = getps()
            for q in range(QB):
                mm(ps[:, q * 128:(q + 1) * 128], bdv(l)[:, q, :], bdv(r)[:, q, :], br(l) + br(r), ps, inc=(q == QB - 1))
            o = wk.get()
            tt("dve", o[:], q128(ps), mask[:], ALU.mult, [ps, mask], [o])
            return o

        N_ = prod("bT", "aT", mUs)
        yield
        L_ = prod("aT", "bT", mLs)
        yield
        Mak = prod("kT", "aT", mUs)
        yield
        Mrb = prod("bT", "rT", mUi)
        yield
        Mrk = prod("kT", "rT", mUi)
        yield
        rTc = wk.get()
        S.op("pool", lambda E: E.tensor_copy(out=rTc[:], in_=bdv("rT")), br("rT"), [rTc])
        yield
        ck("k1")

        def tr(n):
            ps = getps()
            psb = ps[:].bitcast(BF16)
            for q in range(QB):
                S.op("pe", lambda E, q=q: E.transpose(out=psb[:, q * 128:(q + 1) * 128], in_=bdv(n)[:, q, :],
                                                      identity=ident_b[:]), br(n) + [ident_b], [ps], inc=(q == QB - 1))
            o = wk.get()
            copy_op(ev_eng(), o[:], psb[:, 0:QB * 128].rearrange("p (q t) -> p q t", t=128), [ps], [o])
            return o

        XA = tr("aT")
        yield
        Bg = tr("bgT")
        yield
        Kg = tr("kgT")
        yield
        ck("k2")
        ps = getps()
        for q in range(QB):
            mm(ps[:, q * 64:(q + 1) * 64], bdv("vT")[:, q, :], II_b[:], br("vT") + [II_b], ps, inc=(q == QB - 1))
            yield
        V_ = wk.get()
        copy_op(ev_eng(), V_[:, :, 0:64], q64(ps), [ps], [V_])
        yield ("BDFREE", (c, kind, seg.idx))
        yield
        ps = getps()
        for q in range(QB):
            mm(ps[:, q * 64:(q + 1) * 64], Mak[:, q, :], V_[:, q, 0:64], [Mak, V_], ps, inc=(q == QB - 1))
            yield
        XU = wk.get()
        copy_op(ev_eng(), XU[:, :, 0:64], q64(ps), [ps], [XU])
        yield
        wk.put(Mak)
        ck("k3")
        Nc, Lc = N_, L_
        for lev in range(nlev):
            psA = getps()
            for q in range(QB):
                mm(psA[:, q * 128:(q + 1) * 128], Nc[:, q, :], XA[:, q, :], [Nc, XA], psA, start=True, stop=False, inc=False)
                mm(psA[:, q * 128:(q + 1) * 128], ident_b[:], XA[:, q, :], [ident_b, XA], psA, start=False, stop=True,
                   inc=(q == QB - 1))
            yield
            psU = getps()
            for q in range(QB):
                mm(psU[:, q * 64:(q + 1) * 64], Nc[:, q, :], XU[:, q, 0:64], [Nc, XU], psU, start=True, stop=False, inc=False)
                mm(psU[:, q * 64:(q + 1) * 64], ident_b[:], XU[:, q, 0:64], [ident_b, XU], psU, start=False, stop=True,
                   inc=(q == QB - 1))
            yield
            XA2 = wk.get(); XU2 = wk.get()
            copy_op("act", XA2[:], q128(psA), [psA], [XA2])
            yield
            copy_op("dve", XU2[:, :, 0:64], q64(psU), [psU], [XU2])
            yield
            N2 = L2 = None
            if lev < nlev - 1:
                psN = getps()
                for q in range(QB):
                    mm(psN[:, q * 128:(q + 1) * 128], Lc[:, q, :], Nc[:, q, :], [Lc, Nc], psN, inc=(q == QB - 1))
                    yield
                N2 = wk.get()
                copy_op("act", N2[:], q128(psN), [psN], [N2])
                yield
                if lev < nlev - 2:
                    psL = getps()
                    for q in range(QB):
                        mm(psL[:, q * 128:(q + 1) * 128], Nc[:, q, :], Lc[:, q, :], [Lc, Nc], psL, inc=(q == QB - 1))
                        yield
                    L2 = wk.get()
                    copy_op("act", L2[:], q128(psL), [psL], [L2])
                    yield
            wk.put(XA, XU, Nc, Lc)
            XA, XU, Nc, Lc = XA2, XU2, N2, L2
            if Nc is None:
                Nc = wk.get()
            if Lc is None:
                Lc = wk.get()
        wk.put(Nc, Lc)
        ck("k4")
        psR = getps()
        for q in range(QB):
            mm(psR[:, q * 128:(q + 1) * 128], XA[:, q, :], Mrb[:, q, :], [XA, Mrb], psR, start=True, stop=False, inc=False)
            yield
            mm(psR[:, q * 128:(q + 1) * 128], ident_b[:], rTc[:, q, :], [ident_b, rTc], psR, start=False,
               stop=True, inc=(q == QB - 1))
            yield
        Rh = wk.get()
        copy_op(ev_eng(), Rh[:], q128(psR), [psR], [Rh])
        yield
        psG = getps()
        for q in range(QB):
            mm(psG[:, q * 128:(q + 1) * 128], XA[:, q, :], Bg[:, q, :], [XA, Bg], psG, inc=(q == QB - 1))
            yield
        GT = wk.get()
        copy_op(ev_eng(), GT[:], q128(psG), [psG], [GT])
        yield
        ck("k5")
        if kind == "p":
            yield ("CHAIN", (c, seg.idx - 1))
            chain["init"]()
        psH = getps()
        if kind == "p":
            hA, hB = chain["hA"], chain["hB"]
            for q in range(QB):
                sl = slice(q * 64, (q + 1) * 64)
                mm(psH[:, sl], Bg[:, q, :], XU[:, q, 0:64], [Bg, XU], psH, start=True, stop=False, inc=False)
                yield
                mm(psH[:, sl], Kg[:, q, :], V_[:, q, 0:64], [Kg, V_], psH, start=False, stop=False, inc=False)
                yield
                mm(psH[:, sl], GT[:, q, :], hB[:, q, :], [GT, (hB, q)], psH, start=False, stop=True)
                yield
                S.op("dve", lambda E, q=q, sl=sl: E.scalar_tensor_tensor(
                    out=hA[:, q + 1, :], in0=hA[:, q, :], scalar=gam[:, q:q + 1], in1=psH[:, sl],
                    op0=ALU.mult, op1=ALU.add), [(hA, q), gam, psH], [(hA, q + 1)])
                yield
                S.op("pool", lambda E, q=q: E.tensor_copy(out=hB[:, q + 1, :], in_=hA[:, q + 1, :]), [(hA, q + 1)], [(hB, q + 1)])
                yield
            hBr = [(hB, q) for q in range(QB)]
        else:
            hA, hB, hO = chain["hA"], chain["hB"], chain["hO"]
            for q in range(QB):
                sl = slice(q * 64, (q + 1) * 64)
                mm(psH[:, sl], Bg[:, q, :], XU[:, q, 0:64], [Bg, XU], psH, start=True, stop=False, inc=False)
                yield
                mm(psH[:, sl], Kg[:, q, :], V_[:, q, 0:64], [Kg, V_], psH, start=False, stop=False, inc=False)
                yield
                mm(psH[:, sl], GT[:, q, :], hB[:, q, :], [GT, (hB, q)], psH, start=False, stop=True,
                   inc=(q == QB - 1))
                yield
            tmp = scr.get()
            tmpv = tmp[:, 0:QB * 64].rearrange("p (q t) -> p q t", t=64)
            tt("dve", tmpv, hA[:, 0:QB, :], gam[:].rearrange("p (q o) -> p q o", o=1).to_broadcast([128, QB, 64]),
               ALU.mult, [(hA, q) for q in range(QB)] + [gam], [tmp])
            yield
            tt("dve", hO[:, 0:QB, :], tmpv, q64(psH), ALU.add, [tmp, psH], [hO])
            yield
            scr.put(tmp)
            hBr = [(hB, q) for q in range(QB)]
        ck("k6")
        psY = getps()
        for q in range(QB):
            sl = slice(q * 64, (q + 1) * 64)
            mm(psY[:, sl], Mrb[:, q, :], XU[:, q, 0:64], [Mrb, XU], psY, start=True, stop=False, inc=False)
            yield
            mm(psY[:, sl], Mrk[:, q, :], V_[:, q, 0:64], [Mrk, V_], psY, start=False, stop=False, inc=False)
            yield
            mm(psY[:, sl], Rh[:, q, :], hB[:, q, :], [Rh, (hB, q)], psY, start=False, stop=True, inc=(q == QB - 1))
            yield
        wk.put(Mrb, Mrk, XA, XU, Bg, Kg, V_, Rh, GT, rTc)
        if not isinstance(gam, Off):
            gam_pool.put(gam)
        ck("k7")
        ysb = scr.get()
        ysbv = ysb[:, 0:QB * 64].rearrange("p (q t) -> p q t", t=64)
        S.op("act", lambda E: E.copy(out=ysbv, in_=q64(psY)), [psY], [ysb])
        yield ("TAIL", (c, seg.idx) if kind == "p" else None)
        stq = lambda i: stat[:, i, :]
        ysq = scr.get()
        ysqv = ysq[:, 0:QB * 64].rearrange("p (q t) -> p q t", t=64)
        for q in range(QB):
            S.op("dve", lambda E, q=q: E.bn_stats(out=bst4[:, q, :], in_=ysb[:, q * 64:(q + 1) * 64]), [ysb], [(bst4, q)])
        for q in range(QB):
            S.op("dve", lambda E, q=q: E.bn_aggr(out=mvq[:, q, :], in_=bst4[:, q, :]), [(bst4, q)], [(mvq, q)])
        yield
        mvr = [(mvq, q) for q in range(QB)]
        S.op("pool", lambda E: E.tensor_scalar(out=stq(4), in0=mvq[:, :, 1], scalar1=1.0, scalar2=GN_EPS,
                                               op0=ALU.mult, op1=ALU.add), mvr, [(stat, 4)])
        S.op("pool", lambda E: E.tensor_tensor(out=stq(5), in0=stq(4), in1=negh[:], op=ALU.pow),
             [(stat, 4), negh], [(stat, 5)])
        yield
        tt("dve", ysqv, ysbv, mvq[:, :, 0:1].to_broadcast([128, QB, 64]), ALU.subtract, [ysb] + mvr, [ysq])
        scr.put(ysb)
        yield
        for h in range(2):
            P = slice(h * 64, (h + 1) * 64)
            tt(ew_eng(), ybd[P, :, h, :], ysqv[P], stat[P, 5, :].rearrange("p (q o) -> p q o", o=1).to_broadcast([64, QB, 64]),
               ALU.mult, [ysq, (stat, 5)], [(ybd, h)])
            yield
        scr.put(ysq)
        ck("k8")
        psT = getps()
        ybv = ybd[:].rearrange("p q h t -> p q (h t)")
        for q in range(QB):
            mm(psT[:, q * 64:(q + 1) * 64], ybv[:, q, :], II_f[:], [(ybd, 0), (ybd, 1), II_f], psT, inc=(q == QB - 1))
            yield
        ck("k9")
        T_, nb = seg.T, seg.nb
        nt = nb * T_

        def dv(t_):
            return t_[:, 0:nt].rearrange("p (b t) -> p b t", t=T_)

        if kind == "p":
            ysrc = psT[:, 0:QB * 64].rearrange("p (b t) -> p b t", b=1)
        else:
            ysrc = q64(psT)[:, :, 0:DEC]
        yo = scr.get()
        act(dv(yo), ysrc, AF.Identity, [psT, vecs], [yo], bias=vcol(V_LNXB, c), scale=vcol(V_LNXG, c))
        yield
        tt("dve", dv(yo), dv(yo), seg.tokv(s1["bonus"][:, 0:nco]), ALU.add, [yo, s1["bonus"]], [yo])
        yield
        tt("pool", dv(yo), dv(yo), seg.tokv(s1["g"][:, 0:nco]), ALU.mult, [yo, s1["g"]], [yo])
        yield
        S.op("dve", lambda E: E.scalar_tensor_tensor(out=dv(yo), in0=dv(s1["gA"]), scalar=1.0, in1=dv(yo),
                                                     op0=ALU.add, op1=ALU.mult), [yo, s1["gA"]], [yo])
        yield
        mv = merged[:, c, seg.m0:seg.m0 + seg.mcols].rearrange("p (b t) -> p b t", t=T_)
        S.op("dve", lambda E: E.scalar_tensor_tensor(out=mv, in0=dv(s1["m2"]), scalar=0.25, in1=dv(yo),
                                                     op0=ALU.mult, op1=ALU.add), [yo, s1["m2"]], [(merged, c)])
        yield
        scr.put(yo)
        if not isinstance(s1["bonus"], Off):
            scr.put(s1["bonus"], s1["g"], s1["gA"], s1["m2"])

    def state_out(c, hsrc_ap, hres, dst_ap):
        for h in range(2):
            P = slice(h * 64, (h + 1) * 64)
            S.op(ew_eng(), lambda E: E.tensor_copy(out=hbd[P, h, :], in_=hsrc_ap[P, :]), [hres], [(hbd, h)])
        ps = getps()
        mm(ps[:, 0:64], hbd[:].rearrange("p h t -> p (h t)"), II_f[:], [(hbd, 0), (hbd, 1), II_f], ps)
        so = scr.get()
        copy_op(ev_eng(), so[:, 0:64], ps[:, 0:64], [ps], [so])
        S.dma(dst_ap, so[:, 0:64], reads=[so])
        scr.put(so)

    for g in range(3):
        S.op("pool", lambda E, g=g: E.memset(zbuf[g][:], 0.0), writes=[zbuf[g]])

    def s2full(c, seg, s1, par):
        if seg.kind == "p":
            hA, hB = H32[par], Hbf[par]

            def chain_init():
                if seg.idx == 0:
                    S.op("pool", lambda E: E.memset(hA[:, 0, :], 0.0), writes=[(hA, 0)])
                    S.op("pool", lambda E: E.memset(hB[:, 0, :], 0.0), writes=[(hB, 0)])
                else:
                    pA, pB = H32[1 - par], Hbf[1 - par]
                    S.op("pool", lambda E: E.tensor_copy(out=hA[:, 0, :], in_=pA[:, QB, :]), [(pA, QB)], [(hA, 0)])
                    S.op("pool", lambda E: E.tensor_copy(out=hB[:, 0, :], in_=pB[:, QB, :]), [(pB, QB)], [(hB, 0)])

            yield from stage2(c, seg, s1, dict(hA=hA, hB=hB, init=chain_init))
            if seg.idx == NSEGP - 1:
                state_out(c, hA[:, QB, :], (hA, QB),
                          o_wkv_p[2 * c:2 * c + 2, :, :].rearrange("h i j -> (h i) j"))
                yield
        else:
            sp_ = seg.idx % 2
            hA, hB, hO = H32s[sp_][0], Hbfs[sp_], H32s[sp_][1]
            b0 = seg.idx * QB
            for q in range(QB):
                sin = scr.get()
                S.dma(sin[:, 0:64], s0[b0 + q, 2 * c:2 * c + 2, :, :].rearrange("h i j -> (h i) j"), writes=[sin])
                for h in range(2):
                    P = slice(h * 64, (h + 1) * 64)
                    S.op(ew_eng(), lambda E: E.tensor_copy(out=sbd[P, h, :], in_=sin[P, 0:64]), [sin], [(sbd, h)])
                ps = getps()
                mm(ps[:, 0:64], sbd[:].rearrange("p h t -> p (h t)"), II_f[:], [(sbd, 0), (sbd, 1), II_f], ps)
                S.op("act", lambda E: E.copy(out=hA[:, q, :], in_=ps[:, 0:64]), [ps], [(hA, q)])
                S.op("dve", lambda E: E.tensor_copy(out=hB[:, q, :], in_=ps[:, 0:64]), [ps], [(hB, q)])
                scr.put(sin)
                yield
            yield from stage2(c, seg, s1, dict(hA=hA, hB=hB, hO=hO))
            for q in range(QB):
                state_out(c, hO[:, q, :], hO,
                          o_wkv_s[b0 + q, 2 * c:2 * c + 2, :, :].rearrange("h i j -> (h i) j"))
                yield

    NSTREAM = 2
    mains = []
    tails = []
    SID = ("A", "B")

    chain_done = set()
    bd_free = set()

    def step_main(m):
        if m[2] == "TAILWAIT":
            if tails:
                return True
            mains.remove(m)
            tails.append(m)
            return False
        if m[2] is not None:
            if m[2][1] >= 0 and m[2] not in chain_done:
                return True
            m[2] = None
        cur_alloc["who"] = "s2" + SID[m[1]]
        try:
            v = next(m[0])
        except StopIteration:
            mains.remove(m)
            return False
        if isinstance(v, tuple) and v[0] == "BDFREE":
            bd_free.add(v[1])
            return True
        if isinstance(v, tuple) and v[0] == "CHAIN":
            m[2] = v[1]
            return True
        if isinstance(v, tuple) and v[0] == "TAIL":
            if v[1] is not None:
                chain_done.add(v[1])
            if tails:
                m[2] = "TAILWAIT"
                return True
            mains.remove(m)
            tails.append(m)
            return False
        return True

    def step_tails():
        for m in list(tails):
            cur_alloc["who"] = "tail" + SID[m[1]]
            try:
                next(m[0])
            except StopIteration:
                tails.remove(m)

    def step_bg():
        for m in list(mains):
            step_main(m)
        step_tails()

    def run_stage1(g1):
        acc = 0.0
        while True:
            step_bg()
            acc += RSCALE if (mains or tails) else 8.0
            while acc >= 1.0:
                acc -= 1.0
                cur_alloc["who"] = "s1"
                try:
                    next(g1)
                except StopIteration as e_:
                    cur_alloc["who"] = None
                    return e_.value

    def start_main(gen):
        while len(mains) >= NSTREAM:
            step_bg()
        used = {m[1] for m in mains} | {m[1] for m in tails}
        while len(used) >= 2:
            step_bg()
            used = {m[1] for m in mains} | {m[1] for m in tails}
        sid = 0 if 0 not in used else 1
        mains.append([gen, sid, None])

    def drain_all():
        while mains or tails:
            step_bg()
        cur_alloc["who"] = None

    def load_weights(c):
        for g in range(7):
            stg = stage[g % 3]
            S.dma(stg[:], w1[c, :, g, :, :], writes=[stg])
            S.op("pool", lambda E, g=g, stg=stg: E.tensor_copy(out=wcb[:, g, :, :], in_=stg[:]), [stg], [(wcb, g)])

    def finalize_after(gen, fn):
        yield from gen
        fn()

    try:
        load_weights(0)
        for c in range(NCH):
            par = 0
            for segi, seg in enumerate(segs):
                if stop_after == "seg%d" % segi:
                    raise StopBuild()
                cur["segi"] = segi
                s1 = run_stage1(stage1(c, seg))
                if seg.kind == "p":
                    start_main(s2full(c, seg, s1, par))
                    par = 1 - par
                else:
                    if c + 1 < NCH and stop_after != "c0":
                        load_weights(c + 1)
                    nbatch = NSB // QB
                    for b in range(nbatch):
                        sub = SubSeg(b)
                        while b >= 2 and (c, "s", b - 2) not in bd_free:
                            step_bg()
                        cur_alloc["who"] = "s1"
                        for _ in bd_write(sub, bd["s"][b % 2], s1["keep"], b * QB * 5):
                            pass
                        s1b = dict(bonus=Off(s1["bonus"], b * QB * 5, QB * 5), g=Off(s1["g"], b * QB * 5, QB * 5),
                                   gA=Off(s1["gA"], b * QB * DEC, QB * DEC), m2=Off(s1["m2"], b * QB * DEC, QB * DEC),
                                   gam=Off(s1["gam"], b * QB, QB))
                        gen = s2full(c, sub, s1b, 0)
                        if b == nbatch - 1:
                            K_ = s1["keep"]
                            scr.put(K_["kk"], K_["E3"], K_["tb"], K_["E2"], K_["E4"], K_["kf"], K_["zr"], K_["E1"], K_["zv"])

                            def fin_(s1=s1):
                                scr.put(s1["bonus"], s1["g"], s1["gA"], s1["m2"])
                                gam_pool.put(s1["gam"])

                            gen = finalize_after(gen, fin_)
                        start_main(gen)
            if stop_after == "c0":
                break
        drain_all()
    except StopBuild:
        pass
    dump("merged", merged[:, 0, :], [(merged, 0)])
    dump("merged7", merged[:, 7, :], [(merged, 7)])
    dump("fin", fin[:, 0, :], [(fin, 0)])

    print("nops", S.nops, "cnt", {k: v for k, v in S.cnt.items() if not k.startswith("d")})
    if stop_after is not None:
        S.finish()
        p1.close()
        es.close()
        return nc
    psF = [getps(), getps()]
    for c in range(NCH):
        hf, k4 = divmod(c, 4)
        S.op("pe", lambda E: E.transpose(out=psF[hf][0:NFIN, k4 * 128:(k4 + 1) * 128], in_=fin[:, c, :],
                                         identity=ident_f[:]), [(fin, c), ident_f], [psF[hf]])
    S.barrier()
    p1.close()
    p2 = ExitStack()
    lnvs = sb("lnvs", [128, 2, D], F32, p2)
    h1T = sb("h1T", [128, 8, MCOLS], BF16, p2)
    stg2 = [sb("stg2_%d" % i, [128, D], F32, p2) for i in range(3)]
    big = [sb("big%d" % i, [128, D], F32, p2) for i in range(4)]
    bst = sb("bst", [128, 2, 6], F32, p2)
    mv_ = sb("mv_", [128, 8], F32, p2)
    aI = sb("aI", [128, 2, 128], BF16, p2)
    finT = sb("finT", [NFIN, D], F32, p2)
    p2a = ExitStack()
    wo_b = sb("wo_b", [128, 8, D], BF16, p2a)
    a_hi = float(np.float32(ALPHA).astype(ml_bf16).astype(np.float32))
    a_lo = float(np.float32(ALPHA - a_hi).astype(ml_bf16).astype(np.float32))
    S.op("pool", lambda E: E.tensor_scalar(out=aI[:, 0, :], in0=ident_f[:], scalar1=a_hi, scalar2=0.0,
                                           op0=ALU.mult, op1=ALU.add), [ident_f], [(aI, 0)])
    S.op("pool", lambda E: E.tensor_scalar(out=aI[:, 1, :], in0=ident_f[:], scalar1=a_lo, scalar2=0.0,
                                           op0=ALU.mult, op1=ALU.add), [ident_f], [(aI, 1)])
    for hf in range(2):
        copy_op(ev_eng(), finT[:, hf * 512:(hf + 1) * 512], psF[hf][0:NFIN, :], [psF[hf]], [finT])
    S.dma(o_fin[:, :], finT[:, :], reads=[finT])
    S.dma(o_shift[0:1, :], xp[SEQ - 1:SEQ, :])
    S.dma(o_shift[1:1 + NSB, :], xst.rearrange("(b t) d -> b t d", t=DEC)[:, DEC - 1, :])
    S.dma(lnvs[:], lnv[:, 0:2, :], writes=[lnvs])
    for kc in range(8):
        stg = stg2[kc % 3]
        S.dma(stg[:], wo[:, kc, :], writes=[stg])
        S.op("pool", lambda E, kc=kc, stg=stg: E.tensor_copy(out=wo_b[:, kc, :], in_=stg[:]), [stg], [(wo_b, kc)])

    def layer_norm(src, dst, gi, rows):
        R = slice(0, rows)
        for j in range(2):
            S.op("dve", lambda E, j=j: E.bn_stats(out=bst[R, j, :], in_=src[R, j * 512:(j + 1) * 512]), [src], [bst])
        S.op("dve", lambda E: E.bn_aggr(out=mv_[R, 0:2], in_=bst[R, :, :]), [bst], [(mv_, 0)])
        S.op("dve", lambda E: E.tensor_scalar(out=mv_[R, 2:3], in0=mv_[R, 1:2], scalar1=LN_EPS, scalar2=None,
                                              op0=ALU.add), [(mv_, 0)], [(mv_, 2)])
        act(mv_[R, 3:4], mv_[R, 2:3], AF.Sqrt, [(mv_, 2)], [(mv_, 3)])
        S.op("dve", lambda E: E.reciprocal(out=mv_[R, 4:5], in_=mv_[R, 3:4]), [(mv_, 3)], [(mv_, 4)])
        S.op("dve", lambda E: E.scalar_tensor_tensor(out=mv_[R, 5:6], in0=mv_[R, 0:1], scalar=-1.0, in1=mv_[R, 4:5],
                                                     op0=ALU.mult, op1=ALU.mult), [(mv_, 0), (mv_, 4)], [(mv_, 5)])
        act(dst[R, :], src[R, :], AF.Identity, [src, (mv_, 4), (mv_, 5)], [dst], bias=mv_[R, 5:6], scale=mv_[R, 4:5])
        tt("pool", dst[R, :], dst[R, :], lnvs[R, 0, :], ALU.mult, [dst, lnvs], [dst])
        tt("dve", dst[R, :], dst[R, :], lnvs[R, 1, :], ALU.add, [dst, lnvs], [dst])

    ttiles = [(t * 128, 128, yp[t * 128:(t + 1) * 128, :], xp[t * 128:(t + 1) * 128, :]) for t in range(SEQ // 128)]
    ttiles.append((SEQ, NSB * DEC, ys[:, :], xst[:, :]))

    for ti, (m0, rows, _, xrows) in enumerate(ttiles):
        R = slice(0, rows)
        xtm = big[ti % 2]
        s1t = big[2 + ti % 2]
        S.dma(xtm[R, :], xrows, writes=[xtm])
        pss = [getps(), getps()]
        for hf in range(2):
            for kc in range(8):
                mm(pss[hf][R, :], merged[:, kc, m0:m0 + rows], wo_b[:, kc, hf * 512:(hf + 1) * 512],
                   [(merged, kc), (wo_b, kc)], pss[hf], start=(kc == 0), stop=(kc == 7), inc=(kc == 7))
        for hf in range(2):
            S.op("dve", lambda E, hf=hf: E.scalar_tensor_tensor(
                out=s1t[R, hf * 512:(hf + 1) * 512], in0=xtm[R, hf * 512:(hf + 1) * 512], scalar=ALPHA,
                in1=pss[hf][R, :], op0=ALU.mult, op1=ALU.add), [xtm, pss[hf]], [s1t])
        layer_norm(s1t, xtm, 0, rows)
        pst = [getps(), getps()]
        for kc in range(8):
            hf, k4 = divmod(kc, 4)
            S.op("pe", lambda E, hf=hf, k4=k4, kc=kc: E.transpose(
                out=pst[hf][:, k4 * 128:k4 * 128 + rows], in_=xtm[R, kc * 128:(kc + 1) * 128],
                identity=ident_f[R, R]), [xtm, ident_f], [pst[hf]], inc=(k4 == 3))
        for hf in range(2):
            copy_op(ev_eng(), h1T[:, hf * 4:(hf + 1) * 4, m0:m0 + rows],
                    pst[hf][:].rearrange("p (k t) -> p k t", t=128)[:, :, 0:rows], [pst[hf]],
                    [(h1T, hf * 4 + k) for k in range(4)])
    if "h1T" in dbg_out:
        S.dma(dbg_out["h1T"], h1T[:, 0, :], reads=[(h1T, 0)])

    S.barrier()
    p2a.close()
    S.dma(lnvs[:], lnv[:, 2:4, :], writes=[lnvs])
    wd_b = sb("wd_b", [128, NFC, D], BF16, p2)
    NTH = 1088
    NFA = 15
    uTa = T(merged[:].rearrange("p a b -> p (a b)")[:, 0:NFA * NTH].rearrange("p (f t) -> p f t", t=NTH), "uTa")
    uTb = sb("uTb", [128, NFC - NFA, NTH], BF16, p2)

    class _UT:
        def __getitem__(self, k):
            p_, fc_, cols_ = k
            if fc_ < NFA:
                return uTa.h[p_, fc_, cols_]
            return uTb.h[p_, fc_ - NFA, cols_]

    uT = _UT()
    wgub = [sb("wgub%d" % i, [128, 2, 8, 128], BF16, p2) for i in range(2)]
    sgt = [sb("sgt%d" % i, [128, 512], F32, p2) for i in range(2)]
    for fc in range(NFC):
        stg = stg2[fc % 3]
        S.dma(stg[:], wd[fc, :, :], writes=[stg])
        S.op("pool", lambda E, fc=fc, stg=stg: E.tensor_copy(out=wd_b[:, fc, :], in_=stg[:]), [stg], [(wd_b, fc)])
    halves = [(0, 1024, ttiles[0:8]), (1024, 1088, ttiles[8:17])]
    for (c0, ncols, tls) in halves:
        blocks = [(b0_, min(512, ncols - b0_)) for b0_ in range(0, ncols, 512)]
        for fc in range(NFC):
            wb = wgub[fc % 2]
            for j in range(2):
                stg = stg2[(2 * fc + j) % 3]
                S.dma(stg[:].rearrange("p (a b) -> p a b", b=128), wgu[fc, :, j, :, :], writes=[stg])
                S.op("pool", lambda E, j=j, stg=stg: E.tensor_copy(
                    out=wb[:, j, :, :], in_=stg[:].rearrange("p (a b) -> p a b", b=128)), [stg], [(wb, j)])
            for bi, (b0_, bn) in enumerate(blocks):
                psg = getps()
                psu = getps()
                for j, ps_ in ((0, psg), (1, psu)):
                    for kc in range(8):
                        mm(ps_[:, 0:bn], wb[:, j, kc, :], h1T[:, kc, c0 + b0_:c0 + b0_ + bn], [(wb, j), (h1T, kc)], ps_,
                           start=(kc == 0), stop=(kc == 7), inc=(kc == 7))
                sg_ = sgt[(fc * 3 + bi) % 2]
                act(sg_[:, 0:bn], psg[:, 0:bn], AF.Silu, [psg], [sg_])
                tt("dve", uT[:, fc, b0_:b0_ + bn], psu[:, 0:bn], sg_[:, 0:bn], ALU.mult, [psu, sg_], [(uT, fc)])
        for ti, (m0, rows, yout, _) in enumerate(tls):
            R = slice(0, rows)
            l0 = m0 - c0
            pss = [getps(), getps()]
            for hf in range(2):
                for fc in range(NFC):
                    mm(pss[hf][R, :], uT[:, fc, l0:l0 + rows], wd_b[:, fc, hf * 512:(hf + 1) * 512],
                       [(uT, fc), (wd_b, fc)], pss[hf], start=(fc == 0), stop=False, inc=False)
                for k4 in range(4):
                    kc = hf * 4 + k4
                    for a in range(2):
                        last = (k4 == 3 and a == 1)
                        mm(pss[hf][R, k4 * 128:(k4 + 1) * 128], h1T[:, kc, m0:m0 + rows], aI[:, a, :],
                           [(h1T, kc), (aI, a)], pss[hf], start=False, stop=last, inc=last)
            s2t = big[ti % 2]
            yt = big[2 + ti % 2]
            for hf in range(2):
                copy_op(ev_eng(), s2t[R, hf * 512:(hf + 1) * 512], pss[hf][R, :], [pss[hf]], [s2t])
            layer_norm(s2t, yt, 2, rows)
            S.dma(yout, yt[R, :], reads=[yt])

    print("nops", S.nops)
    S.finish()
    p2.close()
    es.close()
    return nc


def kernel(**inputs):
    maps = prep_inputs(inputs)
    nc = build_program()
    res = run_bass_kernel_spmd(nc, maps, core_ids=list(range(NCORES)))
    R = res.results
    f32 = np.float32
    y_p = np.stack([R[i]["yp"] for i in range(NCORES)]).astype(f32)
    y_s = np.concatenate([R[i]["ys"].reshape(NSB, DEC, D) for i in range(NCORES)]).astype(f32)
    fin = [R[i]["o_fin"] for i in range(NCORES)]
    shf = [R[i]["o_shift"] for i in range(NCORES)]
    new_shift_p = np.stack([s[0] for s in shf])[None].astype(f32)
    new_shift_s = np.concatenate([s[1:] for s in shf])[None].astype(f32)
    new_wkv_p = np.stack([R[i]["o_wkv_p"] for i in range(NCORES)])[None].astype(f32)
    new_wkv_s = np.concatenate([R[i]["o_wkv_s"] for i in range(NCORES)])[None].astype(f32)
    new_conv_p = np.stack([f[0:3] for f in fin])[None].astype(f32)
    new_lru_p = np.stack([f[3] for f in fin])[None].astype(f32)
    new_conv_s = np.concatenate([f[4:52].reshape(NSB, 3, D) for f in fin])[None].astype(f32)
    new_lru_s = np.concatenate([f[52:68] for f in fin])[None].astype(f32)
    return (y_p, y_s, new_shift_p, new_wkv_p, new_conv_p, new_lru_p,
            new_shift_s, new_wkv_s, new_conv_s, new_lru_s)


def prep_inputs(inp):
    f = lambda a: np.ascontiguousarray(np.asarray(a, dtype=np.float32))
    w_in = f(inp["w_in"])[0]
    bases = [0, 1024, 2048, 3328, 4352, 5376, 6400]
    w1 = np.stack([w_in[:, b:b + 1024].reshape(8, 128, 8, 128).transpose(2, 1, 0, 3) for b in bases], axis=2)
    wl = w_in[:, 3072:3328].reshape(8, 128, 2, 128).transpose(1, 2, 0, 3)
    wo = f(inp["w_o"])[0].reshape(8, 128, D).transpose(1, 0, 2)
    wg = f(inp["w_ffn_gate"])[0].reshape(8, 128, NFC, 128).transpose(2, 1, 0, 3)
    wu = f(inp["w_ffn_up"])[0].reshape(8, 128, NFC, 128).transpose(2, 1, 0, 3)
    wgu = np.stack([wg, wu], axis=2)
    wd = f(inp["w_ffn_down"])[0].reshape(NFC, 128, D)
    vec = np.zeros((NV, D), np.float32)
    mu = f(inp["tmix_mu"])[0]
    vec[V_MUR] = mu[0:1024]; vec[V_MUK] = mu[1024:2048]; vec[V_MUV] = mu[2048:3072]
    vec[V_MUL, 0:256] = mu[3072:3328]
    vec[V_W0] = f(inp["w0"])[0]; vec[V_A0] = f(inp["a0"])[0]
    vec[V_KK] = f(inp["k_k"])[0]; vec[V_KA] = f(inp["k_a"])[0]
    vec[V_RK] = f(inp["r_k"])[0].reshape(-1)
    vec[V_LNXG] = f(inp["lnx_g"])[0]; vec[V_LNXB] = f(inp["lnx_b"])[0]
    cw = f(inp["conv_w"])[0]
    for j in range(4):
        vec[V_CW0 + j] = cw[j]
    vec[V_CB] = f(inp["conv_b"])[0]
    vec[V_BA] = f(inp["lru_ba"])[0].reshape(-1); vec[V_BI] = f(inp["lru_bi"])[0].reshape(-1)
    vec[V_LAM] = f(inp["lru_lambda"])[0]
    vecp = np.ascontiguousarray(vec.reshape(NV, 8, 128).transpose(2, 0, 1))
    wab = np.zeros((128, 2, 8, 128), np.float32)
    for j, nm in enumerate(("lru_wa", "lru_wi")):
        w = f(inp[nm])[0]
        for c in range(8):
            for bl in range(2):
                wab[bl * 64:(bl + 1) * 64, j, c, bl * 64:(bl + 1) * 64] = w[2 * c + bl]
    lnv = np.stack([np.broadcast_to(f(inp[n])[0], (128, D)) for n in ("ln1_g", "ln1_b", "ln2_g", "ln2_b")], axis=1)
    shared = dict(w1=w1, wl=wl, wo=wo, wgu=wgu, wd=wd, vec=vecp, w2d=f(inp["w2_decay"])[0],
                  a2d=f(inp["a2_iclr"])[0], g2d=f(inp["g2_gate"])[0], wab=wab, lnv=lnv)
    shared = {k: np.ascontiguousarray(v, dtype=np.float32) for k, v in shared.items()}
    x_prompt = f(inp["x_prompt"]); x_sample = f(inp["x_sample"])
    sh = f(inp["state_shift"])[0]; swkv = f(inp["state_wkv"])[0]
    sconv = f(inp["state_conv"])[0]; slru = f(inp["state_lru"])[0]
    maps = []
    for i in range(NCORES):
        b0 = i * NSB
        xs = np.concatenate([sh[b0:b0 + NSB, None, :], x_sample[b0:b0 + NSB]], axis=1).reshape(NSB * 5, D)
        stt = np.concatenate([sconv[b0:b0 + NSB].reshape(NSB * 3, D), slru[b0:b0 + NSB]], axis=0)
        m = dict(xp=x_prompt[i], xs=xs, xst=x_sample[b0:b0 + NSB].reshape(NSB * DEC, D), st=stt,
                 s0=swkv[b0:b0 + NSB])
        m = {k: np.ascontiguousarray(v, dtype=np.float32) for k, v in m.items()}
        m.update(shared)
        maps.append(m)
    return maps
```

```python
import numpy as np
import ml_dtypes
from contextlib import ExitStack
import concourse.bass as bass
import concourse.mybir as mybir
from concourse.bass_utils import run_bass_kernel_spmd

F32 = mybir.dt.float32
BF16 = mybir.dt.bfloat16
AF = mybir.ActivationFunctionType
ALU = mybir.AluOpType
AX = mybir.AxisListType
ml_bf16 = ml_dtypes.bfloat16

D = 1024
NCORES = 8
SEQ = 2048
NSB = 16
DEC = 4
NCH = 8
D_FF = 2816
NFC = D_FF // 128
KAPPA = float(np.exp(-0.5))
ALPHA = 2.0 ** 0.25
LN_EPS = 1e-5
GN_EPS = 64e-5
CH = 64
PIPE = True
RSCALE = 1.0
TAILPIPE = True
TAILSEL = lambda kind, idx: True
QB = 4
SEGP = QB * CH
NSEGP = SEQ // SEGP
SB = 16
NSEGS = NSB // SB
XS0 = SEQ
XCOLS = SEQ + NSB * 5
MCOLS = SEQ + NSB * DEC
NFIN = 68
(V_MUR, V_MUK, V_MUV, V_W0, V_A0, V_KK, V_KA, V_RK, V_LNXG, V_LNXB, V_CW0, V_CW1, V_CW2, V_CW3,
 V_CB, V_BA, V_BI, V_LAM, V_MUL) = range(19)
NV = 19


class Sched:
    COMPUTE = ("pe", "act", "dve", "pool")

    def __init__(self, nc, es, ndma=14):
        self.nc = nc
        self.eng = {"pe": nc.tensor, "act": nc.scalar, "dve": nc.vector, "pool": nc.gpsimd, "sp": nc.sync}
        self.sem = {}
        self.cnt = {}
        for e in self.COMPUTE:
            self.sem[e] = es.enter_context(nc.semaphore("sem_" + e))
            self.cnt[e] = 0
        self.ndma = ndma
        for i in range(ndma):
            d = "d%d" % i
            self.sem[d] = es.enter_context(nc.semaphore("sem_" + d))
            self.cnt[d] = 0
        self.rr = 0
        self.waited = {e: {} for e in list(self.COMPUTE) + ["sp"]}
        self.W = {}
        self.R = {}
        self.excl = set()
        self.nops = {e: 0 for e in list(self.COMPUTE) + ["sp"]}

    def _collect(self, e, reads, writes):
        need = {}

        def add(d, c, raw):
            if d == e and (e == "pe" or not raw):
                return
            if c > need.get(d, 0):
                need[d] = c

        reads = [getattr(r, "base", r) for r in reads]
        writes = [getattr(w, "base", w) for w in writes]
        for r in reads:
            for d, c in self.W.get(r, {}).items():
                add(d, c, True)
            if r in self.excl:
                for d, c in self.R.get(r, {}).items():
                    add(d, c, False)
        for w in writes:
            for d, c in self.W.get(w, {}).items():
                add(d, c, False)
            for d, c in self.R.get(w, {}).items():
                add(d, c, False)
        return need

    LOG = None

    def _emit_waits(self, e, need):
        wd = self.waited[e]
        for d, c in need.items():
            if wd.get(d, 0) >= c:
                continue
            self.eng[e].wait_ge(self.sem[d], c)
            wd[d] = c
            if self.LOG is not None:
                self.LOG.append((e, d, c, dict(self.cnt)))

    def _record(self, dom, val, reads, writes):
        reads = [getattr(r, "base", r) for r in reads]
        writes = [getattr(w, "base", w) for w in writes]
        for r in reads:
            rr = self.R.setdefault(r, {})
            if val > rr.get(dom, 0):
                rr[dom] = val
        for w in writes:
            if self.R.get(w):
                self.W[w] = {dom: val}
                self.R[w] = {}
            else:
                self.W.setdefault(w, {})[dom] = val

    def op(self, e, fn, reads=(), writes=(), inc=True):
        need = self._collect(e, reads, writes)
        att = None
        if e != "pe":
            wd = self.waited[e]
            pend = [(d, c) for d, c in need.items() if wd.get(d, 0) < c]
            if pend:
                att = pend[-1]
                need = dict(pend[:-1])
        self._emit_waits(e, need)
        ins = fn(self.eng[e])
        if att is not None:
            ins._wait_ge(self.sem[att[0]], att[1])
            self.waited[e][att[0]] = att[1]
        self.nops[e] += 1
        if inc:
            self.cnt[e] += 1
            ins.then_inc(self.sem[e], 1)
            val = self.cnt[e]
        else:
            val = self.cnt[e] + 1
        self._record(e, val, reads, writes)

    def dma(self, out, in_, reads=(), writes=(), q="sp"):
        d = "d%d" % self.rr
        self.rr = (self.rr + 1) % self.ndma
        need = self._collect(q, reads, writes)
        if self.cnt[d] > 0:
            need[d] = max(need.get(d, 0), self.cnt[d])
        self._emit_waits(q, need)
        ins = self.eng[q].dma_start(out=out, in_=in_)
        self.nops[q] += 1
        self.cnt[d] += 16
        ins.then_inc(self.sem[d], 16)
        self._record(d, self.cnt[d], reads, writes)

    def barrier(self):
        for e in list(self.COMPUTE) + ["sp"]:
            need = {d: c for d, c in self.cnt.items() if c > 0 and d != e}
            self._emit_waits(e, need)

    def finish(self):
        for i in range(self.ndma):
            d = "d%d" % i
            if self.cnt[d] > 0:
                self.eng["sp"].wait_ge(self.sem[d], self.cnt[d])
        for e in self.COMPUTE:
            if self.cnt[e] > 0:
                self.eng["sp"].wait_ge(self.sem[e], self.cnt[e])


class StopBuild(Exception):
    pass


class T:
    def __init__(self, h, name):
        self.h = h
        self.name = name

    def __getitem__(self, k):
        return self.h[k]

    def __repr__(self):
        return "T(%s)" % self.name


class Off:
    def __init__(self, base, off, width):
        self.base = base
        self.off = off
        self.width = width

    def __getitem__(self, k):
        if not isinstance(k, tuple):
            return self.base[:, self.off:self.off + self.width]
        p_, c_ = k
        lo = 0 if c_.start is None else c_.start
        hi = self.width if c_.stop is None else c_.stop
        return self.base[p_, self.off + lo:self.off + hi]


class SubSeg:
    kind = "s"

    def __init__(self, b):
        self.idx = b
        self.ncol = QB * 5
        self.nb = QB
        self.T = DEC
        self.tv = DEC
        self.nq = QB
        self.m0 = SEQ + b * QB * DEC
        self.mcols = QB * DEC
        self.nlev = 2

    def tokv(self, ap):
        return ap.rearrange("p (b s) -> p b s", s=5)[:, :, 1:5]

    chv = tokv


class Pool:
    def __init__(self, tiles):
        self.free = list(tiles)
        self.all = list(tiles)

    def get(self):
        return self.free.pop(0)

    def put(self, *ts):
        for t in ts:
            assert t in self.all and t not in self.free
            self.free.append(t)


class Seg:
    def __init__(self, kind, idx):
        self.kind = kind
        self.idx = idx
        if kind == "p":
            self.x0 = idx * SEGP
            self.ncol = SEGP
            self.nb = 1
            self.T = SEGP
            self.tv = CH
            self.m0 = idx * SEGP
            self.mcols = SEGP
            self.nlev = 6
        else:
            self.x0 = XS0 + idx * SB * 5
            self.ncol = SB * 5
            self.nb = SB
            self.T = DEC
            self.tv = DEC
            self.m0 = SEQ + idx * SB * DEC
            self.mcols = SB * DEC
            self.nlev = 2
        self.nq = QB if kind == "p" else SB

    def tokv(self, ap):
        if self.kind == "p":
            return ap.rearrange("p (b t) -> p b t", b=1)
        return ap.rearrange("p (b s) -> p b s", s=5)[:, :, 1:5]

    def chv(self, ap):
        if self.kind == "p":
            return ap.rearrange("p (q t) -> p q t", t=CH)
        return ap.rearrange("p (b s) -> p b s", s=5)[:, :, 1:5]


def build_program(debug=None, stop_after=None):
    nc = bass.Bass("TRN2", target_bir_lowering=False)
    es = ExitStack()
    S = Sched(nc, es)

    def din(name, shape):
        return nc.dram_tensor(name, list(shape), F32, kind="ExternalInput").ap()

    def dout(name, shape):
        return nc.dram_tensor(name, list(shape), F32, kind="ExternalOutput").ap()

    xp = din("xp", [SEQ, D])
    xs = din("xs", [NSB * 5, D])
    xst = din("xst", [NSB * DEC, D])
    st = din("st", [64, D])
    s0 = din("s0", [NSB, 16, 64, 64])
    w1 = din("w1", [NCH, 128, 7, 8, 128])
    wl = din("wl", [128, 2, 8, 128])
    wo = din("wo", [128, 8, D])
    wgu = din("wgu", [NFC, 128, 2, 8, 128])
    wd = din("wd", [NFC, 128, D])
    vec = din("vec", [128, NV, 8])
    w2d = din("w2d", [64, D])
    a2d = din("a2d", [64, D])
    g2d = din("g2d", [128, D])
    wab = din("wab", [128, 2, 8, 128])
    lnv = din("lnv", [128, 4, D])

    yp = dout("yp", [SEQ, D])
    ys = dout("ys", [NSB * DEC, D])
    o_shift = dout("o_shift", [1 + NSB, D])
    o_wkv_p = dout("o_wkv_p", [16, 64, 64])
    o_wkv_s = dout("o_wkv_s", [NSB, 16, 64, 64])
    o_fin = dout("o_fin", [NFIN, D])
    dbg_out = {}
    if debug:
        for name, (shape, dt_) in debug.items():
            dbg_out[name] = nc.dram_tensor("dbg_" + name, list(shape), dt_, kind="ExternalOutput").ap()

    def sb(name, shape, dt=F32, stack=None):
        h = (stack or es).enter_context(nc.sbuf_tensor(name, list(shape), dt))
        return T(h, name)

    psl = [T(es.enter_context(nc.psum_tensor("ps%d" % i, [128, 512], F32)), "ps%d" % i) for i in range(8)]
    ps_state = {"i": 0}
    S.excl.update(psl)

    ps_banks = {None: list(range(8)), "s1": [6, 7], "s1a": [6], "s1b": [7], "s1c": [5], "s2A": [0, 1], "s2B": [2, 3], "tailA": [4], "tailB": [4]}
    ps_ctr = {k: 0 for k in (None, "s1", "s1a", "s1b", "s1c", "s2A", "s2B", "tailA", "tailB")}
    cur_alloc = {"who": None}

    def getps():
        who = cur_alloc["who"]
        lst = ps_banks[who]
        t = psl[lst[ps_ctr[who] % len(lst)]]
        ps_ctr[who] += 1
        return t

    cur = {"segi": -1}

    def ck(label):
        if stop_after == label or stop_after == "%s@%d" % (label, cur["segi"]):
            raise StopBuild()

    ident_f = sb("ident_f", [128, 128])
    ident_b = sb("ident_b", [128, 128], BF16)
    II_f = sb("II_f", [128, 64])
    II_b = sb("II_b", [128, 64], BF16)
    ones_bd = sb("ones_bd", [128, 128])
    mUs = sb("mUs", [128, QB, 128], BF16)
    mUi = sb("mUi", [128, QB, 128], BF16)
    mLs = sb("mLs", [128, QB, 128], BF16)
    rmask_p = sb("rmask_p", [128, SEGP])
    rmask_s = sb("rmask_s", [128, SB * 5])
    vecs = sb("vecs", [128, NV, 8])
    der = sb("der", [128, 12, 8])
    w2b = sb("w2b", [128, D], BF16)
    g2b = sb("g2b", [128, D], BF16)
    wabb = sb("wabb", [128, 2, 8, 128], BF16)
    fin = sb("fin", [128, NCH, NFIN])
    stT = sb("stT", [128, NCH, 64])

    S.op("pool", lambda E: E.memset(ident_f[:], 1.0), writes=[ident_f])
    S.op("pool", lambda E: E.affine_select(out=ident_f[:], in_=ident_f[:], pattern=[[-1, 128]],
                                           compare_op=ALU.is_equal, fill=0.0, base=0, channel_multiplier=1),
         reads=[ident_f], writes=[ident_f])
    S.op("dve", lambda E: E.tensor_copy(out=ident_b[:], in_=ident_f[:]), reads=[ident_f], writes=[ident_b])
    S.op("dve", lambda E: E.tensor_tensor(out=II_f[:], in0=ident_f[:, 0:64], in1=ident_f[:, 64:128], op=ALU.add),
         reads=[ident_f], writes=[II_f])
    S.op("dve", lambda E: E.tensor_copy(out=II_b[:], in_=II_f[:]), reads=[II_f], writes=[II_b])
    S.op("pool", lambda E: E.memset(ones_bd[:], 0.0), writes=[ones_bd])
    S.op("pool", lambda E: E.memset(ones_bd[0:64, 0:64], 1.0), reads=[ones_bd], writes=[ones_bd])
    S.op("pool", lambda E: E.memset(ones_bd[64:128, 64:128], 1.0), reads=[ones_bd], writes=[ones_bd])
    for m, patt, cm, cmp in ((mUs, 1, -1, ALU.is_gt), (mUi, 1, -1, ALU.is_ge), (mLs, -1, 1, ALU.is_gt)):
        S.op("pool", lambda E, m=m: E.memset(m[:], 1.0), writes=[m])
        S.op("pool", lambda E, m=m, patt=patt, cm=cm, cmp=cmp: E.affine_select(
            out=m[:], in_=m[:], pattern=[[0, QB], [patt, 128]], compare_op=cmp, fill=0.0, base=0,
            channel_multiplier=cm), reads=[m], writes=[m])
    S.op("pool", lambda E: E.memset(rmask_p[:], 1.0), writes=[rmask_p])
    S.op("pool", lambda E: E.memset(rmask_p[:].rearrange("p (q t) -> p q t", t=CH)[:, :, 0:1], 0.0),
         reads=[rmask_p], writes=[rmask_p])
    S.op("pool", lambda E: E.memset(rmask_s[:], 1.0), writes=[rmask_s])
    S.op("pool", lambda E: E.memset(rmask_s[:].rearrange("p (b s) -> p b s", s=5)[:, :, 0:2], 0.0),
         reads=[rmask_s], writes=[rmask_s])
    S.op("pool", lambda E: E.memset(fin[:], 0.0), writes=[(fin, c) for c in range(NCH)])

    S.dma(vecs[:], vec[:, :, :], writes=[vecs])

    def vcol(i, c):
        return vecs[:, i, c:c + 1]

    S.op("dve", lambda E: E.tensor_scalar(out=der[:, 0, :], in0=vecs[:, V_KA, :], scalar1=-1.0, scalar2=1.0,
                                          op0=ALU.mult, op1=ALU.add), reads=[vecs], writes=[(der, 0)])
    S.op("act", lambda E: E.activation(out=der[:, 3, :], in_=vecs[:, V_LAM, :], func=AF.Exp, scale=-1.0),
         reads=[vecs], writes=[(der, 3)])
    S.op("act", lambda E: E.activation(out=der[:, 3, :], in_=der[:, 3, :], func=AF.Ln, bias=1.0, scale=1.0),
         reads=[(der, 3)], writes=[(der, 3)])
    S.op("dve", lambda E: E.tensor_scalar(out=der[:, 1, :], in0=der[:, 3, :], scalar1=-8.0, scalar2=None,
                                          op0=ALU.mult), reads=[(der, 3)], writes=[(der, 1)])
    S.op("dve", lambda E: E.tensor_scalar(out=der[:, 2, :], in0=der[:, 3, :], scalar1=-16.0, scalar2=None,
                                          op0=ALU.mult), reads=[(der, 3)], writes=[(der, 2)])
    for di, vi in ((4, V_W0), (5, V_A0), (6, V_BA), (7, V_BI), (8, V_KA)):
        S.op("dve", lambda E, di=di, vi=vi: E.tensor_scalar(out=der[:, di, :], in0=vecs[:, vi, :], scalar1=0.5,
                                                            scalar2=None, op0=ALU.mult), reads=[vecs], writes=[(der, di)])
    S.op("dve", lambda E: E.tensor_scalar(out=der[:, 9, :], in0=vecs[:, V_KA, :], scalar1=-0.5, scalar2=1.0,
                                          op0=ALU.mult, op1=ALU.add), reads=[vecs], writes=[(der, 9)])
    S.op("dve", lambda E: E.tensor_scalar(out=der[:, 10, :], in0=der[:, 3, :], scalar1=-4.0, scalar2=None,
                                          op0=ALU.mult), reads=[(der, 3)], writes=[(der, 10)])
    S.op("dve", lambda E: E.tensor_scalar(out=der[:, 11, :], in0=der[:, 3, :], scalar1=-8.0, scalar2=None,
                                          op0=ALU.mult), reads=[(der, 3)], writes=[(der, 11)])
    negh = sb("negh", [128, QB])
    S.op("pool", lambda E: E.memset(negh[:], -0.5), writes=[negh])

    p1 = ExitStack()
    merged = sb("merged", [128, 8, MCOLS], BF16)
    xsc = nc.dram_tensor("xsc", [128, 8, XCOLS], BF16).ap()
    xtt = [sb("xtt%d" % i, [128, 8, 128], BF16, p1) for i in range(1)]
    xs_pool = Pool([sb("xseg%d" % i, [128, 8, SEGP], BF16, p1) for i in range(2)])
    if stop_after is not None:
        for c_ in range(8):
            S.op("pool", lambda E, c_=c_: E.memset(merged[:, c_, :], 0.0), writes=[(merged, c_)])
    lora0 = sb("lora0", [128, XCOLS], BF16, p1)
    lora1 = sb("lora1", [128, XCOLS], BF16, p1)
    stage = [sb("stage%d" % i, [128, 8, 128], F32, p1) for i in range(3)]
    wcb = sb("wcb", [128, 7, 8, 128], BF16, p1)
    NSCR = 37
    scr = Pool([sb("scr%d" % i, [128, SEGP + 4], F32, p1) for i in range(NSCR)])
    NWK = 27
    wk = Pool([sb("wk%d" % i, [128, QB, 128], BF16, p1) for i in range(NWK)])
    bdn = ("aT", "bT", "kT", "rT", "bgT", "kgT", "vT")
    bd = {}
    for kind in ("p", "s"):
        bd[kind] = []
        for par_ in range(2):
            st_ = {n: sb("bd_%s%d_%s" % (kind, par_, n), [128, QB, 2, 64], BF16, p1) for n in bdn}
            bd[kind].append(st_)
            for n in bdn:
                S.op("pool", lambda E, t=st_[n]: E.memset(t[:], 0.0), writes=[(st_[n], 0), (st_[n], 1)])
    ybd = sb("ybd", [128, QB, 2, 64], F32, p1)
    S.op("pool", lambda E: E.memset(ybd[:], 0.0), writes=[(ybd, 0), (ybd, 1)])
    hbd = sb("hbd", [128, 2, 64], F32, p1)
    S.op("pool", lambda E: E.memset(hbd[:], 0.0), writes=[(hbd, 0), (hbd, 1)])
    zbuf = [sb("zbuf%d" % i, [128, 1 + SEGP], F32, p1) for i in range(3)]
    xbuf = sb("xbuf", [128, SB * (3 + SEGP // 1) if False else max(3 + SEGP, SB * 7)], F32, p1)
    carry3 = sb("carry3", [128, 3], F32, p1)
    hcar = sb("hcar", [128, 1], F32, p1)
    H32 = [sb("H32_%d" % i, [128, QB + 1, 64], F32, p1) for i in range(2)]
    Hbf = [sb("Hbf_%d" % i, [128, QB + 1, 64], BF16, p1) for i in range(2)]
    H32s = [[sb("H32s_%d%d" % (i, j), [128, QB, 64], F32, p1) for j in range(2)] for i in range(2)]
    Hbfs = [sb("Hbfs_%d" % i, [128, QB, 64], BF16, p1) for i in range(2)]
    gamC = sb("gamC", [128, QB], F32, p1)
    stat = sb("stat", [128, 8, QB], F32, p1)
    bst4 = sb("bst4", [128, QB, 6], F32, p1)
    mvq = sb("mvq", [128, QB, 2], F32, p1)

    S.dma(stage[0][0:64, :, :].rearrange("p a b -> p (a b)"), w2d[:, :], writes=[stage[0]])
    S.dma(stage[1][64:128, :, :].rearrange("p a b -> p (a b)"), a2d[:, :], writes=[stage[1]])
    S.op("pool", lambda E: E.tensor_copy(out=w2b[0:64, :], in_=stage[0][0:64, :, :].rearrange("p a b -> p (a b)")),
         reads=[stage[0]], writes=[(w2b, 0)])
    S.op("pool", lambda E: E.tensor_copy(out=w2b[64:128, :], in_=stage[1][64:128, :, :].rearrange("p a b -> p (a b)")),
         reads=[stage[1]], writes=[(w2b, 1)])
    S.dma(stage[2][:, :, :].rearrange("p a b -> p (a b)"), g2d[:, :], writes=[stage[2]])
    S.op("pool", lambda E: E.tensor_copy(out=g2b[:], in_=stage[2][:, :, :].rearrange("p a b -> p (a b)")),
         reads=[stage[2]], writes=[g2b])
    for j in range(2):
        S.dma(stage[j][:], wab[:, j, :, :], writes=[stage[j]])
        S.op("pool", lambda E, j=j: E.tensor_copy(out=wabb[:, j, :, :], in_=stage[j][:]),
             reads=[stage[j]], writes=[(wabb, j)])
    for j in range(2):
        S.dma(stage[j][:], wl[:, j, :, :], writes=[stage[j]])
        S.op("pool", lambda E, j=j: E.tensor_copy(out=wcb[:, j, :, :], in_=stage[j][:]),
             reads=[stage[j]], writes=[(wcb, j)])

    rr = {"ev": 0, "ew": 0}

    def ev_eng():
        return "act"

    def ew_eng():
        rr["ew"] ^= 1
        return "dve" if rr["ew"] else "pool"

    def copy_op(e, out, in_, reads, writes):
        if e == "act":
            S.op("act", lambda E: E.copy(out=out, in_=in_), reads, writes)
        else:
            S.op(e, lambda E: E.tensor_copy(out=out, in_=in_), reads, writes)

    ntile = SEQ // 128
    for t in range(ntile + 2):
        xi = stage[t % 3]
        xiv = xi[:].rearrange("p a b -> p (a b)")
        if t < ntile:
            rows = 128
            S.dma(xiv, xp[t * 128:(t + 1) * 128, :], writes=[xi])
        elif t == ntile:
            rows = NSB * 5
            S.dma(xiv[0:rows, :], xs[:, :], writes=[xi])
        else:
            rows = 64
            S.dma(xiv[0:rows, :], st[:, :], writes=[xi])
        for half in range(2):
            ps = getps()
            for k4 in range(4):
                kc = half * 4 + k4
                S.op("pe", lambda E, ps=ps, k4=k4, kc=kc, xi=xi, rows=rows: E.transpose(
                    out=ps[:, k4 * 128:k4 * 128 + rows], in_=xi[0:rows, kc, :],
                    identity=ident_f[0:rows, 0:rows]), reads=[xi, ident_f], writes=[ps], inc=(k4 == 3))
            src = ps[:].rearrange("p (k t) -> p k t", t=128)[:, :, 0:rows]
            if t <= ntile:
                xo = xtt[0]
                copy_op(ev_eng(), xo[:, half * 4:(half + 1) * 4, 0:rows], src, [ps], [(xo, half)])
                if half == 1:
                    S.dma(xsc[:, :, t * 128:t * 128 + rows], xo[:, :, 0:rows], reads=[(xo, 0), (xo, 1)],
                          writes=[("xsc", t)])
            else:
                dst = stT[:, half * 4:(half + 1) * 4, :]
                copy_op(ev_eng(), dst, src, [ps], [stT])
    xstate = {"tile": {}, "order": [], "pos": 0}

    def _issue_x(seg):
        t_ = xs_pool.get()
        t0_, t1_ = seg.x0 // 128, (seg.x0 + seg.ncol - 1) // 128
        S.dma(t_[:, :, 0:seg.ncol], xsc[:, :, seg.x0:seg.x0 + seg.ncol],
              reads=[("xsc", k) for k in range(t0_, t1_ + 1)], writes=[t_])
        return t_

    def get_x(seg):
        order, pos = xstate["order"], xstate["pos"]
        assert order[pos] is seg
        t_ = xstate["tile"].pop(pos, None)
        if t_ is None:
            t_ = _issue_x(seg)
        if pos + 1 < len(order):
            xstate["tile"][pos + 1] = _issue_x(order[pos + 1])
        xstate["pos"] = pos + 1
        return t_

    def dump(name, ap, reads):
        if name in dbg_out:
            S.dma(dbg_out[name], ap, reads=reads)


    def proj_ps(wsel, seg, xseg):
        ps = getps()
        ncol = seg.ncol
        for kc in range(8):
            lhsT, wres = wsel(kc)
            S.op("pe", lambda E, ps=ps, lhsT=lhsT, kc=kc: E.matmul(
                ps[:, 0:ncol], lhsT=lhsT, rhs=xseg[:, kc, 0:ncol], start=(kc == 0), stop=(kc == 7)),
                reads=[wres, xseg], writes=[ps], inc=(kc == 7))
        return ps

    def shifted(wsel, mu_ap, seg, zb, first, xseg):
        ncol = seg.ncol
        if seg.kind == "p":
            if first:
                S.op("pool", lambda E: E.memset(zb[:, 0:1], 0.0), writes=[zb])
            else:
                S.op("pool", lambda E: E.tensor_copy(out=zb[:, 0:1], in_=zb[:, SEGP:SEGP + 1]), reads=[zb], writes=[zb])
        ps = proj_ps(wsel, seg, xseg)
        S.op("act", lambda E: E.copy(out=zb[:, 1:1 + ncol], in_=ps[:, 0:ncol]), reads=[ps], writes=[zb])
        dt_ = scr.get()
        zm = scr.get()
        S.op("pool", lambda E: E.tensor_tensor(out=dt_[:, 0:ncol], in0=zb[:, 0:ncol], in1=zb[:, 1:1 + ncol],
                                               op=ALU.subtract), reads=[zb], writes=[dt_])
        S.op("dve", lambda E: E.scalar_tensor_tensor(out=zm[:, 0:ncol], in0=dt_[:, 0:ncol], scalar=mu_ap,
                                                     in1=zb[:, 1:1 + ncol], op0=ALU.mult, op1=ALU.add),
             reads=[dt_, zb, vecs], writes=[zm])
        scr.put(dt_)
        return zm

    segs = [Seg("p", i) for i in range(NSEGP)] + [Seg("s", i) for i in range(NSEGS)]

    xstate["order"] = segs * 2 + segs * NCH
    for L in range(2):
        for si, seg in enumerate(segs):
            xseg = get_x(seg)
            zm = shifted(lambda kc, L=L: (wcb[:, L, kc, :], (wcb, L)), vcol(V_MUL, L), seg, zbuf[0],
                         first=(seg.kind == "p" and seg.idx == 0), xseg=xseg)
            xs_pool.put(xseg)
            nco = seg.ncol
            if L == 0:
                S.op("act", lambda E: E.activation(out=lora0[0:64, seg.x0:seg.x0 + nco], in_=zm[0:64, 0:nco],
                                                   func=AF.Tanh), reads=[zm], writes=[(lora0, 0)])
                S.op("act", lambda E: E.copy(out=lora0[64:128, seg.x0:seg.x0 + nco], in_=zm[64:128, 0:nco]),
                     reads=[zm], writes=[(lora0, 1)])
            else:
                S.op("act", lambda E: E.activation(out=lora1[:, seg.x0:seg.x0 + nco], in_=zm[:, 0:nco],
                                                   func=AF.Sigmoid), reads=[zm], writes=[lora1])
            scr.put(zm)
    dump("lora0", lora0[:, :], [(lora0, 0), (lora0, 1)])
    dump("lora1", lora1[:, :], [lora1])

    if stop_after == "lora":
        S.finish()
        p1.close()
        es.close()
        return nc

    gam_pool = Pool([sb("gamC%d" % i, [128, SB], F32, p1) for i in range(5)])
    ubf_pool = Pool([sb("ubf%d" % i, [128, SEGP], BF16, p1) for i in range(2)])
    sbd = sb("sbd", [128, 2, 64], F32, p1)
    S.op("pool", lambda E: E.memset(sbd[:], 0.0), writes=[(sbd, 0), (sbd, 1)])

    print("sbuf remaining in phase 1:", nc.sbuf_bytes_remaining)

    def wsel(g):
        return lambda kc: (wcb[:, g, kc, :], (wcb, g))

    def mm(ps_ap, lhsT, rhs, reads, ps, start=True, stop=True, inc=True):
        S.op("pe", lambda E: E.matmul(ps_ap, lhsT=lhsT, rhs=rhs, start=start, stop=stop),
             reads=reads, writes=[ps], inc=inc)

    def tt(e, out, a, b, op, reads, writes):
        S.op(e, lambda E: E.tensor_tensor(out=out, in0=a, in1=b, op=op), reads, writes)

    def act(out, in_, func, reads, writes, bias=None, scale=None):
        kw = {}
        if bias is not None:
            kw["bias"] = bias
        if scale is not None:
            kw["scale"] = scale
        S.op("act", lambda E: E.activation(out=out, in_=in_, func=func, **kw), reads, writes)

    def bd_write(seg, B, K, col0):
        tv, nco = seg.tv, seg.ncol
        for h in range(2):
            P = slice(h * 64, (h + 1) * 64)

            def dst(n):
                return B[n][P, :, h, 0:tv]

            def cv(t_):
                return seg.chv(t_[P, col0:col0 + nco])

            kk, E1, E2, E3, E4, tb, kf, zr, zv = (K[n] for n in ("kk", "E1", "E2", "E3", "E4", "tb", "kf", "zr", "zv"))
            S.op("dve", lambda E: E.scalar_tensor_tensor(out=dst("aT"), in0=cv(kk), scalar=-1.0, in1=cv(E3),
                                                         op0=ALU.mult, op1=ALU.mult), [kk, E3], [(B["aT"], h)])
            yield
            tt(ew_eng(), dst("bT"), cv(tb), cv(E2), ALU.mult, [tb, E2], [(B["bT"], h)])
            yield
            tt(ew_eng(), dst("bgT"), cv(tb), cv(E4), ALU.mult, [tb, E4], [(B["bgT"], h)])
            yield
            tt(ew_eng(), dst("kT"), cv(kf), cv(E2), ALU.mult, [kf, E2], [(B["kT"], h)])
            yield
            tt(ew_eng(), dst("kgT"), cv(kf), cv(E4), ALU.mult, [kf, E4], [(B["kgT"], h)])
            yield
            tt(ew_eng(), dst("rT"), cv(zr), cv(E1), ALU.mult, [zr, E1], [(B["rT"], h)])
            yield
            S.op("act", lambda E: E.copy(out=dst("vT"), in_=cv(zv)), [zv], [(B["vT"], h)])
            yield

    def stage1c(c, seg, shared):
        nco, kind, tv, nq, x0 = seg.ncol, seg.kind, seg.tv, seg.nq, seg.x0
        first = (kind == "p" and seg.idx == 0)
        B = bd[kind][seg.idx % 2]
        cc = slice(c * 128, (c + 1) * 128)
        ps = getps()
        mm(ps[:, 0:nco], w2b[0:64, cc], lora0[0:64, x0:x0 + nco], [(w2b, 0), (lora0, 0)], ps)
        yield
        sg = scr.get()
        act(sg[:, 0:nco], ps[:, 0:nco], AF.Tanh, [ps, (der, 4)], [sg], bias=der[:, 4, c:c + 1], scale=0.5)
        S.op("pool", lambda E: E.tensor_scalar(out=sg[:, 0:nco], in0=sg[:, 0:nco], scalar1=0.5, scalar2=0.5,
                                               op0=ALU.mult, op1=ALU.add), [sg], [sg])
        yield
        cs = scr.get()
        rmask = rmask_p if kind == "p" else rmask_s
        S.op("dve", lambda E: E.tensor_tensor_scan(out=cs[:, 0:nco], data0=rmask[:, 0:nco], data1=sg[:, 0:nco],
                                                   initial=0.0, op0=ALU.mult, op1=ALU.add),
             reads=[rmask, sg], writes=[cs])
        yield
        E1 = scr.get(); E2 = scr.get(); E3 = scr.get(); E4 = scr.get(); t0 = scr.get()
        act(E1[:, 0:nco], cs[:, 0:nco], AF.Exp, [cs], [E1], scale=-KAPPA)
        yield
        act(E2[:, 0:nco], cs[:, 0:nco], AF.Exp, [cs], [E2], scale=KAPPA)
        yield
        tt("pool", t0[:, 0:nco], cs[:, 0:nco], sg[:, 0:nco], ALU.subtract, [cs, sg], [t0])
        yield
        act(E3[:, 0:nco], t0[:, 0:nco], AF.Exp, [t0], [E3], scale=-KAPPA)
        yield
        csv = seg.chv(cs[:, 0:nco])
        t0v = seg.chv(t0[:, 0:nco])
        tt("dve", t0v, csv[:, :, tv - 1:tv].to_broadcast([128, nq, tv]), csv, ALU.subtract, [cs], [t0])
        yield
        act(seg.chv(E4[:, 0:nco]), t0v, AF.Exp, [t0], [E4], scale=-KAPPA)
        yield
        gam = gam_pool.get()
        act(gam[:, 0:nq].rearrange("p (q o) -> p q o", o=1), csv[:, :, tv - 1:tv], AF.Exp, [cs], [gam], scale=-KAPPA)
        yield
        scr.put(sg, cs, t0)
        ps = getps()
        mm(ps[:, 0:nco], w2b[64:128, cc], lora0[64:128, x0:x0 + nco], [(w2b, 1), (lora0, 1)], ps)
        yield
        a_ = scr.get()
        act(a_[:, 0:nco], ps[:, 0:nco], AF.Tanh, [ps, (der, 5)], [a_], bias=der[:, 5, c:c + 1], scale=0.5)
        S.op("pool", lambda E: E.tensor_scalar(out=a_[:, 0:nco], in0=a_[:, 0:nco], scalar1=0.5, scalar2=0.5,
                                               op0=ALU.mult, op1=ALU.add), [a_], [a_])
        yield
        ps = getps()
        mm(ps[:, 0:nco], g2b[:, cc], lora1[:, x0:x0 + nco], [g2b, lora1], ps)
        yield
        g_ = scr.get()
        S.op("act", lambda E: E.mul(out=g_[:, 0:nco], in_=ps[:, 0:nco], mul=0.5), [ps], [g_])
        yield
        shared.update(E1=E1, E2=E2, E3=E3, E4=E4, a_=a_, g_=g_, gam=gam)

    def stage1a(c, seg, xseg, shared):
        nco, kind, tv, nq, x0 = seg.ncol, seg.kind, seg.tv, seg.nq, seg.x0
        first = (kind == "p" and seg.idx == 0)
        B = bd[kind][seg.idx % 2]
        cc = slice(c * 128, (c + 1) * 128)
        zr = shifted(wsel(0), vcol(V_MUR, c), seg, zbuf[0], first, xseg)
        yield
        zk = shifted(wsel(1), vcol(V_MUK, c), seg, zbuf[1], first, xseg)
        yield
        zv = shifted(wsel(2), vcol(V_MUV, c), seg, zbuf[2], first, xseg)
        yield
        kkr = scr.get(); sq = scr.get(); kk = scr.get()
        S.op("dve", lambda E: E.tensor_scalar(out=kkr[:, 0:nco], in0=zk[:, 0:nco], scalar1=vcol(V_KK, c), scalar2=None,
                                              op0=ALU.mult), [zk, vecs], [kkr])
        yield
        tt("pool", sq[:, 0:nco], kkr[:, 0:nco], kkr[:, 0:nco], ALU.mult, [kkr], [sq])
        yield
        ps = getps()
        mm(ps[:, 0:nco], ones_bd[:], sq[:, 0:nco], [ones_bd, sq], ps)
        yield
        act(sq[:, 0:nco], ps[:, 0:nco], AF.Sqrt, [ps], [sq])
        yield
        S.op("dve", lambda E: E.tensor_scalar(out=sq[:, 0:nco], in0=sq[:, 0:nco], scalar1=1e-12, scalar2=None,
                                              op0=ALU.max), [sq], [sq])
        yield
        S.op("dve", lambda E: E.reciprocal(out=sq[:, 0:nco], in_=sq[:, 0:nco]), [sq], [sq])
        yield
        tt("pool", kk[:, 0:nco], kkr[:, 0:nco], sq[:, 0:nco], ALU.mult, [kkr, sq], [kk])
        yield
        yield "NEEDC"
        E1, E2, E3, E4, a_, g_, gam = (shared[k_] for k_ in ("E1", "E2", "E3", "E4", "a_", "g_", "gam"))
        t1 = scr.get(); kf = scr.get(); bonus = scr.get()
        S.op("dve", lambda E: E.tensor_scalar(out=t1[:, 0:nco], in0=a_[:, 0:nco], scalar1=vcol(V_KA, c),
                                              scalar2=der[:, 0, c:c + 1], op0=ALU.mult, op1=ALU.add),
             [a_, vecs, (der, 0)], [t1])
        yield
        tt("pool", kf[:, 0:nco], zk[:, 0:nco], t1[:, 0:nco], ALU.mult, [zk, t1], [kf])
        yield
        S.op("dve", lambda E: E.scalar_tensor_tensor(out=t1[:, 0:nco], in0=zr[:, 0:nco], scalar=vcol(V_RK, c),
                                                     in1=kf[:, 0:nco], op0=ALU.mult, op1=ALU.mult),
             [zr, kf, vecs, t1], [t1])
        yield
        ps = getps()
        mm(ps[:, 0:nco], ones_bd[:], t1[:, 0:nco], [ones_bd, t1], ps)
        yield
        tt("dve", bonus[:, 0:nco], ps[:, 0:nco], zv[:, 0:nco], ALU.mult, [ps, zv], [bonus])
        yield
        tb = kkr
        tt("pool", tb[:, 0:nco], kk[:, 0:nco], a_[:, 0:nco], ALU.mult, [kk, a_, kkr], [tb])
        yield
        keep = dict(kk=kk, E3=E3, tb=tb, E2=E2, E4=E4, kf=kf, zr=zr, E1=E1, zv=zv)
        if kind == "p":
            yield from bd_write(seg, B, keep, 0)
            scr.put(zr, zk, zv, E1, E2, E3, E4, a_, kkr, sq, kk, t1, kf)
        else:
            scr.put(zk, a_, sq, t1)
        return dict(bonus=bonus, g=g_, gam=gam, keep=keep)

    def stage1b(c, seg, xseg):
        nco, kind, tv, nq, x0 = seg.ncol, seg.kind, seg.tv, seg.nq, seg.x0
        first = (kind == "p" and seg.idx == 0)
        B = bd[kind][seg.idx % 2]
        cc = slice(c * 128, (c + 1) * 128)
        T_, nb = seg.T, seg.nb
        nt = nb * T_
        xbv = xbuf[:, 0:nb * (3 + T_)].rearrange("p (b t) -> p b t", t=3 + T_)

        def dv(t_):
            return t_[:, 0:nt].rearrange("p (b t) -> p b t", t=T_)

        ps = proj_ps(wsel(3), seg, xseg)
        yield
        if kind == "p":
            if seg.idx == 0:
                S.op("pool", lambda E: E.memset(xbv[:, :, 0:3], 0.0), writes=[xbuf])
                yield
            else:
                S.op("pool", lambda E: E.tensor_copy(out=xbv[:, 0, 0:3], in_=carry3[:, :]), [carry3], [xbuf])
                yield
        else:
            b0 = seg.idx * SB
            S.op("pool", lambda E: E.tensor_copy(
                out=xbv[:, :, 0:3], in_=stT[:, c, 0:48].rearrange("p (b j) -> p b j", j=3)[:, b0:b0 + SB, :]),
                [stT], [xbuf])
            yield
        S.op("act", lambda E: E.copy(out=xbv[:, :, 3:3 + T_], in_=seg.tokv(ps[:, 0:nco])), [ps, xbuf], [xbuf])
        yield
        if kind == "p":
            S.op("pool", lambda E: E.tensor_copy(out=carry3[:, :], in_=xbv[:, 0, T_:T_ + 3]), [xbuf], [carry3])
            yield
            if seg.idx == NSEGP - 1:
                S.op("pool", lambda E: E.tensor_copy(out=fin[:, c, 0:3], in_=xbv[:, 0, T_:T_ + 3]), [xbuf], [(fin, c)])
                yield
        else:
            S.op("pool", lambda E: E.tensor_copy(
                out=fin[:, c, 4 + 3 * b0:4 + 3 * (b0 + SB)].rearrange("p (b j) -> p b j", j=3),
                in_=xbv[:, :, T_:T_ + 3]), [xbuf], [(fin, c)])
            yield
        u = scr.get()
        uv = dv(u)
        S.op("dve", lambda E: E.tensor_scalar(out=uv, in0=xbv[:, :, 0:T_], scalar1=vcol(V_CW0, c),
                                              scalar2=vcol(V_CB, c), op0=ALU.mult, op1=ALU.add),
             [xbuf, vecs], [u])
        yield
        for j in range(1, 4):
            S.op("dve", lambda E, j=j: E.scalar_tensor_tensor(out=uv, in0=xbv[:, :, j:j + T_],
                                                              scalar=vcol(V_CW0 + j, c), in1=uv,
                                                              op0=ALU.mult, op1=ALU.add), [xbuf, vecs, u], [u])
            yield
        ubf = ubf_pool.get()
        S.op("act", lambda E: E.copy(out=ubf[:, 0:nt], in_=u[:, 0:nt]), [u], [ubf])
        yield
        rg = scr.get(); ig = scr.get(); al = scr.get(); e2 = scr.get(); hh = scr.get()
        ps = getps()
        mm(ps[:, 0:nt], wabb[:, 0, c, :], ubf[:, 0:nt], [(wabb, 0), ubf], ps)
        yield
        act(rg[:, 0:nt], ps[:, 0:nt], AF.Tanh, [ps, (der, 6)], [rg], bias=der[:, 6, c:c + 1], scale=0.5)
        yield
        ps = getps()
        mm(ps[:, 0:nt], wabb[:, 1, c, :], ubf[:, 0:nt], [(wabb, 1), ubf], ps)
        yield
        act(ig[:, 0:nt], ps[:, 0:nt], AF.Tanh, [ps, (der, 7)], [ig], bias=der[:, 7, c:c + 1], scale=0.5)
        yield
        ubf_pool.put(ubf)
        act(al[:, 0:nt], rg[:, 0:nt], AF.Exp, [rg, (der, 10)], [al], scale=der[:, 10, c:c + 1], bias=der[:, 10, c:c + 1])
        yield
        act(e2[:, 0:nt], rg[:, 0:nt], AF.Exp, [rg, (der, 11)], [e2], scale=der[:, 11, c:c + 1], bias=der[:, 11, c:c + 1])
        yield
        S.op("pool", lambda E: E.tensor_scalar(out=e2[:, 0:nt], in0=e2[:, 0:nt], scalar1=-0.25, scalar2=0.25,
                                               op0=ALU.mult, op1=ALU.add), [e2], [e2])
        yield
        act(e2[:, 0:nt], e2[:, 0:nt], AF.Sqrt, [e2], [e2])
        yield
        if first:
            S.op("pool", lambda E: E.memset(e2[:, 0:1], 0.5), [e2], [e2])
            yield
        S.op("dve", lambda E: E.scalar_tensor_tensor(out=ig[:, 0:nt], in0=ig[:, 0:nt], scalar=1.0, in1=e2[:, 0:nt],
                                                     op0=ALU.add, op1=ALU.mult), [ig, e2], [ig])
        yield
        tt("dve", ig[:, 0:nt], ig[:, 0:nt], u[:, 0:nt], ALU.mult, [ig, u], [ig])
        yield
        if kind == "p":
            init = 0.0 if seg.idx == 0 else hcar[:, 0:1]
            S.op("dve", lambda E: E.tensor_tensor_scan(out=hh[:, 0:nt], data0=al[:, 0:nt], data1=ig[:, 0:nt],
                                                       initial=init, op0=ALU.mult, op1=ALU.add),
                 [al, ig, hcar], [hh])
            yield
            S.op("pool", lambda E: E.tensor_copy(out=hcar[:, 0:1], in_=hh[:, nt - 1:nt]), [hh], [hcar])
            yield
            if seg.idx == NSEGP - 1:
                S.op("pool", lambda E: E.tensor_copy(out=fin[:, c, 3:4], in_=hh[:, nt - 1:nt]), [hh], [(fin, c)])
                yield
        else:
            for b in range(nb):
                S.op("dve", lambda E, b=b: E.tensor_tensor_scan(
                    out=hh[:, b * T_:(b + 1) * T_], data0=al[:, b * T_:(b + 1) * T_], data1=ig[:, b * T_:(b + 1) * T_],
                    initial=stT[:, c, 48 + b0 + b:48 + b0 + b + 1], op0=ALU.mult, op1=ALU.add),
                    [al, ig, stT, hh], [hh])
                yield
            S.op("pool", lambda E: E.tensor_copy(out=fin[:, c, 52 + b0:52 + b0 + SB].rearrange("p (b o) -> p b o", o=1),
                                                 in_=dv(hh)[:, :, T_ - 1:T_]), [hh], [(fin, c)])
            yield
        scr.put(rg, al, e2, u, ig)
        ps = proj_ps(wsel(4), seg, xseg)
        yield
        gbs = scr.get(); p_ = scr.get()
        S.op("act", lambda E: E.copy(out=dv(gbs), in_=seg.tokv(ps[:, 0:nco])), [ps], [gbs])
        yield
        tt("pool", p_[:, 0:nt], gbs[:, 0:nt], gbs[:, 0:nt], ALU.mult, [gbs], [p_])
        yield
        S.op("pool", lambda E: E.tensor_scalar(out=p_[:, 0:nt], in0=p_[:, 0:nt], scalar1=0.044715, scalar2=1.0,
                                               op0=ALU.mult, op1=ALU.add), [p_], [p_])
        yield
        tt("pool", p_[:, 0:nt], p_[:, 0:nt], gbs[:, 0:nt], ALU.mult, [p_, gbs], [p_])
        yield
        act(p_[:, 0:nt], p_[:, 0:nt], AF.Tanh, [p_], [p_], scale=0.7978845608028654)
        yield
        S.op("dve", lambda E: E.scalar_tensor_tensor(out=gbs[:, 0:nt], in0=p_[:, 0:nt], scalar=1.0, in1=gbs[:, 0:nt],
                                                     op0=ALU.add, op1=ALU.mult), [gbs, p_], [gbs])
        yield
        tt("dve", hh[:, 0:nt], hh[:, 0:nt], gbs[:, 0:nt], ALU.mult, [hh, gbs], [hh])
        yield
        scr.put(gbs)
        ps = proj_ps(wsel(6), seg, xseg)
        yield
        act(dv(p_), seg.tokv(ps[:, 0:nco]), AF.Tanh, [ps], [p_], scale=0.5)
        yield
        S.op("dve", lambda E: E.scalar_tensor_tensor(out=hh[:, 0:nt], in0=p_[:, 0:nt], scalar=1.0, in1=hh[:, 0:nt],
                                                     op0=ALU.add, op1=ALU.mult), [hh, p_], [hh])
        yield
        scr.put(p_)
        ps = proj_ps(wsel(5), seg, xseg)
        yield
        gA = scr.get()
        act(dv(gA), seg.tokv(ps[:, 0:nco]), AF.Tanh, [ps], [gA], scale=0.5)
        yield
        return dict(gA=gA, m2=hh)


    def stage1(c, seg):
        xseg = get_x(seg)
        shared = {}
        gens = [stage1a(c, seg, xseg, shared), stage1b(c, seg, xseg), stage1c(c, seg, shared)]
        who = ["s1a", "s1b", "s1c"]
        res = [None, None, None]
        done = [False, False, False]
        hold_a = False
        while not all(done):
            for i_ in range(3):
                if done[i_]:
                    continue
                if i_ == 0 and hold_a:
                    if not done[2]:
                        continue
                    hold_a = False
                cur_alloc["who"] = who[i_]
                try:
                    v_ = next(gens[i_])
                    if v_ == "NEEDC":
                        hold_a = True
                except StopIteration as e_:
                    res[i_] = e_.value
                    done[i_] = True
                yield
        xs_pool.put(xseg)
        res[0].update(res[1])
        return res[0]

    def stage2(c, seg, s1, chain):
        kind, tv, nq, nlev, nco = seg.kind, seg.tv, seg.nq, seg.nlev, seg.ncol
        B = bd[kind][seg.idx % 2]
        gam = s1["gam"]

        def bdv(n):
            return B[n][:].rearrange("p q h t -> p q (h t)")

        def br(n):
            return [(B[n], 0), (B[n], 1)]

        def q128(ps):
            return ps[:].rearrange("p (q t) -> p q t", t=128)

        def q64(ps):
            return ps[:, 0:QB * 64].rearrange("p (q t) -> p q t", t=64)

        def prod(l, r, mask):
            ps = getps()
            for q in range(QB):
                mm(ps[:, q * 128:(q + 1) * 128], bdv(l)[:, q, :], bdv(r)[:, q, :], br(l) + br(r), ps, inc=(q == QB - 1))
            o = wk.get()
            tt("dve", o[:], q128(ps), mask[:], ALU.mult, [ps, mask], [o])
            return o

        N_ = prod("bT", "aT", mUs)
        yield
        L_ = prod("aT", "bT", mLs)
        yield
        Mak = prod("kT", "aT", mUs)
        yield
        Mrb = prod("bT", "rT", mUi)
        yield
        Mrk = prod("kT", "rT", mUi)
        yield
        rTc = wk.get()
        S.op("pool", lambda E: E.tensor_copy(out=rTc[:], in_=bdv("rT")), br("rT"), [rTc])
        yield
        ck("k1")

        def tr(n):
            ps = getps()
            psb = ps[:].bitcast(BF16)
            for q in range(QB):
                S.op("pe", lambda E, q=q: E.transpose(out=psb[:, q * 128:(q + 1) * 128], in_=bdv(n)[:, q, :],
                                                      identity=ident_b[:]), br(n) + [ident_b], [ps], inc=(q == QB - 1))
            o = wk.get()
            copy_op(ev_eng(), o[:], psb[:, 0:QB * 128].rearrange("p (q t) -> p q t", t=128), [ps], [o])
            return o

        XA = tr("aT")
        yield
        Bg = tr("bgT")
        yield
        Kg = tr("kgT")
        yield
        ck("k2")
        ps = getps()
        for q in range(QB):
            mm(ps[:, q * 64:(q + 1) * 64], bdv("vT")[:, q, :], II_b[:], br("vT") + [II_b], ps, inc=(q == QB - 1))
            yield
        V_ = wk.get()
        copy_op(ev_eng(), V_[:, :, 0:64], q64(ps), [ps], [V_])
        yield ("BDFREE", (c, kind, seg.idx))
        yield
        ps = getps()
        for q in range(QB):
            mm(ps[:, q * 64:(q + 1) * 64], Mak[:, q, :], V_[:, q, 0:64], [Mak, V_], ps, inc=(q == QB - 1))
            yield
        XU = wk.get()
        copy_op(ev_eng(), XU[:, :, 0:64], q64(ps), [ps], [XU])
        yield
        wk.put(Mak)
        ck("k3")
        Nc, Lc = N_, L_
        for lev in range(nlev):
            psA = getps()
            for q in range(QB):
                mm(psA[:, q * 128:(q + 1) * 128], Nc[:, q, :], XA[:, q, :], [Nc, XA], psA, start=True, stop=False, inc=False)
                mm(psA[:, q * 128:(q + 1) * 128], ident_b[:], XA[:, q, :], [ident_b, XA], psA, start=False, stop=True,
                   inc=(q == QB - 1))
            yield
            psU = getps()
            for q in range(QB):
                mm(psU[:, q * 64:(q + 1) * 64], Nc[:, q, :], XU[:, q, 0:64], [Nc, XU], psU, start=True, stop=False, inc=False)
                mm(psU[:, q * 64:(q + 1) * 64], ident_b[:], XU[:, q, 0:64], [ident_b, XU], psU, start=False, stop=True,
                   inc=(q == QB - 1))
            yield
            XA2 = wk.get(); XU2 = wk.get()
            copy_op("act", XA2[:], q128(psA), [psA], [XA2])
            yield
            copy_op("act", XU2[:, :, 0:64], q64(psU), [psU], [XU2])
            yield
            N2 = L2 = None
            if lev < nlev - 1:
                psN = getps()
                for q in range(QB):
                    mm(psN[:, q * 128:(q + 1) * 128], Lc[:, q, :], Nc[:, q, :], [Lc, Nc], psN, inc=(q == QB - 1))
                    yield
                N2 = wk.get()
                copy_op("act", N2[:], q128(psN), [psN], [N2])
                yield
                if lev < nlev - 2:
                    psL = getps()
                    for q in range(QB):
                        mm(psL[:, q * 128:(q + 1) * 128], Nc[:, q, :], Lc[:, q, :], [Lc, Nc], psL, inc=(q == QB - 1))
                        yield
                    L2 = wk.get()
                    copy_op("act", L2[:], q128(psL), [psL], [L2])
                    yield
            wk.put(XA, XU, Nc, Lc)
            XA, XU, Nc, Lc = XA2, XU2, N2, L2
            if Nc is None:
                Nc = wk.get()
            if Lc is None:
                Lc = wk.get()
        wk.put(Nc, Lc)
        ck("k4")
        psR = getps()
        for q in range(QB):
            mm(psR[:, q * 128:(q + 1) * 128], XA[:, q, :], Mrb[:, q, :], [XA, Mrb], psR, start=True, stop=False, inc=False)
            yield
            mm(psR[:, q * 128:(q + 1) * 128], ident_b[:], rTc[:, q, :], [ident_b, rTc], psR, start=False,
               stop=True, inc=(q == QB - 1))
            yield
        Rh = wk.get()
        copy_op(ev_eng(), Rh[:], q128(psR), [psR], [Rh])
        yield
        psG = getps()
        for q in range(QB):
            mm(psG[:, q * 128:(q + 1) * 128], XA[:, q, :], Bg[:, q, :], [XA, Bg], psG, inc=(q == QB - 1))
            yield
        GT = wk.get()
        copy_op(ev_eng(), GT[:], q128(psG), [psG], [GT])
        yield
        ck("k5")
        if kind == "p":
            yield ("CHAIN", (c, seg.idx - 1))
            chain["init"]()
        psH = getps()
        if kind == "p":
            hA, hB = chain["hA"], chain["hB"]
            for q in range(QB):
                sl = slice(q * 64, (q + 1) * 64)
                mm(psH[:, sl], Bg[:, q, :], XU[:, q, 0:64], [Bg, XU], psH, start=True, stop=False, inc=False)
                yield
                mm(psH[:, sl], Kg[:, q, :], V_[:, q, 0:64], [Kg, V_], psH, start=False, stop=False, inc=False)
                yield
                mm(psH[:, sl], GT[:, q, :], hB[:, q, :], [GT, (hB, q)], psH, start=False, stop=True)
                yield
                S.op("dve", lambda E, q=q, sl=sl: E.scalar_tensor_tensor(
                    out=hA[:, q + 1, :], in0=hA[:, q, :], scalar=gam[:, q:q + 1], in1=psH[:, sl],
                    op0=ALU.mult, op1=ALU.add), [(hA, q), gam, psH], [(hA, q + 1)])
                yield
                S.op("act", lambda E, q=q: E.copy(out=hB[:, q + 1, :], in_=hA[:, q + 1, :]), [(hA, q + 1)], [(hB, q + 1)])
                yield
            hBr = [(hB, q) for q in range(QB)]
        else:
            hA, hB, hO = chain["hA"], chain["hB"], chain["hO"]
            for q in range(QB):
                sl = slice(q * 64, (q + 1) * 64)
                mm(psH[:, sl], Bg[:, q, :], XU[:, q, 0:64], [Bg, XU], psH, start=True, stop=False, inc=False)
                yield
                mm(psH[:, sl], Kg[:, q, :], V_[:, q, 0:64], [Kg, V_], psH, start=False, stop=False, inc=False)
                yield
                mm(psH[:, sl], GT[:, q, :], hB[:, q, :], [GT, (hB, q)], psH, start=False, stop=True,
                   inc=(q == QB - 1))
                yield
            tmp = scr.get()
            tmpv = tmp[:, 0:QB * 64].rearrange("p (q t) -> p q t", t=64)
            tt("dve", tmpv, hA[:, 0:QB, :], gam[:].rearrange("p (q o) -> p q o", o=1).to_broadcast([128, QB, 64]),
               ALU.mult, [(hA, q) for q in range(QB)] + [gam], [tmp])
            yield
            tt("dve", hO[:, 0:QB, :], tmpv, q64(psH), ALU.add, [tmp, psH], [hO])
            yield
            scr.put(tmp)
            hBr = [(hB, q) for q in range(QB)]
        ck("k6")
        psY = getps()
        for q in range(QB):
            sl = slice(q * 64, (q + 1) * 64)
            mm(psY[:, sl], Mrb[:, q, :], XU[:, q, 0:64], [Mrb, XU], psY, start=True, stop=False, inc=False)
            yield
            mm(psY[:, sl], Mrk[:, q, :], V_[:, q, 0:64], [Mrk, V_], psY, start=False, stop=False, inc=False)
            yield
            mm(psY[:, sl], Rh[:, q, :], hB[:, q, :], [Rh, (hB, q)], psY, start=False, stop=True, inc=(q == QB - 1))
            yield
        wk.put(Mrb, Mrk, XA, XU, Bg, Kg, V_, Rh, GT, rTc)
        if not isinstance(gam, Off):
            gam_pool.put(gam)
        ck("k7")
        ysb = scr.get()
        ysbv = ysb[:, 0:QB * 64].rearrange("p (q t) -> p q t", t=64)
        S.op("act", lambda E: E.copy(out=ysbv, in_=q64(psY)), [psY], [ysb])
        yield ("TAIL", (c, seg.idx) if kind == "p" else None)
        stq = lambda i: stat[:, i, :]
        ysq = scr.get()
        ysqv = ysq[:, 0:QB * 64].rearrange("p (q t) -> p q t", t=64)
        for q in range(QB):
            S.op("dve", lambda E, q=q: E.bn_stats(out=bst4[:, q, :], in_=ysb[:, q * 64:(q + 1) * 64]), [ysb], [(bst4, q)])
        for q in range(QB):
            S.op("dve", lambda E, q=q: E.bn_aggr(out=mvq[:, q, :], in_=bst4[:, q, :]), [(bst4, q)], [(mvq, q)])
        yield
        mvr = [(mvq, q) for q in range(QB)]
        S.op("pool", lambda E: E.tensor_scalar(out=stq(4), in0=mvq[:, :, 1], scalar1=1.0, scalar2=GN_EPS,
                                               op0=ALU.mult, op1=ALU.add), mvr, [(stat, 4)])
        S.op("pool", lambda E: E.tensor_tensor(out=stq(5), in0=stq(4), in1=negh[:], op=ALU.pow),
             [(stat, 4), negh], [(stat, 5)])
        yield
        tt("dve", ysqv, ysbv, mvq[:, :, 0:1].to_broadcast([128, QB, 64]), ALU.subtract, [ysb] + mvr, [ysq])
        scr.put(ysb)
        yield
        for h in range(2):
            P = slice(h * 64, (h + 1) * 64)
            tt(ew_eng(), ybd[P, :, h, :], ysqv[P], stat[P, 5, :].rearrange("p (q o) -> p q o", o=1).to_broadcast([64, QB, 64]),
               ALU.mult, [ysq, (stat, 5)], [(ybd, h)])
            yield
        scr.put(ysq)
        ck("k8")
        psT = getps()
        ybv = ybd[:].rearrange("p q h t -> p q (h t)")
        for q in range(QB):
            mm(psT[:, q * 64:(q + 1) * 64], ybv[:, q, :], II_f[:], [(ybd, 0), (ybd, 1), II_f], psT, inc=(q == QB - 1))
            yield
        ck("k9")
        T_, nb = seg.T, seg.nb
        nt = nb * T_

        def dv(t_):
            return t_[:, 0:nt].rearrange("p (b t) -> p b t", t=T_)

        if kind == "p":
            ysrc = psT[:, 0:QB * 64].rearrange("p (b t) -> p b t", b=1)
        else:
            ysrc = q64(psT)[:, :, 0:DEC]
        yo = scr.get()
        act(dv(yo), ysrc, AF.Identity, [psT, vecs], [yo], bias=vcol(V_LNXB, c), scale=vcol(V_LNXG, c))
        yield
        tt("dve", dv(yo), dv(yo), seg.tokv(s1["bonus"][:, 0:nco]), ALU.add, [yo, s1["bonus"]], [yo])
        yield
        tt("pool", dv(yo), dv(yo), seg.tokv(s1["g"][:, 0:nco]), ALU.mult, [yo, s1["g"]], [yo])
        yield
        S.op("dve", lambda E: E.scalar_tensor_tensor(out=dv(yo), in0=dv(s1["gA"]), scalar=1.0, in1=dv(yo),
                                                     op0=ALU.add, op1=ALU.mult), [yo, s1["gA"]], [yo])
        yield
        mv = merged[:, c, seg.m0:seg.m0 + seg.mcols].rearrange("p (b t) -> p b t", t=T_)
        S.op("dve", lambda E: E.scalar_tensor_tensor(out=mv, in0=dv(s1["m2"]), scalar=0.25, in1=dv(yo),
                                                     op0=ALU.mult, op1=ALU.add), [yo, s1["m2"]], [(merged, c)])
        yield
        scr.put(yo)
        if not isinstance(s1["bonus"], Off):
            scr.put(s1["bonus"], s1["g"], s1["gA"], s1["m2"])

    def state_out(c, hsrc_ap, hres, dst_ap):
        for h in range(2):
            P = slice(h * 64, (h + 1) * 64)
            S.op(ew_eng(), lambda E: E.tensor_copy(out=hbd[P, h, :], in_=hsrc_ap[P, :]), [hres], [(hbd, h)])
        ps = getps()
        mm(ps[:, 0:64], hbd[:].rearrange("p h t -> p (h t)"), II_f[:], [(hbd, 0), (hbd, 1), II_f], ps)
        so = scr.get()
        copy_op(ev_eng(), so[:, 0:64], ps[:, 0:64], [ps], [so])
        S.dma(dst_ap, so[:, 0:64], reads=[so], q="act")
        scr.put(so)

    for g in range(3):
        S.op("pool", lambda E, g=g: E.memset(zbuf[g][:], 0.0), writes=[zbuf[g]])

    def s2full(c, seg, s1, par):
        if seg.kind == "p":
            hA, hB = H32[par], Hbf[par]

            def chain_init():
                if seg.idx == 0:
                    S.op("pool", lambda E: E.memset(hA[:, 0, :], 0.0), writes=[(hA, 0)])
                    S.op("pool", lambda E: E.memset(hB[:, 0, :], 0.0), writes=[(hB, 0)])
                else:
                    pA, pB = H32[1 - par], Hbf[1 - par]
                    S.op("pool", lambda E: E.tensor_copy(out=hA[:, 0, :], in_=pA[:, QB, :]), [(pA, QB)], [(hA, 0)])
                    S.op("pool", lambda E: E.tensor_copy(out=hB[:, 0, :], in_=pB[:, QB, :]), [(pB, QB)], [(hB, 0)])

            yield from stage2(c, seg, s1, dict(hA=hA, hB=hB, init=chain_init))
            if seg.idx == NSEGP - 1:
                state_out(c, hA[:, QB, :], (hA, QB),
                          o_wkv_p[2 * c:2 * c + 2, :, :].rearrange("h i j -> (h i) j"))
                yield
        else:
            sp_ = seg.idx % 2
            hA, hB, hO = H32s[sp_][0], Hbfs[sp_], H32s[sp_][1]
            b0 = seg.idx * QB
            for q in range(QB):
                sin = scr.get()
                S.dma(sin[:, 0:64], s0[b0 + q, 2 * c:2 * c + 2, :, :].rearrange("h i j -> (h i) j"), writes=[sin])
                for h in range(2):
                    P = slice(h * 64, (h + 1) * 64)
                    S.op(ew_eng(), lambda E: E.tensor_copy(out=sbd[P, h, :], in_=sin[P, 0:64]), [sin], [(sbd, h)])
                ps = getps()
                mm(ps[:, 0:64], sbd[:].rearrange("p h t -> p (h t)"), II_f[:], [(sbd, 0), (sbd, 1), II_f], ps)
                S.op("act", lambda E: E.copy(out=hA[:, q, :], in_=ps[:, 0:64]), [ps], [(hA, q)])
                S.op("dve", lambda E: E.tensor_copy(out=hB[:, q, :], in_=ps[:, 0:64]), [ps], [(hB, q)])
                scr.put(sin)
                yield
            yield from stage2(c, seg, s1, dict(hA=hA, hB=hB, hO=hO))
            for q in range(QB):
                state_out(c, hO[:, q, :], hO,
                          o_wkv_s[b0 + q, 2 * c:2 * c + 2, :, :].rearrange("h i j -> (h i) j"))
                yield

    NSTREAM = 2
    mains = []
    tails = []
    SID = ("A", "B")

    chain_done = set()
    bd_free = set()

    def step_main(m):
        if m[2] == "TAILWAIT":
            if tails:
                return True
            mains.remove(m)
            tails.append(m)
            return False
        if m[2] is not None:
            if m[2][1] >= 0 and m[2] not in chain_done:
                return True
            m[2] = None
        cur_alloc["who"] = "s2" + SID[m[1]]
        try:
            v = next(m[0])
        except StopIteration:
            mains.remove(m)
            return False
        if isinstance(v, tuple) and v[0] == "BDFREE":
            bd_free.add(v[1])
            return True
        if isinstance(v, tuple) and v[0] == "CHAIN":
            m[2] = v[1]
            return True
        if isinstance(v, tuple) and v[0] == "TAIL":
            if v[1] is not None:
                chain_done.add(v[1])
            if tails:
                m[2] = "TAILWAIT"
                return True
            mains.remove(m)
            tails.append(m)
            return False
        return True

    def step_tails():
        for m in list(tails):
            cur_alloc["who"] = "tail" + SID[m[1]]
            try:
                next(m[0])
            except StopIteration:
                tails.remove(m)

    def step_bg():
        for m in list(mains):
            step_main(m)
        step_tails()

    def run_stage1(g1):
        acc = 0.0
        while True:
            step_bg()
            acc += RSCALE if (mains or tails) else 8.0
            while acc >= 1.0:
                acc -= 1.0
                cur_alloc["who"] = "s1"
                try:
                    next(g1)
                except StopIteration as e_:
                    cur_alloc["who"] = None
                    return e_.value

    def start_main(gen):
        while len(mains) >= NSTREAM:
            step_bg()
        used = {m[1] for m in mains} | {m[1] for m in tails}
        while len(used) >= 2:
            step_bg()
            used = {m[1] for m in mains} | {m[1] for m in tails}
        sid = 0 if 0 not in used else 1
        mains.append([gen, sid, None])

    def drain_all():
        while mains or tails:
            step_bg()
        cur_alloc["who"] = None

    def load_weights(c):
        for g in range(7):
            stg = stage[g % 3]
            S.dma(stg[:], w1[c, :, g, :, :], writes=[stg])
            S.op("pool", lambda E, g=g, stg=stg: E.tensor_copy(out=wcb[:, g, :, :], in_=stg[:]), [stg], [(wcb, g)])

    def finalize_after(gen, fn):
        yield from gen
        fn()

    try:
        load_weights(0)
        for c in range(NCH):
            par = 0
            for segi, seg in enumerate(segs):
                if stop_after == "seg%d" % segi:
                    raise StopBuild()
                cur["segi"] = segi
                s1 = run_stage1(stage1(c, seg))
                if seg.kind == "p":
                    start_main(s2full(c, seg, s1, par))
                    par = 1 - par
                else:
                    if c + 1 < NCH and stop_after != "c0":
                        load_weights(c + 1)
                    nbatch = NSB // QB
                    for b in range(nbatch):
                        sub = SubSeg(b)
                        while b >= 2 and (c, "s", b - 2) not in bd_free:
                            step_bg()
                        cur_alloc["who"] = "s1"
                        for _ in bd_write(sub, bd["s"][b % 2], s1["keep"], b * QB * 5):
                            pass
                        s1b = dict(bonus=Off(s1["bonus"], b * QB * 5, QB * 5), g=Off(s1["g"], b * QB * 5, QB * 5),
                                   gA=Off(s1["gA"], b * QB * DEC, QB * DEC), m2=Off(s1["m2"], b * QB * DEC, QB * DEC),
                                   gam=Off(s1["gam"], b * QB, QB))
                        gen = s2full(c, sub, s1b, 0)
                        if b == nbatch - 1:
                            K_ = s1["keep"]
                            scr.put(K_["kk"], K_["E3"], K_["tb"], K_["E2"], K_["E4"], K_["kf"], K_["zr"], K_["E1"], K_["zv"])

                            def fin_(s1=s1):
                                scr.put(s1["bonus"], s1["g"], s1["gA"], s1["m2"])
                                gam_pool.put(s1["gam"])

                            gen = finalize_after(gen, fin_)
                        start_main(gen)
            if stop_after == "c0":
                break
        drain_all()
    except StopBuild:
        pass
    dump("merged", merged[:, 0, :], [(merged, 0)])
    dump("merged7", merged[:, 7, :], [(merged, 7)])
    dump("fin", fin[:, 0, :], [(fin, 0)])

    print("nops", S.nops, "cnt", {k: v for k, v in S.cnt.items() if not k.startswith("d")})
    if stop_after is not None:
        S.finish()
        p1.close()
        es.close()
        return nc
    psF = [getps(), getps()]
    for c in range(NCH):
        hf, k4 = divmod(c, 4)
        S.op("pe", lambda E: E.transpose(out=psF[hf][0:NFIN, k4 * 128:(k4 + 1) * 128], in_=fin[:, c, :],
                                         identity=ident_f[:]), [(fin, c), ident_f], [psF[hf]])
    S.barrier()
    p1.close()
    p2 = ExitStack()
    lnvs = sb("lnvs", [128, 2, D], F32, p2)
    h1T = sb("h1T", [128, 8, MCOLS], BF16, p2)
    stg2 = [sb("stg2_%d" % i, [128, D], F32, p2) for i in range(3)]
    big = [sb("big%d" % i, [128, D], F32, p2) for i in range(4)]
    bst = sb("bst", [128, 2, 6], F32, p2)
    mv_ = sb("mv_", [128, 8], F32, p2)
    aI = sb("aI", [128, 2, 128], BF16, p2)
    finT = sb("finT", [NFIN, D], F32, p2)
    p2a = ExitStack()
    wo_b = sb("wo_b", [128, 8, D], BF16, p2a)
    a_hi = float(np.float32(ALPHA).astype(ml_bf16).astype(np.float32))
    a_lo = float(np.float32(ALPHA - a_hi).astype(ml_bf16).astype(np.float32))
    S.op("pool", lambda E: E.tensor_scalar(out=aI[:, 0, :], in0=ident_f[:], scalar1=a_hi, scalar2=0.0,
                                           op0=ALU.mult, op1=ALU.add), [ident_f], [(aI, 0)])
    S.op("pool", lambda E: E.tensor_scalar(out=aI[:, 1, :], in0=ident_f[:], scalar1=a_lo, scalar2=0.0,
                                           op0=ALU.mult, op1=ALU.add), [ident_f], [(aI, 1)])
    for hf in range(2):
        copy_op(ev_eng(), finT[:, hf * 512:(hf + 1) * 512], psF[hf][0:NFIN, :], [psF[hf]], [finT])
    S.dma(o_fin[:, :], finT[:, :], reads=[finT])
    S.dma(o_shift[0:1, :], xp[SEQ - 1:SEQ, :])
    S.dma(o_shift[1:1 + NSB, :], xst.rearrange("(b t) d -> b t d", t=DEC)[:, DEC - 1, :])
    S.dma(lnvs[:], lnv[:, 0:2, :], writes=[lnvs])
    for kc in range(8):
        stg = stg2[kc % 3]
        S.dma(stg[:], wo[:, kc, :], writes=[stg])
        S.op("pool", lambda E, kc=kc, stg=stg: E.tensor_copy(out=wo_b[:, kc, :], in_=stg[:]), [stg], [(wo_b, kc)])

    def layer_norm(src, dst, gi, rows):
        R = slice(0, rows)
        for j in range(2):
            S.op("dve", lambda E, j=j: E.bn_stats(out=bst[R, j, :], in_=src[R, j * 512:(j + 1) * 512]), [src], [bst])
        S.op("dve", lambda E: E.bn_aggr(out=mv_[R, 0:2], in_=bst[R, :, :]), [bst], [(mv_, 0)])
        S.op("dve", lambda E: E.tensor_scalar(out=mv_[R, 2:3], in0=mv_[R, 1:2], scalar1=LN_EPS, scalar2=None,
                                              op0=ALU.add), [(mv_, 0)], [(mv_, 2)])
        act(mv_[R, 3:4], mv_[R, 2:3], AF.Sqrt, [(mv_, 2)], [(mv_, 3)])
        S.op("dve", lambda E: E.reciprocal(out=mv_[R, 4:5], in_=mv_[R, 3:4]), [(mv_, 3)], [(mv_, 4)])
        S.op("dve", lambda E: E.scalar_tensor_tensor(out=mv_[R, 5:6], in0=mv_[R, 0:1], scalar=-1.0, in1=mv_[R, 4:5],
                                                     op0=ALU.mult, op1=ALU.mult), [(mv_, 0), (mv_, 4)], [(mv_, 5)])
        act(dst[R, :], src[R, :], AF.Identity, [src, (mv_, 4), (mv_, 5)], [dst], bias=mv_[R, 5:6], scale=mv_[R, 4:5])
        tt("pool", dst[R, :], dst[R, :], lnvs[R, 0, :], ALU.mult, [dst, lnvs], [dst])
        tt("dve", dst[R, :], dst[R, :], lnvs[R, 1, :], ALU.add, [dst, lnvs], [dst])

    ttiles = [(t * 128, 128, yp[t * 128:(t + 1) * 128, :], xp[t * 128:(t + 1) * 128, :]) for t in range(SEQ // 128)]
    ttiles.append((SEQ, NSB * DEC, ys[:, :], xst[:, :]))

    for ti, (m0, rows, _, xrows) in enumerate(ttiles):
        R = slice(0, rows)
        xtm = big[ti % 2]
        s1t = big[2 + ti % 2]
        S.dma(xtm[R, :], xrows, writes=[xtm])
        pss = [getps(), getps()]
        for hf in range(2):
            for kc in range(8):
                mm(pss[hf][R, :], merged[:, kc, m0:m0 + rows], wo_b[:, kc, hf * 512:(hf + 1) * 512],
                   [(merged, kc), (wo_b, kc)], pss[hf], start=(kc == 0), stop=(kc == 7), inc=(kc == 7))
        for hf in range(2):
            S.op("dve", lambda E, hf=hf: E.scalar_tensor_tensor(
                out=s1t[R, hf * 512:(hf + 1) * 512], in0=xtm[R, hf * 512:(hf + 1) * 512], scalar=ALPHA,
                in1=pss[hf][R, :], op0=ALU.mult, op1=ALU.add), [xtm, pss[hf]], [s1t])
        layer_norm(s1t, xtm, 0, rows)
        pst = [getps(), getps()]
        for kc in range(8):
            hf, k4 = divmod(kc, 4)
            S.op("pe", lambda E, hf=hf, k4=k4, kc=kc: E.transpose(
                out=pst[hf][:, k4 * 128:k4 * 128 + rows], in_=xtm[R, kc * 128:(kc + 1) * 128],
                identity=ident_f[R, R]), [xtm, ident_f], [pst[hf]], inc=(k4 == 3))
        for hf in range(2):
            copy_op(ev_eng(), h1T[:, hf * 4:(hf + 1) * 4, m0:m0 + rows],
                    pst[hf][:].rearrange("p (k t) -> p k t", t=128)[:, :, 0:rows], [pst[hf]],
                    [(h1T, hf * 4 + k) for k in range(4)])
    if "h1T" in dbg_out:
        S.dma(dbg_out["h1T"], h1T[:, 0, :], reads=[(h1T, 0)])

    S.barrier()
    p2a.close()
    S.dma(lnvs[:], lnv[:, 2:4, :], writes=[lnvs])
    wd_b = sb("wd_b", [128, NFC, D], BF16, p2)
    NTH = 1088
    NFA = 15
    uTa = T(merged[:].rearrange("p a b -> p (a b)")[:, 0:NFA * NTH].rearrange("p (f t) -> p f t", t=NTH), "uTa")
    uTb = sb("uTb", [128, NFC - NFA, NTH], BF16, p2)

    class _UT:
        def __getitem__(self, k):
            p_, fc_, cols_ = k
            if fc_ < NFA:
                return uTa.h[p_, fc_, cols_]
            return uTb.h[p_, fc_ - NFA, cols_]

    uT = _UT()
    wgub = [sb("wgub%d" % i, [128, 2, 8, 128], BF16, p2) for i in range(2)]
    sgt = [sb("sgt%d" % i, [128, 512], F32, p2) for i in range(2)]
    for fc in range(NFC):
        stg = stg2[fc % 3]
        S.dma(stg[:], wd[fc, :, :], writes=[stg])
        S.op("pool", lambda E, fc=fc, stg=stg: E.tensor_copy(out=wd_b[:, fc, :], in_=stg[:]), [stg], [(wd_b, fc)])
    halves = [(0, 1024, ttiles[0:8]), (1024, 1088, ttiles[8:17])]
    for (c0, ncols, tls) in halves:
        blocks = [(b0_, min(512, ncols - b0_)) for b0_ in range(0, ncols, 512)]
        for fc in range(NFC):
            wb = wgub[fc % 2]
            for j in range(2):
                stg = stg2[(2 * fc + j) % 3]
                S.dma(stg[:].rearrange("p (a b) -> p a b", b=128), wgu[fc, :, j, :, :], writes=[stg])
                S.op("pool", lambda E, j=j, stg=stg: E.tensor_copy(
                    out=wb[:, j, :, :], in_=stg[:].rearrange("p (a b) -> p a b", b=128)), [stg], [(wb, j)])
            for bi, (b0_, bn) in enumerate(blocks):
                psg = getps()
                psu = getps()
                for j, ps_ in ((0, psg), (1, psu)):
                    for kc in range(8):
                        mm(ps_[:, 0:bn], wb[:, j, kc, :], h1T[:, kc, c0 + b0_:c0 + b0_ + bn], [(wb, j), (h1T, kc)], ps_,
                           start=(kc == 0), stop=(kc == 7), inc=(kc == 7))
                sg_ = sgt[(fc * 3 + bi) % 2]
                act(sg_[:, 0:bn], psg[:, 0:bn], AF.Silu, [psg], [sg_])
                tt("dve", uT[:, fc, b0_:b0_ + bn], psu[:, 0:bn], sg_[:, 0:bn], ALU.mult, [psu, sg_], [(uT, fc)])
        for ti, (m0, rows, yout, _) in enumerate(tls):
            R = slice(0, rows)
            l0 = m0 - c0
            pss = [getps(), getps()]
            for hf in range(2):
                for fc in range(NFC):
                    mm(pss[hf][R, :], uT[:, fc, l0:l0 + rows], wd_b[:, fc, hf * 512:(hf + 1) * 512],
                       [(uT, fc), (wd_b, fc)], pss[hf], start=(fc == 0), stop=False, inc=False)
                for k4 in range(4):
                    kc = hf * 4 + k4
                    for a in range(2):
                        last = (k4 == 3 and a == 1)
                        mm(pss[hf][R, k4 * 128:(k4 + 1) * 128], h1T[:, kc, m0:m0 + rows], aI[:, a, :],
                           [(h1T, kc), (aI, a)], pss[hf], start=False, stop=last, inc=last)
            s2t = big[ti % 2]
            yt = big[2 + ti % 2]
            for hf in range(2):
                copy_op(ev_eng(), s2t[R, hf * 512:(hf + 1) * 512], pss[hf][R, :], [pss[hf]], [s2t])
            layer_norm(s2t, yt, 2, rows)
            S.dma(yout, yt[R, :], reads=[yt])

    print("nops", S.nops)
    S.finish()
    p2.close()
    es.close()
    return nc


def kernel(**inputs):
    maps = prep_inputs(inputs)
    nc = build_program()
    res = run_bass_kernel_spmd(nc, maps, core_ids=list(range(NCORES)))
    R = res.results
    f32 = np.float32
    y_p = np.stack([R[i]["yp"] for i in range(NCORES)]).astype(f32)
    y_s = np.concatenate([R[i]["ys"].reshape(NSB, DEC, D) for i in range(NCORES)]).astype(f32)
    fin = [R[i]["o_fin"] for i in range(NCORES)]
    shf = [R[i]["o_shift"] for i in range(NCORES)]
    new_shift_p = np.stack([s[0] for s in shf])[None].astype(f32)
    new_shift_s = np.concatenate([s[1:] for s in shf])[None].astype(f32)
    new_wkv_p = np.stack([R[i]["o_wkv_p"] for i in range(NCORES)])[None].astype(f32)
    new_wkv_s = np.concatenate([R[i]["o_wkv_s"] for i in range(NCORES)])[None].astype(f32)
    new_conv_p = np.stack([f[0:3] for f in fin])[None].astype(f32)
    new_lru_p = np.stack([f[3] for f in fin])[None].astype(f32)
    new_conv_s = np.concatenate([f[4:52].reshape(NSB, 3, D) for f in fin])[None].astype(f32)
    new_lru_s = np.concatenate([f[52:68] for f in fin])[None].astype(f32)
    return (y_p, y_s, new_shift_p, new_wkv_p, new_conv_p, new_lru_p,
            new_shift_s, new_wkv_s, new_conv_s, new_lru_s)


def prep_inputs(inp):
    f = lambda a: np.ascontiguousarray(np.asarray(a, dtype=np.float32))
    w_in = f(inp["w_in"])[0]
    bases = [0, 1024, 2048, 3328, 4352, 5376, 6400]
    w1 = np.stack([w_in[:, b:b + 1024].reshape(8, 128, 8, 128).transpose(2, 1, 0, 3) for b in bases], axis=2)
    wl = w_in[:, 3072:3328].reshape(8, 128, 2, 128).transpose(1, 2, 0, 3)
    wo = f(inp["w_o"])[0].reshape(8, 128, D).transpose(1, 0, 2)
    wg = f(inp["w_ffn_gate"])[0].reshape(8, 128, NFC, 128).transpose(2, 1, 0, 3)
    wu = f(inp["w_ffn_up"])[0].reshape(8, 128, NFC, 128).transpose(2, 1, 0, 3)
    wgu = np.stack([wg, wu], axis=2)
    wd = f(inp["w_ffn_down"])[0].reshape(NFC, 128, D)
    vec = np.zeros((NV, D), np.float32)
    mu = f(inp["tmix_mu"])[0]
    vec[V_MUR] = mu[0:1024]; vec[V_MUK] = mu[1024:2048]; vec[V_MUV] = mu[2048:3072]
    vec[V_MUL, 0:256] = mu[3072:3328]
    vec[V_W0] = f(inp["w0"])[0]; vec[V_A0] = f(inp["a0"])[0]
    vec[V_KK] = f(inp["k_k"])[0]; vec[V_KA] = f(inp["k_a"])[0]
    vec[V_RK] = f(inp["r_k"])[0].reshape(-1)
    vec[V_LNXG] = f(inp["lnx_g"])[0]; vec[V_LNXB] = f(inp["lnx_b"])[0]
    cw = f(inp["conv_w"])[0]
    for j in range(4):
        vec[V_CW0 + j] = cw[j]
    vec[V_CB] = f(inp["conv_b"])[0]
    vec[V_BA] = f(inp["lru_ba"])[0].reshape(-1); vec[V_BI] = f(inp["lru_bi"])[0].reshape(-1)
    vec[V_LAM] = f(inp["lru_lambda"])[0]
    vecp = np.ascontiguousarray(vec.reshape(NV, 8, 128).transpose(2, 0, 1))
    wab = np.zeros((128, 2, 8, 128), np.float32)
    for j, nm in enumerate(("lru_wa", "lru_wi")):
        w = f(inp[nm])[0]
        for c in range(8):
            for bl in range(2):
                wab[bl * 64:(bl + 1) * 64, j, c, bl * 64:(bl + 1) * 64] = w[2 * c + bl]
    lnv = np.stack([np.broadcast_to(f(inp[n])[0], (128, D)) for n in ("ln1_g", "ln1_b", "ln2_g", "ln2_b")], axis=1)
    shared = dict(w1=w1, wl=wl, wo=wo, wgu=wgu, wd=wd, vec=vecp, w2d=f(inp["w2_decay"])[0],
                  a2d=f(inp["a2_iclr"])[0], g2d=f(inp["g2_gate"])[0], wab=wab, lnv=lnv)
    shared = {k: np.ascontiguousarray(v, dtype=np.float32) for k, v in shared.items()}
    x_prompt = f(inp["x_prompt"]); x_sample = f(inp["x_sample"])
    sh = f(inp["state_shift"])[0]; swkv = f(inp["state_wkv"])[0]
    sconv = f(inp["state_conv"])[0]; slru = f(inp["state_lru"])[0]
    maps = []
    for i in range(NCORES):
        b0 = i * NSB
        xs = np.concatenate([sh[b0:b0 + NSB, None, :], x_sample[b0:b0 + NSB]], axis=1).reshape(NSB * 5, D)
        stt = np.concatenate([sconv[b0:b0 + NSB].reshape(NSB * 3, D), slru[b0:b0 + NSB]], axis=0)
        m = dict(xp=x_prompt[i], xs=xs, xst=x_sample[b0:b0 + NSB].reshape(NSB * DEC, D), st=stt,
                 s0=swkv[b0:b0 + NSB])
        m = {k: np.ascontiguousarray(v, dtype=np.float32) for k, v in m.items()}
        m.update(shared)
        maps.append(m)
    return maps
```

```python
import numpy as np
import ml_dtypes
from contextlib import ExitStack
import concourse.bass as bass
import concourse.mybir as mybir
from concourse.bass_utils import run_bass_kernel_spmd

F32 = mybir.dt.float32
BF16 = mybir.dt.bfloat16
AF = mybir.ActivationFunctionType
ALU = mybir.AluOpType
AX = mybir.AxisListType
ml_bf16 = ml_dtypes.bfloat16

D = 1024
NCORES = 8
SEQ = 2048
NSB = 16
DEC = 4
NCH = 8
D_FF = 2816
NFC = D_FF // 128
KAPPA = float(np.exp(-0.5))
ALPHA = 2.0 ** 0.25
LN_EPS = 1e-5
GN_EPS = 64e-5
CH = 64
PIPE = True
RSCALE = 1.0
TAILPIPE = True
TAILSEL = lambda kind, idx: True
QB = 4
SEGP = QB * CH
NSEGP = SEQ // SEGP
SB = 16
NSEGS = NSB // SB
XS0 = SEQ
XCOLS = SEQ + NSB * 5
MCOLS = SEQ + NSB * DEC
NFIN = 68
(V_MUR, V_MUK, V_MUV, V_W0, V_A0, V_KK, V_KA, V_RK, V_LNXG, V_LNXB, V_CW0, V_CW1, V_CW2, V_CW3,
 V_CB, V_BA, V_BI, V_LAM, V_MUL) = range(19)
NV = 19


class Sched:
    COMPUTE = ("pe", "act", "dve", "pool")

    def __init__(self, nc, es, ndma=14):
        self.nc = nc
        self.eng = {"pe": nc.tensor, "act": nc.scalar, "dve": nc.vector, "pool": nc.gpsimd, "sp": nc.sync}
        self.sem = {}
        self.cnt = {}
        for e in self.COMPUTE:
            self.sem[e] = es.enter_context(nc.semaphore("sem_" + e))
            self.cnt[e] = 0
        self.ndma = ndma
        for i in range(ndma):
            d = "d%d" % i
            self.sem[d] = es.enter_context(nc.semaphore("sem_" + d))
            self.cnt[d] = 0
        self.rr = 0
        self.waited = {e: {} for e in list(self.COMPUTE) + ["sp"]}
        self.W = {}
        self.R = {}
        self.excl = set()
        self.nops = {e: 0 for e in list(self.COMPUTE) + ["sp"]}

    def _collect(self, e, reads, writes):
        need = {}

        def add(d, c, raw):
            if d == e and (e == "pe" or not raw):
                return
            if c > need.get(d, 0):
                need[d] = c

        reads = [getattr(r, "base", r) for r in reads]
        writes = [getattr(w, "base", w) for w in writes]
        for r in reads:
            for d, c in self.W.get(r, {}).items():
                add(d, c, True)
            if r in self.excl:
                for d, c in self.R.get(r, {}).items():
                    add(d, c, False)
        for w in writes:
            for d, c in self.W.get(w, {}).items():
                add(d, c, False)
            for d, c in self.R.get(w, {}).items():
                add(d, c, False)
        return need

    LOG = None

    def _emit_waits(self, e, need):
        wd = self.waited[e]
        for d, c in need.items():
            if wd.get(d, 0) >= c:
                continue
            self.eng[e].wait_ge(self.sem[d], c)
            wd[d] = c
            if self.LOG is not None:
                self.LOG.append((e, d, c, dict(self.cnt)))

    def _record(self, dom, val, reads, writes):
        reads = [getattr(r, "base", r) for r in reads]
        writes = [getattr(w, "base", w) for w in writes]
        for r in reads:
            rr = self.R.setdefault(r, {})
            if val > rr.get(dom, 0):
                rr[dom] = val
        for w in writes:
            if self.R.get(w):
                self.W[w] = {dom: val}
                self.R[w] = {}
            else:
                self.W.setdefault(w, {})[dom] = val

    def op(self, e, fn, reads=(), writes=(), inc=True):
        need = self._collect(e, reads, writes)
        att = None
        if e != "pe":
            wd = self.waited[e]
            pend = [(d, c) for d, c in need.items() if wd.get(d, 0) < c]
            if pend:
                att = pend[-1]
                need = dict(pend[:-1])
        self._emit_waits(e, need)
        ins = fn(self.eng[e])
        if att is not None:
            ins._wait_ge(self.sem[att[0]], att[1])
            self.waited[e][att[0]] = att[1]
        self.nops[e] += 1
        if inc:
            self.cnt[e] += 1
            ins.then_inc(self.sem[e], 1)
            val = self.cnt[e]
        else:
            val = self.cnt[e] + 1
        self._record(e, val, reads, writes)

    def dma(self, out, in_, reads=(), writes=(), q="sp"):
        d = "d%d" % self.rr
        self.rr = (self.rr + 1) % self.ndma
        need = self._collect(q, reads, writes)
        if self.cnt[d] > 0:
            need[d] = max(need.get(d, 0), self.cnt[d])
        self._emit_waits(q, need)
        ins = self.eng[q].dma_start(out=out, in_=in_)
        self.nops[q] += 1
        self.cnt[d] += 16
        ins.then_inc(self.sem[d], 16)
        self._record(d, self.cnt[d], reads, writes)

    def barrier(self):
        for e in list(self.COMPUTE) + ["sp"]:
            need = {d: c for d, c in self.cnt.items() if c > 0 and d != e}
            self._emit_waits(e, need)

    def finish(self):
        for i in range(self.ndma):
            d = "d%d" % i
            if self.cnt[d] > 0:
                self.eng["sp"].wait_ge(self.sem[d], self.cnt[d])
        for e in self.COMPUTE:
            if self.cnt[e] > 0:
                self.eng["sp"].wait_ge(self.sem[e], self.cnt[e])


class StopBuild(Exception):
    pass


class T:
    def __init__(self, h, name):
        self.h = h
        self.name = name

    def __getitem__(self, k):
        return self.h[k]

    def __repr__(self):
        return "T(%s)" % self.name


class Off:
    def __init__(self, base, off, width):
        self.base = base
        self.off = off
        self.width = width

    def __getitem__(self, k):
        if not isinstance(k, tuple):
            return self.base[:, self.off:self.off + self.width]
        p_, c_ = k
        lo = 0 if c_.start is None else c_.start
        hi = self.width if c_.stop is None else c_.stop
        return self.base[p_, self.off + lo:self.off + hi]


class SubSeg:
    kind = "s"

    def __init__(self, b):
        self.idx = b
        self.ncol = QB * 5
        self.nb = QB
        self.T = DEC
        self.tv = DEC
        self.nq = QB
        self.m0 = SEQ + b * QB * DEC
        self.mcols = QB * DEC
        self.nlev = 2

    def tokv(self, ap):
        return ap.rearrange("p (b s) -> p b s", s=5)[:, :, 1:5]

    chv = tokv


class Pool:
    def __init__(self, tiles):
        self.free = list(tiles)
        self.all = list(tiles)

    def get(self):
        return self.free.pop(0)

    def put(self, *ts):
        for t in ts:
            assert t in self.all and t not in self.free
            self.free.append(t)


class Seg:
    def __init__(self, kind, idx):
        self.kind = kind
        self.idx = idx
        if kind == "p":
            self.x0 = idx * SEGP
            self.ncol = SEGP
            self.nb = 1
            self.T = SEGP
            self.tv = CH
            self.m0 = idx * SEGP
            self.mcols = SEGP
            self.nlev = 6
        else:
            self.x0 = XS0 + idx * SB * 5
            self.ncol = SB * 5
            self.nb = SB
            self.T = DEC
            self.tv = DEC
            self.m0 = SEQ + idx * SB * DEC
            self.mcols = SB * DEC
            self.nlev = 2
        self.nq = QB if kind == "p" else SB

    def tokv(self, ap):
        if self.kind == "p":
            return ap.rearrange("p (b t) -> p b t", b=1)
        return ap.rearrange("p (b s) -> p b s", s=5)[:, :, 1:5]

    def chv(self, ap):
        if self.kind == "p":
            return ap.rearrange("p (q t) -> p q t", t=CH)
        return ap.rearrange("p (b s) -> p b s", s=5)[:, :, 1:5]


def build_program(debug=None, stop_after=None):
    nc = bass.Bass("TRN2", target_bir_lowering=False)
    es = ExitStack()
    S = Sched(nc, es)

    def din(name, shape):
        return nc.dram_tensor(name, list(shape), F32, kind="ExternalInput").ap()

    def dout(name, shape):
        return nc.dram_tensor(name, list(shape), F32, kind="ExternalOutput").ap()

    xp = din("xp", [SEQ, D])
    xs = din("xs", [NSB * 5, D])
    xst = din("xst", [NSB * DEC, D])
    st = din("st", [64, D])
    s0 = din("s0", [NSB, 16, 64, 64])
    w1 = din("w1", [NCH, 128, 7, 8, 128])
    wl = din("wl", [128, 2, 8, 128])
    wo = din("wo", [128, 8, D])
    wgu = din("wgu", [NFC, 128, 2, 8, 128])
    wd = din("wd", [NFC, 128, D])
    vec = din("vec", [128, NV, 8])
    w2d = din("w2d", [64, D])
    a2d = din("a2d", [64, D])
    g2d = din("g2d", [128, D])
    wab = din("wab", [128, 2, 8, 128])
    lnv = din("lnv", [128, 4, D])

    yp = dout("yp", [SEQ, D])
    ys = dout("ys", [NSB * DEC, D])
    o_shift = dout("o_shift", [1 + NSB, D])
    o_wkv_p = dout("o_wkv_p", [16, 64, 64])
    o_wkv_s = dout("o_wkv_s", [NSB, 16, 64, 64])
    o_fin = dout("o_fin", [NFIN, D])
    dbg_out = {}
    if debug:
        for name, (shape, dt_) in debug.items():
            dbg_out[name] = nc.dram_tensor("dbg_" + name, list(shape), dt_, kind="ExternalOutput").ap()

    def sb(name, shape, dt=F32, stack=None):
        h = (stack or es).enter_context(nc.sbuf_tensor(name, list(shape), dt))
        return T(h, name)

    psl = [T(es.enter_context(nc.psum_tensor("ps%d" % i, [128, 512], F32)), "ps%d" % i) for i in range(8)]
    ps_state = {"i": 0}
    S.excl.update(psl)

    ps_banks = {None: list(range(8)), "s1": [6, 7], "s1a": [6], "s1b": [7], "s1c": [5], "s2A": [0, 1], "s2B": [2, 3], "tailA": [4], "tailB": [4]}
    ps_ctr = {k: 0 for k in (None, "s1", "s1a", "s1b", "s1c", "s2A", "s2B", "tailA", "tailB")}
    cur_alloc = {"who": None}

    def getps():
        who = cur_alloc["who"]
        lst = ps_banks[who]
        t = psl[lst[ps_ctr[who] % len(lst)]]
        ps_ctr[who] += 1
        return t

    cur = {"segi": -1}

    def ck(label):
        if stop_after == label or stop_after == "%s@%d" % (label, cur["segi"]):
            raise StopBuild()

    ident_f = sb("ident_f", [128, 128])
    ident_b = sb("ident_b", [128, 128], BF16)
    II_f = sb("II_f", [128, 64])
    II_b = sb("II_b", [128, 64], BF16)
    ones_bd = sb("ones_bd", [128, 128])
    mUs = sb("mUs", [128, QB, 128], BF16)
    mUi = sb("mUi", [128, QB, 128], BF16)
    mLs = sb("mLs", [128, QB, 128], BF16)
    rmask_p = sb("rmask_p", [128, SEGP])
    rmask_s = sb("rmask_s", [128, SB * 5])
    vecs = sb("vecs", [128, NV, 8])
    der = sb("der", [128, 12, 8])
    w2b = sb("w2b", [128, D], BF16)
    g2b = sb("g2b", [128, D], BF16)
    wabb = sb("wabb", [128, 2, 8, 128], BF16)
    fin = sb("fin", [128, NCH, NFIN])
    stT = sb("stT", [128, NCH, 64])

    S.op("pool", lambda E: E.memset(ident_f[:], 1.0), writes=[ident_f])
    S.op("pool", lambda E: E.affine_select(out=ident_f[:], in_=ident_f[:], pattern=[[-1, 128]],
                                           compare_op=ALU.is_equal, fill=0.0, base=0, channel_multiplier=1),
         reads=[ident_f], writes=[ident_f])
    S.op("dve", lambda E: E.tensor_copy(out=ident_b[:], in_=ident_f[:]), reads=[ident_f], writes=[ident_b])
    S.op("dve", lambda E: E.tensor_tensor(out=II_f[:], in0=ident_f[:, 0:64], in1=ident_f[:, 64:128], op=ALU.add),
         reads=[ident_f], writes=[II_f])
    S.op("dve", lambda E: E.tensor_copy(out=II_b[:], in_=II_f[:]), reads=[II_f], writes=[II_b])
    S.op("pool", lambda E: E.memset(ones_bd[:], 0.0), writes=[ones_bd])
    S.op("pool", lambda E: E.memset(ones_bd[0:64, 0:64], 1.0), reads=[ones_bd], writes=[ones_bd])
    S.op("pool", lambda E: E.memset(ones_bd[64:128, 64:128], 1.0), reads=[ones_bd], writes=[ones_bd])
    for m, patt, cm, cmp in ((mUs, 1, -1, ALU.is_gt), (mUi, 1, -1, ALU.is_ge), (mLs, -1, 1, ALU.is_gt)):
        S.op("pool", lambda E, m=m: E.memset(m[:], 1.0), writes=[m])
        S.op("pool", lambda E, m=m, patt=patt, cm=cm, cmp=cmp: E.affine_select(
            out=m[:], in_=m[:], pattern=[[0, QB], [patt, 128]], compare_op=cmp, fill=0.0, base=0,
            channel_multiplier=cm), reads=[m], writes=[m])
    S.op("pool", lambda E: E.memset(rmask_p[:], 1.0), writes=[rmask_p])
    S.op("pool", lambda E: E.memset(rmask_p[:].rearrange("p (q t) -> p q t", t=CH)[:, :, 0:1], 0.0),
         reads=[rmask_p], writes=[rmask_p])
    S.op("pool", lambda E: E.memset(rmask_s[:], 1.0), writes=[rmask_s])
    S.op("pool", lambda E: E.memset(rmask_s[:].rearrange("p (b s) -> p b s", s=5)[:, :, 0:2], 0.0),
         reads=[rmask_s], writes=[rmask_s])
    S.op("pool", lambda E: E.memset(fin[:], 0.0), writes=[(fin, c) for c in range(NCH)])

    S.dma(vecs[:], vec[:, :, :], writes=[vecs])

    def vcol(i, c):
        return vecs[:, i, c:c + 1]

    S.op("dve", lambda E: E.tensor_scalar(out=der[:, 0, :], in0=vecs[:, V_KA, :], scalar1=-1.0, scalar2=1.0,
                                          op0=ALU.mult, op1=ALU.add), reads=[vecs], writes=[(der, 0)])
    S.op("act", lambda E: E.activation(out=der[:, 3, :], in_=vecs[:, V_LAM, :], func=AF.Exp, scale=-1.0),
         reads=[vecs], writes=[(der, 3)])
    S.op("act", lambda E: E.activation(out=der[:, 3, :], in_=der[:, 3, :], func=AF.Ln, bias=1.0, scale=1.0),
         reads=[(der, 3)], writes=[(der, 3)])
    S.op("dve", lambda E: E.tensor_scalar(out=der[:, 1, :], in0=der[:, 3, :], scalar1=-8.0, scalar2=None,
                                          op0=ALU.mult), reads=[(der, 3)], writes=[(der, 1)])
    S.op("dve", lambda E: E.tensor_scalar(out=der[:, 2, :], in0=der[:, 3, :], scalar1=-16.0, scalar2=None,
                                          op0=ALU.mult), reads=[(der, 3)], writes=[(der, 2)])
    for di, vi in ((4, V_W0), (5, V_A0), (6, V_BA), (7, V_BI), (8, V_KA)):
        S.op("dve", lambda E, di=di, vi=vi: E.tensor_scalar(out=der[:, di, :], in0=vecs[:, vi, :], scalar1=0.5,
                                                            scalar2=None, op0=ALU.mult), reads=[vecs], writes=[(der, di)])
    S.op("dve", lambda E: E.tensor_scalar(out=der[:, 9, :], in0=vecs[:, V_KA, :], scalar1=-0.5, scalar2=1.0,
                                          op0=ALU.mult, op1=ALU.add), reads=[vecs], writes=[(der, 9)])
    S.op("dve", lambda E: E.tensor_scalar(out=der[:, 10, :], in0=der[:, 3, :], scalar1=-4.0, scalar2=None,
                                          op0=ALU.mult), reads=[(der, 3)], writes=[(der, 10)])
    S.op("dve", lambda E: E.tensor_scalar(out=der[:, 11, :], in0=der[:, 3, :], scalar1=-8.0, scalar2=None,
                                          op0=ALU.mult), reads=[(der, 3)], writes=[(der, 11)])
    negh = sb("negh", [128, QB])
    S.op("pool", lambda E: E.memset(negh[:], -0.5), writes=[negh])

    p1 = ExitStack()
    merged = sb("merged", [128, 8, MCOLS], BF16)
    xsc = nc.dram_tensor("xsc", [128, 8, XCOLS], BF16).ap()
    xtt = [sb("xtt%d" % i, [128, 8, 128], BF16, p1) for i in range(1)]
    xs_pool = Pool([sb("xseg%d" % i, [128, 8, SEGP], BF16, p1) for i in range(2)])
    if stop_after is not None:
        for c_ in range(8):
            S.op("pool", lambda E, c_=c_: E.memset(merged[:, c_, :], 0.0), writes=[(merged, c_)])
    lora0 = sb("lora0", [128, XCOLS], BF16, p1)
    lora1 = sb("lora1", [128, XCOLS], BF16, p1)
    stage = [sb("stage%d" % i, [128, 8, 128], F32, p1) for i in range(3)]
    wcb = sb("wcb", [128, 7, 8, 128], BF16, p1)
    NSCR = 37
    scr = Pool([sb("scr%d" % i, [128, SEGP + 4], F32, p1) for i in range(NSCR)])
    NWK = 27
    wk = Pool([sb("wk%d" % i, [128, QB, 128], BF16, p1) for i in range(NWK)])
    bdn = ("aT", "bT", "kT", "rT", "bgT", "kgT", "vT")
    bd = {}
    for kind in ("p", "s"):
        bd[kind] = []
        for par_ in range(2):
            st_ = {n: sb("bd_%s%d_%s" % (kind, par_, n), [128, QB, 2, 64], BF16, p1) for n in bdn}
            bd[kind].append(st_)
            for n in bdn:
                S.op("pool", lambda E, t=st_[n]: E.memset(t[:], 0.0), writes=[(st_[n], 0), (st_[n], 1)])
    ybd = sb("ybd", [128, QB, 2, 64], F32, p1)
    S.op("pool", lambda E: E.memset(ybd[:], 0.0), writes=[(ybd, 0), (ybd, 1)])
    hbd = sb("hbd", [128, 2, 64], F32, p1)
    S.op("pool", lambda E: E.memset(hbd[:], 0.0), writes=[(hbd, 0), (hbd, 1)])
    zbuf = [sb("zbuf%d" % i, [128, 1 + SEGP], F32, p1) for i in range(3)]
    xbuf = sb("xbuf", [128, SB * (3 + SEGP // 1) if False else max(3 + SEGP, SB * 7)], F32, p1)
    carry3 = sb("carry3", [128, 3], F32, p1)
    hcar = sb("hcar", [128, 1], F32, p1)
    H32 = [sb("H32_%d" % i, [128, QB + 1, 64], F32, p1) for i in range(2)]
    Hbf = [sb("Hbf_%d" % i, [128, QB + 1, 64], BF16, p1) for i in range(2)]
    H32s = [[sb("H32s_%d%d" % (i, j), [128, QB, 64], F32, p1) for j in range(2)] for i in range(2)]
    Hbfs = [sb("Hbfs_%d" % i, [128, QB, 64], BF16, p1) for i in range(2)]
    gamC = sb("gamC", [128, QB], F32, p1)
    stat = sb("stat", [128, 8, QB], F32, p1)
    bst4 = sb("bst4", [128, QB, 6], F32, p1)
    mvq = sb("mvq", [128, QB, 2], F32, p1)

    S.dma(stage[0][0:64, :, :].rearrange("p a b -> p (a b)"), w2d[:, :], writes=[stage[0]])
    S.dma(stage[1][64:128, :, :].rearrange("p a b -> p (a b)"), a2d[:, :], writes=[stage[1]])
    S.op("pool", lambda E: E.tensor_copy(out=w2b[0:64, :], in_=stage[0][0:64, :, :].rearrange("p a b -> p (a b)")),
         reads=[stage[0]], writes=[(w2b, 0)])
    S.op("pool", lambda E: E.tensor_copy(out=w2b[64:128, :], in_=stage[1][64:128, :, :].rearrange("p a b -> p (a b)")),
         reads=[stage[1]], writes=[(w2b, 1)])
    S.dma(stage[2][:, :, :].rearrange("p a b -> p (a b)"), g2d[:, :], writes=[stage[2]])
    S.op("pool", lambda E: E.tensor_copy(out=g2b[:], in_=stage[2][:, :, :].rearrange("p a b -> p (a b)")),
         reads=[stage[2]], writes=[g2b])
    for j in range(2):
        S.dma(stage[j][:], wab[:, j, :, :], writes=[stage[j]])
        S.op("pool", lambda E, j=j: E.tensor_copy(out=wabb[:, j, :, :], in_=stage[j][:]),
             reads=[stage[j]], writes=[(wabb, j)])
    for j in range(2):
        S.dma(stage[j][:], wl[:, j, :, :], writes=[stage[j]])
        S.op("pool", lambda E, j=j: E.tensor_copy(out=wcb[:, j, :, :], in_=stage[j][:]),
             reads=[stage[j]], writes=[(wcb, j)])

    rr = {"ev": 0, "ew": 0}

    def ev_eng():
        return "act"

    def ew_eng():
        rr["ew"] ^= 1
        return "dve" if rr["ew"] else "pool"

    def copy_op(e, out, in_, reads, writes):
        if e == "act":
            S.op("act", lambda E: E.copy(out=out, in_=in_), reads, writes)
        else:
            S.op(e, lambda E: E.tensor_copy(out=out, in_=in_), reads, writes)

    ntile = SEQ // 128
    for t in range(ntile + 2):
        xi = stage[t % 3]
        xiv = xi[:].rearrange("p a b -> p (a b)")
        if t < ntile:
            rows = 128
            S.dma(xiv, xp[t * 128:(t + 1) * 128, :], writes=[xi])
        elif t == ntile:
            rows = NSB * 5
            S.dma(xiv[0:rows, :], xs[:, :], writes=[xi])
        else:
            rows = 64
            S.dma(xiv[0:rows, :], st[:, :], writes=[xi])
        for half in range(2):
            ps = getps()
            for k4 in range(4):
                kc = half * 4 + k4
                S.op("pe", lambda E, ps=ps, k4=k4, kc=kc, xi=xi, rows=rows: E.transpose(
                    out=ps[:, k4 * 128:k4 * 128 + rows], in_=xi[0:rows, kc, :],
                    identity=ident_f[0:rows, 0:rows]), reads=[xi, ident_f], writes=[ps], inc=(k4 == 3))
            src = ps[:].rearrange("p (k t) -> p k t", t=128)[:, :, 0:rows]
            if t <= ntile:
                xo = xtt[0]
                copy_op(ev_eng(), xo[:, half * 4:(half + 1) * 4, 0:rows], src, [ps], [(xo, half)])
                if half == 1:
                    S.dma(xsc[:, :, t * 128:t * 128 + rows], xo[:, :, 0:rows], reads=[(xo, 0), (xo, 1)],
                          writes=[("xsc", t)])
            else:
                dst = stT[:, half * 4:(half + 1) * 4, :]
                copy_op(ev_eng(), dst, src, [ps], [stT])
    xstate = {"tile": {}, "order": [], "pos": 0}

    def _issue_x(seg):
        t_ = xs_pool.get()
        t0_, t1_ = seg.x0 // 128, (seg.x0 + seg.ncol - 1) // 128
        S.dma(t_[:, :, 0:seg.ncol], xsc[:, :, seg.x0:seg.x0 + seg.ncol],
              reads=[("xsc", k) for k in range(t0_, t1_ + 1)], writes=[t_])
        return t_

    def get_x(seg):
        order, pos = xstate["order"], xstate["pos"]
        assert order[pos] is seg
        t_ = xstate["tile"].pop(pos, None)
        if t_ is None:
            t_ = _issue_x(seg)
        if pos + 1 < len(order):
            xstate["tile"][pos + 1] = _issue_x(order[pos + 1])
        xstate["pos"] = pos + 1
        return t_

    def dump(name, ap, reads):
        if name in dbg_out:
            S.dma(dbg_out[name], ap, reads=reads)


    def proj_ps(wsel, seg, xseg):
        ps = getps()
        ncol = seg.ncol
        for kc in range(8):
            lhsT, wres = wsel(kc)
            S.op("pe", lambda E, ps=ps, lhsT=lhsT, kc=kc: E.matmul(
                ps[:, 0:ncol], lhsT=lhsT, rhs=xseg[:, kc, 0:ncol], start=(kc == 0), stop=(kc == 7)),
                reads=[wres, xseg], writes=[ps], inc=(kc == 7))
        return ps

    def shifted(wsel, mu_ap, seg, zb, first, xseg):
        ncol = seg.ncol
        if seg.kind == "p":
            if first:
                S.op("pool", lambda E: E.memset(zb[:, 0:1], 0.0), writes=[zb])
            else:
                S.op("pool", lambda E: E.tensor_copy(out=zb[:, 0:1], in_=zb[:, SEGP:SEGP + 1]), reads=[zb], writes=[zb])
        ps = proj_ps(wsel, seg, xseg)
        S.op("act", lambda E: E.copy(out=zb[:, 1:1 + ncol], in_=ps[:, 0:ncol]), reads=[ps], writes=[zb])
        dt_ = scr.get()
        zm = scr.get()
        S.op("pool", lambda E: E.tensor_tensor(out=dt_[:, 0:ncol], in0=zb[:, 0:ncol], in1=zb[:, 1:1 + ncol],
                                               op=ALU.subtract), reads=[zb], writes=[dt_])
        S.op("dve", lambda E: E.scalar_tensor_tensor(out=zm[:, 0:ncol], in0=dt_[:, 0:ncol], scalar=mu_ap,
                                                     in1=zb[:, 1:1 + ncol], op0=ALU.mult, op1=ALU.add),
             reads=[dt_, zb, vecs], writes=[zm])
        scr.put(dt_)
        return zm

    segs = [Seg("p", i) for i in range(NSEGP)] + [Seg("s", i) for i in range(NSEGS)]

    xstate["order"] = segs * 2 + segs * NCH
    for L in range(2):
        for si, seg in enumerate(segs):
            xseg = get_x(seg)
            zm = shifted(lambda kc, L=L: (wcb[:, L, kc, :], (wcb, L)), vcol(V_MUL, L), seg, zbuf[0],
                         first=(seg.kind == "p" and seg.idx == 0), xseg=xseg)
            xs_pool.put(xseg)
            nco = seg.ncol
            if L == 0:
                S.op("act", lambda E: E.activation(out=lora0[0:64, seg.x0:seg.x0 + nco], in_=zm[0:64, 0:nco],
                                                   func=AF.Tanh), reads=[zm], writes=[(lora0, 0)])
                S.op("act", lambda E: E.copy(out=lora0[64:128, seg.x0:seg.x0 + nco], in_=zm[64:128, 0:nco]),
                     reads=[zm], writes=[(lora0, 1)])
            else:
                S.op("act", lambda E: E.activation(out=lora1[:, seg.x0:seg.x0 + nco], in_=zm[:, 0:nco],
                                                   func=AF.Sigmoid), reads=[zm], writes=[lora1])
            scr.put(zm)
    dump("lora0", lora0[:, :], [(lora0, 0), (lora0, 1)])
    dump("lora1", lora1[:, :], [lora1])

    if stop_after == "lora":
        S.finish()
        p1.close()
        es.close()
        return nc

    gam_pool = Pool([sb("gamC%d" % i, [128, SB], F32, p1) for i in range(5)])
    ubf_pool = Pool([sb("ubf%d" % i, [128, SEGP], BF16, p1) for i in range(2)])
    sbd = sb("sbd", [128, 2, 64], F32, p1)
    sin_tiles = [sb("sin%d" % i, [128, 64], F32, p1) for i in range(3)]
    sin_ctr = {"i": 0}
    S.op("pool", lambda E: E.memset(sbd[:], 0.0), writes=[(sbd, 0), (sbd, 1)])

    print("sbuf remaining in phase 1:", nc.sbuf_bytes_remaining)

    def wsel(g):
        return lambda kc: (wcb[:, g, kc, :], (wcb, g))

    def mm(ps_ap, lhsT, rhs, reads, ps, start=True, stop=True, inc=True):
        S.op("pe", lambda E: E.matmul(ps_ap, lhsT=lhsT, rhs=rhs, start=start, stop=stop),
             reads=reads, writes=[ps], inc=inc)

    def tt(e, out, a, b, op, reads, writes):
        S.op(e, lambda E: E.tensor_tensor(out=out, in0=a, in1=b, op=op), reads, writes)

    def act(out, in_, func, reads, writes, bias=None, scale=None):
        kw = {}
        if bias is not None:
            kw["bias"] = bias
        if scale is not None:
            kw["scale"] = scale
        S.op("act", lambda E: E.activation(out=out, in_=in_, func=func, **kw), reads, writes)

    def bd_write(seg, B, K, col0):
        tv, nco = seg.tv, seg.ncol
        for h in range(2):
            P = slice(h * 64, (h + 1) * 64)

            def dst(n):
                return B[n][P, :, h, 0:tv]

            def cv(t_):
                return seg.chv(t_[P, col0:col0 + nco])

            kk, E1, E2, E3, E4, tb, kf, zr, zv = (K[n] for n in ("kk", "E1", "E2", "E3", "E4", "tb", "kf", "zr", "zv"))
            S.op("dve", lambda E: E.scalar_tensor_tensor(out=dst("aT"), in0=cv(kk), scalar=-1.0, in1=cv(E3),
                                                         op0=ALU.mult, op1=ALU.mult), [kk, E3], [(B["aT"], h)])
            yield
            tt(ew_eng(), dst("bT"), cv(tb), cv(E2), ALU.mult, [tb, E2], [(B["bT"], h)])
            yield
            tt(ew_eng(), dst("bgT"), cv(tb), cv(E4), ALU.mult, [tb, E4], [(B["bgT"], h)])
            yield
            tt(ew_eng(), dst("kT"), cv(kf), cv(E2), ALU.mult, [kf, E2], [(B["kT"], h)])
            yield
            tt(ew_eng(), dst("kgT"), cv(kf), cv(E4), ALU.mult, [kf, E4], [(B["kgT"], h)])
            yield
            tt(ew_eng(), dst("rT"), cv(zr), cv(E1), ALU.mult, [zr, E1], [(B["rT"], h)])
            yield
            S.op("act", lambda E: E.copy(out=dst("vT"), in_=cv(zv)), [zv], [(B["vT"], h)])
            yield

    def stage1c(c, seg, shared):
        nco, kind, tv, nq, x0 = seg.ncol, seg.kind, seg.tv, seg.nq, seg.x0
        first = (kind == "p" and seg.idx == 0)
        B = bd[kind][seg.idx % 2]
        cc = slice(c * 128, (c + 1) * 128)
        ps = getps()
        mm(ps[:, 0:nco], w2b[0:64, cc], lora0[0:64, x0:x0 + nco], [(w2b, 0), (lora0, 0)], ps)
        yield
        sg = scr.get()
        act(sg[:, 0:nco], ps[:, 0:nco], AF.Tanh, [ps, (der, 4)], [sg], bias=der[:, 4, c:c + 1], scale=0.5)
        S.op("pool", lambda E: E.tensor_scalar(out=sg[:, 0:nco], in0=sg[:, 0:nco], scalar1=0.5, scalar2=0.5,
                                               op0=ALU.mult, op1=ALU.add), [sg], [sg])
        yield
        cs = scr.get()
        rmask = rmask_p if kind == "p" else rmask_s
        S.op("dve", lambda E: E.tensor_tensor_scan(out=cs[:, 0:nco], data0=rmask[:, 0:nco], data1=sg[:, 0:nco],
                                                   initial=0.0, op0=ALU.mult, op1=ALU.add),
             reads=[rmask, sg], writes=[cs])
        yield
        E1 = scr.get(); E2 = scr.get(); E3 = scr.get(); E4 = scr.get(); t0 = scr.get()
        act(E1[:, 0:nco], cs[:, 0:nco], AF.Exp, [cs], [E1], scale=-KAPPA)
        yield
        act(E2[:, 0:nco], cs[:, 0:nco], AF.Exp, [cs], [E2], scale=KAPPA)
        yield
        tt("pool", t0[:, 0:nco], cs[:, 0:nco], sg[:, 0:nco], ALU.subtract, [cs, sg], [t0])
        yield
        act(E3[:, 0:nco], t0[:, 0:nco], AF.Exp, [t0], [E3], scale=-KAPPA)
        yield
        csv = seg.chv(cs[:, 0:nco])
        t0v = seg.chv(t0[:, 0:nco])
        tt("dve", t0v, csv[:, :, tv - 1:tv].to_broadcast([128, nq, tv]), csv, ALU.subtract, [cs], [t0])
        yield
        act(seg.chv(E4[:, 0:nco]), t0v, AF.Exp, [t0], [E4], scale=-KAPPA)
        yield
        gam = gam_pool.get()
        act(gam[:, 0:nq].rearrange("p (q o) -> p q o", o=1), csv[:, :, tv - 1:tv], AF.Exp, [cs], [gam], scale=-KAPPA)
        yield
        scr.put(sg, cs, t0)
        ps = getps()
        mm(ps[:, 0:nco], w2b[64:128, cc], lora0[64:128, x0:x0 + nco], [(w2b, 1), (lora0, 1)], ps)
        yield
        a_ = scr.get()
        act(a_[:, 0:nco], ps[:, 0:nco], AF.Tanh, [ps, (der, 5)], [a_], bias=der[:, 5, c:c + 1], scale=0.5)
        S.op("pool", lambda E: E.tensor_scalar(out=a_[:, 0:nco], in0=a_[:, 0:nco], scalar1=0.5, scalar2=0.5,
                                               op0=ALU.mult, op1=ALU.add), [a_], [a_])
        yield
        ps = getps()
        mm(ps[:, 0:nco], g2b[:, cc], lora1[:, x0:x0 + nco], [g2b, lora1], ps)
        yield
        g_ = scr.get()
        S.op("act", lambda E: E.mul(out=g_[:, 0:nco], in_=ps[:, 0:nco], mul=0.5), [ps], [g_])
        yield
        shared.update(E1=E1, E2=E2, E3=E3, E4=E4, a_=a_, g_=g_, gam=gam)

    def stage1a(c, seg, xseg, shared):
        nco, kind, tv, nq, x0 = seg.ncol, seg.kind, seg.tv, seg.nq, seg.x0
        first = (kind == "p" and seg.idx == 0)
        B = bd[kind][seg.idx % 2]
        cc = slice(c * 128, (c + 1) * 128)
        zr = shifted(wsel(0), vcol(V_MUR, c), seg, zbuf[0], first, xseg)
        yield
        zk = shifted(wsel(1), vcol(V_MUK, c), seg, zbuf[1], first, xseg)
        yield
        zv = shifted(wsel(2), vcol(V_MUV, c), seg, zbuf[2], first, xseg)
        yield
        kkr = scr.get(); sq = scr.get(); kk = scr.get()
        S.op("dve", lambda E: E.tensor_scalar(out=kkr[:, 0:nco], in0=zk[:, 0:nco], scalar1=vcol(V_KK, c), scalar2=None,
                                              op0=ALU.mult), [zk, vecs], [kkr])
        yield
        tt("pool", sq[:, 0:nco], kkr[:, 0:nco], kkr[:, 0:nco], ALU.mult, [kkr], [sq])
        yield
        ps = getps()
        mm(ps[:, 0:nco], ones_bd[:], sq[:, 0:nco], [ones_bd, sq], ps)
        yield
        act(sq[:, 0:nco], ps[:, 0:nco], AF.Sqrt, [ps], [sq])
        yield
        S.op("dve", lambda E: E.tensor_scalar(out=sq[:, 0:nco], in0=sq[:, 0:nco], scalar1=1e-12, scalar2=None,
                                              op0=ALU.max), [sq], [sq])
        yield
        S.op("dve", lambda E: E.reciprocal(out=sq[:, 0:nco], in_=sq[:, 0:nco]), [sq], [sq])
        yield
        tt("pool", kk[:, 0:nco], kkr[:, 0:nco], sq[:, 0:nco], ALU.mult, [kkr, sq], [kk])
        yield
        yield "NEEDC"
        E1, E2, E3, E4, a_, g_, gam = (shared[k_] for k_ in ("E1", "E2", "E3", "E4", "a_", "g_", "gam"))
        t1 = scr.get(); kf = scr.get(); bonus = scr.get()
        S.op("dve", lambda E: E.tensor_scalar(out=t1[:, 0:nco], in0=a_[:, 0:nco], scalar1=vcol(V_KA, c),
                                              scalar2=der[:, 0, c:c + 1], op0=ALU.mult, op1=ALU.add),
             [a_, vecs, (der, 0)], [t1])
        yield
        tt("pool", kf[:, 0:nco], zk[:, 0:nco], t1[:, 0:nco], ALU.mult, [zk, t1], [kf])
        yield
        S.op("dve", lambda E: E.scalar_tensor_tensor(out=t1[:, 0:nco], in0=zr[:, 0:nco], scalar=vcol(V_RK, c),
                                                     in1=kf[:, 0:nco], op0=ALU.mult, op1=ALU.mult),
             [zr, kf, vecs, t1], [t1])
        yield
        ps = getps()
        mm(ps[:, 0:nco], ones_bd[:], t1[:, 0:nco], [ones_bd, t1], ps)
        yield
        tt("dve", bonus[:, 0:nco], ps[:, 0:nco], zv[:, 0:nco], ALU.mult, [ps, zv], [bonus])
        yield
        tb = kkr
        tt("pool", tb[:, 0:nco], kk[:, 0:nco], a_[:, 0:nco], ALU.mult, [kk, a_, kkr], [tb])
        yield
        keep = dict(kk=kk, E3=E3, tb=tb, E2=E2, E4=E4, kf=kf, zr=zr, E1=E1, zv=zv)
        if kind == "p":
            yield from bd_write(seg, B, keep, 0)
            scr.put(zr, zk, zv, E1, E2, E3, E4, a_, kkr, sq, kk, t1, kf)
        else:
            scr.put(zk, a_, sq, t1)
        return dict(bonus=bonus, g=g_, gam=gam, keep=keep)

    def stage1b(c, seg, xseg):
        nco, kind, tv, nq, x0 = seg.ncol, seg.kind, seg.tv, seg.nq, seg.x0
        first = (kind == "p" and seg.idx == 0)
        B = bd[kind][seg.idx % 2]
        cc = slice(c * 128, (c + 1) * 128)
        T_, nb = seg.T, seg.nb
        nt = nb * T_
        xbv = xbuf[:, 0:nb * (3 + T_)].rearrange("p (b t) -> p b t", t=3 + T_)

        def dv(t_):
            return t_[:, 0:nt].rearrange("p (b t) -> p b t", t=T_)

        ps = proj_ps(wsel(3), seg, xseg)
        yield
        if kind == "p":
            if seg.idx == 0:
                S.op("pool", lambda E: E.memset(xbv[:, :, 0:3], 0.0), writes=[xbuf])
                yield
            else:
                S.op("pool", lambda E: E.tensor_copy(out=xbv[:, 0, 0:3], in_=carry3[:, :]), [carry3], [xbuf])
                yield
        else:
            b0 = seg.idx * SB
            S.op("pool", lambda E: E.tensor_copy(
                out=xbv[:, :, 0:3], in_=stT[:, c, 0:48].rearrange("p (b j) -> p b j", j=3)[:, b0:b0 + SB, :]),
                [stT], [xbuf])
            yield
        S.op("act", lambda E: E.copy(out=xbv[:, :, 3:3 + T_], in_=seg.tokv(ps[:, 0:nco])), [ps, xbuf], [xbuf])
        yield
        if kind == "p":
            S.op("pool", lambda E: E.tensor_copy(out=carry3[:, :], in_=xbv[:, 0, T_:T_ + 3]), [xbuf], [carry3])
            yield
            if seg.idx == NSEGP - 1:
                S.op("pool", lambda E: E.tensor_copy(out=fin[:, c, 0:3], in_=xbv[:, 0, T_:T_ + 3]), [xbuf], [(fin, c)])
                yield
        else:
            S.op("pool", lambda E: E.tensor_copy(
                out=fin[:, c, 4 + 3 * b0:4 + 3 * (b0 + SB)].rearrange("p (b j) -> p b j", j=3),
                in_=xbv[:, :, T_:T_ + 3]), [xbuf], [(fin, c)])
            yield
        u = scr.get()
        uv = dv(u)
        S.op("dve", lambda E: E.tensor_scalar(out=uv, in0=xbv[:, :, 0:T_], scalar1=vcol(V_CW0, c),
                                              scalar2=vcol(V_CB, c), op0=ALU.mult, op1=ALU.add),
             [xbuf, vecs], [u])
        yield
        for j in range(1, 4):
            S.op("dve", lambda E, j=j: E.scalar_tensor_tensor(out=uv, in0=xbv[:, :, j:j + T_],
                                                              scalar=vcol(V_CW0 + j, c), in1=uv,
                                                              op0=ALU.mult, op1=ALU.add), [xbuf, vecs, u], [u])
            yield
        ubf = ubf_pool.get()
        S.op("act", lambda E: E.copy(out=ubf[:, 0:nt], in_=u[:, 0:nt]), [u], [ubf])
        yield
        rg = scr.get(); ig = scr.get(); al = scr.get(); e2 = scr.get(); hh = scr.get()
        ps = getps()
        mm(ps[:, 0:nt], wabb[:, 0, c, :], ubf[:, 0:nt], [(wabb, 0), ubf], ps)
        yield
        act(rg[:, 0:nt], ps[:, 0:nt], AF.Tanh, [ps, (der, 6)], [rg], bias=der[:, 6, c:c + 1], scale=0.5)
        yield
        ps = getps()
        mm(ps[:, 0:nt], wabb[:, 1, c, :], ubf[:, 0:nt], [(wabb, 1), ubf], ps)
        yield
        act(ig[:, 0:nt], ps[:, 0:nt], AF.Tanh, [ps, (der, 7)], [ig], bias=der[:, 7, c:c + 1], scale=0.5)
        yield
        ubf_pool.put(ubf)
        act(al[:, 0:nt], rg[:, 0:nt], AF.Exp, [rg, (der, 10)], [al], scale=der[:, 10, c:c + 1], bias=der[:, 10, c:c + 1])
        yield
        act(e2[:, 0:nt], rg[:, 0:nt], AF.Exp, [rg, (der, 11)], [e2], scale=der[:, 11, c:c + 1], bias=der[:, 11, c:c + 1])
        yield
        S.op("pool", lambda E: E.tensor_scalar(out=e2[:, 0:nt], in0=e2[:, 0:nt], scalar1=-0.25, scalar2=0.25,
                                               op0=ALU.mult, op1=ALU.add), [e2], [e2])
        yield
        act(e2[:, 0:nt], e2[:, 0:nt], AF.Sqrt, [e2], [e2])
        yield
        if first:
            S.op("pool", lambda E: E.memset(e2[:, 0:1], 0.5), [e2], [e2])
            yield
        S.op("dve", lambda E: E.scalar_tensor_tensor(out=ig[:, 0:nt], in0=ig[:, 0:nt], scalar=1.0, in1=e2[:, 0:nt],
                                                     op0=ALU.add, op1=ALU.mult), [ig, e2], [ig])
        yield
        tt("dve", ig[:, 0:nt], ig[:, 0:nt], u[:, 0:nt], ALU.mult, [ig, u], [ig])
        yield
        if kind == "p":
            init = 0.0 if seg.idx == 0 else hcar[:, 0:1]
            S.op("dve", lambda E: E.tensor_tensor_scan(out=hh[:, 0:nt], data0=al[:, 0:nt], data1=ig[:, 0:nt],
                                                       initial=init, op0=ALU.mult, op1=ALU.add),
                 [al, ig, hcar], [hh])
            yield
            S.op("pool", lambda E: E.tensor_copy(out=hcar[:, 0:1], in_=hh[:, nt - 1:nt]), [hh], [hcar])
            yield
            if seg.idx == NSEGP - 1:
                S.op("pool", lambda E: E.tensor_copy(out=fin[:, c, 3:4], in_=hh[:, nt - 1:nt]), [hh], [(fin, c)])
                yield
        else:
            for b in range(nb):
                S.op("dve", lambda E, b=b: E.tensor_tensor_scan(
                    out=hh[:, b * T_:(b + 1) * T_], data0=al[:, b * T_:(b + 1) * T_], data1=ig[:, b * T_:(b + 1) * T_],
                    initial=stT[:, c, 48 + b0 + b:48 + b0 + b + 1], op0=ALU.mult, op1=ALU.add),
                    [al, ig, stT, hh], [hh])
                yield
            S.op("pool", lambda E: E.tensor_copy(out=fin[:, c, 52 + b0:52 + b0 + SB].rearrange("p (b o) -> p b o", o=1),
                                                 in_=dv(hh)[:, :, T_ - 1:T_]), [hh], [(fin, c)])
            yield
        scr.put(rg, al, e2, u, ig)
        ps = proj_ps(wsel(4), seg, xseg)
        yield
        gbs = scr.get(); p_ = scr.get()
        S.op("act", lambda E: E.copy(out=dv(gbs), in_=seg.tokv(ps[:, 0:nco])), [ps], [gbs])
        yield
        tt("pool", p_[:, 0:nt], gbs[:, 0:nt], gbs[:, 0:nt], ALU.mult, [gbs], [p_])
        yield
        S.op("pool", lambda E: E.tensor_scalar(out=p_[:, 0:nt], in0=p_[:, 0:nt], scalar1=0.044715, scalar2=1.0,
                                               op0=ALU.mult, op1=ALU.add), [p_], [p_])
        yield
        tt("pool", p_[:, 0:nt], p_[:, 0:nt], gbs[:, 0:nt], ALU.mult, [p_, gbs], [p_])
        yield
        act(p_[:, 0:nt], p_[:, 0:nt], AF.Tanh, [p_], [p_], scale=0.7978845608028654)
        yield
        S.op("dve", lambda E: E.scalar_tensor_tensor(out=gbs[:, 0:nt], in0=p_[:, 0:nt], scalar=1.0, in1=gbs[:, 0:nt],
                                                     op0=ALU.add, op1=ALU.mult), [gbs, p_], [gbs])
        yield
        tt("dve", hh[:, 0:nt], hh[:, 0:nt], gbs[:, 0:nt], ALU.mult, [hh, gbs], [hh])
        yield
        scr.put(gbs)
        ps = proj_ps(wsel(6), seg, xseg)
        yield
        act(dv(p_), seg.tokv(ps[:, 0:nco]), AF.Tanh, [ps], [p_], scale=0.5)
        yield
        S.op("dve", lambda E: E.scalar_tensor_tensor(out=hh[:, 0:nt], in0=p_[:, 0:nt], scalar=1.0, in1=hh[:, 0:nt],
                                                     op0=ALU.add, op1=ALU.mult), [hh, p_], [hh])
        yield
        scr.put(p_)
        ps = proj_ps(wsel(5), seg, xseg)
        yield
        gA = scr.get()
        act(dv(gA), seg.tokv(ps[:, 0:nco]), AF.Tanh, [ps], [gA], scale=0.5)
        yield
        return dict(gA=gA, m2=hh)


    def stage1(c, seg):
        xseg = get_x(seg)
        shared = {}
        gens = [stage1a(c, seg, xseg, shared), stage1b(c, seg, xseg), stage1c(c, seg, shared)]
        who = ["s1a", "s1b", "s1c"]
        res = [None, None, None]
        done = [False, False, False]
        hold_a = False
        while not all(done):
            for i_ in range(3):
                if done[i_]:
                    continue
                if i_ == 0 and hold_a:
                    if not done[2]:
                        continue
                    hold_a = False
                cur_alloc["who"] = who[i_]
                try:
                    v_ = next(gens[i_])
                    if v_ == "NEEDC":
                        hold_a = True
                except StopIteration as e_:
                    res[i_] = e_.value
                    done[i_] = True
                yield
        xs_pool.put(xseg)
        res[0].update(res[1])
        return res[0]

    def stage2(c, seg, s1, chain):
        kind, tv, nq, nlev, nco = seg.kind, seg.tv, seg.nq, seg.nlev, seg.ncol
        B = bd[kind][seg.idx % 2]
        gam = s1["gam"]

        def bdv(n):
            return B[n][:].rearrange("p q h t -> p q (h t)")

        def br(n):
            return [(B[n], 0), (B[n], 1)]

        def q128(ps):
            return ps[:].rearrange("p (q t) -> p q t", t=128)

        def q64(ps):
            return ps[:, 0:QB * 64].rearrange("p (q t) -> p q t", t=64)

        def prod(l, r, mask):
            ps = getps()
            for q in range(QB):
                mm(ps[:, q * 128:(q + 1) * 128], bdv(l)[:, q, :], bdv(r)[:, q, :], br(l) + br(r), ps, inc=(q == QB - 1))
            o = wk.get()
            tt("dve", o[:], q128(ps), mask[:], ALU.mult, [ps, mask], [o])
            return o

        N_ = prod("bT", "aT", mUs)
        yield
        L_ = prod("aT", "bT", mLs)
        yield
        Mak = prod("kT", "aT", mUs)
        yield
        Mrb = prod("bT", "rT", mUi)
        yield
        Mrk = prod("kT", "rT", mUi)
        yield
        rTc = wk.get()
        S.op("pool", lambda E: E.tensor_copy(out=rTc[:], in_=bdv("rT")), br("rT"), [rTc])
        yield
        ck("k1")

        def tr(n):
            ps = getps()
            psb = ps[:].bitcast(BF16)
            for q in range(QB):
                S.op("pe", lambda E, q=q: E.transpose(out=psb[:, q * 128:(q + 1) * 128], in_=bdv(n)[:, q, :],
                                                      identity=ident_b[:]), br(n) + [ident_b], [ps], inc=(q == QB - 1))
            o = wk.get()
            copy_op(ev_eng(), o[:], psb[:, 0:QB * 128].rearrange("p (q t) -> p q t", t=128), [ps], [o])
            return o

        XA = tr("aT")
        yield
        Bg = tr("bgT")
        yield
        Kg = tr("kgT")
        yield
        ck("k2")
        ps = getps()
        for q in range(QB):
            mm(ps[:, q * 64:(q + 1) * 64], bdv("vT")[:, q, :], II_b[:], br("vT") + [II_b], ps, inc=(q == QB - 1))
            yield
        V_ = wk.get()
        copy_op(ev_eng(), V_[:, :, 0:64], q64(ps), [ps], [V_])
        yield ("BDFREE", (c, kind, seg.idx))
        yield
        ps = getps()
        for q in range(QB):
            mm(ps[:, q * 64:(q + 1) * 64], Mak[:, q, :], V_[:, q, 0:64], [Mak, V_], ps, inc=(q == QB - 1))
            yield
        XU = wk.get()
        copy_op(ev_eng(), XU[:, :, 0:64], q64(ps), [ps], [XU])
        yield
        wk.put(Mak)
        ck("k3")
        Nc, Lc = N_, L_
        for lev in range(nlev):
            psA = getps()
            for q in range(QB):
                mm(psA[:, q * 128:(q + 1) * 128], Nc[:, q, :], XA[:, q, :], [Nc, XA], psA, start=True, stop=False, inc=False)
                mm(psA[:, q * 128:(q + 1) * 128], ident_b[:], XA[:, q, :], [ident_b, XA], psA, start=False, stop=True,
                   inc=(q == QB - 1))
            yield
            psU = getps()
            for q in range(QB):
                mm(psU[:, q * 64:(q + 1) * 64], Nc[:, q, :], XU[:, q, 0:64], [Nc, XU], psU, start=True, stop=False, inc=False)
                mm(psU[:, q * 64:(q + 1) * 64], ident_b[:], XU[:, q, 0:64], [ident_b, XU], psU, start=False, stop=True,
                   inc=(q == QB - 1))
            yield
            XA2 = wk.get(); XU2 = wk.get()
            copy_op("act", XA2[:], q128(psA), [psA], [XA2])
            yield
            copy_op("act", XU2[:, :, 0:64], q64(psU), [psU], [XU2])
            yield
            N2 = L2 = None
            if lev < nlev - 1:
                psN = getps()
                for q in range(QB):
                    mm(psN[:, q * 128:(q + 1) * 128], Lc[:, q, :], Nc[:, q, :], [Lc, Nc], psN, inc=(q == QB - 1))
                    yield
                N2 = wk.get()
                copy_op("act", N2[:], q128(psN), [psN], [N2])
                yield
                if lev < nlev - 2:
                    psL = getps()
                    for q in range(QB):
                        mm(psL[:, q * 128:(q + 1) * 128], Nc[:, q, :], Lc[:, q, :], [Lc, Nc], psL, inc=(q == QB - 1))
                        yield
                    L2 = wk.get()
                    copy_op("act", L2[:], q128(psL), [psL], [L2])
                    yield
            wk.put(XA, XU, Nc, Lc)
            XA, XU, Nc, Lc = XA2, XU2, N2, L2
            if Nc is None:
                Nc = wk.get()
            if Lc is None:
                Lc = wk.get()
        wk.put(Nc, Lc)
        ck("k4")
        psR = getps()
        for q in range(QB):
            mm(psR[:, q * 128:(q + 1) * 128], XA[:, q, :], Mrb[:, q, :], [XA, Mrb], psR, start=True, stop=False, inc=False)
            yield
            mm(psR[:, q * 128:(q + 1) * 128], ident_b[:], rTc[:, q, :], [ident_b, rTc], psR, start=False,
               stop=True, inc=(q == QB - 1))
            yield
        Rh = wk.get()
        copy_op(ev_eng(), Rh[:], q128(psR), [psR], [Rh])
        yield
        psG = getps()
        for q in range(QB):
            mm(psG[:, q * 128:(q + 1) * 128], XA[:, q, :], Bg[:, q, :], [XA, Bg], psG, inc=(q == QB - 1))
            yield
        GT = wk.get()
        copy_op(ev_eng(), GT[:], q128(psG), [psG], [GT])
        yield
        ck("k5")
        if kind == "p":
            yield ("CHAIN", (c, seg.idx - 1))
            chain["init"]()
        psH = getps()
        if kind == "p":
            hA, hB = chain["hA"], chain["hB"]
            for q in range(QB):
                sl = slice(q * 64, (q + 1) * 64)
                mm(psH[:, sl], Bg[:, q, :], XU[:, q, 0:64], [Bg, XU], psH, start=True, stop=False, inc=False)
                yield
                mm(psH[:, sl], Kg[:, q, :], V_[:, q, 0:64], [Kg, V_], psH, start=False, stop=False, inc=False)
                yield
                mm(psH[:, sl], GT[:, q, :], hB[:, q, :], [GT, (hB, q)], psH, start=False, stop=True)
                yield
                S.op("dve", lambda E, q=q, sl=sl: E.scalar_tensor_tensor(
                    out=hA[:, q + 1, :], in0=hA[:, q, :], scalar=gam[:, q:q + 1], in1=psH[:, sl],
                    op0=ALU.mult, op1=ALU.add), [(hA, q), gam, psH], [(hA, q + 1)])
                yield
                S.op("act", lambda E, q=q: E.copy(out=hB[:, q + 1, :], in_=hA[:, q + 1, :]), [(hA, q + 1)], [(hB, q + 1)])
                yield
            hBr = [(hB, q) for q in range(QB)]
        else:
            hA, hB, hO = chain["hA"], chain["hB"], chain["hO"]
            for q in range(QB):
                sl = slice(q * 64, (q + 1) * 64)
                mm(psH[:, sl], Bg[:, q, :], XU[:, q, 0:64], [Bg, XU], psH, start=True, stop=False, inc=False)
                yield
                mm(psH[:, sl], Kg[:, q, :], V_[:, q, 0:64], [Kg, V_], psH, start=False, stop=False, inc=False)
                yield
                mm(psH[:, sl], GT[:, q, :], hB[:, q, :], [GT, (hB, q)], psH, start=False, stop=True,
                   inc=(q == QB - 1))
                yield
            tmp = scr.get()
            tmpv = tmp[:, 0:QB * 64].rearrange("p (q t) -> p q t", t=64)
            tt("dve", tmpv, hA[:, 0:QB, :], gam[:].rearrange("p (q o) -> p q o", o=1).to_broadcast([128, QB, 64]),
               ALU.mult, [(hA, q) for q in range(QB)] + [gam], [tmp])
            yield
            tt("dve", hO[:, 0:QB, :], tmpv, q64(psH), ALU.add, [tmp, psH], [hO])
            yield
            scr.put(tmp)
            hBr = [(hB, q) for q in range(QB)]
        ck("k6")
        psY = getps()
        for q in range(QB):
            sl = slice(q * 64, (q + 1) * 64)
            mm(psY[:, sl], Mrb[:, q, :], XU[:, q, 0:64], [Mrb, XU], psY, start=True, stop=False, inc=False)
            yield
            mm(psY[:, sl], Mrk[:, q, :], V_[:, q, 0:64], [Mrk, V_], psY, start=False, stop=False, inc=False)
            yield
            mm(psY[:, sl], Rh[:, q, :], hB[:, q, :], [Rh, (hB, q)], psY, start=False, stop=True, inc=(q == QB - 1))
            yield
        wk.put(Mrb, Mrk, XA, XU, Bg, Kg, V_, Rh, GT, rTc)
        if not isinstance(gam, Off):
            gam_pool.put(gam)
        ck("k7")
        ysb = scr.get()
        ysbv = ysb[:, 0:QB * 64].rearrange("p (q t) -> p q t", t=64)
        S.op("act", lambda E: E.copy(out=ysbv, in_=q64(psY)), [psY], [ysb])
        yield ("TAIL", (c, seg.idx) if kind == "p" else None)
        stq = lambda i: stat[:, i, :]
        ysq = scr.get()
        ysqv = ysq[:, 0:QB * 64].rearrange("p (q t) -> p q t", t=64)
        for q in range(QB):
            S.op("dve", lambda E, q=q: E.bn_stats(out=bst4[:, q, :], in_=ysb[:, q * 64:(q + 1) * 64]), [ysb], [(bst4, q)])
        for q in range(QB):
            S.op("dve", lambda E, q=q: E.bn_aggr(out=mvq[:, q, :], in_=bst4[:, q, :]), [(bst4, q)], [(mvq, q)])
        yield
        mvr = [(mvq, q) for q in range(QB)]
        S.op("pool", lambda E: E.tensor_scalar(out=stq(4), in0=mvq[:, :, 1], scalar1=1.0, scalar2=GN_EPS,
                                               op0=ALU.mult, op1=ALU.add), mvr, [(stat, 4)])
        S.op("pool", lambda E: E.tensor_tensor(out=stq(5), in0=stq(4), in1=negh[:], op=ALU.pow),
             [(stat, 4), negh], [(stat, 5)])
        yield
        tt("dve", ysqv, ysbv, mvq[:, :, 0:1].to_broadcast([128, QB, 64]), ALU.subtract, [ysb] + mvr, [ysq])
        scr.put(ysb)
        yield
        for h in range(2):
            P = slice(h * 64, (h + 1) * 64)
            tt(ew_eng(), ybd[P, :, h, :], ysqv[P], stat[P, 5, :].rearrange("p (q o) -> p q o", o=1).to_broadcast([64, QB, 64]),
               ALU.mult, [ysq, (stat, 5)], [(ybd, h)])
            yield
        scr.put(ysq)
        ck("k8")
        psT = getps()
        ybv = ybd[:].rearrange("p q h t -> p q (h t)")
        for q in range(QB):
            mm(psT[:, q * 64:(q + 1) * 64], ybv[:, q, :], II_f[:], [(ybd, 0), (ybd, 1), II_f], psT, inc=(q == QB - 1))
            yield
        ck("k9")
        T_, nb = seg.T, seg.nb
        nt = nb * T_

        def dv(t_):
            return t_[:, 0:nt].rearrange("p (b t) -> p b t", t=T_)

        if kind == "p":
            ysrc = psT[:, 0:QB * 64].rearrange("p (b t) -> p b t", b=1)
        else:
            ysrc = q64(psT)[:, :, 0:DEC]
        yo = scr.get()
        act(dv(yo), ysrc, AF.Identity, [psT, vecs], [yo], bias=vcol(V_LNXB, c), scale=vcol(V_LNXG, c))
        yield
        tt("dve", dv(yo), dv(yo), seg.tokv(s1["bonus"][:, 0:nco]), ALU.add, [yo, s1["bonus"]], [yo])
        yield
        tt("pool", dv(yo), dv(yo), seg.tokv(s1["g"][:, 0:nco]), ALU.mult, [yo, s1["g"]], [yo])
        yield
        S.op("dve", lambda E: E.scalar_tensor_tensor(out=dv(yo), in0=dv(s1["gA"]), scalar=1.0, in1=dv(yo),
                                                     op0=ALU.add, op1=ALU.mult), [yo, s1["gA"]], [yo])
        yield
        mv = merged[:, c, seg.m0:seg.m0 + seg.mcols].rearrange("p (b t) -> p b t", t=T_)
        S.op("dve", lambda E: E.scalar_tensor_tensor(out=mv, in0=dv(s1["m2"]), scalar=0.25, in1=dv(yo),
                                                     op0=ALU.mult, op1=ALU.add), [yo, s1["m2"]], [(merged, c)])
        yield
        scr.put(yo)
        if not isinstance(s1["bonus"], Off):
            scr.put(s1["bonus"], s1["g"], s1["gA"], s1["m2"])

    def state_out(c, hsrc_ap, hres, dst_ap):
        for h in range(2):
            P = slice(h * 64, (h + 1) * 64)
            S.op(ew_eng(), lambda E: E.tensor_copy(out=hbd[P, h, :], in_=hsrc_ap[P, :]), [hres], [(hbd, h)])
        ps = getps()
        mm(ps[:, 0:64], hbd[:].rearrange("p h t -> p (h t)"), II_f[:], [(hbd, 0), (hbd, 1), II_f], ps)
        so = scr.get()
        copy_op(ev_eng(), so[:, 0:64], ps[:, 0:64], [ps], [so])
        S.dma(dst_ap, so[:, 0:64], reads=[so], q="act")
        scr.put(so)

    for g in range(3):
        S.op("pool", lambda E, g=g: E.memset(zbuf[g][:], 0.0), writes=[zbuf[g]])

    def s2full(c, seg, s1, par):
        if seg.kind == "p":
            hA, hB = H32[par], Hbf[par]

            def chain_init():
                if seg.idx == 0:
                    S.op("pool", lambda E: E.memset(hA[:, 0, :], 0.0), writes=[(hA, 0)])
                    S.op("pool", lambda E: E.memset(hB[:, 0, :], 0.0), writes=[(hB, 0)])
                else:
                    pA, pB = H32[1 - par], Hbf[1 - par]
                    S.op("pool", lambda E: E.tensor_copy(out=hA[:, 0, :], in_=pA[:, QB, :]), [(pA, QB)], [(hA, 0)])
                    S.op("pool", lambda E: E.tensor_copy(out=hB[:, 0, :], in_=pB[:, QB, :]), [(pB, QB)], [(hB, 0)])

            yield from stage2(c, seg, s1, dict(hA=hA, hB=hB, init=chain_init))
            if seg.idx == NSEGP - 1:
                state_out(c, hA[:, QB, :], (hA, QB),
                          o_wkv_p[2 * c:2 * c + 2, :, :].rearrange("h i j -> (h i) j"))
                yield
        else:
            sp_ = seg.idx % 2
            hA, hB, hO = H32s[sp_][0], Hbfs[sp_], H32s[sp_][1]
            b0 = seg.idx * QB
            for q in range(QB):
                sin = sin_tiles[sin_ctr["i"] % 3]
                sin_ctr["i"] += 1
                S.dma(sin[:, 0:64], s0[b0 + q, 2 * c:2 * c + 2, :, :].rearrange("h i j -> (h i) j"), writes=[sin])
                for h in range(2):
                    P = slice(h * 64, (h + 1) * 64)
                    S.op(ew_eng(), lambda E: E.tensor_copy(out=sbd[P, h, :], in_=sin[P, 0:64]), [sin], [(sbd, h)])
                ps = getps()
                mm(ps[:, 0:64], sbd[:].rearrange("p h t -> p (h t)"), II_f[:], [(sbd, 0), (sbd, 1), II_f], ps)
                S.op("act", lambda E: E.copy(out=hA[:, q, :], in_=ps[:, 0:64]), [ps], [(hA, q)])
                S.op("dve", lambda E: E.tensor_copy(out=hB[:, q, :], in_=ps[:, 0:64]), [ps], [(hB, q)])
                yield
            yield from stage2(c, seg, s1, dict(hA=hA, hB=hB, hO=hO))
            for q in range(QB):
                state_out(c, hO[:, q, :], hO,
                          o_wkv_s[b0 + q, 2 * c:2 * c + 2, :, :].rearrange("h i j -> (h i) j"))
                yield

    NSTREAM = 2
    mains = []
    tails = []
    SID = ("A", "B")

    chain_done = set()
    bd_free = set()

    def step_main(m):
        if m[2] == "TAILWAIT":
            if tails:
                return True
            mains.remove(m)
            tails.append(m)
            return False
        if m[2] is not None:
            if m[2][1] >= 0 and m[2] not in chain_done:
                return True
            m[2] = None
        cur_alloc["who"] = "s2" + SID[m[1]]
        try:
            v = next(m[0])
        except StopIteration:
            mains.remove(m)
            return False
        if isinstance(v, tuple) and v[0] == "BDFREE":
            bd_free.add(v[1])
            return True
        if isinstance(v, tuple) and v[0] == "CHAIN":
            m[2] = v[1]
            return True
        if isinstance(v, tuple) and v[0] == "TAIL":
            if v[1] is not None:
                chain_done.add(v[1])
            if tails:
                m[2] = "TAILWAIT"
                return True
            mains.remove(m)
            tails.append(m)
            return False
        return True

    def step_tails():
        for m in list(tails):
            cur_alloc["who"] = "tail" + SID[m[1]]
            try:
                next(m[0])
            except StopIteration:
                tails.remove(m)

    def step_bg():
        for m in list(mains):
            step_main(m)
        step_tails()
        step_wgen()

    def run_stage1(g1):
        acc = 0.0
        while True:
            step_bg()
            acc += RSCALE if (mains or tails) else 8.0
            while acc >= 1.0:
                acc -= 1.0
                cur_alloc["who"] = "s1"
                try:
                    next(g1)
                except StopIteration as e_:
                    cur_alloc["who"] = None
                    return e_.value

    def start_main(gen):
        while len(mains) >= NSTREAM:
            step_bg()
        used = {m[1] for m in mains} | {m[1] for m in tails}
        while len(used) >= 2:
            step_bg()
            used = {m[1] for m in mains} | {m[1] for m in tails}
        sid = 0 if 0 not in used else 1
        mains.append([gen, sid, None])

    def drain_all():
        while mains or tails:
            step_bg()
        cur_alloc["who"] = None

    def load_weights(c):
        for g in range(7):
            stg = stage[g % 3]
            S.dma(stg[:], w1[c, :, g, :, :], writes=[stg])
            S.op("dve", lambda E, g=g, stg=stg: E.tensor_copy(out=wcb[:, g, :, :], in_=stg[:]), [stg], [(wcb, g)])

    def load_weights_gen(c, gap=7):
        for g in range(7):
            stg = stage[g % 3]
            S.dma(stg[:], w1[c, :, g, :, :], writes=[stg])
            for _ in range(gap):
                yield
            S.op("dve", lambda E, g=g, stg=stg: E.tensor_copy(out=wcb[:, g, :, :], in_=stg[:]), [stg], [(wcb, g)])
            yield

    wgen = {"g": None}

    def step_wgen():
        if wgen["g"] is not None:
            try:
                next(wgen["g"])
            except StopIteration:
                wgen["g"] = None

    def finish_wgen():
        while wgen["g"] is not None:
            step_wgen()

    def finalize_after(gen, fn):
        yield from gen
        fn()

    try:
        load_weights(0)
        for c in range(NCH):
            par = 0
            for segi, seg in enumerate(segs):
                if stop_after == "seg%d" % segi:
                    raise StopBuild()
                cur["segi"] = segi
                if segi == 0:
                    finish_wgen()
                s1 = run_stage1(stage1(c, seg))
                if seg.kind == "p":
                    start_main(s2full(c, seg, s1, par))
                    par = 1 - par
                else:
                    if c + 1 < NCH and stop_after != "c0":
                        wgen["g"] = load_weights_gen(c + 1)
                    nbatch = NSB // QB
                    for b in range(nbatch):
                        sub = SubSeg(b)
                        while b >= 2 and (c, "s", b - 2) not in bd_free:
                            step_bg()
                        cur_alloc["who"] = "s1"
                        for _ in bd_write(sub, bd["s"][b % 2], s1["keep"], b * QB * 5):
                            pass
                        s1b = dict(bonus=Off(s1["bonus"], b * QB * 5, QB * 5), g=Off(s1["g"], b * QB * 5, QB * 5),
                                   gA=Off(s1["gA"], b * QB * DEC, QB * DEC), m2=Off(s1["m2"], b * QB * DEC, QB * DEC),
                                   gam=Off(s1["gam"], b * QB, QB))
                        gen = s2full(c, sub, s1b, 0)
                        if b == nbatch - 1:
                            K_ = s1["keep"]
                            scr.put(K_["kk"], K_["E3"], K_["tb"], K_["E2"], K_["E4"], K_["kf"], K_["zr"], K_["E1"], K_["zv"])

                            def fin_(s1=s1):
                                scr.put(s1["bonus"], s1["g"], s1["gA"], s1["m2"])
                                gam_pool.put(s1["gam"])

                            gen = finalize_after(gen, fin_)
                        start_main(gen)
            if stop_after == "c0":
                break
        drain_all()
    except StopBuild:
        pass
    dump("merged", merged[:, 0, :], [(merged, 0)])
    dump("merged7", merged[:, 7, :], [(merged, 7)])
    dump("fin", fin[:, 0, :], [(fin, 0)])

    print("nops", S.nops, "cnt", {k: v for k, v in S.cnt.items() if not k.startswith("d")})
    if stop_after is not None:
        S.finish()
        p1.close()
        es.close()
        return nc
    psF = [getps(), getps()]
    for c in range(NCH):
        hf, k4 = divmod(c, 4)
        S.op("pe", lambda E: E.transpose(out=psF[hf][0:NFIN, k4 * 128:(k4 + 1) * 128], in_=fin[:, c, :],
                                         identity=ident_f[:]), [(fin, c), ident_f], [psF[hf]])
    S.barrier()
    p1.close()
    p2 = ExitStack()
    lnvs = sb("lnvs", [128, 2, D], F32, p2)
    h1T = sb("h1T", [128, 8, MCOLS], BF16, p2)
    stg2 = [sb("stg2_%d" % i, [128, D], F32, p2) for i in range(3)]
    big = [sb("big%d" % i, [128, D], F32, p2) for i in range(4)]
    bst = sb("bst", [128, 2, 6], F32, p2)
    mv_ = sb("mv_", [128, 8], F32, p2)
    aI = sb("aI", [128, 2, 128], BF16, p2)
    finT = sb("finT", [NFIN, D], F32, p2)
    p2a = ExitStack()
    wo_b = sb("wo_b", [128, 8, D], BF16, p2a)
    a_hi = float(np.float32(ALPHA).astype(ml_bf16).astype(np.float32))
    a_lo = float(np.float32(ALPHA - a_hi).astype(ml_bf16).astype(np.float32))
    S.op("pool", lambda E: E.tensor_scalar(out=aI[:, 0, :], in0=ident_f[:], scalar1=a_hi, scalar2=0.0,
                                           op0=ALU.mult, op1=ALU.add), [ident_f], [(aI, 0)])
    S.op("pool", lambda E: E.tensor_scalar(out=aI[:, 1, :], in0=ident_f[:], scalar1=a_lo, scalar2=0.0,
                                           op0=ALU.mult, op1=ALU.add), [ident_f], [(aI, 1)])
    for hf in range(2):
        copy_op(ev_eng(), finT[:, hf * 512:(hf + 1) * 512], psF[hf][0:NFIN, :], [psF[hf]], [finT])
    S.dma(o_fin[:, :], finT[:, :], reads=[finT])
    S.dma(o_shift[0:1, :], xp[SEQ - 1:SEQ, :])
    S.dma(o_shift[1:1 + NSB, :], xst.rearrange("(b t) d -> b t d", t=DEC)[:, DEC - 1, :])
    S.dma(lnvs[:], lnv[:, 0:2, :], writes=[lnvs])
    for kc in range(8):
        stg = stg2[kc % 3]
        S.dma(stg[:], wo[:, kc, :], writes=[stg])
        S.op("dve", lambda E, kc=kc, stg=stg: E.tensor_copy(out=wo_b[:, kc, :], in_=stg[:]), [stg], [(wo_b, kc)])

    def layer_norm(src, dst, gi, rows):
        R = slice(0, rows)
        for j in range(2):
            S.op("dve", lambda E, j=j: E.bn_stats(out=bst[R, j, :], in_=src[R, j * 512:(j + 1) * 512]), [src], [bst])
        S.op("dve", lambda E: E.bn_aggr(out=mv_[R, 0:2], in_=bst[R, :, :]), [bst], [(mv_, 0)])
        S.op("dve", lambda E: E.tensor_scalar(out=mv_[R, 2:3], in0=mv_[R, 1:2], scalar1=LN_EPS, scalar2=None,
                                              op0=ALU.add), [(mv_, 0)], [(mv_, 2)])
        act(mv_[R, 3:4], mv_[R, 2:3], AF.Sqrt, [(mv_, 2)], [(mv_, 3)])
        S.op("dve", lambda E: E.reciprocal(out=mv_[R, 4:5], in_=mv_[R, 3:4]), [(mv_, 3)], [(mv_, 4)])
        S.op("dve", lambda E: E.scalar_tensor_tensor(out=mv_[R, 5:6], in0=mv_[R, 0:1], scalar=-1.0, in1=mv_[R, 4:5],
                                                     op0=ALU.mult, op1=ALU.mult), [(mv_, 0), (mv_, 4)], [(mv_, 5)])
        act(dst[R, :], src[R, :], AF.Identity, [src, (mv_, 4), (mv_, 5)], [dst], bias=mv_[R, 5:6], scale=mv_[R, 4:5])
        tt("pool", dst[R, :], dst[R, :], lnvs[R, 0, :], ALU.mult, [dst, lnvs], [dst])
        tt("dve", dst[R, :], dst[R, :], lnvs[R, 1, :], ALU.add, [dst, lnvs], [dst])

    ttiles = [(t * 128, 128, yp[t * 128:(t + 1) * 128, :], xp[t * 128:(t + 1) * 128, :]) for t in range(SEQ // 128)]
    ttiles.append((SEQ, NSB * DEC, ys[:, :], xst[:, :]))

    for ti, (m0, rows, _, xrows) in enumerate(ttiles):
        R = slice(0, rows)
        xtm = big[ti % 2]
        s1t = big[2 + ti % 2]
        S.dma(xtm[R, :], xrows, writes=[xtm])
        pss = [getps(), getps()]
        for hf in range(2):
            for kc in range(8):
                mm(pss[hf][R, :], merged[:, kc, m0:m0 + rows], wo_b[:, kc, hf * 512:(hf + 1) * 512],
                   [(merged, kc), (wo_b, kc)], pss[hf], start=(kc == 0), stop=(kc == 7), inc=(kc == 7))
        for hf in range(2):
            S.op("dve", lambda E, hf=hf: E.scalar_tensor_tensor(
                out=s1t[R, hf * 512:(hf + 1) * 512], in0=xtm[R, hf * 512:(hf + 1) * 512], scalar=ALPHA,
                in1=pss[hf][R, :], op0=ALU.mult, op1=ALU.add), [xtm, pss[hf]], [s1t])
        layer_norm(s1t, xtm, 0, rows)
        pst = [getps(), getps()]
        for kc in range(8):
            hf, k4 = divmod(kc, 4)
            S.op("pe", lambda E, hf=hf, k4=k4, kc=kc: E.transpose(
                out=pst[hf][:, k4 * 128:k4 * 128 + rows], in_=xtm[R, kc * 128:(kc + 1) * 128],
                identity=ident_f[R, R]), [xtm, ident_f], [pst[hf]], inc=(k4 == 3))
        for hf in range(2):
            copy_op(ev_eng(), h1T[:, hf * 4:(hf + 1) * 4, m0:m0 + rows],
                    pst[hf][:].rearrange("p (k t) -> p k t", t=128)[:, :, 0:rows], [pst[hf]],
                    [(h1T, hf * 4 + k) for k in range(4)])
    if "h1T" in dbg_out:
        S.dma(dbg_out["h1T"], h1T[:, 0, :], reads=[(h1T, 0)])

    S.barrier()
    p2a.close()
    S.dma(lnvs[:], lnv[:, 2:4, :], writes=[lnvs])
    wd_b = sb("wd_b", [128, NFC, D], BF16, p2)
    NTH = 1088
    NFA = 15
    uTa = T(merged[:].rearrange("p a b -> p (a b)")[:, 0:NFA * NTH].rearrange("p (f t) -> p f t", t=NTH), "uTa")
    uTb = sb("uTb", [128, NFC - NFA, NTH], BF16, p2)

    class _UT:
        def __getitem__(self, k):
            p_, fc_, cols_ = k
            if fc_ < NFA:
                return uTa.h[p_, fc_, cols_]
            return uTb.h[p_, fc_ - NFA, cols_]

    uT = _UT()
    wgub = [sb("wgub%d" % i, [128, 2, 8, 128], BF16, p2) for i in range(2)]
    sgt = [sb("sgt%d" % i, [128, 512], F32, p2) for i in range(2)]
    for fc in range(NFC):
        stg = stg2[fc % 3]
        S.dma(stg[:], wd[fc, :, :], writes=[stg])
        S.op("dve", lambda E, fc=fc, stg=stg: E.tensor_copy(out=wd_b[:, fc, :], in_=stg[:]), [stg], [(wd_b, fc)])
    halves = [(0, 1024, ttiles[0:8]), (1024, 1088, ttiles[8:17])]
    for (c0, ncols, tls) in halves:
        blocks = [(b0_, min(512, ncols - b0_)) for b0_ in range(0, ncols, 512)]
        for fc in range(NFC):
            wb = wgub[fc % 2]
            for j in range(2):
                stg = stg2[(2 * fc + j) % 3]
                S.dma(stg[:].rearrange("p (a b) -> p a b", b=128), wgu[fc, :, j, :, :], writes=[stg])
                S.op("dve", lambda E, j=j, stg=stg: E.tensor_copy(
                    out=wb[:, j, :, :], in_=stg[:].rearrange("p (a b) -> p a b", b=128)), [stg], [(wb, j)])
            for bi, (b0_, bn) in enumerate(blocks):
                psg = getps()
                psu = getps()
                for j, ps_ in ((0, psg), (1, psu)):
                    for kc in range(8):
                        mm(ps_[:, 0:bn], wb[:, j, kc, :], h1T[:, kc, c0 + b0_:c0 + b0_ + bn], [(wb, j), (h1T, kc)], ps_,
                           start=(kc == 0), stop=(kc == 7), inc=(kc == 7))
                sg_ = sgt[(fc * 3 + bi) % 2]
                act(sg_[:, 0:bn], psg[:, 0:bn], AF.Silu, [psg], [sg_])
                tt("dve", uT[:, fc, b0_:b0_ + bn], psu[:, 0:bn], sg_[:, 0:bn], ALU.mult, [psu, sg_], [(uT, fc)])
        for ti, (m0, rows, yout, _) in enumerate(tls):
            R = slice(0, rows)
            l0 = m0 - c0
            pss = [getps(), getps()]
            for hf in range(2):
                for fc in range(NFC):
                    mm(pss[hf][R, :], uT[:, fc, l0:l0 + rows], wd_b[:, fc, hf * 512:(hf + 1) * 512],
                       [(uT, fc), (wd_b, fc)], pss[hf], start=(fc == 0), stop=False, inc=False)
                for k4 in range(4):
                    kc = hf * 4 + k4
                    for a in range(2):
                        last = (k4 == 3 and a == 1)
                        mm(pss[hf][R, k4 * 128:(k4 + 1) * 128], h1T[:, kc, m0:m0 + rows], aI[:, a, :],
                           [(h1T, kc), (aI, a)], pss[hf], start=False, stop=last, inc=last)
            s2t = big[ti % 2]
            yt = big[2 + ti % 2]
            for hf in range(2):
                copy_op(ev_eng(), s2t[R, hf * 512:(hf + 1) * 512], pss[hf][R, :], [pss[hf]], [s2t])
            layer_norm(s2t, yt, 2, rows)
            S.dma(yout, yt[R, :], reads=[yt])

    print("nops", S.nops)
    S.finish()
    p2.close()
    es.close()
    return nc


def kernel(**inputs):
    maps = prep_inputs(inputs)
    nc = build_program()
    res = run_bass_kernel_spmd(nc, maps, core_ids=list(range(NCORES)))
    R = res.results
    f32 = np.float32
    y_p = np.stack([R[i]["yp"] for i in range(NCORES)]).astype(f32)
    y_s = np.concatenate([R[i]["ys"].reshape(NSB, DEC, D) for i in range(NCORES)]).astype(f32)
    fin = [R[i]["o_fin"] for i in range(NCORES)]
    shf = [R[i]["o_shift"] for i in range(NCORES)]
    new_shift_p = np.stack([s[0] for s in shf])[None].astype(f32)
    new_shift_s = np.concatenate([s[1:] for s in shf])[None].astype(f32)
    new_wkv_p = np.stack([R[i]["o_wkv_p"] for i in range(NCORES)])[None].astype(f32)
    new_wkv_s = np.concatenate([R[i]["o_wkv_s"] for i in range(NCORES)])[None].astype(f32)
    new_conv_p = np.stack([f[0:3] for f in fin])[None].astype(f32)
    new_lru_p = np.stack([f[3] for f in fin])[None].astype(f32)
    new_conv_s = np.concatenate([f[4:52].reshape(NSB, 3, D) for f in fin])[None].astype(f32)
    new_lru_s = np.concatenate([f[52:68] for f in fin])[None].astype(f32)
    return (y_p, y_s, new_shift_p, new_wkv_p, new_conv_p, new_lru_p,
            new_shift_s, new_wkv_s, new_conv_s, new_lru_s)


def prep_inputs(inp):
    f = lambda a: np.ascontiguousarray(np.asarray(a, dtype=np.float32))
    w_in = f(inp["w_in"])[0]
    bases = [0, 1024, 2048, 3328, 4352, 5376, 6400]
    w1 = np.stack([w_in[:, b:b + 1024].reshape(8, 128, 8, 128).transpose(2, 1, 0, 3) for b in bases], axis=2)
    wl = w_in[:, 3072:3328].reshape(8, 128, 2, 128).transpose(1, 2, 0, 3)
    wo = f(inp["w_o"])[0].reshape(8, 128, D).transpose(1, 0, 2)
    wg = f(inp["w_ffn_gate"])[0].reshape(8, 128, NFC, 128).transpose(2, 1, 0, 3)
    wu = f(inp["w_ffn_up"])[0].reshape(8, 128, NFC, 128).transpose(2, 1, 0, 3)
    wgu = np.stack([wg, wu], axis=2)
    wd = f(inp["w_ffn_down"])[0].reshape(NFC, 128, D)
    vec = np.zeros((NV, D), np.float32)
    mu = f(inp["tmix_mu"])[0]
    vec[V_MUR] = mu[0:1024]; vec[V_MUK] = mu[1024:2048]; vec[V_MUV] = mu[2048:3072]
    vec[V_MUL, 0:256] = mu[3072:3328]
    vec[V_W0] = f(inp["w0"])[0]; vec[V_A0] = f(inp["a0"])[0]
    vec[V_KK] = f(inp["k_k"])[0]; vec[V_KA] = f(inp["k_a"])[0]
    vec[V_RK] = f(inp["r_k"])[0].reshape(-1)
    vec[V_LNXG] = f(inp["lnx_g"])[0]; vec[V_LNXB] = f(inp["lnx_b"])[0]
    cw = f(inp["conv_w"])[0]
    for j in range(4):
        vec[V_CW0 + j] = cw[j]
    vec[V_CB] = f(inp["conv_b"])[0]
    vec[V_BA] = f(inp["lru_ba"])[0].reshape(-1); vec[V_BI] = f(inp["lru_bi"])[0].reshape(-1)
    vec[V_LAM] = f(inp["lru_lambda"])[0]
    vecp = np.ascontiguousarray(vec.reshape(NV, 8, 128).transpose(2, 0, 1))
    wab = np.zeros((128, 2, 8, 128), np.float32)
    for j, nm in enumerate(("lru_wa", "lru_wi")):
        w = f(inp[nm])[0]
        for c in range(8):
            for bl in range(2):
                wab[bl * 64:(bl + 1) * 64, j, c, bl * 64:(bl + 1) * 64] = w[2 * c + bl]
    lnv = np.stack([np.broadcast_to(f(inp[n])[0], (128, D)) for n in ("ln1_g", "ln1_b", "ln2_g", "ln2_b")], axis=1)
    shared = dict(w1=w1, wl=wl, wo=wo, wgu=wgu, wd=wd, vec=vecp, w2d=f(inp["w2_decay"])[0],
                  a2d=f(inp["a2_iclr"])[0], g2d=f(inp["g2_gate"])[0], wab=wab, lnv=lnv)
    shared = {k: np.ascontiguousarray(v, dtype=np.float32) for k, v in shared.items()}
    x_prompt = f(inp["x_prompt"]); x_sample = f(inp["x_sample"])
    sh = f(inp["state_shift"])[0]; swkv = f(inp["state_wkv"])[0]
    sconv = f(inp["state_conv"])[0]; slru = f(inp["state_lru"])[0]
    maps = []
    for i in range(NCORES):
        b0 = i * NSB
        xs = np.concatenate([sh[b0:b0 + NSB, None, :], x_sample[b0:b0 + NSB]], axis=1).reshape(NSB * 5, D)
        stt = np.concatenate([sconv[b0:b0 + NSB].reshape(NSB * 3, D), slru[b0:b0 + NSB]], axis=0)
        m = dict(xp=x_prompt[i], xs=xs, xst=x_sample[b0:b0 + NSB].reshape(NSB * DEC, D), st=stt,
                 s0=swkv[b0:b0 + NSB])
        m = {k: np.ascontiguousarray(v, dtype=np.float32) for k, v in m.items()}
        m.update(shared)
        maps.append(m)
    return maps
```

```python
import numpy as np
import ml_dtypes
from contextlib import ExitStack
import concourse.bass as bass
import concourse.mybir as mybir
from concourse.bass_utils import run_bass_kernel_spmd

F32 = mybir.dt.float32
BF16 = mybir.dt.bfloat16
AF = mybir.ActivationFunctionType
ALU = mybir.AluOpType
AX = mybir.AxisListType
ml_bf16 = ml_dtypes.bfloat16

D = 1024
NCORES = 8
SEQ = 2048
NSB = 16
DEC = 4
NCH = 8
D_FF = 2816
NFC = D_FF // 128
KAPPA = float(np.exp(-0.5))
ALPHA = 2.0 ** 0.25
LN_EPS = 1e-5
GN_EPS = 64e-5
CH = 64
PIPE = True
RSCALE = 1.0
TAILPIPE = True
TAILSEL = lambda kind, idx: True
QB = 4
SEGP = QB * CH
NSEGP = SEQ // SEGP
SB = 16
NSEGS = NSB // SB
XS0 = SEQ
XCOLS = SEQ + NSB * 5
MCOLS = SEQ + NSB * DEC
NFIN = 68
(V_MUR, V_MUK, V_MUV, V_W0, V_A0, V_KK, V_KA, V_RK, V_LNXG, V_LNXB, V_CW0, V_CW1, V_CW2, V_CW3,
 V_CB, V_BA, V_BI, V_LAM, V_MUL) = range(19)
NV = 19


class Sched:
    COMPUTE = ("pe", "act", "dve", "pool")

    def __init__(self, nc, es, ndma=14):
        self.nc = nc
        self.eng = {"pe": nc.tensor, "act": nc.scalar, "dve": nc.vector, "pool": nc.gpsimd, "sp": nc.sync}
        self.sem = {}
        self.cnt = {}
        for e in self.COMPUTE:
            self.sem[e] = es.enter_context(nc.semaphore("sem_" + e))
            self.cnt[e] = 0
        self.ndma = ndma
        for i in range(ndma):
            d = "d%d" % i
            self.sem[d] = es.enter_context(nc.semaphore("sem_" + d))
            self.cnt[d] = 0
        self.rr = 0
        self.waited = {e: {} for e in list(self.COMPUTE) + ["sp"]}
        self.W = {}
        self.R = {}
        self.excl = set()
        self.nops = {e: 0 for e in list(self.COMPUTE) + ["sp"]}

    def _collect(self, e, reads, writes):
        need = {}

        def add(d, c, raw):
            if d == e and (e == "pe" or not raw):
                return
            if c > need.get(d, 0):
                need[d] = c

        reads = [getattr(r, "base", r) for r in reads]
        writes = [getattr(w, "base", w) for w in writes]
        for r in reads:
            for d, c in self.W.get(r, {}).items():
                add(d, c, True)
            if r in self.excl:
                for d, c in self.R.get(r, {}).items():
                    add(d, c, False)
        for w in writes:
            for d, c in self.W.get(w, {}).items():
                add(d, c, False)
            for d, c in self.R.get(w, {}).items():
                add(d, c, False)
        return need

    LOG = None

    def _emit_waits(self, e, need):
        wd = self.waited[e]
        for d, c in need.items():
            if wd.get(d, 0) >= c:
                continue
            self.eng[e].wait_ge(self.sem[d], c)
            wd[d] = c
            if self.LOG is not None:
                self.LOG.append((e, d, c, dict(self.cnt)))

    def _record(self, dom, val, reads, writes):
        reads = [getattr(r, "base", r) for r in reads]
        writes = [getattr(w, "base", w) for w in writes]
        for r in reads:
            rr = self.R.setdefault(r, {})
            if val > rr.get(dom, 0):
                rr[dom] = val
        for w in writes:
            if self.R.get(w):
                self.W[w] = {dom: val}
                self.R[w] = {}
            else:
                self.W.setdefault(w, {})[dom] = val

    def op(self, e, fn, reads=(), writes=(), inc=True):
        need = self._collect(e, reads, writes)
        att = None
        if e != "pe":
            wd = self.waited[e]
            pend = [(d, c) for d, c in need.items() if wd.get(d, 0) < c]
            if pend:
                att = pend[-1]
                need = dict(pend[:-1])
        self._emit_waits(e, need)
        ins = fn(self.eng[e])
        if att is not None:
            ins._wait_ge(self.sem[att[0]], att[1])
            self.waited[e][att[0]] = att[1]
        self.nops[e] += 1
        if inc:
            self.cnt[e] += 1
            ins.then_inc(self.sem[e], 1)
            val = self.cnt[e]
        else:
            val = self.cnt[e] + 1
        self._record(e, val, reads, writes)

    def dma(self, out, in_, reads=(), writes=(), q="sp"):
        d = "d%d" % self.rr
        self.rr = (self.rr + 1) % self.ndma
        need = self._collect(q, reads, writes)
        if self.cnt[d] > 0:
            need[d] = max(need.get(d, 0), self.cnt[d])
        self._emit_waits(q, need)
        ins = self.eng[q].dma_start(out=out, in_=in_)
        self.nops[q] += 1
        self.cnt[d] += 16
        ins.then_inc(self.sem[d], 16)
        self._record(d, self.cnt[d], reads, writes)

    def barrier(self):
        for e in list(self.COMPUTE) + ["sp"]:
            need = {d: c for d, c in self.cnt.items() if c > 0 and d != e}
            self._emit_waits(e, need)

    def finish(self):
        for i in range(self.ndma):
            d = "d%d" % i
            if self.cnt[d] > 0:
                self.eng["sp"].wait_ge(self.sem[d], self.cnt[d])
        for e in self.COMPUTE:
            if self.cnt[e] > 0:
                self.eng["sp"].wait_ge(self.sem[e], self.cnt[e])


class StopBuild(Exception):
    pass


class T:
    def __init__(self, h, name):
        self.h = h
        self.name = name

    def __getitem__(self, k):
        return self.h[k]

    def __repr__(self):
        return "T(%s)" % self.name


class Off:
    def __init__(self, base, off, width):
        self.base = base
        self.off = off
        self.width = width

    def __getitem__(self, k):
        if not isinstance(k, tuple):
            return self.base[:, self.off:self.off + self.width]
        p_, c_ = k
        lo = 0 if c_.start is None else c_.start
        hi = self.width if c_.stop is None else c_.stop
        return self.base[p_, self.off + lo:self.off + hi]


class SubSeg:
    kind = "s"

    def __init__(self, b):
        self.idx = b
        self.ncol = QB * 5
        self.nb = QB
        self.T = DEC
        self.tv = DEC
        self.nq = QB
        self.m0 = SEQ + b * QB * DEC
        self.mcols = QB * DEC
        self.nlev = 2

    def tokv(self, ap):
        return ap.rearrange("p (b s) -> p b s", s=5)[:, :, 1:5]

    chv = tokv


class Pool:
    def __init__(self, tiles):
        self.free = list(tiles)
        self.all = list(tiles)

    def get(self):
        return self.free.pop(0)

    def put(self, *ts):
        for t in ts:
            assert t in self.all and t not in self.free
            self.free.append(t)


class Seg:
    def __init__(self, kind, idx):
        self.kind = kind
        self.idx = idx
        if kind == "p":
            self.x0 = idx * SEGP
            self.ncol = SEGP
            self.nb = 1
            self.T = SEGP
            self.tv = CH
            self.m0 = idx * SEGP
            self.mcols = SEGP
            self.nlev = 6
        else:
            self.x0 = XS0 + idx * SB * 5
            self.ncol = SB * 5
            self.nb = SB
            self.T = DEC
            self.tv = DEC
            self.m0 = SEQ + idx * SB * DEC
            self.mcols = SB * DEC
            self.nlev = 2
        self.nq = QB if kind == "p" else SB

    def tokv(self, ap):
        if self.kind == "p":
            return ap.rearrange("p (b t) -> p b t", b=1)
        return ap.rearrange("p (b s) -> p b s", s=5)[:, :, 1:5]

    def chv(self, ap):
        if self.kind == "p":
            return ap.rearrange("p (q t) -> p q t", t=CH)
        return ap.rearrange("p (b s) -> p b s", s=5)[:, :, 1:5]


def build_program(debug=None, stop_after=None):
    nc = bass.Bass("TRN2", target_bir_lowering=False)
    es = ExitStack()
    S = Sched(nc, es)

    def din(name, shape):
        return nc.dram_tensor(name, list(shape), F32, kind="ExternalInput").ap()

    def dout(name, shape):
        return nc.dram_tensor(name, list(shape), F32, kind="ExternalOutput").ap()

    xp = din("xp", [SEQ, D])
    xs = din("xs", [NSB * 5, D])
    xst = din("xst", [NSB * DEC, D])
    st = din("st", [64, D])
    s0 = din("s0", [NSB, 16, 64, 64])
    w1 = din("w1", [NCH, 128, 7, 8, 128])
    wl = din("wl", [128, 2, 8, 128])
    wo = din("wo", [128, 8, D])
    wgu = din("wgu", [NFC, 128, 2, 8, 128])
    wd = din("wd", [NFC, 128, D])
    vec = din("vec", [128, NV, 8])
    w2d = din("w2d", [64, D])
    a2d = din("a2d", [64, D])
    g2d = din("g2d", [128, D])
    wab = din("wab", [128, 2, 8, 128])
    lnv = din("lnv", [128, 4, D])

    yp = dout("yp", [SEQ, D])
    ys = dout("ys", [NSB * DEC, D])
    o_shift = dout("o_shift", [1 + NSB, D])
    o_wkv_p = dout("o_wkv_p", [16, 64, 64])
    o_wkv_s = dout("o_wkv_s", [NSB, 16, 64, 64])
    o_fin = dout("o_fin", [NFIN, D])
    dbg_out = {}
    if debug:
        for name, (shape, dt_) in debug.items():
            dbg_out[name] = nc.dram_tensor("dbg_" + name, list(shape), dt_, kind="ExternalOutput").ap()

    def sb(name, shape, dt=F32, stack=None):
        h = (stack or es).enter_context(nc.sbuf_tensor(name, list(shape), dt))
        return T(h, name)

    psl = [T(es.enter_context(nc.psum_tensor("ps%d" % i, [128, 512], F32)), "ps%d" % i) for i in range(8)]
    ps_state = {"i": 0}
    S.excl.update(psl)

    ps_banks = {None: list(range(8)), "s1": [6, 7], "s1a": [6], "s1b": [7], "s1c": [5], "s2A": [0, 1], "s2B": [2, 3], "tailA": [4], "tailB": [4]}
    ps_ctr = {k: 0 for k in (None, "s1", "s1a", "s1b", "s1c", "s2A", "s2B", "tailA", "tailB")}
    cur_alloc = {"who": None}

    def getps():
        who = cur_alloc["who"]
        lst = ps_banks[who]
        t = psl[lst[ps_ctr[who] % len(lst)]]
        ps_ctr[who] += 1
        return t

    cur = {"segi": -1}

    def ck(label):
        if stop_after == label or stop_after == "%s@%d" % (label, cur["segi"]):
            raise StopBuild()

    ident_f = sb("ident_f", [128, 128])
    ident_b = sb("ident_b", [128, 128], BF16)
    II_f = sb("II_f", [128, 64])
    II_b = sb("II_b", [128, 64], BF16)
    ones_bd = sb("ones_bd", [128, 128])
    mUs = sb("mUs", [128, QB, 128], BF16)
    mUi = sb("mUi", [128, QB, 128], BF16)
    mLs = sb("mLs", [128, QB, 128], BF16)
    rmask_p = sb("rmask_p", [128, SEGP])
    rmask_s = sb("rmask_s", [128, SB * 5])
    vecs = sb("vecs", [128, NV, 8])
    der = sb("der", [128, 12, 8])
    w2b = sb("w2b", [128, D], BF16)
    g2b = sb("g2b", [128, D], BF16)
    wabb = sb("wabb", [128, 2, 8, 128], BF16)
    fin = sb("fin", [128, NCH, NFIN])
    stT = sb("stT", [128, NCH, 64])

    S.op("pool", lambda E: E.memset(ident_f[:], 1.0), writes=[ident_f])
    S.op("pool", lambda E: E.affine_select(out=ident_f[:], in_=ident_f[:], pattern=[[-1, 128]],
                                           compare_op=ALU.is_equal, fill=0.0, base=0, channel_multiplier=1),
         reads=[ident_f], writes=[ident_f])
    S.op("dve", lambda E: E.tensor_copy(out=ident_b[:], in_=ident_f[:]), reads=[ident_f], writes=[ident_b])
    S.op("dve", lambda E: E.tensor_tensor(out=II_f[:], in0=ident_f[:, 0:64], in1=ident_f[:, 64:128], op=ALU.add),
         reads=[ident_f], writes=[II_f])
    S.op("dve", lambda E: E.tensor_copy(out=II_b[:], in_=II_f[:]), reads=[II_f], writes=[II_b])
    S.op("pool", lambda E: E.memset(ones_bd[:], 0.0), writes=[ones_bd])
    S.op("pool", lambda E: E.memset(ones_bd[0:64, 0:64], 1.0), reads=[ones_bd], writes=[ones_bd])
    S.op("pool", lambda E: E.memset(ones_bd[64:128, 64:128], 1.0), reads=[ones_bd], writes=[ones_bd])
    for m, patt, cm, cmp in ((mUs, 1, -1, ALU.is_gt), (mUi, 1, -1, ALU.is_ge), (mLs, -1, 1, ALU.is_gt)):
        S.op("pool", lambda E, m=m: E.memset(m[:], 1.0), writes=[m])
        S.op("pool", lambda E, m=m, patt=patt, cm=cm, cmp=cmp: E.affine_select(
            out=m[:], in_=m[:], pattern=[[0, QB], [patt, 128]], compare_op=cmp, fill=0.0, base=0,
            channel_multiplier=cm), reads=[m], writes=[m])
    S.op("pool", lambda E: E.memset(rmask_p[:], 1.0), writes=[rmask_p])
    S.op("pool", lambda E: E.memset(rmask_p[:].rearrange("p (q t) -> p q t", t=CH)[:, :, 0:1], 0.0),
         reads=[rmask_p], writes=[rmask_p])
    S.op("pool", lambda E: E.memset(rmask_s[:], 1.0), writes=[rmask_s])
    S.op("pool", lambda E: E.memset(rmask_s[:].rearrange("p (b s) -> p b s", s=5)[:, :, 0:2], 0.0),
         reads=[rmask_s], writes=[rmask_s])
    S.op("pool", lambda E: E.memset(fin[:], 0.0), writes=[(fin, c) for c in range(NCH)])

    S.dma(vecs[:], vec[:, :, :], writes=[vecs])

    def vcol(i, c):
        return vecs[:, i, c:c + 1]

    S.op("dve", lambda E: E.tensor_scalar(out=der[:, 0, :], in0=vecs[:, V_KA, :], scalar1=-1.0, scalar2=1.0,
                                          op0=ALU.mult, op1=ALU.add), reads=[vecs], writes=[(der, 0)])
    S.op("act", lambda E: E.activation(out=der[:, 3, :], in_=vecs[:, V_LAM, :], func=AF.Exp, scale=-1.0),
         reads=[vecs], writes=[(der, 3)])
    S.op("act", lambda E: E.activation(out=der[:, 3, :], in_=der[:, 3, :], func=AF.Ln, bias=1.0, scale=1.0),
         reads=[(der, 3)], writes=[(der, 3)])
    S.op("dve", lambda E: E.tensor_scalar(out=der[:, 1, :], in0=der[:, 3, :], scalar1=-8.0, scalar2=None,
                                          op0=ALU.mult), reads=[(der, 3)], writes=[(der, 1)])
    S.op("dve", lambda E: E.tensor_scalar(out=der[:, 2, :], in0=der[:, 3, :], scalar1=-16.0, scalar2=None,
                                          op0=ALU.mult), reads=[(der, 3)], writes=[(der, 2)])
    for di, vi in ((4, V_W0), (5, V_A0), (6, V_BA), (7, V_BI), (8, V_KA)):
        S.op("dve", lambda E, di=di, vi=vi: E.tensor_scalar(out=der[:, di, :], in0=vecs[:, vi, :], scalar1=0.5,
                                                            scalar2=None, op0=ALU.mult), reads=[vecs], writes=[(der, di)])
    S.op("dve", lambda E: E.tensor_scalar(out=der[:, 9, :], in0=vecs[:, V_KA, :], scalar1=-0.5, scalar2=1.0,
                                          op0=ALU.mult, op1=ALU.add), reads=[vecs], writes=[(der, 9)])
    S.op("dve", lambda E: E.tensor_scalar(out=der[:, 10, :], in0=der[:, 3, :], scalar1=-4.0, scalar2=None,
                                          op0=ALU.mult), reads=[(der, 3)], writes=[(der, 10)])
    S.op("dve", lambda E: E.tensor_scalar(out=der[:, 11, :], in0=der[:, 3, :], scalar1=-8.0, scalar2=None,
                                          op0=ALU.mult), reads=[(der, 3)], writes=[(der, 11)])
    negh = sb("negh", [128, QB])
    S.op("pool", lambda E: E.memset(negh[:], -0.5), writes=[negh])

    p1 = ExitStack()
    merged = sb("merged", [128, 8, MCOLS], BF16)
    xsc = nc.dram_tensor("xsc", [128, 8, XCOLS], BF16).ap()
    xtt = [sb("xtt%d" % i, [128, 8, 128], BF16, p1) for i in range(1)]
    xs_pool = Pool([sb("xseg%d" % i, [128, 8, SEGP], BF16, p1) for i in range(2)])
    if stop_after is not None:
        for c_ in range(8):
            S.op("pool", lambda E, c_=c_: E.memset(merged[:, c_, :], 0.0), writes=[(merged, c_)])
    lora0 = sb("lora0", [128, XCOLS], BF16, p1)
    lora1 = sb("lora1", [128, XCOLS], BF16, p1)
    stage = [sb("stage%d" % i, [128, 8, 128], F32, p1) for i in range(3)]
    wcb = sb("wcb", [128, 7, 8, 128], BF16, p1)
    NSCR = 37
    scr = Pool([sb("scr%d" % i, [128, SEGP + 4], F32, p1) for i in range(NSCR)])
    NWK = 27
    wk = Pool([sb("wk%d" % i, [128, QB, 128], BF16, p1) for i in range(NWK)])
    bdn = ("aT", "bT", "kT", "rT", "bgT", "kgT", "vT")
    bd = {}
    for kind in ("p", "s"):
        bd[kind] = []
        for par_ in range(2):
            st_ = {n: sb("bd_%s%d_%s" % (kind, par_, n), [128, QB, 2, 64], BF16, p1) for n in bdn}
            bd[kind].append(st_)
            for n in bdn:
                S.op("pool", lambda E, t=st_[n]: E.memset(t[:], 0.0), writes=[(st_[n], 0), (st_[n], 1)])
    ybd = sb("ybd", [128, QB, 2, 64], F32, p1)
    S.op("pool", lambda E: E.memset(ybd[:], 0.0), writes=[(ybd, 0), (ybd, 1)])
    hbd = sb("hbd", [128, 2, 64], F32, p1)
    S.op("pool", lambda E: E.memset(hbd[:], 0.0), writes=[(hbd, 0), (hbd, 1)])
    zbuf = [sb("zbuf%d" % i, [128, 1 + SEGP], F32, p1) for i in range(3)]
    xbuf = sb("xbuf", [128, SB * (3 + SEGP // 1) if False else max(3 + SEGP, SB * 7)], F32, p1)
    carry3 = sb("carry3", [128, 3], F32, p1)
    hcar = sb("hcar", [128, 1], F32, p1)
    H32 = [sb("H32_%d" % i, [128, QB + 1, 64], F32, p1) for i in range(2)]
    Hbf = [sb("Hbf_%d" % i, [128, QB + 1, 64], BF16, p1) for i in range(2)]
    H32s = [[sb("H32s_%d%d" % (i, j), [128, QB, 64], F32, p1) for j in range(2)] for i in range(2)]
    Hbfs = [sb("Hbfs_%d" % i, [128, QB, 64], BF16, p1) for i in range(2)]
    gamC = sb("gamC", [128, QB], F32, p1)
    stat = sb("stat", [128, 8, QB], F32, p1)
    bst4 = sb("bst4", [128, QB, 6], F32, p1)
    mvq = sb("mvq", [128, QB, 2], F32, p1)

    S.dma(stage[0][0:64, :, :].rearrange("p a b -> p (a b)"), w2d[:, :], writes=[stage[0]])
    S.dma(stage[1][64:128, :, :].rearrange("p a b -> p (a b)"), a2d[:, :], writes=[stage[1]])
    S.op("pool", lambda E: E.tensor_copy(out=w2b[0:64, :], in_=stage[0][0:64, :, :].rearrange("p a b -> p (a b)")),
         reads=[stage[0]], writes=[(w2b, 0)])
    S.op("pool", lambda E: E.tensor_copy(out=w2b[64:128, :], in_=stage[1][64:128, :, :].rearrange("p a b -> p (a b)")),
         reads=[stage[1]], writes=[(w2b, 1)])
    S.dma(stage[2][:, :, :].rearrange("p a b -> p (a b)"), g2d[:, :], writes=[stage[2]])
    S.op("pool", lambda E: E.tensor_copy(out=g2b[:], in_=stage[2][:, :, :].rearrange("p a b -> p (a b)")),
         reads=[stage[2]], writes=[g2b])
    for j in range(2):
        S.dma(stage[j][:], wab[:, j, :, :], writes=[stage[j]])
        S.op("pool", lambda E, j=j: E.tensor_copy(out=wabb[:, j, :, :], in_=stage[j][:]),
             reads=[stage[j]], writes=[(wabb, j)])
    for j in range(2):
        S.dma(stage[j][:], wl[:, j, :, :], writes=[stage[j]])
        S.op("pool", lambda E, j=j: E.tensor_copy(out=wcb[:, j, :, :], in_=stage[j][:]),
             reads=[stage[j]], writes=[(wcb, j)])

    rr = {"ev": 0, "ew": 0}

    def ev_eng():
        return "act"

    def ew_eng():
        rr["ew"] ^= 1
        return "dve" if rr["ew"] else "pool"

    def copy_op(e, out, in_, reads, writes):
        if e == "act":
            S.op("act", lambda E: E.copy(out=out, in_=in_), reads, writes)
        else:
            S.op(e, lambda E: E.tensor_copy(out=out, in_=in_), reads, writes)

    ntile = SEQ // 128
    for t in range(ntile + 2):
        xi = stage[t % 3]
        xiv = xi[:].rearrange("p a b -> p (a b)")
        if t < ntile:
            rows = 128
            S.dma(xiv, xp[t * 128:(t + 1) * 128, :], writes=[xi])
        elif t == ntile:
            rows = NSB * 5
            S.dma(xiv[0:rows, :], xs[:, :], writes=[xi])
        else:
            rows = 64
            S.dma(xiv[0:rows, :], st[:, :], writes=[xi])
        for half in range(2):
            ps = getps()
            for k4 in range(4):
                kc = half * 4 + k4
                S.op("pe", lambda E, ps=ps, k4=k4, kc=kc, xi=xi, rows=rows: E.transpose(
                    out=ps[:, k4 * 128:k4 * 128 + rows], in_=xi[0:rows, kc, :],
                    identity=ident_f[0:rows, 0:rows]), reads=[xi, ident_f], writes=[ps], inc=(k4 == 3))
            src = ps[:].rearrange("p (k t) -> p k t", t=128)[:, :, 0:rows]
            if t <= ntile:
                xo = xtt[0]
                copy_op(ev_eng(), xo[:, half * 4:(half + 1) * 4, 0:rows], src, [ps], [(xo, half)])
                if half == 1:
                    S.dma(xsc[:, :, t * 128:t * 128 + rows], xo[:, :, 0:rows], reads=[(xo, 0), (xo, 1)],
                          writes=[("xsc", t)])
            else:
                dst = stT[:, half * 4:(half + 1) * 4, :]
                copy_op(ev_eng(), dst, src, [ps], [stT])
    xstate = {"tile": {}, "order": [], "pos": 0}

    def _issue_x(seg):
        t_ = xs_pool.get()
        t0_, t1_ = seg.x0 // 128, (seg.x0 + seg.ncol - 1) // 128
        S.dma(t_[:, :, 0:seg.ncol], xsc[:, :, seg.x0:seg.x0 + seg.ncol],
              reads=[("xsc", k) for k in range(t0_, t1_ + 1)], writes=[t_])
        return t_

    def get_x(seg):
        order, pos = xstate["order"], xstate["pos"]
        assert order[pos] is seg
        t_ = xstate["tile"].pop(pos, None)
        if t_ is None:
            t_ = _issue_x(seg)
        if pos + 1 < len(order):
            xstate["tile"][pos + 1] = _issue_x(order[pos + 1])
        xstate["pos"] = pos + 1
        return t_

    def dump(name, ap, reads):
        if name in dbg_out:
            S.dma(dbg_out[name], ap, reads=reads)


    def proj_ps(wsel, seg, xseg):
        ps = getps()
        ncol = seg.ncol
        for kc in range(8):
            lhsT, wres = wsel(kc)
            S.op("pe", lambda E, ps=ps, lhsT=lhsT, kc=kc: E.matmul(
                ps[:, 0:ncol], lhsT=lhsT, rhs=xseg[:, kc, 0:ncol], start=(kc == 0), stop=(kc == 7)),
                reads=[wres, xseg], writes=[ps], inc=(kc == 7))
        return ps

    def shifted(wsel, mu_ap, seg, zb, first, xseg):
        ncol = seg.ncol
        if seg.kind == "p":
            if first:
                S.op("pool", lambda E: E.memset(zb[:, 0:1], 0.0), writes=[zb])
            else:
                S.op("pool", lambda E: E.tensor_copy(out=zb[:, 0:1], in_=zb[:, SEGP:SEGP + 1]), reads=[zb], writes=[zb])
        ps = proj_ps(wsel, seg, xseg)
        S.op("act", lambda E: E.copy(out=zb[:, 1:1 + ncol], in_=ps[:, 0:ncol]), reads=[ps], writes=[zb])
        dt_ = scr.get()
        zm = scr.get()
        S.op("pool", lambda E: E.tensor_tensor(out=dt_[:, 0:ncol], in0=zb[:, 0:ncol], in1=zb[:, 1:1 + ncol],
                                               op=ALU.subtract), reads=[zb], writes=[dt_])
        S.op("dve", lambda E: E.scalar_tensor_tensor(out=zm[:, 0:ncol], in0=dt_[:, 0:ncol], scalar=mu_ap,
                                                     in1=zb[:, 1:1 + ncol], op0=ALU.mult, op1=ALU.add),
             reads=[dt_, zb, vecs], writes=[zm])
        scr.put(dt_)
        return zm

    segs = [Seg("p", i) for i in range(NSEGP)] + [Seg("s", i) for i in range(NSEGS)]

    xstate["order"] = segs * 2 + segs * NCH
    for L in range(2):
        for si, seg in enumerate(segs):
            xseg = get_x(seg)
            zm = shifted(lambda kc, L=L: (wcb[:, L, kc, :], (wcb, L)), vcol(V_MUL, L), seg, zbuf[0],
                         first=(seg.kind == "p" and seg.idx == 0), xseg=xseg)
            xs_pool.put(xseg)
            nco = seg.ncol
            if L == 0:
                S.op("act", lambda E: E.activation(out=lora0[0:64, seg.x0:seg.x0 + nco], in_=zm[0:64, 0:nco],
                                                   func=AF.Tanh), reads=[zm], writes=[(lora0, 0)])
                S.op("act", lambda E: E.copy(out=lora0[64:128, seg.x0:seg.x0 + nco], in_=zm[64:128, 0:nco]),
                     reads=[zm], writes=[(lora0, 1)])
            else:
                S.op("act", lambda E: E.activation(out=lora1[:, seg.x0:seg.x0 + nco], in_=zm[:, 0:nco],
                                                   func=AF.Sigmoid), reads=[zm], writes=[lora1])
            scr.put(zm)
    dump("lora0", lora0[:, :], [(lora0, 0), (lora0, 1)])
    dump("lora1", lora1[:, :], [lora1])

    if stop_after == "lora":
        S.finish()
        p1.close()
        es.close()
        return nc

    gam_pool = Pool([sb("gamC%d" % i, [128, SB], F32, p1) for i in range(5)])
    ubf_pool = Pool([sb("ubf%d" % i, [128, SEGP], BF16, p1) for i in range(2)])
    sbd = sb("sbd", [128, 2, 64], F32, p1)
    sin_tiles = [sb("sin%d" % i, [128, 64], F32, p1) for i in range(3)]
    sin_ctr = {"i": 0}
    S.op("pool", lambda E: E.memset(sbd[:], 0.0), writes=[(sbd, 0), (sbd, 1)])

    print("sbuf remaining in phase 1:", nc.sbuf_bytes_remaining)

    def wsel(g):
        return lambda kc: (wcb[:, g, kc, :], (wcb, g))

    def mm(ps_ap, lhsT, rhs, reads, ps, start=True, stop=True, inc=True):
        S.op("pe", lambda E: E.matmul(ps_ap, lhsT=lhsT, rhs=rhs, start=start, stop=stop),
             reads=reads, writes=[ps], inc=inc)

    def tt(e, out, a, b, op, reads, writes):
        S.op(e, lambda E: E.tensor_tensor(out=out, in0=a, in1=b, op=op), reads, writes)

    def act(out, in_, func, reads, writes, bias=None, scale=None):
        kw = {}
        if bias is not None:
            kw["bias"] = bias
        if scale is not None:
            kw["scale"] = scale
        S.op("act", lambda E: E.activation(out=out, in_=in_, func=func, **kw), reads, writes)

    def bd_write(seg, B, K, col0):
        tv, nco = seg.tv, seg.ncol
        for h in range(2):
            P = slice(h * 64, (h + 1) * 64)

            def dst(n):
                return B[n][P, :, h, 0:tv]

            def cv(t_):
                return seg.chv(t_[P, col0:col0 + nco])

            kk, E1, E2, E3, E4, tb, kf, zr, zv = (K[n] for n in ("kk", "E1", "E2", "E3", "E4", "tb", "kf", "zr", "zv"))
            S.op("dve", lambda E: E.scalar_tensor_tensor(out=dst("aT"), in0=cv(kk), scalar=-1.0, in1=cv(E3),
                                                         op0=ALU.mult, op1=ALU.mult), [kk, E3], [(B["aT"], h)])
            yield
            tt(ew_eng(), dst("bT"), cv(tb), cv(E2), ALU.mult, [tb, E2], [(B["bT"], h)])
            yield
            tt(ew_eng(), dst("bgT"), cv(tb), cv(E4), ALU.mult, [tb, E4], [(B["bgT"], h)])
            yield
            tt(ew_eng(), dst("kT"), cv(kf), cv(E2), ALU.mult, [kf, E2], [(B["kT"], h)])
            yield
            tt(ew_eng(), dst("kgT"), cv(kf), cv(E4), ALU.mult, [kf, E4], [(B["kgT"], h)])
            yield
            tt(ew_eng(), dst("rT"), cv(zr), cv(E1), ALU.mult, [zr, E1], [(B["rT"], h)])
            yield
            S.op("act", lambda E: E.copy(out=dst("vT"), in_=cv(zv)), [zv], [(B["vT"], h)])
            yield

    def stage1c(c, seg, shared):
        nco, kind, tv, nq, x0 = seg.ncol, seg.kind, seg.tv, seg.nq, seg.x0
        first = (kind == "p" and seg.idx == 0)
        B = bd[kind][seg.idx % 2]
        cc = slice(c * 128, (c + 1) * 128)
        ps = getps()
        mm(ps[:, 0:nco], w2b[0:64, cc], lora0[0:64, x0:x0 + nco], [(w2b, 0), (lora0, 0)], ps)
        yield
        sg = scr.get()
        act(sg[:, 0:nco], ps[:, 0:nco], AF.Tanh, [ps, (der, 4)], [sg], bias=der[:, 4, c:c + 1], scale=0.5)
        S.op("pool", lambda E: E.tensor_scalar(out=sg[:, 0:nco], in0=sg[:, 0:nco], scalar1=0.5, scalar2=0.5,
                                               op0=ALU.mult, op1=ALU.add), [sg], [sg])
        yield
        cs = scr.get()
        rmask = rmask_p if kind == "p" else rmask_s
        S.op("dve", lambda E: E.tensor_tensor_scan(out=cs[:, 0:nco], data0=rmask[:, 0:nco], data1=sg[:, 0:nco],
                                                   initial=0.0, op0=ALU.mult, op1=ALU.add),
             reads=[rmask, sg], writes=[cs])
        yield
        E1 = scr.get(); E2 = scr.get(); E3 = scr.get(); E4 = scr.get(); t0 = scr.get()
        act(E1[:, 0:nco], cs[:, 0:nco], AF.Exp, [cs], [E1], scale=-KAPPA)
        yield
        act(E2[:, 0:nco], cs[:, 0:nco], AF.Exp, [cs], [E2], scale=KAPPA)
        yield
        tt("pool", t0[:, 0:nco], cs[:, 0:nco], sg[:, 0:nco], ALU.subtract, [cs, sg], [t0])
        yield
        act(E3[:, 0:nco], t0[:, 0:nco], AF.Exp, [t0], [E3], scale=-KAPPA)
        yield
        csv = seg.chv(cs[:, 0:nco])
        t0v = seg.chv(t0[:, 0:nco])
        tt("dve", t0v, csv[:, :, tv - 1:tv].to_broadcast([128, nq, tv]), csv, ALU.subtract, [cs], [t0])
        yield
        act(seg.chv(E4[:, 0:nco]), t0v, AF.Exp, [t0], [E4], scale=-KAPPA)
        yield
        gam = gam_pool.get()
        act(gam[:, 0:nq].rearrange("p (q o) -> p q o", o=1), csv[:, :, tv - 1:tv], AF.Exp, [cs], [gam], scale=-KAPPA)
        yield
        scr.put(sg, cs, t0)
        ps = getps()
        mm(ps[:, 0:nco], w2b[64:128, cc], lora0[64:128, x0:x0 + nco], [(w2b, 1), (lora0, 1)], ps)
        yield
        a_ = scr.get()
        act(a_[:, 0:nco], ps[:, 0:nco], AF.Tanh, [ps, (der, 5)], [a_], bias=der[:, 5, c:c + 1], scale=0.5)
        S.op("pool", lambda E: E.tensor_scalar(out=a_[:, 0:nco], in0=a_[:, 0:nco], scalar1=0.5, scalar2=0.5,
                                               op0=ALU.mult, op1=ALU.add), [a_], [a_])
        yield
        ps = getps()
        mm(ps[:, 0:nco], g2b[:, cc], lora1[:, x0:x0 + nco], [g2b, lora1], ps)
        yield
        g_ = scr.get()
        S.op("act", lambda E: E.mul(out=g_[:, 0:nco], in_=ps[:, 0:nco], mul=0.5), [ps], [g_])
        yield
        shared.update(E1=E1, E2=E2, E3=E3, E4=E4, a_=a_, g_=g_, gam=gam)

    def stage1a(c, seg, xseg, shared):
        nco, kind, tv, nq, x0 = seg.ncol, seg.kind, seg.tv, seg.nq, seg.x0
        first = (kind == "p" and seg.idx == 0)
        B = bd[kind][seg.idx % 2]
        cc = slice(c * 128, (c + 1) * 128)
        zr = shifted(wsel(0), vcol(V_MUR, c), seg, zbuf[0], first, xseg)
        yield
        zk = shifted(wsel(1), vcol(V_MUK, c), seg, zbuf[1], first, xseg)
        yield
        zv = shifted(wsel(2), vcol(V_MUV, c), seg, zbuf[2], first, xseg)
        yield
        kkr = scr.get(); sq = scr.get(); kk = scr.get()
        S.op("dve", lambda E: E.tensor_scalar(out=kkr[:, 0:nco], in0=zk[:, 0:nco], scalar1=vcol(V_KK, c), scalar2=None,
                                              op0=ALU.mult), [zk, vecs], [kkr])
        yield
        tt("pool", sq[:, 0:nco], kkr[:, 0:nco], kkr[:, 0:nco], ALU.mult, [kkr], [sq])
        yield
        ps = getps()
        mm(ps[:, 0:nco], ones_bd[:], sq[:, 0:nco], [ones_bd, sq], ps)
        yield
        act(sq[:, 0:nco], ps[:, 0:nco], AF.Sqrt, [ps], [sq])
        yield
        S.op("dve", lambda E: E.tensor_scalar(out=sq[:, 0:nco], in0=sq[:, 0:nco], scalar1=1e-12, scalar2=None,
                                              op0=ALU.max), [sq], [sq])
        yield
        S.op("dve", lambda E: E.reciprocal(out=sq[:, 0:nco], in_=sq[:, 0:nco]), [sq], [sq])
        yield
        tt("pool", kk[:, 0:nco], kkr[:, 0:nco], sq[:, 0:nco], ALU.mult, [kkr, sq], [kk])
        yield
        yield "NEEDC"
        E1, E2, E3, E4, a_, g_, gam = (shared[k_] for k_ in ("E1", "E2", "E3", "E4", "a_", "g_", "gam"))
        t1 = scr.get(); kf = scr.get(); bonus = scr.get()
        S.op("dve", lambda E: E.tensor_scalar(out=t1[:, 0:nco], in0=a_[:, 0:nco], scalar1=vcol(V_KA, c),
                                              scalar2=der[:, 0, c:c + 1], op0=ALU.mult, op1=ALU.add),
             [a_, vecs, (der, 0)], [t1])
        yield
        tt("pool", kf[:, 0:nco], zk[:, 0:nco], t1[:, 0:nco], ALU.mult, [zk, t1], [kf])
        yield
        S.op("dve", lambda E: E.scalar_tensor_tensor(out=t1[:, 0:nco], in0=zr[:, 0:nco], scalar=vcol(V_RK, c),
                                                     in1=kf[:, 0:nco], op0=ALU.mult, op1=ALU.mult),
             [zr, kf, vecs, t1], [t1])
        yield
        ps = getps()
        mm(ps[:, 0:nco], ones_bd[:], t1[:, 0:nco], [ones_bd, t1], ps)
        yield
        tt("dve", bonus[:, 0:nco], ps[:, 0:nco], zv[:, 0:nco], ALU.mult, [ps, zv], [bonus])
        yield
        tb = kkr
        tt("pool", tb[:, 0:nco], kk[:, 0:nco], a_[:, 0:nco], ALU.mult, [kk, a_, kkr], [tb])
        yield
        keep = dict(kk=kk, E3=E3, tb=tb, E2=E2, E4=E4, kf=kf, zr=zr, E1=E1, zv=zv)
        if kind == "p":
            yield from bd_write(seg, B, keep, 0)
            scr.put(zr, zk, zv, E1, E2, E3, E4, a_, kkr, sq, kk, t1, kf)
        else:
            scr.put(zk, a_, sq, t1)
        return dict(bonus=bonus, g=g_, gam=gam, keep=keep)

    def stage1b(c, seg, xseg):
        nco, kind, tv, nq, x0 = seg.ncol, seg.kind, seg.tv, seg.nq, seg.x0
        first = (kind == "p" and seg.idx == 0)
        B = bd[kind][seg.idx % 2]
        cc = slice(c * 128, (c + 1) * 128)
        T_, nb = seg.T, seg.nb
        nt = nb * T_
        xbv = xbuf[:, 0:nb * (3 + T_)].rearrange("p (b t) -> p b t", t=3 + T_)

        def dv(t_):
            return t_[:, 0:nt].rearrange("p (b t) -> p b t", t=T_)

        ps = proj_ps(wsel(3), seg, xseg)
        yield
        if kind == "p":
            if seg.idx == 0:
                S.op("pool", lambda E: E.memset(xbv[:, :, 0:3], 0.0), writes=[xbuf])
                yield
            else:
                S.op("pool", lambda E: E.tensor_copy(out=xbv[:, 0, 0:3], in_=carry3[:, :]), [carry3], [xbuf])
                yield
        else:
            b0 = seg.idx * SB
            S.op("pool", lambda E: E.tensor_copy(
                out=xbv[:, :, 0:3], in_=stT[:, c, 0:48].rearrange("p (b j) -> p b j", j=3)[:, b0:b0 + SB, :]),
                [stT], [xbuf])
            yield
        S.op("act", lambda E: E.copy(out=xbv[:, :, 3:3 + T_], in_=seg.tokv(ps[:, 0:nco])), [ps, xbuf], [xbuf])
        yield
        if kind == "p":
            S.op("pool", lambda E: E.tensor_copy(out=carry3[:, :], in_=xbv[:, 0, T_:T_ + 3]), [xbuf], [carry3])
            yield
            if seg.idx == NSEGP - 1:
                S.op("pool", lambda E: E.tensor_copy(out=fin[:, c, 0:3], in_=xbv[:, 0, T_:T_ + 3]), [xbuf], [(fin, c)])
                yield
        else:
            S.op("pool", lambda E: E.tensor_copy(
                out=fin[:, c, 4 + 3 * b0:4 + 3 * (b0 + SB)].rearrange("p (b j) -> p b j", j=3),
                in_=xbv[:, :, T_:T_ + 3]), [xbuf], [(fin, c)])
            yield
        u = scr.get()
        uv = dv(u)
        S.op("dve", lambda E: E.tensor_scalar(out=uv, in0=xbv[:, :, 0:T_], scalar1=vcol(V_CW0, c),
                                              scalar2=vcol(V_CB, c), op0=ALU.mult, op1=ALU.add),
             [xbuf, vecs], [u])
        yield
        for j in range(1, 4):
            S.op("dve", lambda E, j=j: E.scalar_tensor_tensor(out=uv, in0=xbv[:, :, j:j + T_],
                                                              scalar=vcol(V_CW0 + j, c), in1=uv,
                                                              op0=ALU.mult, op1=ALU.add), [xbuf, vecs, u], [u])
            yield
        ubf = ubf_pool.get()
        S.op("act", lambda E: E.copy(out=ubf[:, 0:nt], in_=u[:, 0:nt]), [u], [ubf])
        yield
        rg = scr.get(); ig = scr.get(); al = scr.get(); e2 = scr.get(); hh = scr.get()
        ps = getps()
        mm(ps[:, 0:nt], wabb[:, 0, c, :], ubf[:, 0:nt], [(wabb, 0), ubf], ps)
        yield
        act(rg[:, 0:nt], ps[:, 0:nt], AF.Tanh, [ps, (der, 6)], [rg], bias=der[:, 6, c:c + 1], scale=0.5)
        yield
        ps = getps()
        mm(ps[:, 0:nt], wabb[:, 1, c, :], ubf[:, 0:nt], [(wabb, 1), ubf], ps)
        yield
        act(ig[:, 0:nt], ps[:, 0:nt], AF.Tanh, [ps, (der, 7)], [ig], bias=der[:, 7, c:c + 1], scale=0.5)
        yield
        ubf_pool.put(ubf)
        act(al[:, 0:nt], rg[:, 0:nt], AF.Exp, [rg, (der, 10)], [al], scale=der[:, 10, c:c + 1], bias=der[:, 10, c:c + 1])
        yield
        act(e2[:, 0:nt], rg[:, 0:nt], AF.Exp, [rg, (der, 11)], [e2], scale=der[:, 11, c:c + 1], bias=der[:, 11, c:c + 1])
        yield
        S.op("pool", lambda E: E.tensor_scalar(out=e2[:, 0:nt], in0=e2[:, 0:nt], scalar1=-0.25, scalar2=0.25,
                                               op0=ALU.mult, op1=ALU.add), [e2], [e2])
        yield
        act(e2[:, 0:nt], e2[:, 0:nt], AF.Sqrt, [e2], [e2])
        yield
        if first:
            S.op("pool", lambda E: E.memset(e2[:, 0:1], 0.5), [e2], [e2])
            yield
        S.op("dve", lambda E: E.scalar_tensor_tensor(out=ig[:, 0:nt], in0=ig[:, 0:nt], scalar=1.0, in1=e2[:, 0:nt],
                                                     op0=ALU.add, op1=ALU.mult), [ig, e2], [ig])
        yield
        tt("dve", ig[:, 0:nt], ig[:, 0:nt], u[:, 0:nt], ALU.mult, [ig, u], [ig])
        yield
        if kind == "p":
            init = 0.0 if seg.idx == 0 else hcar[:, 0:1]
            S.op("dve", lambda E: E.tensor_tensor_scan(out=hh[:, 0:nt], data0=al[:, 0:nt], data1=ig[:, 0:nt],
                                                       initial=init, op0=ALU.mult, op1=ALU.add),
                 [al, ig, hcar], [hh])
            yield
            S.op("pool", lambda E: E.tensor_copy(out=hcar[:, 0:1], in_=hh[:, nt - 1:nt]), [hh], [hcar])
            yield
            if seg.idx == NSEGP - 1:
                S.op("pool", lambda E: E.tensor_copy(out=fin[:, c, 3:4], in_=hh[:, nt - 1:nt]), [hh], [(fin, c)])
                yield
        else:
            for b in range(nb):
                S.op("dve", lambda E, b=b: E.tensor_tensor_scan(
                    out=hh[:, b * T_:(b + 1) * T_], data0=al[:, b * T_:(b + 1) * T_], data1=ig[:, b * T_:(b + 1) * T_],
                    initial=stT[:, c, 48 + b0 + b:48 + b0 + b + 1], op0=ALU.mult, op1=ALU.add),
                    [al, ig, stT, hh], [hh])
                yield
            S.op("pool", lambda E: E.tensor_copy(out=fin[:, c, 52 + b0:52 + b0 + SB].rearrange("p (b o) -> p b o", o=1),
                                                 in_=dv(hh)[:, :, T_ - 1:T_]), [hh], [(fin, c)])
            yield
        scr.put(rg, al, e2, u, ig)
        ps = proj_ps(wsel(4), seg, xseg)
        yield
        gbs = scr.get(); p_ = scr.get()
        S.op("act", lambda E: E.copy(out=dv(gbs), in_=seg.tokv(ps[:, 0:nco])), [ps], [gbs])
        yield
        tt("pool", p_[:, 0:nt], gbs[:, 0:nt], gbs[:, 0:nt], ALU.mult, [gbs], [p_])
        yield
        S.op("pool", lambda E: E.tensor_scalar(out=p_[:, 0:nt], in0=p_[:, 0:nt], scalar1=0.044715, scalar2=1.0,
                                               op0=ALU.mult, op1=ALU.add), [p_], [p_])
        yield
        tt("pool", p_[:, 0:nt], p_[:, 0:nt], gbs[:, 0:nt], ALU.mult, [p_, gbs], [p_])
        yield
        act(p_[:, 0:nt], p_[:, 0:nt], AF.Tanh, [p_], [p_], scale=0.7978845608028654)
        yield
        S.op("dve", lambda E: E.scalar_tensor_tensor(out=gbs[:, 0:nt], in0=p_[:, 0:nt], scalar=1.0, in1=gbs[:, 0:nt],
                                                     op0=ALU.add, op1=ALU.mult), [gbs, p_], [gbs])
        yield
        tt("dve", hh[:, 0:nt], hh[:, 0:nt], gbs[:, 0:nt], ALU.mult, [hh, gbs], [hh])
        yield
        scr.put(gbs)
        ps = proj_ps(wsel(6), seg, xseg)
        yield
        act(dv(p_), seg.tokv(ps[:, 0:nco]), AF.Tanh, [ps], [p_], scale=0.5)
        yield
        S.op("dve", lambda E: E.scalar_tensor_tensor(out=hh[:, 0:nt], in0=p_[:, 0:nt], scalar=1.0, in1=hh[:, 0:nt],
                                                     op0=ALU.add, op1=ALU.mult), [hh, p_], [hh])
        yield
        scr.put(p_)
        ps = proj_ps(wsel(5), seg, xseg)
        yield
        gA = scr.get()
        act(dv(gA), seg.tokv(ps[:, 0:nco]), AF.Tanh, [ps], [gA], scale=0.5)
        yield
        return dict(gA=gA, m2=hh)


    def stage1(c, seg):
        xseg = get_x(seg)
        shared = {}
        gens = [stage1a(c, seg, xseg, shared), stage1b(c, seg, xseg), stage1c(c, seg, shared)]
        who = ["s1a", "s1b", "s1c"]
        res = [None, None, None]
        done = [False, False, False]
        hold_a = False
        while not all(done):
            for i_ in range(3):
                if done[i_]:
                    continue
                if i_ == 0 and hold_a:
                    if not done[2]:
                        continue
                    hold_a = False
                cur_alloc["who"] = who[i_]
                try:
                    v_ = next(gens[i_])
                    if v_ == "NEEDC":
                        hold_a = True
                except StopIteration as e_:
                    res[i_] = e_.value
                    done[i_] = True
                yield
        xs_pool.put(xseg)
        res[0].update(res[1])
        return res[0]

    def stage2(c, seg, s1, chain):
        kind, tv, nq, nlev, nco = seg.kind, seg.tv, seg.nq, seg.nlev, seg.ncol
        B = bd[kind][seg.idx % 2]
        gam = s1["gam"]

        def bdv(n):
            return B[n][:].rearrange("p q h t -> p q (h t)")

        def br(n):
            return [(B[n], 0), (B[n], 1)]

        def q128(ps):
            return ps[:].rearrange("p (q t) -> p q t", t=128)

        def q64(ps):
            return ps[:, 0:QB * 64].rearrange("p (q t) -> p q t", t=64)

        def prod(l, r, mask):
            ps = getps()
            for q in range(QB):
                mm(ps[:, q * 128:(q + 1) * 128], bdv(l)[:, q, :], bdv(r)[:, q, :], br(l) + br(r), ps, inc=(q == QB - 1))
            o = wk.get()
            tt("dve", o[:], q128(ps), mask[:], ALU.mult, [ps, mask], [o])
            return o

        N_ = prod("bT", "aT", mUs)
        yield
        L_ = prod("aT", "bT", mLs)
        yield
        Mak = prod("kT", "aT", mUs)
        yield
        Mrb = prod("bT", "rT", mUi)
        yield
        Mrk = prod("kT", "rT", mUi)
        yield
        rTc = wk.get()
        S.op("pool", lambda E: E.tensor_copy(out=rTc[:], in_=bdv("rT")), br("rT"), [rTc])
        yield
        ck("k1")

        def tr(n):
            ps = getps()
            psb = ps[:].bitcast(BF16)
            for q in range(QB):
                S.op("pe", lambda E, q=q: E.transpose(out=psb[:, q * 128:(q + 1) * 128], in_=bdv(n)[:, q, :],
                                                      identity=ident_b[:]), br(n) + [ident_b], [ps], inc=(q == QB - 1))
            o = wk.get()
            copy_op(ev_eng(), o[:], psb[:, 0:QB * 128].rearrange("p (q t) -> p q t", t=128), [ps], [o])
            return o

        XA = tr("aT")
        yield
        Bg = tr("bgT")
        yield
        Kg = tr("kgT")
        yield
        ck("k2")
        ps = getps()
        for q in range(QB):
            mm(ps[:, q * 64:(q + 1) * 64], bdv("vT")[:, q, :], II_b[:], br("vT") + [II_b], ps, inc=(q == QB - 1))
            yield
        V_ = wk.get()
        copy_op(ev_eng(), V_[:, :, 0:64], q64(ps), [ps], [V_])
        yield ("BDFREE", (c, kind, seg.idx))
        yield
        ps = getps()
        for q in range(QB):
            mm(ps[:, q * 64:(q + 1) * 64], Mak[:, q, :], V_[:, q, 0:64], [Mak, V_], ps, inc=(q == QB - 1))
            yield
        XU = wk.get()
        copy_op(ev_eng(), XU[:, :, 0:64], q64(ps), [ps], [XU])
        yield
        wk.put(Mak)
        ck("k3")
        Nc, Lc = N_, L_
        for lev in range(nlev):
            psA = getps()
            for q in range(QB):
                mm(psA[:, q * 128:(q + 1) * 128], Nc[:, q, :], XA[:, q, :], [Nc, XA], psA, start=True, stop=False, inc=False)
                mm(psA[:, q * 128:(q + 1) * 128], ident_b[:], XA[:, q, :], [ident_b, XA], psA, start=False, stop=True,
                   inc=(q == QB - 1))
            yield
            psU = getps()
            for q in range(QB):
                mm(psU[:, q * 64:(q + 1) * 64], Nc[:, q, :], XU[:, q, 0:64], [Nc, XU], psU, start=True, stop=False, inc=False)
                mm(psU[:, q * 64:(q + 1) * 64], ident_b[:], XU[:, q, 0:64], [ident_b, XU], psU, start=False, stop=True,
                   inc=(q == QB - 1))
            yield
            XA2 = wk.get(); XU2 = wk.get()
            copy_op("act", XA2[:], q128(psA), [psA], [XA2])
            yield
            copy_op("act", XU2[:, :, 0:64], q64(psU), [psU], [XU2])
            yield
            N2 = L2 = None
            if lev < nlev - 1:
                psN = getps()
                for q in range(QB):
                    mm(psN[:, q * 128:(q + 1) * 128], Lc[:, q, :], Nc[:, q, :], [Lc, Nc], psN, inc=(q == QB - 1))
                    yield
                N2 = wk.get()
                copy_op("act", N2[:], q128(psN), [psN], [N2])
                yield
                if lev < nlev - 2:
                    psL = getps()
                    for q in range(QB):
                        mm(psL[:, q * 128:(q + 1) * 128], Nc[:, q, :], Lc[:, q, :], [Lc, Nc], psL, inc=(q == QB - 1))
                        yield
                    L2 = wk.get()
                    copy_op("act", L2[:], q128(psL), [psL], [L2])
                    yield
            wk.put(XA, XU, Nc, Lc)
            XA, XU, Nc, Lc = XA2, XU2, N2, L2
            if Nc is None:
                Nc = wk.get()
            if Lc is None:
                Lc = wk.get()
        wk.put(Nc, Lc)
        ck("k4")
        psR = getps()
        for q in range(QB):
            mm(psR[:, q * 128:(q + 1) * 128], XA[:, q, :], Mrb[:, q, :], [XA, Mrb], psR, start=True, stop=False, inc=False)
            yield
            mm(psR[:, q * 128:(q + 1) * 128], ident_b[:], rTc[:, q, :], [ident_b, rTc], psR, start=False,
               stop=True, inc=(q == QB - 1))
            yield
        Rh = wk.get()
        copy_op(ev_eng(), Rh[:], q128(psR), [psR], [Rh])
        yield
        psG = getps()
        for q in range(QB):
            mm(psG[:, q * 128:(q + 1) * 128], XA[:, q, :], Bg[:, q, :], [XA, Bg], psG, inc=(q == QB - 1))
            yield
        GT = wk.get()
        copy_op(ev_eng(), GT[:], q128(psG), [psG], [GT])
        yield
        ck("k5")
        if kind == "p":
            yield ("CHAIN", (c, seg.idx - 1))
            chain["init"]()
        psH = getps()
        if kind == "p":
            hA, hB = chain["hA"], chain["hB"]
            for q in range(QB):
                sl = slice(q * 64, (q + 1) * 64)
                mm(psH[:, sl], Bg[:, q, :], XU[:, q, 0:64], [Bg, XU], psH, start=True, stop=False, inc=False)
                yield
                mm(psH[:, sl], Kg[:, q, :], V_[:, q, 0:64], [Kg, V_], psH, start=False, stop=False, inc=False)
                yield
                mm(psH[:, sl], GT[:, q, :], hB[:, q, :], [GT, (hB, q)], psH, start=False, stop=True)
                yield
                S.op("dve", lambda E, q=q, sl=sl: E.scalar_tensor_tensor(
                    out=hA[:, q + 1, :], in0=hA[:, q, :], scalar=gam[:, q:q + 1], in1=psH[:, sl],
                    op0=ALU.mult, op1=ALU.add), [(hA, q), gam, psH], [(hA, q + 1)])
                yield
                S.op("act", lambda E, q=q: E.copy(out=hB[:, q + 1, :], in_=hA[:, q + 1, :]), [(hA, q + 1)], [(hB, q + 1)])
                yield
            hBr = [(hB, q) for q in range(QB)]
        else:
            hA, hB, hO = chain["hA"], chain["hB"], chain["hO"]
            for q in range(QB):
                sl = slice(q * 64, (q + 1) * 64)
                mm(psH[:, sl], Bg[:, q, :], XU[:, q, 0:64], [Bg, XU], psH, start=True, stop=False, inc=False)
                yield
                mm(psH[:, sl], Kg[:, q, :], V_[:, q, 0:64], [Kg, V_], psH, start=False, stop=False, inc=False)
                yield
                mm(psH[:, sl], GT[:, q, :], hB[:, q, :], [GT, (hB, q)], psH, start=False, stop=True,
                   inc=(q == QB - 1))
                yield
            tmp = scr.get()
            tmpv = tmp[:, 0:QB * 64].rearrange("p (q t) -> p q t", t=64)
            tt("dve", tmpv, hA[:, 0:QB, :], gam[:].rearrange("p (q o) -> p q o", o=1).to_broadcast([128, QB, 64]),
               ALU.mult, [(hA, q) for q in range(QB)] + [gam], [tmp])
            yield
            tt("dve", hO[:, 0:QB, :], tmpv, q64(psH), ALU.add, [tmp, psH], [hO])
            yield
            scr.put(tmp)
            hBr = [(hB, q) for q in range(QB)]
        ck("k6")
        psY = getps()
        for q in range(QB):
            sl = slice(q * 64, (q + 1) * 64)
            mm(psY[:, sl], Mrb[:, q, :], XU[:, q, 0:64], [Mrb, XU], psY, start=True, stop=False, inc=False)
            yield
            mm(psY[:, sl], Mrk[:, q, :], V_[:, q, 0:64], [Mrk, V_], psY, start=False, stop=False, inc=False)
            yield
            mm(psY[:, sl], Rh[:, q, :], hB[:, q, :], [Rh, (hB, q)], psY, start=False, stop=True, inc=(q == QB - 1))
            yield
        wk.put(Mrb, Mrk, XA, XU, Bg, Kg, V_, Rh, GT, rTc)
        if not isinstance(gam, Off):
            gam_pool.put(gam)
        ck("k7")
        ysb = scr.get()
        ysbv = ysb[:, 0:QB * 64].rearrange("p (q t) -> p q t", t=64)
        S.op("act", lambda E: E.copy(out=ysbv, in_=q64(psY)), [psY], [ysb])
        yield ("TAIL", (c, seg.idx) if kind == "p" else None)
        stq = lambda i: stat[:, i, :]
        ysq = scr.get()
        ysqv = ysq[:, 0:QB * 64].rearrange("p (q t) -> p q t", t=64)
        for q in range(QB):
            S.op("dve", lambda E, q=q: E.bn_stats(out=bst4[:, q, :], in_=ysb[:, q * 64:(q + 1) * 64]), [ysb], [(bst4, q)])
        for q in range(QB):
            S.op("dve", lambda E, q=q: E.bn_aggr(out=mvq[:, q, :], in_=bst4[:, q, :]), [(bst4, q)], [(mvq, q)])
        yield
        mvr = [(mvq, q) for q in range(QB)]
        S.op("pool", lambda E: E.tensor_scalar(out=stq(4), in0=mvq[:, :, 1], scalar1=1.0, scalar2=GN_EPS,
                                               op0=ALU.mult, op1=ALU.add), mvr, [(stat, 4)])
        S.op("pool", lambda E: E.tensor_tensor(out=stq(5), in0=stq(4), in1=negh[:], op=ALU.pow),
             [(stat, 4), negh], [(stat, 5)])
        yield
        tt("dve", ysqv, ysbv, mvq[:, :, 0:1].to_broadcast([128, QB, 64]), ALU.subtract, [ysb] + mvr, [ysq])
        scr.put(ysb)
        yield
        for h in range(2):
            P = slice(h * 64, (h + 1) * 64)
            tt(ew_eng(), ybd[P, :, h, :], ysqv[P], stat[P, 5, :].rearrange("p (q o) -> p q o", o=1).to_broadcast([64, QB, 64]),
               ALU.mult, [ysq, (stat, 5)], [(ybd, h)])
            yield
        scr.put(ysq)
        ck("k8")
        psT = getps()
        ybv = ybd[:].rearrange("p q h t -> p q (h t)")
        for q in range(QB):
            mm(psT[:, q * 64:(q + 1) * 64], ybv[:, q, :], II_f[:], [(ybd, 0), (ybd, 1), II_f], psT, inc=(q == QB - 1))
            yield
        ck("k9")
        T_, nb = seg.T, seg.nb
        nt = nb * T_

        def dv(t_):
            return t_[:, 0:nt].rearrange("p (b t) -> p b t", t=T_)

        if kind == "p":
            ysrc = psT[:, 0:QB * 64].rearrange("p (b t) -> p b t", b=1)
        else:
            ysrc = q64(psT)[:, :, 0:DEC]
        yo = scr.get()
        act(dv(yo), ysrc, AF.Identity, [psT, vecs], [yo], bias=vcol(V_LNXB, c), scale=vcol(V_LNXG, c))
        yield
        tt("dve", dv(yo), dv(yo), seg.tokv(s1["bonus"][:, 0:nco]), ALU.add, [yo, s1["bonus"]], [yo])
        yield
        tt("pool", dv(yo), dv(yo), seg.tokv(s1["g"][:, 0:nco]), ALU.mult, [yo, s1["g"]], [yo])
        yield
        S.op("dve", lambda E: E.scalar_tensor_tensor(out=dv(yo), in0=dv(s1["gA"]), scalar=1.0, in1=dv(yo),
                                                     op0=ALU.add, op1=ALU.mult), [yo, s1["gA"]], [yo])
        yield
        mv = merged[:, c, seg.m0:seg.m0 + seg.mcols].rearrange("p (b t) -> p b t", t=T_)
        S.op("dve", lambda E: E.scalar_tensor_tensor(out=mv, in0=dv(s1["m2"]), scalar=0.25, in1=dv(yo),
                                                     op0=ALU.mult, op1=ALU.add), [yo, s1["m2"]], [(merged, c)])
        yield
        scr.put(yo)
        if not isinstance(s1["bonus"], Off):
            scr.put(s1["bonus"], s1["g"], s1["gA"], s1["m2"])

    def state_out(c, hsrc_ap, hres, dst_ap):
        for h in range(2):
            P = slice(h * 64, (h + 1) * 64)
            S.op(ew_eng(), lambda E: E.tensor_copy(out=hbd[P, h, :], in_=hsrc_ap[P, :]), [hres], [(hbd, h)])
        ps = getps()
        mm(ps[:, 0:64], hbd[:].rearrange("p h t -> p (h t)"), II_f[:], [(hbd, 0), (hbd, 1), II_f], ps)
        so = scr.get()
        copy_op(ev_eng(), so[:, 0:64], ps[:, 0:64], [ps], [so])
        S.dma(dst_ap, so[:, 0:64], reads=[so], q="act")
        scr.put(so)

    for g in range(3):
        S.op("pool", lambda E, g=g: E.memset(zbuf[g][:], 0.0), writes=[zbuf[g]])

    def s2full(c, seg, s1, par):
        if seg.kind == "p":
            hA, hB = H32[par], Hbf[par]

            def chain_init():
                if seg.idx == 0:
                    S.op("pool", lambda E: E.memset(hA[:, 0, :], 0.0), writes=[(hA, 0)])
                    S.op("pool", lambda E: E.memset(hB[:, 0, :], 0.0), writes=[(hB, 0)])
                else:
                    pA, pB = H32[1 - par], Hbf[1 - par]
                    S.op("pool", lambda E: E.tensor_copy(out=hA[:, 0, :], in_=pA[:, QB, :]), [(pA, QB)], [(hA, 0)])
                    S.op("pool", lambda E: E.tensor_copy(out=hB[:, 0, :], in_=pB[:, QB, :]), [(pB, QB)], [(hB, 0)])

            yield from stage2(c, seg, s1, dict(hA=hA, hB=hB, init=chain_init))
            if seg.idx == NSEGP - 1:
                state_out(c, hA[:, QB, :], (hA, QB),
                          o_wkv_p[2 * c:2 * c + 2, :, :].rearrange("h i j -> (h i) j"))
                yield
        else:
            sp_ = seg.idx % 2
            hA, hB, hO = H32s[sp_][0], Hbfs[sp_], H32s[sp_][1]
            b0 = seg.idx * QB
            for q in range(QB):
                sin = sin_tiles[sin_ctr["i"] % 3]
                sin_ctr["i"] += 1
                S.dma(sin[:, 0:64], s0[b0 + q, 2 * c:2 * c + 2, :, :].rearrange("h i j -> (h i) j"), writes=[sin])
                for h in range(2):
                    P = slice(h * 64, (h + 1) * 64)
                    S.op(ew_eng(), lambda E: E.tensor_copy(out=sbd[P, h, :], in_=sin[P, 0:64]), [sin], [(sbd, h)])
                ps = getps()
                mm(ps[:, 0:64], sbd[:].rearrange("p h t -> p (h t)"), II_f[:], [(sbd, 0), (sbd, 1), II_f], ps)
                S.op("act", lambda E: E.copy(out=hA[:, q, :], in_=ps[:, 0:64]), [ps], [(hA, q)])
                S.op("dve", lambda E: E.tensor_copy(out=hB[:, q, :], in_=ps[:, 0:64]), [ps], [(hB, q)])
                yield
            yield from stage2(c, seg, s1, dict(hA=hA, hB=hB, hO=hO))
            for q in range(QB):
                state_out(c, hO[:, q, :], hO,
                          o_wkv_s[b0 + q, 2 * c:2 * c + 2, :, :].rearrange("h i j -> (h i) j"))
                yield

    NSTREAM = 2
    mains = []
    tails = []
    SID = ("A", "B")

    chain_done = set()
    bd_free = set()

    def step_main(m):
        if m[2] == "TAILWAIT":
            if tails:
                return True
            mains.remove(m)
            tails.append(m)
            return False
        if m[2] is not None:
            if m[2][1] >= 0 and m[2] not in chain_done:
                return True
            m[2] = None
        cur_alloc["who"] = "s2" + SID[m[1]]
        try:
            v = next(m[0])
        except StopIteration:
            mains.remove(m)
            return False
        if isinstance(v, tuple) and v[0] == "BDFREE":
            bd_free.add(v[1])
            return True
        if isinstance(v, tuple) and v[0] == "CHAIN":
            m[2] = v[1]
            return True
        if isinstance(v, tuple) and v[0] == "TAIL":
            if v[1] is not None:
                chain_done.add(v[1])
            if tails:
                m[2] = "TAILWAIT"
                return True
            mains.remove(m)
            tails.append(m)
            return False
        return True

    def step_tails():
        for m in list(tails):
            cur_alloc["who"] = "tail" + SID[m[1]]
            try:
                next(m[0])
            except StopIteration:
                tails.remove(m)

    def step_bg():
        for m in list(mains):
            step_main(m)
        step_tails()
        step_wgen()

    def run_stage1(g1):
        acc = 0.0
        while True:
            step_bg()
            acc += RSCALE if (mains or tails) else 8.0
            while acc >= 1.0:
                acc -= 1.0
                cur_alloc["who"] = "s1"
                try:
                    next(g1)
                except StopIteration as e_:
                    cur_alloc["who"] = None
                    return e_.value

    def start_main(gen):
        while len(mains) >= NSTREAM:
            step_bg()
        used = {m[1] for m in mains} | {m[1] for m in tails}
        while len(used) >= 2:
            step_bg()
            used = {m[1] for m in mains} | {m[1] for m in tails}
        sid = 0 if 0 not in used else 1
        mains.append([gen, sid, None])

    def drain_all():
        while mains or tails:
            step_bg()
        cur_alloc["who"] = None

    def load_weights(c):
        for g in range(7):
            stg = stage[g % 3]
            S.dma(stg[:], w1[c, :, g, :, :], writes=[stg])
            S.op("dve", lambda E, g=g, stg=stg: E.tensor_copy(out=wcb[:, g, :, :], in_=stg[:]), [stg], [(wcb, g)])

    def load_weights_gen(c, gap=7):
        for g in range(7):
            stg = stage[g % 3]
            S.dma(stg[:], w1[c, :, g, :, :], writes=[stg])
            for _ in range(gap):
                yield
            S.op("dve", lambda E, g=g, stg=stg: E.tensor_copy(out=wcb[:, g, :, :], in_=stg[:]), [stg], [(wcb, g)])
            yield

    wgen = {"g": None}

    def step_wgen():
        if wgen["g"] is not None:
            try:
                next(wgen["g"])
            except StopIteration:
                wgen["g"] = None

    def finish_wgen():
        while wgen["g"] is not None:
            step_wgen()

    def finalize_after(gen, fn):
        yield from gen
        fn()

    try:
        load_weights(0)
        for c in range(NCH):
            par = 0
            for segi, seg in enumerate(segs):
                if stop_after == "seg%d" % segi:
                    raise StopBuild()
                cur["segi"] = segi
                if segi == 0:
                    finish_wgen()
                s1 = run_stage1(stage1(c, seg))
                if seg.kind == "p":
                    start_main(s2full(c, seg, s1, par))
                    par = 1 - par
                else:
                    if c + 1 < NCH and stop_after != "c0":
                        wgen["g"] = load_weights_gen(c + 1)
                    nbatch = NSB // QB
                    for b in range(nbatch):
                        sub = SubSeg(b)
                        while b >= 2 and (c, "s", b - 2) not in bd_free:
                            step_bg()
                        cur_alloc["who"] = "s1"
                        for _ in bd_write(sub, bd["s"][b % 2], s1["keep"], b * QB * 5):
                            pass
                        s1b = dict(bonus=Off(s1["bonus"], b * QB * 5, QB * 5), g=Off(s1["g"], b * QB * 5, QB * 5),
                                   gA=Off(s1["gA"], b * QB * DEC, QB * DEC), m2=Off(s1["m2"], b * QB * DEC, QB * DEC),
                                   gam=Off(s1["gam"], b * QB, QB))
                        gen = s2full(c, sub, s1b, 0)
                        if b == nbatch - 1:
                            K_ = s1["keep"]
                            scr.put(K_["kk"], K_["E3"], K_["tb"], K_["E2"], K_["E4"], K_["kf"], K_["zr"], K_["E1"], K_["zv"])

                            def fin_(s1=s1):
                                scr.put(s1["bonus"], s1["g"], s1["gA"], s1["m2"])
                                gam_pool.put(s1["gam"])

                            gen = finalize_after(gen, fin_)
                        start_main(gen)
            if stop_after == "c0":
                break
        drain_all()
    except StopBuild:
        pass
    dump("merged", merged[:, 0, :], [(merged, 0)])
    dump("merged7", merged[:, 7, :], [(merged, 7)])
    dump("fin", fin[:, 0, :], [(fin, 0)])

    print("nops", S.nops, "cnt", {k: v for k, v in S.cnt.items() if not k.startswith("d")})
    if stop_after is not None:
        S.finish()
        p1.close()
        es.close()
        return nc
    psF = [getps(), getps()]
    for c in range(NCH):
        hf, k4 = divmod(c, 4)
        S.op("pe", lambda E: E.transpose(out=psF[hf][0:NFIN, k4 * 128:(k4 + 1) * 128], in_=fin[:, c, :],
                                         identity=ident_f[:]), [(fin, c), ident_f], [psF[hf]])
    S.barrier()
    p1.close()
    p2 = ExitStack()
    lnvs = sb("lnvs", [128, 2, D], F32, p2)
    h1T = sb("h1T", [128, 8, MCOLS], BF16, p2)
    stg2 = [sb("stg2_%d" % i, [128, D], F32, p2) for i in range(3)]
    big = [sb("big%d" % i, [128, D], F32, p2) for i in range(4)]
    bst = sb("bst", [128, 2, 6], F32, p2)
    mv_ = sb("mv_", [128, 8], F32, p2)
    aI = sb("aI", [128, 2, 128], BF16, p2)
    finT = sb("finT", [NFIN, D], F32, p2)
    p2a = ExitStack()
    wo_b = sb("wo_b", [128, 8, D], BF16, p2a)
    a_hi = float(np.float32(ALPHA).astype(ml_bf16).astype(np.float32))
    a_lo = float(np.float32(ALPHA - a_hi).astype(ml_bf16).astype(np.float32))
    S.op("pool", lambda E: E.tensor_scalar(out=aI[:, 0, :], in0=ident_f[:], scalar1=a_hi, scalar2=0.0,
                                           op0=ALU.mult, op1=ALU.add), [ident_f], [(aI, 0)])
    S.op("pool", lambda E: E.tensor_scalar(out=aI[:, 1, :], in0=ident_f[:], scalar1=a_lo, scalar2=0.0,
                                           op0=ALU.mult, op1=ALU.add), [ident_f], [(aI, 1)])
    for hf in range(2):
        copy_op(ev_eng(), finT[:, hf * 512:(hf + 1) * 512], psF[hf][0:NFIN, :], [psF[hf]], [finT])
    S.dma(o_fin[:, :], finT[:, :], reads=[finT])
    S.dma(o_shift[0:1, :], xp[SEQ - 1:SEQ, :])
    S.dma(o_shift[1:1 + NSB, :], xst.rearrange("(b t) d -> b t d", t=DEC)[:, DEC - 1, :])
    S.dma(lnvs[:], lnv[:, 0:2, :], writes=[lnvs])
    for kc in range(8):
        stg = stg2[kc % 3]
        S.dma(stg[:], wo[:, kc, :], writes=[stg])
        S.op("dve", lambda E, kc=kc, stg=stg: E.tensor_copy(out=wo_b[:, kc, :], in_=stg[:]), [stg], [(wo_b, kc)])

    def layer_norm(src, dst, gi, rows):
        R = slice(0, rows)
        for j in range(2):
            S.op("dve", lambda E, j=j: E.bn_stats(out=bst[R, j, :], in_=src[R, j * 512:(j + 1) * 512]), [src], [bst])
        S.op("dve", lambda E: E.bn_aggr(out=mv_[R, 0:2], in_=bst[R, :, :]), [bst], [(mv_, 0)])
        S.op("dve", lambda E: E.tensor_scalar(out=mv_[R, 2:3], in0=mv_[R, 1:2], scalar1=LN_EPS, scalar2=None,
                                              op0=ALU.add), [(mv_, 0)], [(mv_, 2)])
        act(mv_[R, 3:4], mv_[R, 2:3], AF.Sqrt, [(mv_, 2)], [(mv_, 3)])
        S.op("dve", lambda E: E.reciprocal(out=mv_[R, 4:5], in_=mv_[R, 3:4]), [(mv_, 3)], [(mv_, 4)])
        S.op("dve", lambda E: E.scalar_tensor_tensor(out=mv_[R, 5:6], in0=mv_[R, 0:1], scalar=-1.0, in1=mv_[R, 4:5],
                                                     op0=ALU.mult, op1=ALU.mult), [(mv_, 0), (mv_, 4)], [(mv_, 5)])
        act(dst[R, :], src[R, :], AF.Identity, [src, (mv_, 4), (mv_, 5)], [dst], bias=mv_[R, 5:6], scale=mv_[R, 4:5])
        tt("pool", dst[R, :], dst[R, :], lnvs[R, 0, :], ALU.mult, [dst, lnvs], [dst])
        tt("dve", dst[R, :], dst[R, :], lnvs[R, 1, :], ALU.add, [dst, lnvs], [dst])

    ttiles = [(t * 128, 128, yp[t * 128:(t + 1) * 128, :], xp[t * 128:(t + 1) * 128, :]) for t in range(SEQ // 128)]
    ttiles.append((SEQ, NSB * DEC, ys[:, :], xst[:, :]))

    for ti, (m0, rows, _, xrows) in enumerate(ttiles):
        R = slice(0, rows)
        xtm = big[ti % 2]
        s1t = big[2 + ti % 2]
        S.dma(xtm[R, :], xrows, writes=[xtm])
        pss = [getps(), getps()]
        for hf in range(2):
            for kc in range(8):
                mm(pss[hf][R, :], merged[:, kc, m0:m0 + rows], wo_b[:, kc, hf * 512:(hf + 1) * 512],
                   [(merged, kc), (wo_b, kc)], pss[hf], start=(kc == 0), stop=(kc == 7), inc=(kc == 7))
        for hf in range(2):
            S.op("dve", lambda E, hf=hf: E.scalar_tensor_tensor(
                out=s1t[R, hf * 512:(hf + 1) * 512], in0=xtm[R, hf * 512:(hf + 1) * 512], scalar=ALPHA,
                in1=pss[hf][R, :], op0=ALU.mult, op1=ALU.add), [xtm, pss[hf]], [s1t])
        layer_norm(s1t, xtm, 0, rows)
        pst = [getps(), getps()]
        for kc in range(8):
            hf, k4 = divmod(kc, 4)
            S.op("pe", lambda E, hf=hf, k4=k4, kc=kc: E.transpose(
                out=pst[hf][:, k4 * 128:k4 * 128 + rows], in_=xtm[R, kc * 128:(kc + 1) * 128],
                identity=ident_f[R, R]), [xtm, ident_f], [pst[hf]], inc=(k4 == 3))
        for hf in range(2):
            copy_op(ev_eng(), h1T[:, hf * 4:(hf + 1) * 4, m0:m0 + rows],
                    pst[hf][:].rearrange("p (k t) -> p k t", t=128)[:, :, 0:rows], [pst[hf]],
                    [(h1T, hf * 4 + k) for k in range(4)])
    if "h1T" in dbg_out:
        S.dma(dbg_out["h1T"], h1T[:, 0, :], reads=[(h1T, 0)])

    S.barrier()
    p2a.close()
    S.dma(lnvs[:], lnv[:, 2:4, :], writes=[lnvs])
    wd_b = sb("wd_b", [128, NFC, D], BF16, p2)
    NTH = 1088
    NFA = 15
    uTa = T(merged[:].rearrange("p a b -> p (a b)")[:, 0:NFA * NTH].rearrange("p (f t) -> p f t", t=NTH), "uTa")
    uTb = sb("uTb", [128, NFC - NFA, NTH], BF16, p2)

    class _UT:
        def __getitem__(self, k):
            p_, fc_, cols_ = k
            if fc_ < NFA:
                return uTa.h[p_, fc_, cols_]
            return uTb.h[p_, fc_ - NFA, cols_]

    uT = _UT()
    wgub = [sb("wgub%d" % i, [128, 2, 8, 128], BF16, p2) for i in range(2)]
    sgt = [sb("sgt%d" % i, [128, 512], F32, p2) for i in range(2)]
    for fc in range(NFC):
        stg = stg2[fc % 3]
        S.dma(stg[:], wd[fc, :, :], writes=[stg])
        S.op("dve", lambda E, fc=fc, stg=stg: E.tensor_copy(out=wd_b[:, fc, :], in_=stg[:]), [stg], [(wd_b, fc)])
    halves = [(0, 1024, ttiles[0:8]), (1024, 1088, ttiles[8:17])]
    for (c0, ncols, tls) in halves:
        blocks = [(b0_, min(512, ncols - b0_)) for b0_ in range(0, ncols, 512)]
        def load_fc(fc):
            wb_ = wgub[fc % 2]
            for j in range(2):
                stg = stg2[(2 * fc + j) % 3]
                S.dma(stg[:].rearrange("p (a b) -> p a b", b=128), wgu[fc, :, j, :, :], writes=[stg])
                S.op("dve", lambda E, j=j, stg=stg: E.tensor_copy(
                    out=wb_[:, j, :, :], in_=stg[:].rearrange("p (a b) -> p a b", b=128)), [stg], [(wb_, j)])

        load_fc(0)
        for fc in range(NFC):
            wb = wgub[fc % 2]
            if fc + 1 < NFC:
                load_fc(fc + 1)
            for bi, (b0_, bn) in enumerate(blocks):
                psg = getps()
                psu = getps()
                for j, ps_ in ((0, psg), (1, psu)):
                    for kc in range(8):
                        mm(ps_[:, 0:bn], wb[:, j, kc, :], h1T[:, kc, c0 + b0_:c0 + b0_ + bn], [(wb, j), (h1T, kc)], ps_,
                           start=(kc == 0), stop=(kc == 7), inc=(kc == 7))
                sg_ = sgt[(fc * 3 + bi) % 2]
                act(sg_[:, 0:bn], psg[:, 0:bn], AF.Silu, [psg], [sg_])
                tt("dve", uT[:, fc, b0_:b0_ + bn], psu[:, 0:bn], sg_[:, 0:bn], ALU.mult, [psu, sg_], [(uT, fc)])
        for ti, (m0, rows, yout, _) in enumerate(tls):
            R = slice(0, rows)
            l0 = m0 - c0
            pss = [getps(), getps()]
            for hf in range(2):
                for fc in range(NFC):
                    mm(pss[hf][R, :], uT[:, fc, l0:l0 + rows], wd_b[:, fc, hf * 512:(hf + 1) * 512],
                       [(uT, fc), (wd_b, fc)], pss[hf], start=(fc == 0), stop=False, inc=False)
                for k4 in range(4):
                    kc = hf * 4 + k4
                    for a in range(2):
                        last = (k4 == 3 and a == 1)
                        mm(pss[hf][R, k4 * 128:(k4 + 1) * 128], h1T[:, kc, m0:m0 + rows], aI[:, a, :],
                           [(h1T, kc), (aI, a)], pss[hf], start=False, stop=last, inc=last)
            s2t = big[ti % 2]
            yt = big[2 + ti % 2]
            for hf in range(2):
                copy_op(ev_eng(), s2t[R, hf * 512:(hf + 1) * 512], pss[hf][R, :], [pss[hf]], [s2t])
            layer_norm(s2t, yt, 2, rows)
            S.dma(yout, yt[R, :], reads=[yt])

    print("nops", S.nops)
    S.finish()
    p2.close()
    es.close()
    return nc


def kernel(**inputs):
    maps = prep_inputs(inputs)
    nc = build_program()
    res = run_bass_kernel_spmd(nc, maps, core_ids=list(range(NCORES)))
    R = res.results
    f32 = np.float32
    y_p = np.stack([R[i]["yp"] for i in range(NCORES)]).astype(f32)
    y_s = np.concatenate([R[i]["ys"].reshape(NSB, DEC, D) for i in range(NCORES)]).astype(f32)
    fin = [R[i]["o_fin"] for i in range(NCORES)]
    shf = [R[i]["o_shift"] for i in range(NCORES)]
    new_shift_p = np.stack([s[0] for s in shf])[None].astype(f32)
    new_shift_s = np.concatenate([s[1:] for s in shf])[None].astype(f32)
    new_wkv_p = np.stack([R[i]["o_wkv_p"] for i in range(NCORES)])[None].astype(f32)
    new_wkv_s = np.concatenate([R[i]["o_wkv_s"] for i in range(NCORES)])[None].astype(f32)
    new_conv_p = np.stack([f[0:3] for f in fin])[None].astype(f32)
    new_lru_p = np.stack([f[3] for f in fin])[None].astype(f32)
    new_conv_s = np.concatenate([f[4:52].reshape(NSB, 3, D) for f in fin])[None].astype(f32)
    new_lru_s = np.concatenate([f[52:68] for f in fin])[None].astype(f32)
    return (y_p, y_s, new_shift_p, new_wkv_p, new_conv_p, new_lru_p,
            new_shift_s, new_wkv_s, new_conv_s, new_lru_s)


def prep_inputs(inp):
    f = lambda a: np.ascontiguousarray(np.asarray(a, dtype=np.float32))
    w_in = f(inp["w_in"])[0]
    bases = [0, 1024, 2048, 3328, 4352, 5376, 6400]
    w1 = np.stack([w_in[:, b:b + 1024].reshape(8, 128, 8, 128).transpose(2, 1, 0, 3) for b in bases], axis=2)
    wl = w_in[:, 3072:3328].reshape(8, 128, 2, 128).transpose(1, 2, 0, 3)
    wo = f(inp["w_o"])[0].reshape(8, 128, D).transpose(1, 0, 2)
    wg = f(inp["w_ffn_gate"])[0].reshape(8, 128, NFC, 128).transpose(2, 1, 0, 3)
    wu = f(inp["w_ffn_up"])[0].reshape(8, 128, NFC, 128).transpose(2, 1, 0, 3)
    wgu = np.stack([wg, wu], axis=2)
    wd = f(inp["w_ffn_down"])[0].reshape(NFC, 128, D)
    vec = np.zeros((NV, D), np.float32)
    mu = f(inp["tmix_mu"])[0]
    vec[V_MUR] = mu[0:1024]; vec[V_MUK] = mu[1024:2048]; vec[V_MUV] = mu[2048:3072]
    vec[V_MUL, 0:256] = mu[3072:3328]
    vec[V_W0] = f(inp["w0"])[0]; vec[V_A0] = f(inp["a0"])[0]
    vec[V_KK] = f(inp["k_k"])[0]; vec[V_KA] = f(inp["k_a"])[0]
    vec[V_RK] = f(inp["r_k"])[0].reshape(-1)
    vec[V_LNXG] = f(inp["lnx_g"])[0]; vec[V_LNXB] = f(inp["lnx_b"])[0]
    cw = f(inp["conv_w"])[0]
    for j in range(4):
        vec[V_CW0 + j] = cw[j]
    vec[V_CB] = f(inp["conv_b"])[0]
    vec[V_BA] = f(inp["lru_ba"])[0].reshape(-1); vec[V_BI] = f(inp["lru_bi"])[0].reshape(-1)
    vec[V_LAM] = f(inp["lru_lambda"])[0]
    vecp = np.ascontiguousarray(vec.reshape(NV, 8, 128).transpose(2, 0, 1))
    wab = np.zeros((128, 2, 8, 128), np.float32)
    for j, nm in enumerate(("lru_wa", "lru_wi")):
        w = f(inp[nm])[0]
        for c in range(8):
            for bl in range(2):
                wab[bl * 64:(bl + 1) * 64, j, c, bl * 64:(bl + 1) * 64] = w[2 * c + bl]
    lnv = np.stack([np.broadcast_to(f(inp[n])[0], (128, D)) for n in ("ln1_g", "ln1_b", "ln2_g", "ln2_b")], axis=1)
    shared = dict(w1=w1, wl=wl, wo=wo, wgu=wgu, wd=wd, vec=vecp, w2d=f(inp["w2_decay"])[0],
                  a2d=f(inp["a2_iclr"])[0], g2d=f(inp["g2_gate"])[0], wab=wab, lnv=lnv)
    shared = {k: np.ascontiguousarray(v, dtype=np.float32) for k, v in shared.items()}
    x_prompt = f(inp["x_prompt"]); x_sample = f(inp["x_sample"])
    sh = f(inp["state_shift"])[0]; swkv = f(inp["state_wkv"])[0]
    sconv = f(inp["state_conv"])[0]; slru = f(inp["state_lru"])[0]
    maps = []
    for i in range(NCORES):
        b0 = i * NSB
        xs = np.concatenate([sh[b0:b0 + NSB, None, :], x_sample[b0:b0 + NSB]], axis=1).reshape(NSB * 5, D)
        stt = np.concatenate([sconv[b0:b0 + NSB].reshape(NSB * 3, D), slru[b0:b0 + NSB]], axis=0)
        m = dict(xp=x_prompt[i], xs=xs, xst=x_sample[b0:b0 + NSB].reshape(NSB * DEC, D), st=stt,
                 s0=swkv[b0:b0 + NSB])
        m = {k: np.ascontiguousarray(v, dtype=np.float32) for k, v in m.items()}
        m.update(shared)
        maps.append(m)
    return maps
```

```python
import numpy as np
import ml_dtypes
from contextlib import ExitStack
import concourse.bass as bass
import concourse.mybir as mybir
from concourse.bass_utils import run_bass_kernel_spmd

F32 = mybir.dt.float32
BF16 = mybir.dt.bfloat16
AF = mybir.ActivationFunctionType
ALU = mybir.AluOpType
AX = mybir.AxisListType
ml_bf16 = ml_dtypes.bfloat16

D = 1024
NCORES = 8
SEQ = 2048
NSB = 16
DEC = 4
NCH = 8
D_FF = 2816
NFC = D_FF // 128
KAPPA = float(np.exp(-0.5))
ALPHA = 2.0 ** 0.25
LN_EPS = 1e-5
GN_EPS = 64e-5
CH = 64
PIPE = True
RSCALE = 1.0
TAILPIPE = True
TAILSEL = lambda kind, idx: True
QB = 4
SEGP = QB * CH
NSEGP = SEQ // SEGP
SB = 16
NSEGS = NSB // SB
XS0 = SEQ
XCOLS = SEQ + NSB * 5
MCOLS = SEQ + NSB * DEC
NFIN = 68
(V_MUR, V_MUK, V_MUV, V_W0, V_A0, V_KK, V_KA, V_RK, V_LNXG, V_LNXB, V_CW0, V_CW1, V_CW2, V_CW3,
 V_CB, V_BA, V_BI, V_LAM, V_MUL) = range(19)
NV = 19


class Sched:
    COMPUTE = ("pe", "act", "dve", "pool")

    def __init__(self, nc, es, ndma=14):
        self.nc = nc
        self.eng = {"pe": nc.tensor, "act": nc.scalar, "dve": nc.vector, "pool": nc.gpsimd, "sp": nc.sync}
        self.sem = {}
        self.cnt = {}
        for e in self.COMPUTE:
            self.sem[e] = es.enter_context(nc.semaphore("sem_" + e))
            self.cnt[e] = 0
        self.ndma = ndma
        for i in range(ndma):
            d = "d%d" % i
            self.sem[d] = es.enter_context(nc.semaphore("sem_" + d))
            self.cnt[d] = 0
        self.rr = 0
        self.waited = {e: {} for e in list(self.COMPUTE) + ["sp"]}
        self.W = {}
        self.R = {}
        self.excl = set()
        self.nops = {e: 0 for e in list(self.COMPUTE) + ["sp"]}

    def _collect(self, e, reads, writes):
        need = {}

        def add(d, c, raw):
            if d == e and (e == "pe" or (not raw and e != "pool")):
                return
            if c > need.get(d, 0):
                need[d] = c

        reads = [getattr(r, "base", r) for r in reads]
        writes = [getattr(w, "base", w) for w in writes]
        for r in reads:
            for d, c in self.W.get(r, {}).items():
                add(d, c, True)
            if r in self.excl:
                for d, c in self.R.get(r, {}).items():
                    add(d, c, False)
        for w in writes:
            for d, c in self.W.get(w, {}).items():
                add(d, c, False)
            for d, c in self.R.get(w, {}).items():
                add(d, c, False)
        return need

    LOG = None

    def _emit_waits(self, e, need):
        wd = self.waited[e]
        for d, c in need.items():
            if wd.get(d, 0) >= c:
                continue
            self.eng[e].wait_ge(self.sem[d], c)
            wd[d] = c
            if self.LOG is not None:
                self.LOG.append((e, d, c, dict(self.cnt)))

    def _record(self, dom, val, reads, writes):
        reads = [getattr(r, "base", r) for r in reads]
        writes = [getattr(w, "base", w) for w in writes]
        for r in reads:
            rr = self.R.setdefault(r, {})
            if val > rr.get(dom, 0):
                rr[dom] = val
        for w in writes:
            if self.R.get(w):
                self.W[w] = {dom: val}
                self.R[w] = {}
            else:
                self.W.setdefault(w, {})[dom] = val

    def op(self, e, fn, reads=(), writes=(), inc=True):
        need = self._collect(e, reads, writes)
        att = None
        if e != "pe":
            wd = self.waited[e]
            pend = [(d, c) for d, c in need.items() if wd.get(d, 0) < c]
            if pend:
                att = pend[-1]
                need = dict(pend[:-1])
        self._emit_waits(e, need)
        ins = fn(self.eng[e])
        if att is not None:
            ins._wait_ge(self.sem[att[0]], att[1])
            self.waited[e][att[0]] = att[1]
        self.nops[e] += 1
        if inc:
            self.cnt[e] += 1
            ins.then_inc(self.sem[e], 1)
            val = self.cnt[e]
        else:
            val = self.cnt[e] + 1
        self._record(e, val, reads, writes)

    def dma(self, out, in_, reads=(), writes=(), q="sp"):
        d = "d%d" % self.rr
        self.rr = (self.rr + 1) % self.ndma
        need = self._collect(q, reads, writes)
        if self.cnt[d] > 0:
            need[d] = max(need.get(d, 0), self.cnt[d])
        self._emit_waits(q, need)
        ins = self.eng[q].dma_start(out=out, in_=in_)
        self.nops[q] += 1
        self.cnt[d] += 16
        ins.then_inc(self.sem[d], 16)
        self._record(d, self.cnt[d], reads, writes)

    def barrier(self):
        for e in list(self.COMPUTE) + ["sp"]:
            need = {d: c for d, c in self.cnt.items() if c > 0 and d != e}
            self._emit_waits(e, need)

    def finish(self):
        for i in range(self.ndma):
            d = "d%d" % i
            if self.cnt[d] > 0:
                self.eng["sp"].wait_ge(self.sem[d], self.cnt[d])
        for e in self.COMPUTE:
            if self.cnt[e] > 0:
                self.eng["sp"].wait_ge(self.sem[e], self.cnt[e])


class StopBuild(Exception):
    pass


class T:
    def __init__(self, h, name):
        self.h = h
        self.name = name

    def __getitem__(self, k):
        return self.h[k]

    def __repr__(self):
        return "T(%s)" % self.name


class Off:
    def __init__(self, base, off, width):
        self.base = base
        self.off = off
        self.width = width

    def __getitem__(self, k):
        if not isinstance(k, tuple):
            return self.base[:, self.off:self.off + self.width]
        p_, c_ = k
        lo = 0 if c_.start is None else c_.start
        hi = self.width if c_.stop is None else c_.stop
        return self.base[p_, self.off + lo:self.off + hi]


class SubSeg:
    kind = "s"

    def __init__(self, b):
        self.idx = b
        self.ncol = QB * 5
        self.nb = QB
        self.T = DEC
        self.tv = DEC
        self.nq = QB
        self.m0 = SEQ + b * QB * DEC
        self.mcols = QB * DEC
        self.nlev = 2

    def tokv(self, ap):
        return ap.rearrange("p (b s) -> p b s", s=5)[:, :, 1:5]

    chv = tokv


class Pool:
    def __init__(self, tiles):
        self.free = list(tiles)
        self.all = list(tiles)

    def get(self):
        return self.free.pop(0)

    def put(self, *ts):
        for t in ts:
            assert t in self.all and t not in self.free
            self.free.append(t)


class Seg:
    def __init__(self, kind, idx):
        self.kind = kind
        self.idx = idx
        if kind == "p":
            self.x0 = idx * SEGP
            self.ncol = SEGP
            self.nb = 1
            self.T = SEGP
            self.tv = CH
            self.m0 = idx * SEGP
            self.mcols = SEGP
            self.nlev = 6
        else:
            self.x0 = XS0 + idx * SB * 5
            self.ncol = SB * 5
            self.nb = SB
            self.T = DEC
            self.tv = DEC
            self.m0 = SEQ + idx * SB * DEC
            self.mcols = SB * DEC
            self.nlev = 2
        self.nq = QB if kind == "p" else SB

    def tokv(self, ap):
        if self.kind == "p":
            return ap.rearrange("p (b t) -> p b t", b=1)
        return ap.rearrange("p (b s) -> p b s", s=5)[:, :, 1:5]

    def chv(self, ap):
        if self.kind == "p":
            return ap.rearrange("p (q t) -> p q t", t=CH)
        return ap.rearrange("p (b s) -> p b s", s=5)[:, :, 1:5]


def build_program(debug=None, stop_after=None):
    nc = bass.Bass("TRN2", target_bir_lowering=False)
    es = ExitStack()
    S = Sched(nc, es)

    def din(name, shape):
        return nc.dram_tensor(name, list(shape), F32, kind="ExternalInput").ap()

    def dout(name, shape):
        return nc.dram_tensor(name, list(shape), F32, kind="ExternalOutput").ap()

    xp = din("xp", [SEQ, D])
    xs = din("xs", [NSB * 5, D])
    xst = din("xst", [NSB * DEC, D])
    st = din("st", [64, D])
    s0 = din("s0", [NSB, 16, 64, 64])
    w1 = din("w1", [NCH, 128, 7, 8, 128])
    wl = din("wl", [128, 2, 8, 128])
    wo = din("wo", [128, 8, D])
    wgu = din("wgu", [NFC, 128, 2, 8, 128])
    wd = din("wd", [NFC, 128, D])
    vec = din("vec", [128, NV, 8])
    w2d = din("w2d", [64, D])
    a2d = din("a2d", [64, D])
    g2d = din("g2d", [128, D])
    wab = din("wab", [128, 2, 8, 128])
    lnv = din("lnv", [128, 4, D])

    yp = dout("yp", [SEQ, D])
    ys = dout("ys", [NSB * DEC, D])
    o_shift = dout("o_shift", [1 + NSB, D])
    o_wkv_p = dout("o_wkv_p", [16, 64, 64])
    o_wkv_s = dout("o_wkv_s", [NSB, 16, 64, 64])
    o_fin = dout("o_fin", [NFIN, D])
    dbg_out = {}
    if debug:
        for name, (shape, dt_) in debug.items():
            dbg_out[name] = nc.dram_tensor("dbg_" + name, list(shape), dt_, kind="ExternalOutput").ap()

    def sb(name, shape, dt=F32, stack=None):
        h = (stack or es).enter_context(nc.sbuf_tensor(name, list(shape), dt))
        return T(h, name)

    psl = [T(es.enter_context(nc.psum_tensor("ps%d" % i, [128, 512], F32)), "ps%d" % i) for i in range(8)]
    ps_state = {"i": 0}
    S.excl.update(psl)

    ps_banks = {None: list(range(8)), "s1": [6, 7], "s1a": [6], "s1b": [7], "s1c": [5], "s2A": [0, 1], "s2B": [2, 3], "tailA": [4], "tailB": [4]}
    ps_ctr = {k: 0 for k in (None, "s1", "s1a", "s1b", "s1c", "s2A", "s2B", "tailA", "tailB")}
    cur_alloc = {"who": None}

    def getps():
        who = cur_alloc["who"]
        lst = ps_banks[who]
        t = psl[lst[ps_ctr[who] % len(lst)]]
        ps_ctr[who] += 1
        return t

    cur = {"segi": -1}

    def ck(label):
        if stop_after == label or stop_after == "%s@%d" % (label, cur["segi"]):
            raise StopBuild()

    ident_f = sb("ident_f", [128, 128])
    ident_b = sb("ident_b", [128, 128], BF16)
    II_f = sb("II_f", [128, 64])
    II_b = sb("II_b", [128, 64], BF16)
    ones_bd = sb("ones_bd", [128, 128])
    mUs = sb("mUs", [128, QB, 128], BF16)
    mUi = sb("mUi", [128, QB, 128], BF16)
    mLs = sb("mLs", [128, QB, 128], BF16)
    rmask_p = sb("rmask_p", [128, SEGP])
    rmask_s = sb("rmask_s", [128, SB * 5])
    vecs = sb("vecs", [128, NV, 8])
    der = sb("der", [128, 12, 8])
    w2b = sb("w2b", [128, D], BF16)
    g2b = sb("g2b", [128, D], BF16)
    wabb = sb("wabb", [128, 2, 8, 128], BF16)
    fin = sb("fin", [128, NCH, NFIN])
    stT = sb("stT", [128, NCH, 64])

    S.op("pool", lambda E: E.memset(ident_f[:], 1.0), writes=[ident_f])
    S.op("pool", lambda E: E.affine_select(out=ident_f[:], in_=ident_f[:], pattern=[[-1, 128]],
                                           compare_op=ALU.is_equal, fill=0.0, base=0, channel_multiplier=1),
         reads=[ident_f], writes=[ident_f])
    S.op("dve", lambda E: E.tensor_copy(out=ident_b[:], in_=ident_f[:]), reads=[ident_f], writes=[ident_b])
    S.op("dve", lambda E: E.tensor_tensor(out=II_f[:], in0=ident_f[:, 0:64], in1=ident_f[:, 64:128], op=ALU.add),
         reads=[ident_f], writes=[II_f])
    S.op("dve", lambda E: E.tensor_copy(out=II_b[:], in_=II_f[:]), reads=[II_f], writes=[II_b])
    S.op("pool", lambda E: E.memset(ones_bd[:], 0.0), writes=[ones_bd])
    S.op("pool", lambda E: E.memset(ones_bd[0:64, 0:64], 1.0), reads=[ones_bd], writes=[ones_bd])
    S.op("pool", lambda E: E.memset(ones_bd[64:128, 64:128], 1.0), reads=[ones_bd], writes=[ones_bd])
    for m, patt, cm, cmp in ((mUs, 1, -1, ALU.is_gt), (mUi, 1, -1, ALU.is_ge), (mLs, -1, 1, ALU.is_gt)):
        S.op("pool", lambda E, m=m: E.memset(m[:], 1.0), writes=[m])
        S.op("pool", lambda E, m=m, patt=patt, cm=cm, cmp=cmp: E.affine_select(
            out=m[:], in_=m[:], pattern=[[0, QB], [patt, 128]], compare_op=cmp, fill=0.0, base=0,
            channel_multiplier=cm), reads=[m], writes=[m])
    S.op("pool", lambda E: E.memset(rmask_p[:], 1.0), writes=[rmask_p])
    S.op("pool", lambda E: E.memset(rmask_p[:].rearrange("p (q t) -> p q t", t=CH)[:, :, 0:1], 0.0),
         reads=[rmask_p], writes=[rmask_p])
    S.op("pool", lambda E: E.memset(rmask_s[:], 1.0), writes=[rmask_s])
    S.op("pool", lambda E: E.memset(rmask_s[:].rearrange("p (b s) -> p b s", s=5)[:, :, 0:2], 0.0),
         reads=[rmask_s], writes=[rmask_s])
    S.op("pool", lambda E: E.memset(fin[:], 0.0), writes=[(fin, c) for c in range(NCH)])

    S.dma(vecs[:], vec[:, :, :], writes=[vecs])

    def vcol(i, c):
        return vecs[:, i, c:c + 1]

    S.op("dve", lambda E: E.tensor_scalar(out=der[:, 0, :], in0=vecs[:, V_KA, :], scalar1=-1.0, scalar2=1.0,
                                          op0=ALU.mult, op1=ALU.add), reads=[vecs], writes=[(der, 0)])
    S.op("act", lambda E: E.activation(out=der[:, 3, :], in_=vecs[:, V_LAM, :], func=AF.Exp, scale=-1.0),
         reads=[vecs], writes=[(der, 3)])
    S.op("act", lambda E: E.activation(out=der[:, 3, :], in_=der[:, 3, :], func=AF.Ln, bias=1.0, scale=1.0),
         reads=[(der, 3)], writes=[(der, 3)])
    S.op("dve", lambda E: E.tensor_scalar(out=der[:, 1, :], in0=der[:, 3, :], scalar1=-8.0, scalar2=None,
                                          op0=ALU.mult), reads=[(der, 3)], writes=[(der, 1)])
    S.op("dve", lambda E: E.tensor_scalar(out=der[:, 2, :], in0=der[:, 3, :], scalar1=-16.0, scalar2=None,
                                          op0=ALU.mult), reads=[(der, 3)], writes=[(der, 2)])
    for di, vi in ((4, V_W0), (5, V_A0), (6, V_BA), (7, V_BI), (8, V_KA)):
        S.op("dve", lambda E, di=di, vi=vi: E.tensor_scalar(out=der[:, di, :], in0=vecs[:, vi, :], scalar1=0.5,
                                                            scalar2=None, op0=ALU.mult), reads=[vecs], writes=[(der, di)])
    S.op("dve", lambda E: E.tensor_scalar(out=der[:, 9, :], in0=vecs[:, V_KA, :], scalar1=-0.5, scalar2=1.0,
                                          op0=ALU.mult, op1=ALU.add), reads=[vecs], writes=[(der, 9)])
    S.op("dve", lambda E: E.tensor_scalar(out=der[:, 10, :], in0=der[:, 3, :], scalar1=-4.0, scalar2=None,
                                          op0=ALU.mult), reads=[(der, 3)], writes=[(der, 10)])
    S.op("dve", lambda E: E.tensor_scalar(out=der[:, 11, :], in0=der[:, 3, :], scalar1=-8.0, scalar2=None,
                                          op0=ALU.mult), reads=[(der, 3)], writes=[(der, 11)])
    negh = sb("negh", [128, QB])
    S.op("pool", lambda E: E.memset(negh[:], -0.5), writes=[negh])

    p1 = ExitStack()
    merged = sb("merged", [128, 8, MCOLS], BF16)
    xsc = nc.dram_tensor("xsc", [128, 8, XCOLS], BF16).ap()
    xtt = [sb("xtt%d" % i, [128, 8, 128], BF16, p1) for i in range(1)]
    xs_pool = Pool([sb("xseg%d" % i, [128, 8, SEGP], BF16, p1) for i in range(2)])
    if stop_after is not None:
        for c_ in range(8):
            S.op("pool", lambda E, c_=c_: E.memset(merged[:, c_, :], 0.0), writes=[(merged, c_)])
    lora0 = sb("lora0", [128, XCOLS], BF16, p1)
    lora1 = sb("lora1", [128, XCOLS], BF16, p1)
    stage = [sb("stage%d" % i, [128, 8, 128], F32, p1) for i in range(3)]
    wcb = sb("wcb", [128, 7, 8, 128], BF16, p1)
    NSCR = 37
    scr = Pool([sb("scr%d" % i, [128, SEGP + 4], F32, p1) for i in range(NSCR)])
    NWK = 27
    wk = Pool([sb("wk%d" % i, [128, QB, 128], BF16, p1) for i in range(NWK)])
    bdn = ("aT", "bT", "kT", "rT", "bgT", "kgT", "vT")
    bd = {}
    for kind in ("p", "s"):
        bd[kind] = []
        for par_ in range(2):
            st_ = {n: sb("bd_%s%d_%s" % (kind, par_, n), [128, QB, 2, 64], BF16, p1) for n in bdn}
            bd[kind].append(st_)
            for n in bdn:
                S.op("pool", lambda E, t=st_[n]: E.memset(t[:], 0.0), writes=[(st_[n], 0), (st_[n], 1)])
    ybd = sb("ybd", [128, QB, 2, 64], F32, p1)
    S.op("pool", lambda E: E.memset(ybd[:], 0.0), writes=[(ybd, 0), (ybd, 1)])
    hbd = sb("hbd", [128, 2, 64], F32, p1)
    S.op("pool", lambda E: E.memset(hbd[:], 0.0), writes=[(hbd, 0), (hbd, 1)])
    zbuf = [sb("zbuf%d" % i, [128, 1 + SEGP], F32, p1) for i in range(3)]
    xbuf = sb("xbuf", [128, SB * (3 + SEGP // 1) if False else max(3 + SEGP, SB * 7)], F32, p1)
    carry3 = sb("carry3", [128, 3], F32, p1)
    hcar = sb("hcar", [128, 1], F32, p1)
    H32 = [sb("H32_%d" % i, [128, QB + 1, 64], F32, p1) for i in range(2)]
    Hbf = [sb("Hbf_%d" % i, [128, QB + 1, 64], BF16, p1) for i in range(2)]
    H32s = [[sb("H32s_%d%d" % (i, j), [128, QB, 64], F32, p1) for j in range(2)] for i in range(2)]
    Hbfs = [sb("Hbfs_%d" % i, [128, QB, 64], BF16, p1) for i in range(2)]
    gamC = sb("gamC", [128, QB], F32, p1)
    stat = sb("stat", [128, 8, QB], F32, p1)
    bst4 = sb("bst4", [128, QB, 6], F32, p1)
    mvq = sb("mvq", [128, QB, 2], F32, p1)

    S.dma(stage[0][0:64, :, :].rearrange("p a b -> p (a b)"), w2d[:, :], writes=[stage[0]])
    S.dma(stage[1][64:128, :, :].rearrange("p a b -> p (a b)"), a2d[:, :], writes=[stage[1]])
    S.op("pool", lambda E: E.tensor_copy(out=w2b[0:64, :], in_=stage[0][0:64, :, :].rearrange("p a b -> p (a b)")),
         reads=[stage[0]], writes=[(w2b, 0)])
    S.op("pool", lambda E: E.tensor_copy(out=w2b[64:128, :], in_=stage[1][64:128, :, :].rearrange("p a b -> p (a b)")),
         reads=[stage[1]], writes=[(w2b, 1)])
    S.dma(stage[2][:, :, :].rearrange("p a b -> p (a b)"), g2d[:, :], writes=[stage[2]])
    S.op("pool", lambda E: E.tensor_copy(out=g2b[:], in_=stage[2][:, :, :].rearrange("p a b -> p (a b)")),
         reads=[stage[2]], writes=[g2b])
    for j in range(2):
        S.dma(stage[j][:], wab[:, j, :, :], writes=[stage[j]])
        S.op("pool", lambda E, j=j: E.tensor_copy(out=wabb[:, j, :, :], in_=stage[j][:]),
             reads=[stage[j]], writes=[(wabb, j)])
    for j in range(2):
        S.dma(stage[j][:], wl[:, j, :, :], writes=[stage[j]])
        S.op("pool", lambda E, j=j: E.tensor_copy(out=wcb[:, j, :, :], in_=stage[j][:]),
             reads=[stage[j]], writes=[(wcb, j)])

    rr = {"ev": 0, "ew": 0}

    def ev_eng():
        return "act"

    def ew_eng():
        rr["ew"] ^= 1
        return "dve" if rr["ew"] else "pool"

    def copy_op(e, out, in_, reads, writes):
        if e == "act":
            S.op("act", lambda E: E.copy(out=out, in_=in_), reads, writes)
        else:
            S.op(e, lambda E: E.tensor_copy(out=out, in_=in_), reads, writes)

    ntile = SEQ // 128
    for t in range(ntile + 2):
        xi = stage[t % 3]
        xiv = xi[:].rearrange("p a b -> p (a b)")
        if t < ntile:
            rows = 128
            S.dma(xiv, xp[t * 128:(t + 1) * 128, :], writes=[xi])
        elif t == ntile:
            rows = NSB * 5
            S.dma(xiv[0:rows, :], xs[:, :], writes=[xi])
        else:
            rows = 64
            S.dma(xiv[0:rows, :], st[:, :], writes=[xi])
        for half in range(2):
            ps = getps()
            for k4 in range(4):
                kc = half * 4 + k4
                S.op("pe", lambda E, ps=ps, k4=k4, kc=kc, xi=xi, rows=rows: E.transpose(
                    out=ps[:, k4 * 128:k4 * 128 + rows], in_=xi[0:rows, kc, :],
                    identity=ident_f[0:rows, 0:rows]), reads=[xi, ident_f], writes=[ps], inc=(k4 == 3))
            src = ps[:].rearrange("p (k t) -> p k t", t=128)[:, :, 0:rows]
            if t <= ntile:
                xo = xtt[0]
                copy_op(ev_eng(), xo[:, half * 4:(half + 1) * 4, 0:rows], src, [ps], [(xo, half)])
                if half == 1:
                    S.dma(xsc[:, :, t * 128:t * 128 + rows], xo[:, :, 0:rows], reads=[(xo, 0), (xo, 1)],
                          writes=[("xsc", t)])
            else:
                dst = stT[:, half * 4:(half + 1) * 4, :]
                copy_op(ev_eng(), dst, src, [ps], [stT])
    xstate = {"tile": {}, "order": [], "pos": 0}

    def _issue_x(seg):
        t_ = xs_pool.get()
        t0_, t1_ = seg.x0 // 128, (seg.x0 + seg.ncol - 1) // 128
        S.dma(t_[:, :, 0:seg.ncol], xsc[:, :, seg.x0:seg.x0 + seg.ncol],
              reads=[("xsc", k) for k in range(t0_, t1_ + 1)], writes=[t_])
        return t_

    def get_x(seg):
        order, pos = xstate["order"], xstate["pos"]
        assert order[pos] is seg
        t_ = xstate["tile"].pop(pos, None)
        if t_ is None:
            t_ = _issue_x(seg)
        if pos + 1 < len(order):
            xstate["tile"][pos + 1] = _issue_x(order[pos + 1])
        xstate["pos"] = pos + 1
        return t_

    def dump(name, ap, reads):
        if name in dbg_out:
            S.dma(dbg_out[name], ap, reads=reads)


    def proj_ps(wsel, seg, xseg):
        ps = getps()
        ncol = seg.ncol
        for kc in range(8):
            lhsT, wres = wsel(kc)
            S.op("pe", lambda E, ps=ps, lhsT=lhsT, kc=kc: E.matmul(
                ps[:, 0:ncol], lhsT=lhsT, rhs=xseg[:, kc, 0:ncol], start=(kc == 0), stop=(kc == 7)),
                reads=[wres, xseg], writes=[ps], inc=(kc == 7))
        return ps

    def shifted(wsel, mu_ap, seg, zb, first, xseg):
        ncol = seg.ncol
        if seg.kind == "p":
            if first:
                S.op("pool", lambda E: E.memset(zb[:, 0:1], 0.0), writes=[zb])
            else:
                S.op("pool", lambda E: E.tensor_copy(out=zb[:, 0:1], in_=zb[:, SEGP:SEGP + 1]), reads=[zb], writes=[zb])
        ps = proj_ps(wsel, seg, xseg)
        S.op("act", lambda E: E.copy(out=zb[:, 1:1 + ncol], in_=ps[:, 0:ncol]), reads=[ps], writes=[zb])
        dt_ = scr.get()
        zm = scr.get()
        S.op("pool", lambda E: E.tensor_tensor(out=dt_[:, 0:ncol], in0=zb[:, 0:ncol], in1=zb[:, 1:1 + ncol],
                                               op=ALU.subtract), reads=[zb], writes=[dt_])
        S.op("dve", lambda E: E.scalar_tensor_tensor(out=zm[:, 0:ncol], in0=dt_[:, 0:ncol], scalar=mu_ap,
                                                     in1=zb[:, 1:1 + ncol], op0=ALU.mult, op1=ALU.add),
             reads=[dt_, zb, vecs], writes=[zm])
        scr.put(dt_)
        return zm

    segs = [Seg("p", i) for i in range(NSEGP)] + [Seg("s", i) for i in range(NSEGS)]

    xstate["order"] = segs * 2 + segs * NCH
    for L in range(2):
        for si, seg in enumerate(segs):
            xseg = get_x(seg)
            zm = shifted(lambda kc, L=L: (wcb[:, L, kc, :], (wcb, L)), vcol(V_MUL, L), seg, zbuf[0],
                         first=(seg.kind == "p" and seg.idx == 0), xseg=xseg)
            xs_pool.put(xseg)
            nco = seg.ncol
            if L == 0:
                S.op("act", lambda E: E.activation(out=lora0[0:64, seg.x0:seg.x0 + nco], in_=zm[0:64, 0:nco],
                                                   func=AF.Tanh), reads=[zm], writes=[(lora0, 0)])
                S.op("act", lambda E: E.copy(out=lora0[64:128, seg.x0:seg.x0 + nco], in_=zm[64:128, 0:nco]),
                     reads=[zm], writes=[(lora0, 1)])
            else:
                S.op("act", lambda E: E.activation(out=lora1[:, seg.x0:seg.x0 + nco], in_=zm[:, 0:nco],
                                                   func=AF.Sigmoid), reads=[zm], writes=[lora1])
            scr.put(zm)
    dump("lora0", lora0[:, :], [(lora0, 0), (lora0, 1)])
    dump("lora1", lora1[:, :], [lora1])

    if stop_after == "lora":
        S.finish()
        p1.close()
        es.close()
        return nc

    gam_pool = Pool([sb("gamC%d" % i, [128, SB], F32, p1) for i in range(5)])
    ubf_pool = Pool([sb("ubf%d" % i, [128, SEGP], BF16, p1) for i in range(2)])
    sbd = sb("sbd", [128, 2, 64], F32, p1)
    sin_tiles = [sb("sin%d" % i, [128, 64], F32, p1) for i in range(3)]
    sin_ctr = {"i": 0}
    S.op("pool", lambda E: E.memset(sbd[:], 0.0), writes=[(sbd, 0), (sbd, 1)])

    print("sbuf remaining in phase 1:", nc.sbuf_bytes_remaining)

    def wsel(g):
        return lambda kc: (wcb[:, g, kc, :], (wcb, g))

    def mm(ps_ap, lhsT, rhs, reads, ps, start=True, stop=True, inc=True):
        S.op("pe", lambda E: E.matmul(ps_ap, lhsT=lhsT, rhs=rhs, start=start, stop=stop),
             reads=reads, writes=[ps], inc=inc)

    def tt(e, out, a, b, op, reads, writes):
        S.op(e, lambda E: E.tensor_tensor(out=out, in0=a, in1=b, op=op), reads, writes)

    def act(out, in_, func, reads, writes, bias=None, scale=None):
        kw = {}
        if bias is not None:
            kw["bias"] = bias
        if scale is not None:
            kw["scale"] = scale
        S.op("act", lambda E: E.activation(out=out, in_=in_, func=func, **kw), reads, writes)

    def bd_write(seg, B, K, col0):
        tv, nco = seg.tv, seg.ncol
        for h in range(2):
            P = slice(h * 64, (h + 1) * 64)

            def dst(n):
                return B[n][P, :, h, 0:tv]

            def cv(t_):
                return seg.chv(t_[P, col0:col0 + nco])

            kk, E1, E2, E3, E4, tb, kf, zr, zv = (K[n] for n in ("kk", "E1", "E2", "E3", "E4", "tb", "kf", "zr", "zv"))
            S.op("dve", lambda E: E.scalar_tensor_tensor(out=dst("aT"), in0=cv(kk), scalar=-1.0, in1=cv(E3),
                                                         op0=ALU.mult, op1=ALU.mult), [kk, E3], [(B["aT"], h)])
            yield
            tt(ew_eng(), dst("bT"), cv(tb), cv(E2), ALU.mult, [tb, E2], [(B["bT"], h)])
            yield
            tt(ew_eng(), dst("bgT"), cv(tb), cv(E4), ALU.mult, [tb, E4], [(B["bgT"], h)])
            yield
            tt(ew_eng(), dst("kT"), cv(kf), cv(E2), ALU.mult, [kf, E2], [(B["kT"], h)])
            yield
            tt(ew_eng(), dst("kgT"), cv(kf), cv(E4), ALU.mult, [kf, E4], [(B["kgT"], h)])
            yield
            tt(ew_eng(), dst("rT"), cv(zr), cv(E1), ALU.mult, [zr, E1], [(B["rT"], h)])
            yield
            S.op("act", lambda E: E.copy(out=dst("vT"), in_=cv(zv)), [zv], [(B["vT"], h)])
            yield

    def stage1c(c, seg, shared):
        nco, kind, tv, nq, x0 = seg.ncol, seg.kind, seg.tv, seg.nq, seg.x0
        first = (kind == "p" and seg.idx == 0)
        B = bd[kind][seg.idx % 2]
        cc = slice(c * 128, (c + 1) * 128)
        ps = getps()
        mm(ps[:, 0:nco], w2b[0:64, cc], lora0[0:64, x0:x0 + nco], [(w2b, 0), (lora0, 0)], ps)
        yield
        sg = scr.get()
        act(sg[:, 0:nco], ps[:, 0:nco], AF.Tanh, [ps, (der, 4)], [sg], bias=der[:, 4, c:c + 1], scale=0.5)
        S.op("pool", lambda E: E.tensor_scalar(out=sg[:, 0:nco], in0=sg[:, 0:nco], scalar1=0.5, scalar2=0.5,
                                               op0=ALU.mult, op1=ALU.add), [sg], [sg])
        yield
        cs = scr.get()
        rmask = rmask_p if kind == "p" else rmask_s
        S.op("dve", lambda E: E.tensor_tensor_scan(out=cs[:, 0:nco], data0=rmask[:, 0:nco], data1=sg[:, 0:nco],
                                                   initial=0.0, op0=ALU.mult, op1=ALU.add),
             reads=[rmask, sg], writes=[cs])
        yield
        E1 = scr.get(); E2 = scr.get(); E3 = scr.get(); E4 = scr.get(); t0 = scr.get()
        act(E1[:, 0:nco], cs[:, 0:nco], AF.Exp, [cs], [E1], scale=-KAPPA)
        yield
        act(E2[:, 0:nco], cs[:, 0:nco], AF.Exp, [cs], [E2], scale=KAPPA)
        yield
        tt("pool", t0[:, 0:nco], cs[:, 0:nco], sg[:, 0:nco], ALU.subtract, [cs, sg], [t0])
        yield
        act(E3[:, 0:nco], t0[:, 0:nco], AF.Exp, [t0], [E3], scale=-KAPPA)
        yield
        csv = seg.chv(cs[:, 0:nco])
        t0v = seg.chv(t0[:, 0:nco])
        tt("dve", t0v, csv[:, :, tv - 1:tv].to_broadcast([128, nq, tv]), csv, ALU.subtract, [cs], [t0])
        yield
        act(seg.chv(E4[:, 0:nco]), t0v, AF.Exp, [t0], [E4], scale=-KAPPA)
        yield
        gam = gam_pool.get()
        act(gam[:, 0:nq].rearrange("p (q o) -> p q o", o=1), csv[:, :, tv - 1:tv], AF.Exp, [cs], [gam], scale=-KAPPA)
        yield
        scr.put(sg, cs, t0)
        ps = getps()
        mm(ps[:, 0:nco], w2b[64:128, cc], lora0[64:128, x0:x0 + nco], [(w2b, 1), (lora0, 1)], ps)
        yield
        a_ = scr.get()
        act(a_[:, 0:nco], ps[:, 0:nco], AF.Tanh, [ps, (der, 5)], [a_], bias=der[:, 5, c:c + 1], scale=0.5)
        S.op("pool", lambda E: E.tensor_scalar(out=a_[:, 0:nco], in0=a_[:, 0:nco], scalar1=0.5, scalar2=0.5,
                                               op0=ALU.mult, op1=ALU.add), [a_], [a_])
        yield
        ps = getps()
        mm(ps[:, 0:nco], g2b[:, cc], lora1[:, x0:x0 + nco], [g2b, lora1], ps)
        yield
        g_ = scr.get()
        S.op("act", lambda E: E.mul(out=g_[:, 0:nco], in_=ps[:, 0:nco], mul=0.5), [ps], [g_])
        yield
        shared.update(E1=E1, E2=E2, E3=E3, E4=E4, a_=a_, g_=g_, gam=gam)

    def stage1a(c, seg, xseg, shared):
        nco, kind, tv, nq, x0 = seg.ncol, seg.kind, seg.tv, seg.nq, seg.x0
        first = (kind == "p" and seg.idx == 0)
        B = bd[kind][seg.idx % 2]
        cc = slice(c * 128, (c + 1) * 128)
        zr = shifted(wsel(0), vcol(V_MUR, c), seg, zbuf[0], first, xseg)
        yield
        zk = shifted(wsel(1), vcol(V_MUK, c), seg, zbuf[1], first, xseg)
        yield
        zv = shifted(wsel(2), vcol(V_MUV, c), seg, zbuf[2], first, xseg)
        yield
        kkr = scr.get(); sq = scr.get(); kk = scr.get()
        S.op("dve", lambda E: E.tensor_scalar(out=kkr[:, 0:nco], in0=zk[:, 0:nco], scalar1=vcol(V_KK, c), scalar2=None,
                                              op0=ALU.mult), [zk, vecs], [kkr])
        yield
        tt("pool", sq[:, 0:nco], kkr[:, 0:nco], kkr[:, 0:nco], ALU.mult, [kkr], [sq])
        yield
        ps = getps()
        mm(ps[:, 0:nco], ones_bd[:], sq[:, 0:nco], [ones_bd, sq], ps)
        yield
        act(sq[:, 0:nco], ps[:, 0:nco], AF.Sqrt, [ps], [sq])
        yield
        S.op("dve", lambda E: E.tensor_scalar(out=sq[:, 0:nco], in0=sq[:, 0:nco], scalar1=1e-12, scalar2=None,
                                              op0=ALU.max), [sq], [sq])
        yield
        S.op("dve", lambda E: E.reciprocal(out=sq[:, 0:nco], in_=sq[:, 0:nco]), [sq], [sq])
        yield
        tt("pool", kk[:, 0:nco], kkr[:, 0:nco], sq[:, 0:nco], ALU.mult, [kkr, sq], [kk])
        yield
        yield "NEEDC"
        E1, E2, E3, E4, a_, g_, gam = (shared[k_] for k_ in ("E1", "E2", "E3", "E4", "a_", "g_", "gam"))
        t1 = scr.get(); kf = scr.get(); bonus = scr.get()
        S.op("dve", lambda E: E.tensor_scalar(out=t1[:, 0:nco], in0=a_[:, 0:nco], scalar1=vcol(V_KA, c),
                                              scalar2=der[:, 0, c:c + 1], op0=ALU.mult, op1=ALU.add),
             [a_, vecs, (der, 0)], [t1])
        yield
        tt("pool", kf[:, 0:nco], zk[:, 0:nco], t1[:, 0:nco], ALU.mult, [zk, t1], [kf])
        yield
        S.op("dve", lambda E: E.scalar_tensor_tensor(out=t1[:, 0:nco], in0=zr[:, 0:nco], scalar=vcol(V_RK, c),
                                                     in1=kf[:, 0:nco], op0=ALU.mult, op1=ALU.mult),
             [zr, kf, vecs, t1], [t1])
        yield
        ps = getps()
        mm(ps[:, 0:nco], ones_bd[:], t1[:, 0:nco], [ones_bd, t1], ps)
        yield
        tt("dve", bonus[:, 0:nco], ps[:, 0:nco], zv[:, 0:nco], ALU.mult, [ps, zv], [bonus])
        yield
        tb = kkr
        tt("pool", tb[:, 0:nco], kk[:, 0:nco], a_[:, 0:nco], ALU.mult, [kk, a_, kkr], [tb])
        yield
        keep = dict(kk=kk, E3=E3, tb=tb, E2=E2, E4=E4, kf=kf, zr=zr, E1=E1, zv=zv)
        if kind == "p":
            yield from bd_write(seg, B, keep, 0)
            scr.put(zr, zk, zv, E1, E2, E3, E4, a_, kkr, sq, kk, t1, kf)
        else:
            scr.put(zk, a_, sq, t1)
        return dict(bonus=bonus, g=g_, gam=gam, keep=keep)

    def stage1b(c, seg, xseg):
        nco, kind, tv, nq, x0 = seg.ncol, seg.kind, seg.tv, seg.nq, seg.x0
        first = (kind == "p" and seg.idx == 0)
        B = bd[kind][seg.idx % 2]
        cc = slice(c * 128, (c + 1) * 128)
        T_, nb = seg.T, seg.nb
        nt = nb * T_
        xbv = xbuf[:, 0:nb * (3 + T_)].rearrange("p (b t) -> p b t", t=3 + T_)

        def dv(t_):
            return t_[:, 0:nt].rearrange("p (b t) -> p b t", t=T_)

        ps = proj_ps(wsel(3), seg, xseg)
        yield
        if kind == "p":
            if seg.idx == 0:
                S.op("pool", lambda E: E.memset(xbv[:, :, 0:3], 0.0), writes=[xbuf])
                yield
            else:
                S.op("pool", lambda E: E.tensor_copy(out=xbv[:, 0, 0:3], in_=carry3[:, :]), [carry3], [xbuf])
                yield
        else:
            b0 = seg.idx * SB
            S.op("pool", lambda E: E.tensor_copy(
                out=xbv[:, :, 0:3], in_=stT[:, c, 0:48].rearrange("p (b j) -> p b j", j=3)[:, b0:b0 + SB, :]),
                [stT], [xbuf])
            yield
        S.op("act", lambda E: E.copy(out=xbv[:, :, 3:3 + T_], in_=seg.tokv(ps[:, 0:nco])), [ps, xbuf], [xbuf])
        yield
        if kind == "p":
            S.op("pool", lambda E: E.tensor_copy(out=carry3[:, :], in_=xbv[:, 0, T_:T_ + 3]), [xbuf], [carry3])
            yield
            if seg.idx == NSEGP - 1:
                S.op("pool", lambda E: E.tensor_copy(out=fin[:, c, 0:3], in_=xbv[:, 0, T_:T_ + 3]), [xbuf], [(fin, c)])
                yield
        else:
            S.op("pool", lambda E: E.tensor_copy(
                out=fin[:, c, 4 + 3 * b0:4 + 3 * (b0 + SB)].rearrange("p (b j) -> p b j", j=3),
                in_=xbv[:, :, T_:T_ + 3]), [xbuf], [(fin, c)])
            yield
        u = scr.get()
        uv = dv(u)
        S.op("dve", lambda E: E.tensor_scalar(out=uv, in0=xbv[:, :, 0:T_], scalar1=vcol(V_CW0, c),
                                              scalar2=vcol(V_CB, c), op0=ALU.mult, op1=ALU.add),
             [xbuf, vecs], [u])
        yield
        for j in range(1, 4):
            S.op("dve", lambda E, j=j: E.scalar_tensor_tensor(out=uv, in0=xbv[:, :, j:j + T_],
                                                              scalar=vcol(V_CW0 + j, c), in1=uv,
                                                              op0=ALU.mult, op1=ALU.add), [xbuf, vecs, u], [u])
            yield
        ubf = ubf_pool.get()
        S.op("act", lambda E: E.copy(out=ubf[:, 0:nt], in_=u[:, 0:nt]), [u], [ubf])
        yield
        rg = scr.get(); ig = scr.get(); al = scr.get(); e2 = scr.get(); hh = scr.get()
        ps = getps()
        mm(ps[:, 0:nt], wabb[:, 0, c, :], ubf[:, 0:nt], [(wabb, 0), ubf], ps)
        yield
        act(rg[:, 0:nt], ps[:, 0:nt], AF.Tanh, [ps, (der, 6)], [rg], bias=der[:, 6, c:c + 1], scale=0.5)
        yield
        ps = getps()
        mm(ps[:, 0:nt], wabb[:, 1, c, :], ubf[:, 0:nt], [(wabb, 1), ubf], ps)
        yield
        act(ig[:, 0:nt], ps[:, 0:nt], AF.Tanh, [ps, (der, 7)], [ig], bias=der[:, 7, c:c + 1], scale=0.5)
        yield
        ubf_pool.put(ubf)
        act(al[:, 0:nt], rg[:, 0:nt], AF.Exp, [rg, (der, 10)], [al], scale=der[:, 10, c:c + 1], bias=der[:, 10, c:c + 1])
        yield
        act(e2[:, 0:nt], rg[:, 0:nt], AF.Exp, [rg, (der, 11)], [e2], scale=der[:, 11, c:c + 1], bias=der[:, 11, c:c + 1])
        yield
        S.op("pool", lambda E: E.tensor_scalar(out=e2[:, 0:nt], in0=e2[:, 0:nt], scalar1=-0.25, scalar2=0.25,
                                               op0=ALU.mult, op1=ALU.add), [e2], [e2])
        yield
        act(e2[:, 0:nt], e2[:, 0:nt], AF.Sqrt, [e2], [e2])
        yield
        if first:
            S.op("pool", lambda E: E.memset(e2[:, 0:1], 0.5), [e2], [e2])
            yield
        S.op("dve", lambda E: E.scalar_tensor_tensor(out=ig[:, 0:nt], in0=ig[:, 0:nt], scalar=1.0, in1=e2[:, 0:nt],
                                                     op0=ALU.add, op1=ALU.mult), [ig, e2], [ig])
        yield
        tt("dve", ig[:, 0:nt], ig[:, 0:nt], u[:, 0:nt], ALU.mult, [ig, u], [ig])
        yield
        if kind == "p":
            init = 0.0 if seg.idx == 0 else hcar[:, 0:1]
            S.op("dve", lambda E: E.tensor_tensor_scan(out=hh[:, 0:nt], data0=al[:, 0:nt], data1=ig[:, 0:nt],
                                                       initial=init, op0=ALU.mult, op1=ALU.add),
                 [al, ig, hcar], [hh])
            yield
            S.op("pool", lambda E: E.tensor_copy(out=hcar[:, 0:1], in_=hh[:, nt - 1:nt]), [hh], [hcar])
            yield
            if seg.idx == NSEGP - 1:
                S.op("pool", lambda E: E.tensor_copy(out=fin[:, c, 3:4], in_=hh[:, nt - 1:nt]), [hh], [(fin, c)])
                yield
        else:
            for b in range(nb):
                S.op("dve", lambda E, b=b: E.tensor_tensor_scan(
                    out=hh[:, b * T_:(b + 1) * T_], data0=al[:, b * T_:(b + 1) * T_], data1=ig[:, b * T_:(b + 1) * T_],
                    initial=stT[:, c, 48 + b0 + b:48 + b0 + b + 1], op0=ALU.mult, op1=ALU.add),
                    [al, ig, stT, hh], [hh])
                yield
            S.op("pool", lambda E: E.tensor_copy(out=fin[:, c, 52 + b0:52 + b0 + SB].rearrange("p (b o) -> p b o", o=1),
                                                 in_=dv(hh)[:, :, T_ - 1:T_]), [hh], [(fin, c)])
            yield
        scr.put(rg, al, e2, u, ig)
        ps = proj_ps(wsel(4), seg, xseg)
        yield
        gbs = scr.get(); p_ = scr.get()
        S.op("act", lambda E: E.copy(out=dv(gbs), in_=seg.tokv(ps[:, 0:nco])), [ps], [gbs])
        yield
        tt("pool", p_[:, 0:nt], gbs[:, 0:nt], gbs[:, 0:nt], ALU.mult, [gbs], [p_])
        yield
        S.op("pool", lambda E: E.tensor_scalar(out=p_[:, 0:nt], in0=p_[:, 0:nt], scalar1=0.044715, scalar2=1.0,
                                               op0=ALU.mult, op1=ALU.add), [p_], [p_])
        yield
        tt("pool", p_[:, 0:nt], p_[:, 0:nt], gbs[:, 0:nt], ALU.mult, [p_, gbs], [p_])
        yield
        act(p_[:, 0:nt], p_[:, 0:nt], AF.Tanh, [p_], [p_], scale=0.7978845608028654)
        yield
        S.op("dve", lambda E: E.scalar_tensor_tensor(out=gbs[:, 0:nt], in0=p_[:, 0:nt], scalar=1.0, in1=gbs[:, 0:nt],
                                                     op0=ALU.add, op1=ALU.mult), [gbs, p_], [gbs])
        yield
        tt("dve", hh[:, 0:nt], hh[:, 0:nt], gbs[:, 0:nt], ALU.mult, [hh, gbs], [hh])
        yield
        scr.put(gbs)
        ps = proj_ps(wsel(6), seg, xseg)
        yield
        act(dv(p_), seg.tokv(ps[:, 0:nco]), AF.Tanh, [ps], [p_], scale=0.5)
        yield
        S.op("dve", lambda E: E.scalar_tensor_tensor(out=hh[:, 0:nt], in0=p_[:, 0:nt], scalar=1.0, in1=hh[:, 0:nt],
                                                     op0=ALU.add, op1=ALU.mult), [hh, p_], [hh])
        yield
        scr.put(p_)
        ps = proj_ps(wsel(5), seg, xseg)
        yield
        gA = scr.get()
        act(dv(gA), seg.tokv(ps[:, 0:nco]), AF.Tanh, [ps], [gA], scale=0.5)
        yield
        return dict(gA=gA, m2=hh)


    def stage1(c, seg):
        xseg = get_x(seg)
        shared = {}
        gens = [stage1a(c, seg, xseg, shared), stage1b(c, seg, xseg), stage1c(c, seg, shared)]
        who = ["s1a", "s1b", "s1c"]
        res = [None, None, None]
        done = [False, False, False]
        hold_a = False
        while not all(done):
            for i_ in range(3):
                if done[i_]:
                    continue
                if i_ == 0 and hold_a:
                    if not done[2]:
                        continue
                    hold_a = False
                cur_alloc["who"] = who[i_]
                try:
                    v_ = next(gens[i_])
                    if v_ == "NEEDC":
                        hold_a = True
                except StopIteration as e_:
                    res[i_] = e_.value
                    done[i_] = True
                yield
        xs_pool.put(xseg)
        res[0].update(res[1])
        return res[0]

    def stage2(c, seg, s1, chain):
        kind, tv, nq, nlev, nco = seg.kind, seg.tv, seg.nq, seg.nlev, seg.ncol
        B = bd[kind][seg.idx % 2]
        gam = s1["gam"]

        def bdv(n):
            return B[n][:].rearrange("p q h t -> p q (h t)")

        def br(n):
            return [(B[n], 0), (B[n], 1)]

        def q128(ps):
            return ps[:].rearrange("p (q t) -> p q t", t=128)

        def q64(ps):
            return ps[:, 0:QB * 64].rearrange("p (q t) -> p q t", t=64)

        def prod(l, r, mask):
            ps = getps()
            for q in range(QB):
                mm(ps[:, q * 128:(q + 1) * 128], bdv(l)[:, q, :], bdv(r)[:, q, :], br(l) + br(r), ps, inc=(q == QB - 1))
            o = wk.get()
            tt("dve", o[:], q128(ps), mask[:], ALU.mult, [ps, mask], [o])
            return o

        N_ = prod("bT", "aT", mUs)
        yield
        L_ = prod("aT", "bT", mLs)
        yield
        Mak = prod("kT", "aT", mUs)
        yield
        Mrb = prod("bT", "rT", mUi)
        yield
        Mrk = prod("kT", "rT", mUi)
        yield
        rTc = wk.get()
        S.op("pool", lambda E: E.tensor_copy(out=rTc[:], in_=bdv("rT")), br("rT"), [rTc])
        yield
        ck("k1")

        def tr(n):
            ps = getps()
            psb = ps[:].bitcast(BF16)
            for q in range(QB):
                S.op("pe", lambda E, q=q: E.transpose(out=psb[:, q * 128:(q + 1) * 128], in_=bdv(n)[:, q, :],
                                                      identity=ident_b[:]), br(n) + [ident_b], [ps], inc=(q == QB - 1))
            o = wk.get()
            copy_op(ev_eng(), o[:], psb[:, 0:QB * 128].rearrange("p (q t) -> p q t", t=128), [ps], [o])
            return o

        XA = tr("aT")
        yield
        Bg = tr("bgT")
        yield
        Kg = tr("kgT")
        yield
        ck("k2")
        ps = getps()
        for q in range(QB):
            mm(ps[:, q * 64:(q + 1) * 64], bdv("vT")[:, q, :], II_b[:], br("vT") + [II_b], ps, inc=(q == QB - 1))
            yield
        V_ = wk.get()
        copy_op(ev_eng(), V_[:, :, 0:64], q64(ps), [ps], [V_])
        yield ("BDFREE", (c, kind, seg.idx))
        yield
        ps = getps()
        for q in range(QB):
            mm(ps[:, q * 64:(q + 1) * 64], Mak[:, q, :], V_[:, q, 0:64], [Mak, V_], ps, inc=(q == QB - 1))
            yield
        XU = wk.get()
        copy_op(ev_eng(), XU[:, :, 0:64], q64(ps), [ps], [XU])
        yield
        wk.put(Mak)
        ck("k3")
        Nc, Lc = N_, L_
        for lev in range(nlev):
            psA = getps()
            for q in range(QB):
                mm(psA[:, q * 128:(q + 1) * 128], Nc[:, q, :], XA[:, q, :], [Nc, XA], psA, start=True, stop=False, inc=False)
                mm(psA[:, q * 128:(q + 1) * 128], ident_b[:], XA[:, q, :], [ident_b, XA], psA, start=False, stop=True,
                   inc=(q == QB - 1))
            yield
            psU = getps()
            for q in range(QB):
                mm(psU[:, q * 64:(q + 1) * 64], Nc[:, q, :], XU[:, q, 0:64], [Nc, XU], psU, start=True, stop=False, inc=False)
                mm(psU[:, q * 64:(q + 1) * 64], ident_b[:], XU[:, q, 0:64], [ident_b, XU], psU, start=False, stop=True,
                   inc=(q == QB - 1))
            yield
            XA2 = wk.get(); XU2 = wk.get()
            copy_op("act", XA2[:], q128(psA), [psA], [XA2])
            yield
            copy_op("act", XU2[:, :, 0:64], q64(psU), [psU], [XU2])
            yield
            N2 = L2 = None
            if lev < nlev - 1:
                psN = getps()
                for q in range(QB):
                    mm(psN[:, q * 128:(q + 1) * 128], Lc[:, q, :], Nc[:, q, :], [Lc, Nc], psN, inc=(q == QB - 1))
                    yield
                N2 = wk.get()
                copy_op("act", N2[:], q128(psN), [psN], [N2])
                yield
                if lev < nlev - 2:
                    psL = getps()
                    for q in range(QB):
                        mm(psL[:, q * 128:(q + 1) * 128], Nc[:, q, :], Lc[:, q, :], [Lc, Nc], psL, inc=(q == QB - 1))
                        yield
                    L2 = wk.get()
                    copy_op("act", L2[:], q128(psL), [psL], [L2])
                    yield
            wk.put(XA, XU, Nc, Lc)
            XA, XU, Nc, Lc = XA2, XU2, N2, L2
            if Nc is None:
                Nc = wk.get()
            if Lc is None:
                Lc = wk.get()
        wk.put(Nc, Lc)
        ck("k4")
        psR = getps()
        for q in range(QB):
            mm(psR[:, q * 128:(q + 1) * 128], XA[:, q, :], Mrb[:, q, :], [XA, Mrb], psR, start=True, stop=False, inc=False)
            yield
            mm(psR[:, q * 128:(q + 1) * 128], ident_b[:], rTc[:, q, :], [ident_b, rTc], psR, start=False,
               stop=True, inc=(q == QB - 1))
            yield
        Rh = wk.get()
        copy_op(ev_eng(), Rh[:], q128(psR), [psR], [Rh])
        yield
        psG = getps()
        for q in range(QB):
            mm(psG[:, q * 128:(q + 1) * 128], XA[:, q, :], Bg[:, q, :], [XA, Bg], psG, inc=(q == QB - 1))
            yield
        GT = wk.get()
        copy_op(ev_eng(), GT[:], q128(psG), [psG], [GT])
        yield
        ck("k5")
        if kind == "p":
            yield ("CHAIN", (c, seg.idx - 1))
            chain["init"]()
        psH = getps()
        if kind == "p":
            hA, hB = chain["hA"], chain["hB"]
            for q in range(QB):
                sl = slice(q * 64, (q + 1) * 64)
                mm(psH[:, sl], Bg[:, q, :], XU[:, q, 0:64], [Bg, XU], psH, start=True, stop=False, inc=False)
                yield
                mm(psH[:, sl], Kg[:, q, :], V_[:, q, 0:64], [Kg, V_], psH, start=False, stop=False, inc=False)
                yield
                mm(psH[:, sl], GT[:, q, :], hB[:, q, :], [GT, (hB, q)], psH, start=False, stop=True)
                yield
                S.op("dve", lambda E, q=q, sl=sl: E.scalar_tensor_tensor(
                    out=hA[:, q + 1, :], in0=hA[:, q, :], scalar=gam[:, q:q + 1], in1=psH[:, sl],
                    op0=ALU.mult, op1=ALU.add), [(hA, q), gam, psH], [(hA, q + 1)])
                yield
                S.op("act", lambda E, q=q: E.copy(out=hB[:, q + 1, :], in_=hA[:, q + 1, :]), [(hA, q + 1)], [(hB, q + 1)])
                yield
            hBr = [(hB, q) for q in range(QB)]
        else:
            hA, hB, hO = chain["hA"], chain["hB"], chain["hO"]
            for q in range(QB):
                sl = slice(q * 64, (q + 1) * 64)
                mm(psH[:, sl], Bg[:, q, :], XU[:, q, 0:64], [Bg, XU], psH, start=True, stop=False, inc=False)
                yield
                mm(psH[:, sl], Kg[:, q, :], V_[:, q, 0:64], [Kg, V_], psH, start=False, stop=False, inc=False)
                yield
                mm(psH[:, sl], GT[:, q, :], hB[:, q, :], [GT, (hB, q)], psH, start=False, stop=True,
                   inc=(q == QB - 1))
                yield
            tmp = scr.get()
            tmpv = tmp[:, 0:QB * 64].rearrange("p (q t) -> p q t", t=64)
            tt("dve", tmpv, hA[:, 0:QB, :], gam[:].rearrange("p (q o) -> p q o", o=1).to_broadcast([128, QB, 64]),
               ALU.mult, [(hA, q) for q in range(QB)] + [gam], [tmp])
            yield
            tt("dve", hO[:, 0:QB, :], tmpv, q64(psH), ALU.add, [tmp, psH], [hO])
            yield
            scr.put(tmp)
            hBr = [(hB, q) for q in range(QB)]
        ck("k6")
        psY = getps()
        for q in range(QB):
            sl = slice(q * 64, (q + 1) * 64)
            mm(psY[:, sl], Mrb[:, q, :], XU[:, q, 0:64], [Mrb, XU], psY, start=True, stop=False, inc=False)
            yield
            mm(psY[:, sl], Mrk[:, q, :], V_[:, q, 0:64], [Mrk, V_], psY, start=False, stop=False, inc=False)
            yield
            mm(psY[:, sl], Rh[:, q, :], hB[:, q, :], [Rh, (hB, q)], psY, start=False, stop=True, inc=(q == QB - 1))
            yield
        wk.put(Mrb, Mrk, XA, XU, Bg, Kg, V_, Rh, GT, rTc)
        if not isinstance(gam, Off):
            gam_pool.put(gam)
        ck("k7")
        ysb = scr.get()
        ysbv = ysb[:, 0:QB * 64].rearrange("p (q t) -> p q t", t=64)
        S.op("act", lambda E: E.copy(out=ysbv, in_=q64(psY)), [psY], [ysb])
        yield ("TAIL", (c, seg.idx) if kind == "p" else None)
        stq = lambda i: stat[:, i, :]
        ysq = scr.get()
        ysqv = ysq[:, 0:QB * 64].rearrange("p (q t) -> p q t", t=64)
        for q in range(QB):
            S.op("dve", lambda E, q=q: E.bn_stats(out=bst4[:, q, :], in_=ysb[:, q * 64:(q + 1) * 64]), [ysb], [(bst4, q)])
        for q in range(QB):
            S.op("dve", lambda E, q=q: E.bn_aggr(out=mvq[:, q, :], in_=bst4[:, q, :]), [(bst4, q)], [(mvq, q)])
        yield
        mvr = [(mvq, q) for q in range(QB)]
        S.op("pool", lambda E: E.tensor_scalar(out=stq(4), in0=mvq[:, :, 1], scalar1=1.0, scalar2=GN_EPS,
                                               op0=ALU.mult, op1=ALU.add), mvr, [(stat, 4)])
        S.op("pool", lambda E: E.tensor_tensor(out=stq(5), in0=stq(4), in1=negh[:], op=ALU.pow),
             [(stat, 4), negh], [(stat, 5)])
        yield
        tt("dve", ysqv, ysbv, mvq[:, :, 0:1].to_broadcast([128, QB, 64]), ALU.subtract, [ysb] + mvr, [ysq])
        scr.put(ysb)
        yield
        for h in range(2):
            P = slice(h * 64, (h + 1) * 64)
            tt(ew_eng(), ybd[P, :, h, :], ysqv[P], stat[P, 5, :].rearrange("p (q o) -> p q o", o=1).to_broadcast([64, QB, 64]),
               ALU.mult, [ysq, (stat, 5)], [(ybd, h)])
            yield
        scr.put(ysq)
        ck("k8")
        psT = getps()
        ybv = ybd[:].rearrange("p q h t -> p q (h t)")
        for q in range(QB):
            mm(psT[:, q * 64:(q + 1) * 64], ybv[:, q, :], II_f[:], [(ybd, 0), (ybd, 1), II_f], psT, inc=(q == QB - 1))
            yield
        ck("k9")
        T_, nb = seg.T, seg.nb
        nt = nb * T_

        def dv(t_):
            return t_[:, 0:nt].rearrange("p (b t) -> p b t", t=T_)

        if kind == "p":
            ysrc = psT[:, 0:QB * 64].rearrange("p (b t) -> p b t", b=1)
        else:
            ysrc = q64(psT)[:, :, 0:DEC]
        yo = scr.get()
        act(dv(yo), ysrc, AF.Identity, [psT, vecs], [yo], bias=vcol(V_LNXB, c), scale=vcol(V_LNXG, c))
        yield
        tt("dve", dv(yo), dv(yo), seg.tokv(s1["bonus"][:, 0:nco]), ALU.add, [yo, s1["bonus"]], [yo])
        yield
        tt("pool", dv(yo), dv(yo), seg.tokv(s1["g"][:, 0:nco]), ALU.mult, [yo, s1["g"]], [yo])
        yield
        S.op("dve", lambda E: E.scalar_tensor_tensor(out=dv(yo), in0=dv(s1["gA"]), scalar=1.0, in1=dv(yo),
                                                     op0=ALU.add, op1=ALU.mult), [yo, s1["gA"]], [yo])
        yield
        mv = merged[:, c, seg.m0:seg.m0 + seg.mcols].rearrange("p (b t) -> p b t", t=T_)
        S.op("dve", lambda E: E.scalar_tensor_tensor(out=mv, in0=dv(s1["m2"]), scalar=0.25, in1=dv(yo),
                                                     op0=ALU.mult, op1=ALU.add), [yo, s1["m2"]], [(merged, c)])
        yield
        scr.put(yo)
        if not isinstance(s1["bonus"], Off):
            scr.put(s1["bonus"], s1["g"], s1["gA"], s1["m2"])

    def state_out(c, hsrc_ap, hres, dst_ap):
        for h in range(2):
            P = slice(h * 64, (h + 1) * 64)
            S.op(ew_eng(), lambda E: E.tensor_copy(out=hbd[P, h, :], in_=hsrc_ap[P, :]), [hres], [(hbd, h)])
        ps = getps()
        mm(ps[:, 0:64], hbd[:].rearrange("p h t -> p (h t)"), II_f[:], [(hbd, 0), (hbd, 1), II_f], ps)
        so = scr.get()
        copy_op(ev_eng(), so[:, 0:64], ps[:, 0:64], [ps], [so])
        S.dma(dst_ap, so[:, 0:64], reads=[so], q="act")
        scr.put(so)

    for g in range(3):
        S.op("pool", lambda E, g=g: E.memset(zbuf[g][:], 0.0), writes=[zbuf[g]])

    def s2full(c, seg, s1, par):
        if seg.kind == "p":
            hA, hB = H32[par], Hbf[par]

            def chain_init():
                if seg.idx == 0:
                    S.op("pool", lambda E: E.memset(hA[:, 0, :], 0.0), writes=[(hA, 0)])
                    S.op("pool", lambda E: E.memset(hB[:, 0, :], 0.0), writes=[(hB, 0)])
                else:
                    pA, pB = H32[1 - par], Hbf[1 - par]
                    S.op("pool", lambda E: E.tensor_copy(out=hA[:, 0, :], in_=pA[:, QB, :]), [(pA, QB)], [(hA, 0)])
                    S.op("pool", lambda E: E.tensor_copy(out=hB[:, 0, :], in_=pB[:, QB, :]), [(pB, QB)], [(hB, 0)])

            yield from stage2(c, seg, s1, dict(hA=hA, hB=hB, init=chain_init))
            if seg.idx == NSEGP - 1:
                state_out(c, hA[:, QB, :], (hA, QB),
                          o_wkv_p[2 * c:2 * c + 2, :, :].rearrange("h i j -> (h i) j"))
                yield
        else:
            sp_ = seg.idx % 2
            hA, hB, hO = H32s[sp_][0], Hbfs[sp_], H32s[sp_][1]
            b0 = seg.idx * QB
            for q in range(QB):
                sin = sin_tiles[sin_ctr["i"] % 3]
                sin_ctr["i"] += 1
                S.dma(sin[:, 0:64], s0[b0 + q, 2 * c:2 * c + 2, :, :].rearrange("h i j -> (h i) j"), writes=[sin])
                for h in range(2):
                    P = slice(h * 64, (h + 1) * 64)
                    S.op(ew_eng(), lambda E: E.tensor_copy(out=sbd[P, h, :], in_=sin[P, 0:64]), [sin], [(sbd, h)])
                ps = getps()
                mm(ps[:, 0:64], sbd[:].rearrange("p h t -> p (h t)"), II_f[:], [(sbd, 0), (sbd, 1), II_f], ps)
                S.op("act", lambda E: E.copy(out=hA[:, q, :], in_=ps[:, 0:64]), [ps], [(hA, q)])
                S.op("dve", lambda E: E.tensor_copy(out=hB[:, q, :], in_=ps[:, 0:64]), [ps], [(hB, q)])
                yield
            yield from stage2(c, seg, s1, dict(hA=hA, hB=hB, hO=hO))
            for q in range(QB):
                state_out(c, hO[:, q, :], hO,
                          o_wkv_s[b0 + q, 2 * c:2 * c + 2, :, :].rearrange("h i j -> (h i) j"))
                yield

    NSTREAM = 2
    mains = []
    tails = []
    SID = ("A", "B")

    chain_done = set()
    bd_free = set()

    def step_main(m):
        if m[2] == "TAILWAIT":
            if tails:
                return True
            mains.remove(m)
            tails.append(m)
            return False
        if m[2] is not None:
            if m[2][1] >= 0 and m[2] not in chain_done:
                return True
            m[2] = None
        cur_alloc["who"] = "s2" + SID[m[1]]
        try:
            v = next(m[0])
        except StopIteration:
            mains.remove(m)
            return False
        if isinstance(v, tuple) and v[0] == "BDFREE":
            bd_free.add(v[1])
            return True
        if isinstance(v, tuple) and v[0] == "CHAIN":
            m[2] = v[1]
            return True
        if isinstance(v, tuple) and v[0] == "TAIL":
            if v[1] is not None:
                chain_done.add(v[1])
            if tails:
                m[2] = "TAILWAIT"
                return True
            mains.remove(m)
            tails.append(m)
            return False
        return True

    def step_tails():
        for m in list(tails):
            cur_alloc["who"] = "tail" + SID[m[1]]
            try:
                next(m[0])
            except StopIteration:
                tails.remove(m)

    def step_bg():
        for m in list(mains):
            step_main(m)
        step_tails()
        step_wgen()

    def run_stage1(g1):
        acc = 0.0
        while True:
            step_bg()
            acc += RSCALE if (mains or tails) else 8.0
            while acc >= 1.0:
                acc -= 1.0
                cur_alloc["who"] = "s1"
                try:
                    next(g1)
                except StopIteration as e_:
                    cur_alloc["who"] = None
                    return e_.value

    def start_main(gen):
        while len(mains) >= NSTREAM:
            step_bg()
        used = {m[1] for m in mains} | {m[1] for m in tails}
        while len(used) >= 2:
            step_bg()
            used = {m[1] for m in mains} | {m[1] for m in tails}
        sid = 0 if 0 not in used else 1
        mains.append([gen, sid, None])

    def drain_all():
        while mains or tails:
            step_bg()
        cur_alloc["who"] = None

    def load_weights(c):
        for g in range(7):
            stg = stage[g % 3]
            S.dma(stg[:], w1[c, :, g, :, :], writes=[stg])
            S.op("dve", lambda E, g=g, stg=stg: E.tensor_copy(out=wcb[:, g, :, :], in_=stg[:]), [stg], [(wcb, g)])

    def load_weights_gen(c, gap=7):
        for g in range(7):
            stg = stage[g % 3]
            S.dma(stg[:], w1[c, :, g, :, :], writes=[stg])
            for _ in range(gap):
                yield
            S.op("dve", lambda E, g=g, stg=stg: E.tensor_copy(out=wcb[:, g, :, :], in_=stg[:]), [stg], [(wcb, g)])
            yield

    wgen = {"g": None}

    def step_wgen():
        if wgen["g"] is not None:
            try:
                next(wgen["g"])
            except StopIteration:
                wgen["g"] = None

    def finish_wgen():
        while wgen["g"] is not None:
            step_wgen()

    def finalize_after(gen, fn):
        yield from gen
        fn()

    try:
        load_weights(0)
        for c in range(NCH):
            par = 0
            for segi, seg in enumerate(segs):
                if stop_after == "seg%d" % segi:
                    raise StopBuild()
                cur["segi"] = segi
                if segi == 0:
                    finish_wgen()
                s1 = run_stage1(stage1(c, seg))
                if seg.kind == "p":
                    start_main(s2full(c, seg, s1, par))
                    par = 1 - par
                else:
                    if c + 1 < NCH and stop_after != "c0":
                        wgen["g"] = load_weights_gen(c + 1)
                    nbatch = NSB // QB
                    for b in range(nbatch):
                        sub = SubSeg(b)
                        while b >= 2 and (c, "s", b - 2) not in bd_free:
                            step_bg()
                        cur_alloc["who"] = "s1"
                        for _ in bd_write(sub, bd["s"][b % 2], s1["keep"], b * QB * 5):
                            pass
                        s1b = dict(bonus=Off(s1["bonus"], b * QB * 5, QB * 5), g=Off(s1["g"], b * QB * 5, QB * 5),
                                   gA=Off(s1["gA"], b * QB * DEC, QB * DEC), m2=Off(s1["m2"], b * QB * DEC, QB * DEC),
                                   gam=Off(s1["gam"], b * QB, QB))
                        gen = s2full(c, sub, s1b, 0)
                        if b == nbatch - 1:
                            K_ = s1["keep"]
                            scr.put(K_["kk"], K_["E3"], K_["tb"], K_["E2"], K_["E4"], K_["kf"], K_["zr"], K_["E1"], K_["zv"])

                            def fin_(s1=s1):
                                scr.put(s1["bonus"], s1["g"], s1["gA"], s1["m2"])
                                gam_pool.put(s1["gam"])

                            gen = finalize_after(gen, fin_)
                        start_main(gen)
            if stop_after == "c0":
                break
        drain_all()
    except StopBuild:
        pass
    dump("merged", merged[:, 0, :], [(merged, 0)])
    dump("merged7", merged[:, 7, :], [(merged, 7)])
    dump("fin", fin[:, 0, :], [(fin, 0)])

    print("nops", S.nops, "cnt", {k: v for k, v in S.cnt.items() if not k.startswith("d")})
    if stop_after is not None:
        S.finish()
        p1.close()
        es.close()
        return nc
    psF = [getps(), getps()]
    for c in range(NCH):
        hf, k4 = divmod(c, 4)
        S.op("pe", lambda E: E.transpose(out=psF[hf][0:NFIN, k4 * 128:(k4 + 1) * 128], in_=fin[:, c, :],
                                         identity=ident_f[:]), [(fin, c), ident_f], [psF[hf]])
    S.barrier()
    p1.close()
    p2 = ExitStack()
    lnvs = sb("lnvs", [128, 2, D], F32, p2)
    h1T = sb("h1T", [128, 8, MCOLS], BF16, p2)
    stg2 = [sb("stg2_%d" % i, [128, D], F32, p2) for i in range(3)]
    big = [sb("big%d" % i, [128, D], F32, p2) for i in range(4)]
    bst = sb("bst", [128, 2, 6], F32, p2)
    mv_ = sb("mv_", [128, 8], F32, p2)
    aI = sb("aI", [128, 2, 128], BF16, p2)
    finT = sb("finT", [NFIN, D], F32, p2)
    p2a = ExitStack()
    wo_b = sb("wo_b", [128, 8, D], BF16, p2a)
    a_hi = float(np.float32(ALPHA).astype(ml_bf16).astype(np.float32))
    a_lo = float(np.float32(ALPHA - a_hi).astype(ml_bf16).astype(np.float32))
    S.op("pool", lambda E: E.tensor_scalar(out=aI[:, 0, :], in0=ident_f[:], scalar1=a_hi, scalar2=0.0,
                                           op0=ALU.mult, op1=ALU.add), [ident_f], [(aI, 0)])
    S.op("pool", lambda E: E.tensor_scalar(out=aI[:, 1, :], in0=ident_f[:], scalar1=a_lo, scalar2=0.0,
                                           op0=ALU.mult, op1=ALU.add), [ident_f], [(aI, 1)])
    for hf in range(2):
        copy_op(ev_eng(), finT[:, hf * 512:(hf + 1) * 512], psF[hf][0:NFIN, :], [psF[hf]], [finT])
    S.dma(o_fin[:, :], finT[:, :], reads=[finT])
    S.dma(o_shift[0:1, :], xp[SEQ - 1:SEQ, :])
    S.dma(o_shift[1:1 + NSB, :], xst.rearrange("(b t) d -> b t d", t=DEC)[:, DEC - 1, :])
    S.dma(lnvs[:], lnv[:, 0:2, :], writes=[lnvs])
    for kc in range(8):
        stg = stg2[kc % 3]
        S.dma(stg[:], wo[:, kc, :], writes=[stg])
        S.op("dve", lambda E, kc=kc, stg=stg: E.tensor_copy(out=wo_b[:, kc, :], in_=stg[:]), [stg], [(wo_b, kc)])

    def layer_norm(src, dst, gi, rows):
        R = slice(0, rows)
        for j in range(2):
            S.op("dve", lambda E, j=j: E.bn_stats(out=bst[R, j, :], in_=src[R, j * 512:(j + 1) * 512]), [src], [bst])
        S.op("dve", lambda E: E.bn_aggr(out=mv_[R, 0:2], in_=bst[R, :, :]), [bst], [(mv_, 0)])
        S.op("dve", lambda E: E.tensor_scalar(out=mv_[R, 2:3], in0=mv_[R, 1:2], scalar1=LN_EPS, scalar2=None,
                                              op0=ALU.add), [(mv_, 0)], [(mv_, 2)])
        act(mv_[R, 3:4], mv_[R, 2:3], AF.Sqrt, [(mv_, 2)], [(mv_, 3)])
        S.op("dve", lambda E: E.reciprocal(out=mv_[R, 4:5], in_=mv_[R, 3:4]), [(mv_, 3)], [(mv_, 4)])
        S.op("dve", lambda E: E.scalar_tensor_tensor(out=mv_[R, 5:6], in0=mv_[R, 0:1], scalar=-1.0, in1=mv_[R, 4:5],
                                                     op0=ALU.mult, op1=ALU.mult), [(mv_, 0), (mv_, 4)], [(mv_, 5)])
        act(dst[R, :], src[R, :], AF.Identity, [src, (mv_, 4), (mv_, 5)], [dst], bias=mv_[R, 5:6], scale=mv_[R, 4:5])
        tt("pool", dst[R, :], dst[R, :], lnvs[R, 0, :], ALU.mult, [dst, lnvs], [dst])
        tt("dve", dst[R, :], dst[R, :], lnvs[R, 1, :], ALU.add, [dst, lnvs], [dst])

    ttiles = [(t * 128, 128, yp[t * 128:(t + 1) * 128, :], xp[t * 128:(t + 1) * 128, :]) for t in range(SEQ // 128)]
    ttiles.append((SEQ, NSB * DEC, ys[:, :], xst[:, :]))

    for ti, (m0, rows, _, xrows) in enumerate(ttiles):
        R = slice(0, rows)
        xtm = big[ti % 2]
        s1t = big[2 + ti % 2]
        S.dma(xtm[R, :], xrows, writes=[xtm])
        pss = [getps(), getps()]
        for hf in range(2):
            for kc in range(8):
                mm(pss[hf][R, :], merged[:, kc, m0:m0 + rows], wo_b[:, kc, hf * 512:(hf + 1) * 512],
                   [(merged, kc), (wo_b, kc)], pss[hf], start=(kc == 0), stop=(kc == 7), inc=(kc == 7))
        for hf in range(2):
            S.op("dve", lambda E, hf=hf: E.scalar_tensor_tensor(
                out=s1t[R, hf * 512:(hf + 1) * 512], in0=xtm[R, hf * 512:(hf + 1) * 512], scalar=ALPHA,
                in1=pss[hf][R, :], op0=ALU.mult, op1=ALU.add), [xtm, pss[hf]], [s1t])
        layer_norm(s1t, xtm, 0, rows)
        pst = [getps(), getps()]
        for kc in range(8):
            hf, k4 = divmod(kc, 4)
            S.op("pe", lambda E, hf=hf, k4=k4, kc=kc: E.transpose(
                out=pst[hf][:, k4 * 128:k4 * 128 + rows], in_=xtm[R, kc * 128:(kc + 1) * 128],
                identity=ident_f[R, R]), [xtm, ident_f], [pst[hf]], inc=(k4 == 3))
        for hf in range(2):
            copy_op(ev_eng(), h1T[:, hf * 4:(hf + 1) * 4, m0:m0 + rows],
                    pst[hf][:].rearrange("p (k t) -> p k t", t=128)[:, :, 0:rows], [pst[hf]],
                    [(h1T, hf * 4 + k) for k in range(4)])
    if "h1T" in dbg_out:
        S.dma(dbg_out["h1T"], h1T[:, 0, :], reads=[(h1T, 0)])

    S.barrier()
    p2a.close()
    S.dma(lnvs[:], lnv[:, 2:4, :], writes=[lnvs])
    wd_b = sb("wd_b", [128, NFC, D], BF16, p2)
    NTH = 1088
    NFA = 15
    uTa = T(merged[:].rearrange("p a b -> p (a b)")[:, 0:NFA * NTH].rearrange("p (f t) -> p f t", t=NTH), "uTa")
    uTb = sb("uTb", [128, NFC - NFA, NTH], BF16, p2)

    class _UT:
        def __getitem__(self, k):
            p_, fc_, cols_ = k
            if fc_ < NFA:
                return uTa.h[p_, fc_, cols_]
            return uTb.h[p_, fc_ - NFA, cols_]

    uT = _UT()
    wgub = [sb("wgub%d" % i, [128, 2, 8, 128], BF16, p2) for i in range(2)]
    sgt = [sb("sgt%d" % i, [128, 512], F32, p2) for i in range(2)]
    for fc in range(NFC):
        stg = stg2[fc % 3]
        S.dma(stg[:], wd[fc, :, :], writes=[stg])
        S.op("dve", lambda E, fc=fc, stg=stg: E.tensor_copy(out=wd_b[:, fc, :], in_=stg[:]), [stg], [(wd_b, fc)])
    halves = [(0, 1024, ttiles[0:8]), (1024, 1088, ttiles[8:17])]
    for (c0, ncols, tls) in halves:
        blocks = [(b0_, min(512, ncols - b0_)) for b0_ in range(0, ncols, 512)]
        def load_fc(fc):
            wb_ = wgub[fc % 2]
            for j in range(2):
                stg = stg2[(2 * fc + j) % 3]
                S.dma(stg[:].rearrange("p (a b) -> p a b", b=128), wgu[fc, :, j, :, :], writes=[stg])
                S.op("dve", lambda E, j=j, stg=stg: E.tensor_copy(
                    out=wb_[:, j, :, :], in_=stg[:].rearrange("p (a b) -> p a b", b=128)), [stg], [(wb_, j)])

        load_fc(0)
        for fc in range(NFC):
            wb = wgub[fc % 2]
            if fc + 1 < NFC:
                load_fc(fc + 1)
            for bi, (b0_, bn) in enumerate(blocks):
                psg = getps()
                psu = getps()
                for j, ps_ in ((0, psg), (1, psu)):
                    for kc in range(8):
                        mm(ps_[:, 0:bn], wb[:, j, kc, :], h1T[:, kc, c0 + b0_:c0 + b0_ + bn], [(wb, j), (h1T, kc)], ps_,
                           start=(kc == 0), stop=(kc == 7), inc=(kc == 7))
                sg_ = sgt[(fc * 3 + bi) % 2]
                act(sg_[:, 0:bn], psg[:, 0:bn], AF.Silu, [psg], [sg_])
                tt("dve", uT[:, fc, b0_:b0_ + bn], psu[:, 0:bn], sg_[:, 0:bn], ALU.mult, [psu, sg_], [(uT, fc)])
        for ti, (m0, rows, yout, _) in enumerate(tls):
            R = slice(0, rows)
            l0 = m0 - c0
            pss = [getps(), getps()]
            for hf in range(2):
                for fc in range(NFC):
                    mm(pss[hf][R, :], uT[:, fc, l0:l0 + rows], wd_b[:, fc, hf * 512:(hf + 1) * 512],
                       [(uT, fc), (wd_b, fc)], pss[hf], start=(fc == 0), stop=False, inc=False)
                for k4 in range(4):
                    kc = hf * 4 + k4
                    for a in range(2):
                        last = (k4 == 3 and a == 1)
                        mm(pss[hf][R, k4 * 128:(k4 + 1) * 128], h1T[:, kc, m0:m0 + rows], aI[:, a, :],
                           [(h1T, kc), (aI, a)], pss[hf], start=False, stop=last, inc=last)
            s2t = big[ti % 2]
            yt = big[2 + ti % 2]
            for hf in range(2):
                copy_op(ev_eng(), s2t[R, hf * 512:(hf + 1) * 512], pss[hf][R, :], [pss[hf]], [s2t])
            layer_norm(s2t, yt, 2, rows)
            S.dma(yout, yt[R, :], reads=[yt])

    print("nops", S.nops)
    S.finish()
    p2.close()
    es.close()
    return nc


def kernel(**inputs):
    maps = prep_inputs(inputs)
    nc = build_program()
    res = run_bass_kernel_spmd(nc, maps, core_ids=list(range(NCORES)))
    R = res.results
    f32 = np.float32
    y_p = np.stack([R[i]["yp"] for i in range(NCORES)]).astype(f32)
    y_s = np.concatenate([R[i]["ys"].reshape(NSB, DEC, D) for i in range(NCORES)]).astype(f32)
    fin = [R[i]["o_fin"] for i in range(NCORES)]
    shf = [R[i]["o_shift"] for i in range(NCORES)]
    new_shift_p = np.stack([s[0] for s in shf])[None].astype(f32)
    new_shift_s = np.concatenate([s[1:] for s in shf])[None].astype(f32)
    new_wkv_p = np.stack([R[i]["o_wkv_p"] for i in range(NCORES)])[None].astype(f32)
    new_wkv_s = np.concatenate([R[i]["o_wkv_s"] for i in range(NCORES)])[None].astype(f32)
    new_conv_p = np.stack([f[0:3] for f in fin])[None].astype(f32)
    new_lru_p = np.stack([f[3] for f in fin])[None].astype(f32)
    new_conv_s = np.concatenate([f[4:52].reshape(NSB, 3, D) for f in fin])[None].astype(f32)
    new_lru_s = np.concatenate([f[52:68] for f in fin])[None].astype(f32)
    return (y_p, y_s, new_shift_p, new_wkv_p, new_conv_p, new_lru_p,
            new_shift_s, new_wkv_s, new_conv_s, new_lru_s)


def prep_inputs(inp):
    f = lambda a: np.ascontiguousarray(np.asarray(a, dtype=np.float32))
    w_in = f(inp["w_in"])[0]
    bases = [0, 1024, 2048, 3328, 4352, 5376, 6400]
    w1 = np.stack([w_in[:, b:b + 1024].reshape(8, 128, 8, 128).transpose(2, 1, 0, 3) for b in bases], axis=2)
    wl = w_in[:, 3072:3328].reshape(8, 128, 2, 128).transpose(1, 2, 0, 3)
    wo = f(inp["w_o"])[0].reshape(8, 128, D).transpose(1, 0, 2)
    wg = f(inp["w_ffn_gate"])[0].reshape(8, 128, NFC, 128).transpose(2, 1, 0, 3)
    wu = f(inp["w_ffn_up"])[0].reshape(8, 128, NFC, 128).transpose(2, 1, 0, 3)
    wgu = np.stack([wg, wu], axis=2)
    wd = f(inp["w_ffn_down"])[0].reshape(NFC, 128, D)
    vec = np.zeros((NV, D), np.float32)
    mu = f(inp["tmix_mu"])[0]
    vec[V_MUR] = mu[0:1024]; vec[V_MUK] = mu[1024:2048]; vec[V_MUV] = mu[2048:3072]
    vec[V_MUL, 0:256] = mu[3072:3328]
    vec[V_W0] = f(inp["w0"])[0]; vec[V_A0] = f(inp["a0"])[0]
    vec[V_KK] = f(inp["k_k"])[0]; vec[V_KA] = f(inp["k_a"])[0]
    vec[V_RK] = f(inp["r_k"])[0].reshape(-1)
    vec[V_LNXG] = f(inp["lnx_g"])[0]; vec[V_LNXB] = f(inp["lnx_b"])[0]
    cw = f(inp["conv_w"])[0]
    for j in range(4):
        vec[V_CW0 + j] = cw[j]
    vec[V_CB] = f(inp["conv_b"])[0]
    vec[V_BA] = f(inp["lru_ba"])[0].reshape(-1); vec[V_BI] = f(inp["lru_bi"])[0].reshape(-1)
    vec[V_LAM] = f(inp["lru_lambda"])[0]
    vecp = np.ascontiguousarray(vec.reshape(NV, 8, 128).transpose(2, 0, 1))
    wab = np.zeros((128, 2, 8, 128), np.float32)
    for j, nm in enumerate(("lru_wa", "lru_wi")):
        w = f(inp[nm])[0]
        for c in range(8):
            for bl in range(2):
                wab[bl * 64:(bl + 1) * 64, j, c, bl * 64:(bl + 1) * 64] = w[2 * c + bl]
    lnv = np.stack([np.broadcast_to(f(inp[n])[0], (128, D)) for n in ("ln1_g", "ln1_b", "ln2_g", "ln2_b")], axis=1)
    shared = dict(w1=w1, wl=wl, wo=wo, wgu=wgu, wd=wd, vec=vecp, w2d=f(inp["w2_decay"])[0],
                  a2d=f(inp["a2_iclr"])[0], g2d=f(inp["g2_gate"])[0], wab=wab, lnv=lnv)
    shared = {k: np.ascontiguousarray(v, dtype=np.float32) for k, v in shared.items()}
    x_prompt = f(inp["x_prompt"]); x_sample = f(inp["x_sample"])
    sh = f(inp["state_shift"])[0]; swkv = f(inp["state_wkv"])[0]
    sconv = f(inp["state_conv"])[0]; slru = f(inp["state_lru"])[0]
    maps = []
    for i in range(NCORES):
        b0 = i * NSB
        xs = np.concatenate([sh[b0:b0 + NSB, None, :], x_sample[b0:b0 + NSB]], axis=1).reshape(NSB * 5, D)
        stt = np.concatenate([sconv[b0:b0 + NSB].reshape(NSB * 3, D), slru[b0:b0 + NSB]], axis=0)
        m = dict(xp=x_prompt[i], xs=xs, xst=x_sample[b0:b0 + NSB].reshape(NSB * DEC, D), st=stt,
                 s0=swkv[b0:b0 + NSB])
        m = {k: np.ascontiguousarray(v, dtype=np.float32) for k, v in m.items()}
        m.update(shared)
        maps.append(m)
    return maps
```

```python
import numpy as np
import ml_dtypes
from contextlib import ExitStack
import concourse.bass as bass
import concourse.mybir as mybir
from concourse.bass_utils import run_bass_kernel_spmd

F32 = mybir.dt.float32
BF16 = mybir.dt.bfloat16
AF = mybir.ActivationFunctionType
ALU = mybir.AluOpType
AX = mybir.AxisListType
ml_bf16 = ml_dtypes.bfloat16

D = 1024
NCORES = 8
SEQ = 2048
NSB = 16
DEC = 4
NCH = 8
D_FF = 2816
NFC = D_FF // 128
KAPPA = float(np.exp(-0.5))
ALPHA = 2.0 ** 0.25
LN_EPS = 1e-5
GN_EPS = 64e-5
CH = 64
PIPE = True
RSCALE = 1.0
TAILPIPE = True
TAILSEL = lambda kind, idx: True
QB = 4
SEGP = QB * CH
NSEGP = SEQ // SEGP
SB = 16
NSEGS = NSB // SB
XS0 = SEQ
XCOLS = SEQ + NSB * 5
MCOLS = SEQ + NSB * DEC
NFIN = 68
(V_MUR, V_MUK, V_MUV, V_W0, V_A0, V_KK, V_KA, V_RK, V_LNXG, V_LNXB, V_CW0, V_CW1, V_CW2, V_CW3,
 V_CB, V_BA, V_BI, V_LAM, V_MUL) = range(19)
NV = 19


class Sched:
    COMPUTE = ("pe", "act", "dve", "pool")

    def __init__(self, nc, es, ndma=14):
        self.nc = nc
        self.eng = {"pe": nc.tensor, "act": nc.scalar, "dve": nc.vector, "pool": nc.gpsimd, "sp": nc.sync}
        self.sem = {}
        self.cnt = {}
        for e in self.COMPUTE:
            self.sem[e] = es.enter_context(nc.semaphore("sem_" + e))
            self.cnt[e] = 0
        self.ndma = ndma
        for i in range(ndma):
            d = "d%d" % i
            self.sem[d] = es.enter_context(nc.semaphore("sem_" + d))
            self.cnt[d] = 0
        self.rr = 0
        self.waited = {e: {} for e in list(self.COMPUTE) + ["sp"]}
        self.W = {}
        self.R = {}
        self.excl = set()
        self.nops = {e: 0 for e in list(self.COMPUTE) + ["sp"]}

    def _collect(self, e, reads, writes):
        need = {}

        def add(d, c, raw):
            if d == e and (e == "pe" or (not raw and e != "pool")):
                return
            if c > need.get(d, 0):
                need[d] = c

        reads = [getattr(r, "base", r) for r in reads]
        writes = [getattr(w, "base", w) for w in writes]
        for r in reads:
            for d, c in self.W.get(r, {}).items():
                add(d, c, True)
            if r in self.excl:
                for d, c in self.R.get(r, {}).items():
                    add(d, c, False)
        for w in writes:
            for d, c in self.W.get(w, {}).items():
                add(d, c, False)
            for d, c in self.R.get(w, {}).items():
                add(d, c, False)
        return need

    LOG = None

    def _emit_waits(self, e, need):
        wd = self.waited[e]
        for d, c in need.items():
            if wd.get(d, 0) >= c:
                continue
            self.eng[e].wait_ge(self.sem[d], c)
            wd[d] = c
            if self.LOG is not None:
                self.LOG.append((e, d, c, dict(self.cnt)))

    def _record(self, dom, val, reads, writes):
        reads = [getattr(r, "base", r) for r in reads]
        writes = [getattr(w, "base", w) for w in writes]
        for r in reads:
            rr = self.R.setdefault(r, {})
            if val > rr.get(dom, 0):
                rr[dom] = val
        for w in writes:
            if self.R.get(w):
                self.W[w] = {dom: val}
                self.R[w] = {}
            else:
                self.W.setdefault(w, {})[dom] = val

    def op(self, e, fn, reads=(), writes=(), inc=True):
        need = self._collect(e, reads, writes)
        att = None
        if e != "pe":
            wd = self.waited[e]
            pend = [(d, c) for d, c in need.items() if wd.get(d, 0) < c]
            if pend:
                att = pend[-1]
                need = dict(pend[:-1])
        self._emit_waits(e, need)
        ins = fn(self.eng[e])
        if att is not None:
            ins._wait_ge(self.sem[att[0]], att[1])
            self.waited[e][att[0]] = att[1]
        self.nops[e] += 1
        if inc:
            self.cnt[e] += 1
            ins.then_inc(self.sem[e], 1)
            val = self.cnt[e]
        else:
            val = self.cnt[e] + 1
        self._record(e, val, reads, writes)

    def dma(self, out, in_, reads=(), writes=(), q="sp"):
        d = "d%d" % self.rr
        self.rr = (self.rr + 1) % self.ndma
        need = self._collect(q, reads, writes)
        if self.cnt[d] > 0:
            need[d] = max(need.get(d, 0), self.cnt[d])
        self._emit_waits(q, need)
        ins = self.eng[q].dma_start(out=out, in_=in_)
        self.nops[q] += 1
        self.cnt[d] += 16
        ins.then_inc(self.sem[d], 16)
        self._record(d, self.cnt[d], reads, writes)

    def barrier(self):
        for e in list(self.COMPUTE) + ["sp"]:
            need = {d: c for d, c in self.cnt.items() if c > 0 and d != e}
            self._emit_waits(e, need)

    def finish(self):
        for i in range(self.ndma):
            d = "d%d" % i
            if self.cnt[d] > 0:
                self.eng["sp"].wait_ge(self.sem[d], self.cnt[d])
        for e in self.COMPUTE:
            if self.cnt[e] > 0:
                self.eng["sp"].wait_ge(self.sem[e], self.cnt[e])


class StopBuild(Exception):
    pass


class T:
    def __init__(self, h, name):
        self.h = h
        self.name = name

    def __getitem__(self, k):
        return self.h[k]

    def __repr__(self):
        return "T(%s)" % self.name


class Off:
    def __init__(self, base, off, width):
        self.base = base
        self.off = off
        self.width = width

    def __getitem__(self, k):
        if not isinstance(k, tuple):
            return self.base[:, self.off:self.off + self.width]
        p_, c_ = k
        lo = 0 if c_.start is None else c_.start
        hi = self.width if c_.stop is None else c_.stop
        return self.base[p_, self.off + lo:self.off + hi]


class SubSeg:
    kind = "s"

    def __init__(self, b):
        self.idx = b
        self.ncol = QB * 5
        self.nb = QB
        self.T = DEC
        self.tv = DEC
        self.nq = QB
        self.m0 = SEQ + b * QB * DEC
        self.mcols = QB * DEC
        self.nlev = 2

    def tokv(self, ap):
        return ap.rearrange("p (b s) -> p b s", s=5)[:, :, 1:5]

    chv = tokv


class Pool:
    def __init__(self, tiles):
        self.free = list(tiles)
        self.all = list(tiles)

    def get(self):
        return self.free.pop(0)

    def put(self, *ts):
        for t in ts:
            assert t in self.all and t not in self.free
            self.free.append(t)


class Seg:
    def __init__(self, kind, idx):
        self.kind = kind
        self.idx = idx
        if kind == "p":
            self.x0 = idx * SEGP
            self.ncol = SEGP
            self.nb = 1
            self.T = SEGP
            self.tv = CH
            self.m0 = idx * SEGP
            self.mcols = SEGP
            self.nlev = 6
        else:
            self.x0 = XS0 + idx * SB * 5
            self.ncol = SB * 5
            self.nb = SB
            self.T = DEC
            self.tv = DEC
            self.m0 = SEQ + idx * SB * DEC
            self.mcols = SB * DEC
            self.nlev = 2
        self.nq = QB if kind == "p" else SB

    def tokv(self, ap):
        if self.kind == "p":
            return ap.rearrange("p (b t) -> p b t", b=1)
        return ap.rearrange("p (b s) -> p b s", s=5)[:, :, 1:5]

    def chv(self, ap):
        if self.kind == "p":
            return ap.rearrange("p (q t) -> p q t", t=CH)
        return ap.rearrange("p (b s) -> p b s", s=5)[:, :, 1:5]


def build_program(debug=None, stop_after=None):
    nc = bass.Bass("TRN2", target_bir_lowering=False)
    es = ExitStack()
    S = Sched(nc, es)

    def din(name, shape):
        return nc.dram_tensor(name, list(shape), F32, kind="ExternalInput").ap()

    def dout(name, shape):
        return nc.dram_tensor(name, list(shape), F32, kind="ExternalOutput").ap()

    xp = din("xp", [SEQ, D])
    xs = din("xs", [NSB * 5, D])
    xst = din("xst", [NSB * DEC, D])
    st = din("st", [64, D])
    s0 = din("s0", [NSB, 16, 64, 64])
    w1 = din("w1", [NCH, 128, 7, 8, 128])
    wl = din("wl", [128, 2, 8, 128])
    wo = din("wo", [128, 8, D])
    wgu = din("wgu", [NFC, 128, 2, 8, 128])
    wd = din("wd", [NFC, 128, D])
    vec = din("vec", [128, NV, 8])
    w2d = din("w2d", [64, D])
    a2d = din("a2d", [64, D])
    g2d = din("g2d", [128, D])
    wab = din("wab", [128, 2, 8, 128])
    lnv = din("lnv", [128, 4, D])

    yp = dout("yp", [SEQ, D])
    ys = dout("ys", [NSB * DEC, D])
    o_shift = dout("o_shift", [1 + NSB, D])
    o_wkv_p = dout("o_wkv_p", [16, 64, 64])
    o_wkv_s = dout("o_wkv_s", [NSB, 16, 64, 64])
    o_fin = dout("o_fin", [NFIN, D])
    dbg_out = {}
    if debug:
        for name, (shape, dt_) in debug.items():
            dbg_out[name] = nc.dram_tensor("dbg_" + name, list(shape), dt_, kind="ExternalOutput").ap()

    def sb(name, shape, dt=F32, stack=None):
        h = (stack or es).enter_context(nc.sbuf_tensor(name, list(shape), dt))
        return T(h, name)

    psl = [T(es.enter_context(nc.psum_tensor("ps%d" % i, [128, 512], F32)), "ps%d" % i) for i in range(8)]
    ps_state = {"i": 0}
    S.excl.update(psl)

    ps_banks = {None: list(range(8)), "s1": [6, 7], "s1a": [6], "s1b": [7], "s1c": [5], "s2A": [0, 1], "s2B": [2, 3], "tailA": [4], "tailB": [4]}
    ps_ctr = {k: 0 for k in (None, "s1", "s1a", "s1b", "s1c", "s2A", "s2B", "tailA", "tailB")}
    cur_alloc = {"who": None}

    def getps():
        who = cur_alloc["who"]
        lst = ps_banks[who]
        t = psl[lst[ps_ctr[who] % len(lst)]]
        ps_ctr[who] += 1
        return t

    cur = {"segi": -1}

    def ck(label):
        if stop_after == label or stop_after == "%s@%d" % (label, cur["segi"]):
            raise StopBuild()

    ident_f = sb("ident_f", [128, 128])
    ident_b = sb("ident_b", [128, 128], BF16)
    II_f = sb("II_f", [128, 64])
    II_b = sb("II_b", [128, 64], BF16)
    ones_bd = sb("ones_bd", [128, 128])
    mUs = sb("mUs", [128, QB, 128], BF16)
    mUi = sb("mUi", [128, QB, 128], BF16)
    mLs = sb("mLs", [128, QB, 128], BF16)
    rmask_p = sb("rmask_p", [128, SEGP])
    rmask_s = sb("rmask_s", [128, SB * 5])
    vecs = sb("vecs", [128, NV, 8])
    der = sb("der", [128, 12, 8])
    w2b = sb("w2b", [128, D], BF16)
    g2b = sb("g2b", [128, D], BF16)
    wabb = sb("wabb", [128, 2, 8, 128], BF16)
    fin = sb("fin", [128, NCH, NFIN])
    stT = sb("stT", [128, NCH, 64])

    S.op("pool", lambda E: E.memset(ident_f[:], 1.0), writes=[ident_f])
    S.op("pool", lambda E: E.affine_select(out=ident_f[:], in_=ident_f[:], pattern=[[-1, 128]],
                                           compare_op=ALU.is_equal, fill=0.0, base=0, channel_multiplier=1),
         reads=[ident_f], writes=[ident_f])
    S.op("dve", lambda E: E.tensor_copy(out=ident_b[:], in_=ident_f[:]), reads=[ident_f], writes=[ident_b])
    S.op("dve", lambda E: E.tensor_tensor(out=II_f[:], in0=ident_f[:, 0:64], in1=ident_f[:, 64:128], op=ALU.add),
         reads=[ident_f], writes=[II_f])
    S.op("dve", lambda E: E.tensor_copy(out=II_b[:], in_=II_f[:]), reads=[II_f], writes=[II_b])
    S.op("pool", lambda E: E.memset(ones_bd[:], 0.0), writes=[ones_bd])
    S.op("pool", lambda E: E.memset(ones_bd[0:64, 0:64], 1.0), reads=[ones_bd], writes=[ones_bd])
    S.op("pool", lambda E: E.memset(ones_bd[64:128, 64:128], 1.0), reads=[ones_bd], writes=[ones_bd])
    for m, patt, cm, cmp in ((mUs, 1, -1, ALU.is_gt), (mUi, 1, -1, ALU.is_ge), (mLs, -1, 1, ALU.is_gt)):
        S.op("pool", lambda E, m=m: E.memset(m[:], 1.0), writes=[m])
        S.op("pool", lambda E, m=m, patt=patt, cm=cm, cmp=cmp: E.affine_select(
            out=m[:], in_=m[:], pattern=[[0, QB], [patt, 128]], compare_op=cmp, fill=0.0, base=0,
            channel_multiplier=cm), reads=[m], writes=[m])
    S.op("pool", lambda E: E.memset(rmask_p[:], 1.0), writes=[rmask_p])
    S.op("pool", lambda E: E.memset(rmask_p[:].rearrange("p (q t) -> p q t", t=CH)[:, :, 0:1], 0.0),
         reads=[rmask_p], writes=[rmask_p])
    S.op("pool", lambda E: E.memset(rmask_s[:], 1.0), writes=[rmask_s])
    S.op("pool", lambda E: E.memset(rmask_s[:].rearrange("p (b s) -> p b s", s=5)[:, :, 0:2], 0.0),
         reads=[rmask_s], writes=[rmask_s])
    S.op("pool", lambda E: E.memset(fin[:], 0.0), writes=[(fin, c) for c in range(NCH)])

    S.dma(vecs[:], vec[:, :, :], writes=[vecs])

    def vcol(i, c):
        return vecs[:, i, c:c + 1]

    S.op("dve", lambda E: E.tensor_scalar(out=der[:, 0, :], in0=vecs[:, V_KA, :], scalar1=-1.0, scalar2=1.0,
                                          op0=ALU.mult, op1=ALU.add), reads=[vecs], writes=[(der, 0)])
    S.op("act", lambda E: E.activation(out=der[:, 3, :], in_=vecs[:, V_LAM, :], func=AF.Exp, scale=-1.0),
         reads=[vecs], writes=[(der, 3)])
    S.op("act", lambda E: E.activation(out=der[:, 3, :], in_=der[:, 3, :], func=AF.Ln, bias=1.0, scale=1.0),
         reads=[(der, 3)], writes=[(der, 3)])
    S.op("dve", lambda E: E.tensor_scalar(out=der[:, 1, :], in0=der[:, 3, :], scalar1=-8.0, scalar2=None,
                                          op0=ALU.mult), reads=[(der, 3)], writes=[(der, 1)])
    S.op("dve", lambda E: E.tensor_scalar(out=der[:, 2, :], in0=der[:, 3, :], scalar1=-16.0, scalar2=None,
                                          op0=ALU.mult), reads=[(der, 3)], writes=[(der, 2)])
    for di, vi in ((4, V_W0), (5, V_A0), (6, V_BA), (7, V_BI), (8, V_KA)):
        S.op("dve", lambda E, di=di, vi=vi: E.tensor_scalar(out=der[:, di, :], in0=vecs[:, vi, :], scalar1=0.5,
                                                            scalar2=None, op0=ALU.mult), reads=[vecs], writes=[(der, di)])
    S.op("dve", lambda E: E.tensor_scalar(out=der[:, 9, :], in0=vecs[:, V_KA, :], scalar1=-0.5, scalar2=1.0,
                                          op0=ALU.mult, op1=ALU.add), reads=[vecs], writes=[(der, 9)])
    S.op("dve", lambda E: E.tensor_scalar(out=der[:, 10, :], in0=der[:, 3, :], scalar1=-4.0, scalar2=None,
                                          op0=ALU.mult), reads=[(der, 3)], writes=[(der, 10)])
    S.op("dve", lambda E: E.tensor_scalar(out=der[:, 11, :], in0=der[:, 3, :], scalar1=-8.0, scalar2=None,
                                          op0=ALU.mult), reads=[(der, 3)], writes=[(der, 11)])
    negh = sb("negh", [128, QB])
    S.op("pool", lambda E: E.memset(negh[:], -0.5), writes=[negh])

    p1 = ExitStack()
    merged = sb("merged", [128, 8, MCOLS], BF16)
    xsc = nc.dram_tensor("xsc", [128, 8, XCOLS], BF16).ap()
    xtt = [sb("xtt%d" % i, [128, 8, 128], BF16, p1) for i in range(1)]
    xs_pool = Pool([sb("xseg%d" % i, [128, 8, SEGP], BF16, p1) for i in range(2)])
    if stop_after is not None:
        for c_ in range(8):
            S.op("pool", lambda E, c_=c_: E.memset(merged[:, c_, :], 0.0), writes=[(merged, c_)])
    lora0 = sb("lora0", [128, XCOLS], BF16, p1)
    lora1 = sb("lora1", [128, XCOLS], BF16, p1)
    stage = [sb("stage%d" % i, [128, 8, 128], F32, p1) for i in range(3)]
    wcb = sb("wcb", [128, 7, 8, 128], BF16, p1)
    NSCR = 37
    scr = Pool([sb("scr%d" % i, [128, SEGP + 4], F32, p1) for i in range(NSCR)])
    NWK = 27
    wk = Pool([sb("wk%d" % i, [128, QB, 128], BF16, p1) for i in range(NWK)])
    bdn = ("aT", "bT", "kT", "rT", "bgT", "kgT", "vT")
    bd = {}
    for kind in ("p", "s"):
        bd[kind] = []
        for par_ in range(2):
            st_ = {n: sb("bd_%s%d_%s" % (kind, par_, n), [128, QB, 2, 64], BF16, p1) for n in bdn}
            bd[kind].append(st_)
            for n in bdn:
                S.op("pool", lambda E, t=st_[n]: E.memset(t[:], 0.0), writes=[(st_[n], 0), (st_[n], 1)])
    ybd = sb("ybd", [128, QB, 2, 64], F32, p1)
    S.op("pool", lambda E: E.memset(ybd[:], 0.0), writes=[(ybd, 0), (ybd, 1)])
    hbd = sb("hbd", [128, 2, 64], F32, p1)
    S.op("pool", lambda E: E.memset(hbd[:], 0.0), writes=[(hbd, 0), (hbd, 1)])
    zbuf = [sb("zbuf%d" % i, [128, 1 + SEGP], F32, p1) for i in range(3)]
    xbuf = sb("xbuf", [128, SB * (3 + SEGP // 1) if False else max(3 + SEGP, SB * 7)], F32, p1)
    carry3 = sb("carry3", [128, 3], F32, p1)
    hcar = sb("hcar", [128, 1], F32, p1)
    H32 = [sb("H32_%d" % i, [128, QB + 1, 64], F32, p1) for i in range(2)]
    Hbf = [sb("Hbf_%d" % i, [128, QB + 1, 64], BF16, p1) for i in range(2)]
    H32s = [[sb("H32s_%d%d" % (i, j), [128, QB, 64], F32, p1) for j in range(2)] for i in range(2)]
    Hbfs = [sb("Hbfs_%d" % i, [128, QB, 64], BF16, p1) for i in range(2)]
    gamC = sb("gamC", [128, QB], F32, p1)
    stat = sb("stat", [128, 8, QB], F32, p1)
    bst4 = sb("bst4", [128, QB, 6], F32, p1)
    mvq = sb("mvq", [128, QB, 2], F32, p1)

    S.dma(stage[0][0:64, :, :].rearrange("p a b -> p (a b)"), w2d[:, :], writes=[stage[0]])
    S.dma(stage[1][64:128, :, :].rearrange("p a b -> p (a b)"), a2d[:, :], writes=[stage[1]])
    S.op("pool", lambda E: E.tensor_copy(out=w2b[0:64, :], in_=stage[0][0:64, :, :].rearrange("p a b -> p (a b)")),
         reads=[stage[0]], writes=[(w2b, 0)])
    S.op("pool", lambda E: E.tensor_copy(out=w2b[64:128, :], in_=stage[1][64:128, :, :].rearrange("p a b -> p (a b)")),
         reads=[stage[1]], writes=[(w2b, 1)])
    S.dma(stage[2][:, :, :].rearrange("p a b -> p (a b)"), g2d[:, :], writes=[stage[2]])
    S.op("pool", lambda E: E.tensor_copy(out=g2b[:], in_=stage[2][:, :, :].rearrange("p a b -> p (a b)")),
         reads=[stage[2]], writes=[g2b])
    for j in range(2):
        S.dma(stage[j][:], wab[:, j, :, :], writes=[stage[j]])
        S.op("pool", lambda E, j=j: E.tensor_copy(out=wabb[:, j, :, :], in_=stage[j][:]),
             reads=[stage[j]], writes=[(wabb, j)])
    for j in range(2):
        S.dma(stage[j][:], wl[:, j, :, :], writes=[stage[j]])
        S.op("pool", lambda E, j=j: E.tensor_copy(out=wcb[:, j, :, :], in_=stage[j][:]),
             reads=[stage[j]], writes=[(wcb, j)])

    rr = {"ev": 0, "ew": 0}

    def ev_eng():
        return "act"

    def ew_eng():
        rr["ew"] ^= 1
        return "dve" if rr["ew"] else "pool"

    def copy_op(e, out, in_, reads, writes):
        if e == "act":
            S.op("act", lambda E: E.copy(out=out, in_=in_), reads, writes)
        else:
            S.op(e, lambda E: E.tensor_copy(out=out, in_=in_), reads, writes)

    ntile = SEQ // 128
    for t in range(ntile + 2):
        xi = stage[t % 3]
        xiv = xi[:].rearrange("p a b -> p (a b)")
        if t < ntile:
            rows = 128
            S.dma(xiv, xp[t * 128:(t + 1) * 128, :], writes=[xi])
        elif t == ntile:
            rows = NSB * 5
            S.dma(xiv[0:rows, :], xs[:, :], writes=[xi])
        else:
            rows = 64
            S.dma(xiv[0:rows, :], st[:, :], writes=[xi])
        for half in range(2):
            ps = getps()
            for k4 in range(4):
                kc = half * 4 + k4
                S.op("pe", lambda E, ps=ps, k4=k4, kc=kc, xi=xi, rows=rows: E.transpose(
                    out=ps[:, k4 * 128:k4 * 128 + rows], in_=xi[0:rows, kc, :],
                    identity=ident_f[0:rows, 0:rows]), reads=[xi, ident_f], writes=[ps], inc=(k4 == 3))
            src = ps[:].rearrange("p (k t) -> p k t", t=128)[:, :, 0:rows]
            if t <= ntile:
                xo = xtt[0]
                copy_op(ev_eng(), xo[:, half * 4:(half + 1) * 4, 0:rows], src, [ps], [(xo, half)])
                if half == 1:
                    S.dma(xsc[:, :, t * 128:t * 128 + rows], xo[:, :, 0:rows], reads=[(xo, 0), (xo, 1)],
                          writes=[("xsc", t)])
            else:
                dst = stT[:, half * 4:(half + 1) * 4, :]
                copy_op(ev_eng(), dst, src, [ps], [stT])
    xstate = {"tile": {}, "order": [], "pos": 0}

    def _issue_x(seg):
        t_ = xs_pool.get()
        t0_, t1_ = seg.x0 // 128, (seg.x0 + seg.ncol - 1) // 128
        S.dma(t_[:, :, 0:seg.ncol], xsc[:, :, seg.x0:seg.x0 + seg.ncol],
              reads=[("xsc", k) for k in range(t0_, t1_ + 1)], writes=[t_])
        return t_

    def get_x(seg):
        order, pos = xstate["order"], xstate["pos"]
        assert order[pos] is seg
        t_ = xstate["tile"].pop(pos, None)
        if t_ is None:
            t_ = _issue_x(seg)
        if pos + 1 < len(order):
            xstate["tile"][pos + 1] = _issue_x(order[pos + 1])
        xstate["pos"] = pos + 1
        return t_

    def dump(name, ap, reads):
        if name in dbg_out:
            S.dma(dbg_out[name], ap, reads=reads)


    def proj_ps(wsel, seg, xseg):
        ps = getps()
        ncol = seg.ncol
        for kc in range(8):
            lhsT, wres = wsel(kc)
            S.op("pe", lambda E, ps=ps, lhsT=lhsT, kc=kc: E.matmul(
                ps[:, 0:ncol], lhsT=lhsT, rhs=xseg[:, kc, 0:ncol], start=(kc == 0), stop=(kc == 7)),
                reads=[wres, xseg], writes=[ps], inc=(kc == 7))
        return ps

    def shifted(wsel, mu_ap, seg, zb, first, xseg):
        ncol = seg.ncol
        if seg.kind == "p":
            if first:
                S.op("pool", lambda E: E.memset(zb[:, 0:1], 0.0), writes=[zb])
            else:
                S.op("pool", lambda E: E.tensor_copy(out=zb[:, 0:1], in_=zb[:, SEGP:SEGP + 1]), reads=[zb], writes=[zb])
        ps = proj_ps(wsel, seg, xseg)
        S.op("act", lambda E: E.copy(out=zb[:, 1:1 + ncol], in_=ps[:, 0:ncol]), reads=[ps], writes=[zb])
        dt_ = scr.get()
        zm = scr.get()
        S.op("pool", lambda E: E.tensor_tensor(out=dt_[:, 0:ncol], in0=zb[:, 0:ncol], in1=zb[:, 1:1 + ncol],
                                               op=ALU.subtract), reads=[zb], writes=[dt_])
        S.op("dve", lambda E: E.scalar_tensor_tensor(out=zm[:, 0:ncol], in0=dt_[:, 0:ncol], scalar=mu_ap,
                                                     in1=zb[:, 1:1 + ncol], op0=ALU.mult, op1=ALU.add),
             reads=[dt_, zb, vecs], writes=[zm])
        scr.put(dt_)
        return zm

    segs = [Seg("p", i) for i in range(NSEGP)] + [Seg("s", i) for i in range(NSEGS)]

    xstate["order"] = segs * 2 + segs * NCH
    for L in range(2):
        for si, seg in enumerate(segs):
            xseg = get_x(seg)
            zm = shifted(lambda kc, L=L: (wcb[:, L, kc, :], (wcb, L)), vcol(V_MUL, L), seg, zbuf[0],
                         first=(seg.kind == "p" and seg.idx == 0), xseg=xseg)
            xs_pool.put(xseg)
            nco = seg.ncol
            if L == 0:
                S.op("act", lambda E: E.activation(out=lora0[0:64, seg.x0:seg.x0 + nco], in_=zm[0:64, 0:nco],
                                                   func=AF.Tanh), reads=[zm], writes=[(lora0, 0)])
                S.op("act", lambda E: E.copy(out=lora0[64:128, seg.x0:seg.x0 + nco], in_=zm[64:128, 0:nco]),
                     reads=[zm], writes=[(lora0, 1)])
            else:
                S.op("act", lambda E: E.activation(out=lora1[:, seg.x0:seg.x0 + nco], in_=zm[:, 0:nco],
                                                   func=AF.Sigmoid), reads=[zm], writes=[lora1])
            scr.put(zm)
    dump("lora0", lora0[:, :], [(lora0, 0), (lora0, 1)])
    dump("lora1", lora1[:, :], [lora1])

    if stop_after == "lora":
        S.finish()
        p1.close()
        es.close()
        return nc

    gam_pool = Pool([sb("gamC%d" % i, [128, SB], F32, p1) for i in range(5)])
    ubf_pool = Pool([sb("ubf%d" % i, [128, SEGP], BF16, p1) for i in range(2)])
    sbd = sb("sbd", [128, 2, 64], F32, p1)
    sin_tiles = [sb("sin%d" % i, [128, 64], F32, p1) for i in range(3)]
    sin_ctr = {"i": 0}
    S.op("pool", lambda E: E.memset(sbd[:], 0.0), writes=[(sbd, 0), (sbd, 1)])

    print("sbuf remaining in phase 1:", nc.sbuf_bytes_remaining)

    def wsel(g):
        return lambda kc: (wcb[:, g, kc, :], (wcb, g))

    def mm(ps_ap, lhsT, rhs, reads, ps, start=True, stop=True, inc=True):
        S.op("pe", lambda E: E.matmul(ps_ap, lhsT=lhsT, rhs=rhs, start=start, stop=stop),
             reads=reads, writes=[ps], inc=inc)

    def tt(e, out, a, b, op, reads, writes):
        S.op(e, lambda E: E.tensor_tensor(out=out, in0=a, in1=b, op=op), reads, writes)

    def act(out, in_, func, reads, writes, bias=None, scale=None):
        kw = {}
        if bias is not None:
            kw["bias"] = bias
        if scale is not None:
            kw["scale"] = scale
        S.op("act", lambda E: E.activation(out=out, in_=in_, func=func, **kw), reads, writes)

    def bd_write(seg, B, K, col0):
        tv, nco = seg.tv, seg.ncol
        for h in range(2):
            P = slice(h * 64, (h + 1) * 64)

            def dst(n):
                return B[n][P, :, h, 0:tv]

            def cv(t_):
                return seg.chv(t_[P, col0:col0 + nco])

            kk, E1, E2, E3, E4, tb, kf, zr, zv = (K[n] for n in ("kk", "E1", "E2", "E3", "E4", "tb", "kf", "zr", "zv"))
            S.op("dve", lambda E: E.scalar_tensor_tensor(out=dst("aT"), in0=cv(kk), scalar=-1.0, in1=cv(E3),
                                                         op0=ALU.mult, op1=ALU.mult), [kk, E3], [(B["aT"], h)])
            yield
            tt(ew_eng(), dst("bT"), cv(tb), cv(E2), ALU.mult, [tb, E2], [(B["bT"], h)])
            yield
            tt(ew_eng(), dst("bgT"), cv(tb), cv(E4), ALU.mult, [tb, E4], [(B["bgT"], h)])
            yield
            tt(ew_eng(), dst("kT"), cv(kf), cv(E2), ALU.mult, [kf, E2], [(B["kT"], h)])
            yield
            tt(ew_eng(), dst("kgT"), cv(kf), cv(E4), ALU.mult, [kf, E4], [(B["kgT"], h)])
            yield
            tt(ew_eng(), dst("rT"), cv(zr), cv(E1), ALU.mult, [zr, E1], [(B["rT"], h)])
            yield
            S.op("act", lambda E: E.copy(out=dst("vT"), in_=cv(zv)), [zv], [(B["vT"], h)])
            yield

    def stage1c(c, seg, shared):
        nco, kind, tv, nq, x0 = seg.ncol, seg.kind, seg.tv, seg.nq, seg.x0
        first = (kind == "p" and seg.idx == 0)
        B = bd[kind][seg.idx % 2]
        cc = slice(c * 128, (c + 1) * 128)
        ps = getps()
        mm(ps[:, 0:nco], w2b[0:64, cc], lora0[0:64, x0:x0 + nco], [(w2b, 0), (lora0, 0)], ps)
        yield
        sg = scr.get()
        act(sg[:, 0:nco], ps[:, 0:nco], AF.Tanh, [ps, (der, 4)], [sg], bias=der[:, 4, c:c + 1], scale=0.5)
        S.op("pool", lambda E: E.tensor_scalar(out=sg[:, 0:nco], in0=sg[:, 0:nco], scalar1=0.5, scalar2=0.5,
                                               op0=ALU.mult, op1=ALU.add), [sg], [sg])
        yield
        cs = scr.get()
        rmask = rmask_p if kind == "p" else rmask_s
        S.op("dve", lambda E: E.tensor_tensor_scan(out=cs[:, 0:nco], data0=rmask[:, 0:nco], data1=sg[:, 0:nco],
                                                   initial=0.0, op0=ALU.mult, op1=ALU.add),
             reads=[rmask, sg], writes=[cs])
        yield
        E1 = scr.get(); E2 = scr.get(); E3 = scr.get(); E4 = scr.get(); t0 = scr.get()
        act(E1[:, 0:nco], cs[:, 0:nco], AF.Exp, [cs], [E1], scale=-KAPPA)
        yield
        act(E2[:, 0:nco], cs[:, 0:nco], AF.Exp, [cs], [E2], scale=KAPPA)
        yield
        tt("pool", t0[:, 0:nco], cs[:, 0:nco], sg[:, 0:nco], ALU.subtract, [cs, sg], [t0])
        yield
        act(E3[:, 0:nco], t0[:, 0:nco], AF.Exp, [t0], [E3], scale=-KAPPA)
        yield
        csv = seg.chv(cs[:, 0:nco])
        t0v = seg.chv(t0[:, 0:nco])
        tt("dve", t0v, csv[:, :, tv - 1:tv].to_broadcast([128, nq, tv]), csv, ALU.subtract, [cs], [t0])
        yield
        act(seg.chv(E4[:, 0:nco]), t0v, AF.Exp, [t0], [E4], scale=-KAPPA)
        yield
        gam = gam_pool.get()
        act(gam[:, 0:nq].rearrange("p (q o) -> p q o", o=1), csv[:, :, tv - 1:tv], AF.Exp, [cs], [gam], scale=-KAPPA)
        yield
        scr.put(sg, cs, t0)
        ps = getps()
        mm(ps[:, 0:nco], w2b[64:128, cc], lora0[64:128, x0:x0 + nco], [(w2b, 1), (lora0, 1)], ps)
        yield
        a_ = scr.get()
        act(a_[:, 0:nco], ps[:, 0:nco], AF.Tanh, [ps, (der, 5)], [a_], bias=der[:, 5, c:c + 1], scale=0.5)
        S.op("pool", lambda E: E.tensor_scalar(out=a_[:, 0:nco], in0=a_[:, 0:nco], scalar1=0.5, scalar2=0.5,
                                               op0=ALU.mult, op1=ALU.add), [a_], [a_])
        yield
        ps = getps()
        mm(ps[:, 0:nco], g2b[:, cc], lora1[:, x0:x0 + nco], [g2b, lora1], ps)
        yield
        g_ = scr.get()
        S.op("act", lambda E: E.mul(out=g_[:, 0:nco], in_=ps[:, 0:nco], mul=0.5), [ps], [g_])
        yield
        shared.update(E1=E1, E2=E2, E3=E3, E4=E4, a_=a_, g_=g_, gam=gam)

    def stage1a(c, seg, xseg, shared):
        nco, kind, tv, nq, x0 = seg.ncol, seg.kind, seg.tv, seg.nq, seg.x0
        first = (kind == "p" and seg.idx == 0)
        B = bd[kind][seg.idx % 2]
        cc = slice(c * 128, (c + 1) * 128)
        zr = shifted(wsel(0), vcol(V_MUR, c), seg, zbuf[0], first, xseg)
        yield
        zk = shifted(wsel(1), vcol(V_MUK, c), seg, zbuf[1], first, xseg)
        yield
        zv = shifted(wsel(2), vcol(V_MUV, c), seg, zbuf[2], first, xseg)
        yield
        kkr = scr.get(); sq = scr.get(); kk = scr.get()
        S.op("dve", lambda E: E.tensor_scalar(out=kkr[:, 0:nco], in0=zk[:, 0:nco], scalar1=vcol(V_KK, c), scalar2=None,
                                              op0=ALU.mult), [zk, vecs], [kkr])
        yield
        tt("pool", sq[:, 0:nco], kkr[:, 0:nco], kkr[:, 0:nco], ALU.mult, [kkr], [sq])
        yield
        ps = getps()
        mm(ps[:, 0:nco], ones_bd[:], sq[:, 0:nco], [ones_bd, sq], ps)
        yield
        act(sq[:, 0:nco], ps[:, 0:nco], AF.Sqrt, [ps], [sq])
        yield
        S.op("dve", lambda E: E.tensor_scalar(out=sq[:, 0:nco], in0=sq[:, 0:nco], scalar1=1e-12, scalar2=None,
                                              op0=ALU.max), [sq], [sq])
        yield
        S.op("dve", lambda E: E.reciprocal(out=sq[:, 0:nco], in_=sq[:, 0:nco]), [sq], [sq])
        yield
        tt("pool", kk[:, 0:nco], kkr[:, 0:nco], sq[:, 0:nco], ALU.mult, [kkr, sq], [kk])
        yield
        yield "NEEDC"
        E1, E2, E3, E4, a_, g_, gam = (shared[k_] for k_ in ("E1", "E2", "E3", "E4", "a_", "g_", "gam"))
        t1 = scr.get(); kf = scr.get(); bonus = scr.get()
        S.op("dve", lambda E: E.tensor_scalar(out=t1[:, 0:nco], in0=a_[:, 0:nco], scalar1=vcol(V_KA, c),
                                              scalar2=der[:, 0, c:c + 1], op0=ALU.mult, op1=ALU.add),
             [a_, vecs, (der, 0)], [t1])
        yield
        tt("pool", kf[:, 0:nco], zk[:, 0:nco], t1[:, 0:nco], ALU.mult, [zk, t1], [kf])
        yield
        S.op("dve", lambda E: E.scalar_tensor_tensor(out=t1[:, 0:nco], in0=zr[:, 0:nco], scalar=vcol(V_RK, c),
                                                     in1=kf[:, 0:nco], op0=ALU.mult, op1=ALU.mult),
             [zr, kf, vecs, t1], [t1])
        yield
        ps = getps()
        mm(ps[:, 0:nco], ones_bd[:], t1[:, 0:nco], [ones_bd, t1], ps)
        yield
        tt("dve", bonus[:, 0:nco], ps[:, 0:nco], zv[:, 0:nco], ALU.mult, [ps, zv], [bonus])
        yield
        tb = kkr
        tt("pool", tb[:, 0:nco], kk[:, 0:nco], a_[:, 0:nco], ALU.mult, [kk, a_, kkr], [tb])
        yield
        keep = dict(kk=kk, E3=E3, tb=tb, E2=E2, E4=E4, kf=kf, zr=zr, E1=E1, zv=zv)
        if kind == "p":
            yield from bd_write(seg, B, keep, 0)
            scr.put(zr, zk, zv, E1, E2, E3, E4, a_, kkr, sq, kk, t1, kf)
        else:
            scr.put(zk, a_, sq, t1)
        return dict(bonus=bonus, g=g_, gam=gam, keep=keep)

    def stage1b(c, seg, xseg):
        nco, kind, tv, nq, x0 = seg.ncol, seg.kind, seg.tv, seg.nq, seg.x0
        first = (kind == "p" and seg.idx == 0)
        B = bd[kind][seg.idx % 2]
        cc = slice(c * 128, (c + 1) * 128)
        T_, nb = seg.T, seg.nb
        nt = nb * T_
        xbv = xbuf[:, 0:nb * (3 + T_)].rearrange("p (b t) -> p b t", t=3 + T_)

        def dv(t_):
            return t_[:, 0:nt].rearrange("p (b t) -> p b t", t=T_)

        ps = proj_ps(wsel(3), seg, xseg)
        yield
        if kind == "p":
            if seg.idx == 0:
                S.op("pool", lambda E: E.memset(xbv[:, :, 0:3], 0.0), writes=[xbuf])
                yield
            else:
                S.op("pool", lambda E: E.tensor_copy(out=xbv[:, 0, 0:3], in_=carry3[:, :]), [carry3], [xbuf])
                yield
        else:
            b0 = seg.idx * SB
            S.op("pool", lambda E: E.tensor_copy(
                out=xbv[:, :, 0:3], in_=stT[:, c, 0:48].rearrange("p (b j) -> p b j", j=3)[:, b0:b0 + SB, :]),
                [stT], [xbuf])
            yield
        S.op("act", lambda E: E.copy(out=xbv[:, :, 3:3 + T_], in_=seg.tokv(ps[:, 0:nco])), [ps, xbuf], [xbuf])
        yield
        if kind == "p":
            S.op("pool", lambda E: E.tensor_copy(out=carry3[:, :], in_=xbv[:, 0, T_:T_ + 3]), [xbuf], [carry3])
            yield
            if seg.idx == NSEGP - 1:
                S.op("pool", lambda E: E.tensor_copy(out=fin[:, c, 0:3], in_=xbv[:, 0, T_:T_ + 3]), [xbuf], [(fin, c)])
                yield
        else:
            S.op("pool", lambda E: E.tensor_copy(
                out=fin[:, c, 4 + 3 * b0:4 + 3 * (b0 + SB)].rearrange("p (b j) -> p b j", j=3),
                in_=xbv[:, :, T_:T_ + 3]), [xbuf], [(fin, c)])
            yield
        u = scr.get()
        uv = dv(u)
        S.op("dve", lambda E: E.tensor_scalar(out=uv, in0=xbv[:, :, 0:T_], scalar1=vcol(V_CW0, c),
                                              scalar2=vcol(V_CB, c), op0=ALU.mult, op1=ALU.add),
             [xbuf, vecs], [u])
        yield
        for j in range(1, 4):
            S.op("dve", lambda E, j=j: E.scalar_tensor_tensor(out=uv, in0=xbv[:, :, j:j + T_],
                                                              scalar=vcol(V_CW0 + j, c), in1=uv,
                                                              op0=ALU.mult, op1=ALU.add), [xbuf, vecs, u], [u])
            yield
        ubf = ubf_pool.get()
        S.op("act", lambda E: E.copy(out=ubf[:, 0:nt], in_=u[:, 0:nt]), [u], [ubf])
        yield
        rg = scr.get(); ig = scr.get(); al = scr.get(); e2 = scr.get(); hh = scr.get()
        ps = getps()
        mm(ps[:, 0:nt], wabb[:, 0, c, :], ubf[:, 0:nt], [(wabb, 0), ubf], ps)
        yield
        act(rg[:, 0:nt], ps[:, 0:nt], AF.Tanh, [ps, (der, 6)], [rg], bias=der[:, 6, c:c + 1], scale=0.5)
        yield
        ps = getps()
        mm(ps[:, 0:nt], wabb[:, 1, c, :], ubf[:, 0:nt], [(wabb, 1), ubf], ps)
        yield
        act(ig[:, 0:nt], ps[:, 0:nt], AF.Tanh, [ps, (der, 7)], [ig], bias=der[:, 7, c:c + 1], scale=0.5)
        yield
        ubf_pool.put(ubf)
        act(al[:, 0:nt], rg[:, 0:nt], AF.Exp, [rg, (der, 10)], [al], scale=der[:, 10, c:c + 1], bias=der[:, 10, c:c + 1])
        yield
        act(e2[:, 0:nt], rg[:, 0:nt], AF.Exp, [rg, (der, 11)], [e2], scale=der[:, 11, c:c + 1], bias=der[:, 11, c:c + 1])
        yield
        S.op("pool", lambda E: E.tensor_scalar(out=e2[:, 0:nt], in0=e2[:, 0:nt], scalar1=-0.25, scalar2=0.25,
                                               op0=ALU.mult, op1=ALU.add), [e2], [e2])
        yield
        act(e2[:, 0:nt], e2[:, 0:nt], AF.Sqrt, [e2], [e2])
        yield
        if first:
            S.op("pool", lambda E: E.memset(e2[:, 0:1], 0.5), [e2], [e2])
            yield
        S.op("dve", lambda E: E.scalar_tensor_tensor(out=ig[:, 0:nt], in0=ig[:, 0:nt], scalar=1.0, in1=e2[:, 0:nt],
                                                     op0=ALU.add, op1=ALU.mult), [ig, e2], [ig])
        yield
        tt("dve", ig[:, 0:nt], ig[:, 0:nt], u[:, 0:nt], ALU.mult, [ig, u], [ig])
        yield
        if kind == "p":
            init = 0.0 if seg.idx == 0 else hcar[:, 0:1]
            S.op("dve", lambda E: E.tensor_tensor_scan(out=hh[:, 0:nt], data0=al[:, 0:nt], data1=ig[:, 0:nt],
                                                       initial=init, op0=ALU.mult, op1=ALU.add),
                 [al, ig, hcar], [hh])
            yield
            S.op("pool", lambda E: E.tensor_copy(out=hcar[:, 0:1], in_=hh[:, nt - 1:nt]), [hh], [hcar])
            yield
            if seg.idx == NSEGP - 1:
                S.op("pool", lambda E: E.tensor_copy(out=fin[:, c, 3:4], in_=hh[:, nt - 1:nt]), [hh], [(fin, c)])
                yield
        else:
            for b in range(nb):
                S.op("dve", lambda E, b=b: E.tensor_tensor_scan(
                    out=hh[:, b * T_:(b + 1) * T_], data0=al[:, b * T_:(b + 1) * T_], data1=ig[:, b * T_:(b + 1) * T_],
                    initial=stT[:, c, 48 + b0 + b:48 + b0 + b + 1], op0=ALU.mult, op1=ALU.add),
                    [al, ig, stT, hh], [hh])
                yield
            S.op("pool", lambda E: E.tensor_copy(out=fin[:, c, 52 + b0:52 + b0 + SB].rearrange("p (b o) -> p b o", o=1),
                                                 in_=dv(hh)[:, :, T_ - 1:T_]), [hh], [(fin, c)])
            yield
        scr.put(rg, al, e2, u, ig)
        ps = proj_ps(wsel(4), seg, xseg)
        yield
        gbs = scr.get(); p_ = scr.get()
        S.op("act", lambda E: E.copy(out=dv(gbs), in_=seg.tokv(ps[:, 0:nco])), [ps], [gbs])
        yield
        tt("pool", p_[:, 0:nt], gbs[:, 0:nt], gbs[:, 0:nt], ALU.mult, [gbs], [p_])
        yield
        S.op("pool", lambda E: E.tensor_scalar(out=p_[:, 0:nt], in0=p_[:, 0:nt], scalar1=0.044715, scalar2=1.0,
                                               op0=ALU.mult, op1=ALU.add), [p_], [p_])
        yield
        tt("pool", p_[:, 0:nt], p_[:, 0:nt], gbs[:, 0:nt], ALU.mult, [p_, gbs], [p_])
        yield
        act(p_[:, 0:nt], p_[:, 0:nt], AF.Tanh, [p_], [p_], scale=0.7978845608028654)
        yield
        S.op("dve", lambda E: E.scalar_tensor_tensor(out=gbs[:, 0:nt], in0=p_[:, 0:nt], scalar=1.0, in1=gbs[:, 0:nt],
                                                     op0=ALU.add, op1=ALU.mult), [gbs, p_], [gbs])
        yield
        tt("dve", hh[:, 0:nt], hh[:, 0:nt], gbs[:, 0:nt], ALU.mult, [hh, gbs], [hh])
        yield
        scr.put(gbs)
        ps = proj_ps(wsel(6), seg, xseg)
        yield
        act(dv(p_), seg.tokv(ps[:, 0:nco]), AF.Tanh, [ps], [p_], scale=0.5)
        yield
        S.op("dve", lambda E: E.scalar_tensor_tensor(out=hh[:, 0:nt], in0=p_[:, 0:nt], scalar=1.0, in1=hh[:, 0:nt],
                                                     op0=ALU.add, op1=ALU.mult), [hh, p_], [hh])
        yield
        scr.put(p_)
        ps = proj_ps(wsel(5), seg, xseg)
        yield
        gA = scr.get()
        act(dv(gA), seg.tokv(ps[:, 0:nco]), AF.Tanh, [ps], [gA], scale=0.5)
        yield
        return dict(gA=gA, m2=hh)


    def stage1(c, seg):
        xseg = get_x(seg)
        shared = {}
        gens = [stage1a(c, seg, xseg, shared), stage1b(c, seg, xseg), stage1c(c, seg, shared)]
        who = ["s1a", "s1b", "s1c"]
        res = [None, None, None]
        done = [False, False, False]
        hold_a = False
        while not all(done):
            for i_ in range(3):
                if done[i_]:
                    continue
                if i_ == 0 and hold_a:
                    if not done[2]:
                        continue
                    hold_a = False
                cur_alloc["who"] = who[i_]
                try:
                    v_ = next(gens[i_])
                    if v_ == "NEEDC":
                        hold_a = True
                except StopIteration as e_:
                    res[i_] = e_.value
                    done[i_] = True
                yield
        xs_pool.put(xseg)
        res[0].update(res[1])
        return res[0]

    def stage2(c, seg, s1, chain):
        kind, tv, nq, nlev, nco = seg.kind, seg.tv, seg.nq, seg.nlev, seg.ncol
        B = bd[kind][seg.idx % 2]
        gam = s1["gam"]

        def bdv(n):
            return B[n][:].rearrange("p q h t -> p q (h t)")

        def br(n):
            return [(B[n], 0), (B[n], 1)]

        def q128(ps):
            return ps[:].rearrange("p (q t) -> p q t", t=128)

        def q64(ps):
            return ps[:, 0:QB * 64].rearrange("p (q t) -> p q t", t=64)

        def prod(l, r, mask):
            ps = getps()
            for q in range(QB):
                mm(ps[:, q * 128:(q + 1) * 128], bdv(l)[:, q, :], bdv(r)[:, q, :], br(l) + br(r), ps, inc=(q == QB - 1))
            o = wk.get()
            tt("dve", o[:], q128(ps), mask[:], ALU.mult, [ps, mask], [o])
            return o

        N_ = prod("bT", "aT", mUs)
        yield
        L_ = prod("aT", "bT", mLs)
        yield
        Mak = prod("kT", "aT", mUs)
        yield
        Mrb = prod("bT", "rT", mUi)
        yield
        Mrk = prod("kT", "rT", mUi)
        yield
        rTc = wk.get()
        S.op("pool", lambda E: E.tensor_copy(out=rTc[:], in_=bdv("rT")), br("rT"), [rTc])
        yield
        ck("k1")

        def tr(n):
            ps = getps()
            psb = ps[:].bitcast(BF16)
            for q in range(QB):
                S.op("pe", lambda E, q=q: E.transpose(out=psb[:, q * 128:(q + 1) * 128], in_=bdv(n)[:, q, :],
                                                      identity=ident_b[:]), br(n) + [ident_b], [ps], inc=(q == QB - 1))
            o = wk.get()
            copy_op(ev_eng(), o[:], psb[:, 0:QB * 128].rearrange("p (q t) -> p q t", t=128), [ps], [o])
            return o

        XA = tr("aT")
        yield
        Bg = tr("bgT")
        yield
        Kg = tr("kgT")
        yield
        ck("k2")
        ps = getps()
        for q in range(QB):
            mm(ps[:, q * 64:(q + 1) * 64], bdv("vT")[:, q, :], II_b[:], br("vT") + [II_b], ps, inc=(q == QB - 1))
            yield
        V_ = wk.get()
        copy_op(ev_eng(), V_[:, :, 0:64], q64(ps), [ps], [V_])
        yield ("BDFREE", (c, kind, seg.idx))
        yield
        ps = getps()
        for q in range(QB):
            mm(ps[:, q * 64:(q + 1) * 64], Mak[:, q, :], V_[:, q, 0:64], [Mak, V_], ps, inc=(q == QB - 1))
            yield
        XU = wk.get()
        copy_op(ev_eng(), XU[:, :, 0:64], q64(ps), [ps], [XU])
        yield
        wk.put(Mak)
        ck("k3")
        Nc, Lc = N_, L_
        for lev in range(nlev):
            psA = getps()
            for q in range(QB):
                mm(psA[:, q * 128:(q + 1) * 128], Nc[:, q, :], XA[:, q, :], [Nc, XA], psA, start=True, stop=False, inc=False)
                mm(psA[:, q * 128:(q + 1) * 128], ident_b[:], XA[:, q, :], [ident_b, XA], psA, start=False, stop=True,
                   inc=(q == QB - 1))
            yield
            psU = getps()
            for q in range(QB):
                mm(psU[:, q * 64:(q + 1) * 64], Nc[:, q, :], XU[:, q, 0:64], [Nc, XU], psU, start=True, stop=False, inc=False)
                mm(psU[:, q * 64:(q + 1) * 64], ident_b[:], XU[:, q, 0:64], [ident_b, XU], psU, start=False, stop=True,
                   inc=(q == QB - 1))
            yield
            XA2 = wk.get(); XU2 = wk.get()
            copy_op("act", XA2[:], q128(psA), [psA], [XA2])
            yield
            copy_op("act", XU2[:, :, 0:64], q64(psU), [psU], [XU2])
            yield
            N2 = L2 = None
            if lev < nlev - 1:
                psN = getps()
                for q in range(QB):
                    mm(psN[:, q * 128:(q + 1) * 128], Lc[:, q, :], Nc[:, q, :], [Lc, Nc], psN, inc=(q == QB - 1))
                    yield
                N2 = wk.get()
                copy_op("act", N2[:], q128(psN), [psN], [N2])
                yield
                if lev < nlev - 2:
                    psL = getps()
                    for q in range(QB):
                        mm(psL[:, q * 128:(q + 1) * 128], Nc[:, q, :], Lc[:, q, :], [Lc, Nc], psL, inc=(q == QB - 1))
                        yield
                    L2 = wk.get()
                    copy_op("act", L2[:], q128(psL), [psL], [L2])
                    yield
            wk.put(XA, XU, Nc, Lc)
            XA, XU, Nc, Lc = XA2, XU2, N2, L2
            if Nc is None:
                Nc = wk.get()
            if Lc is None:
                Lc = wk.get()
        wk.put(Nc, Lc)
        ck("k4")
        psR = getps()
        for q in range(QB):
            mm(psR[:, q * 128:(q + 1) * 128], XA[:, q, :], Mrb[:, q, :], [XA, Mrb], psR, start=True, stop=False, inc=False)
            yield
            mm(psR[:, q * 128:(q + 1) * 128], ident_b[:], rTc[:, q, :], [ident_b, rTc], psR, start=False,
               stop=True, inc=(q == QB - 1))
            yield
        Rh = wk.get()
        copy_op(ev_eng(), Rh[:], q128(psR), [psR], [Rh])
        yield
        psG = getps()
        for q in range(QB):
            mm(psG[:, q * 128:(q + 1) * 128], XA[:, q, :], Bg[:, q, :], [XA, Bg], psG, inc=(q == QB - 1))
            yield
        GT = wk.get()
        copy_op(ev_eng(), GT[:], q128(psG), [psG], [GT])
        yield
        ck("k5")
        if kind == "p":
            yield ("CHAIN", (c, seg.idx - 1))
            chain["init"]()
        psH = getps()
        if kind == "p":
            hA, hB = chain["hA"], chain["hB"]
            for q in range(QB):
                sl = slice(q * 64, (q + 1) * 64)
                mm(psH[:, sl], Bg[:, q, :], XU[:, q, 0:64], [Bg, XU], psH, start=True, stop=False, inc=False)
                yield
                mm(psH[:, sl], Kg[:, q, :], V_[:, q, 0:64], [Kg, V_], psH, start=False, stop=False, inc=False)
                yield
                mm(psH[:, sl], GT[:, q, :], hB[:, q, :], [GT, (hB, q)], psH, start=False, stop=True)
                yield
                S.op("dve", lambda E, q=q, sl=sl: E.scalar_tensor_tensor(
                    out=hA[:, q + 1, :], in0=hA[:, q, :], scalar=gam[:, q:q + 1], in1=psH[:, sl],
                    op0=ALU.mult, op1=ALU.add), [(hA, q), gam, psH], [(hA, q + 1)])
                yield
                S.op("act", lambda E, q=q: E.copy(out=hB[:, q + 1, :], in_=hA[:, q + 1, :]), [(hA, q + 1)], [(hB, q + 1)])
                yield
            hBr = [(hB, q) for q in range(QB)]
        else:
            hA, hB, hO = chain["hA"], chain["hB"], chain["hO"]
            for q in range(QB):
                sl = slice(q * 64, (q + 1) * 64)
                mm(psH[:, sl], Bg[:, q, :], XU[:, q, 0:64], [Bg, XU], psH, start=True, stop=False, inc=False)
                yield
                mm(psH[:, sl], Kg[:, q, :], V_[:, q, 0:64], [Kg, V_], psH, start=False, stop=False, inc=False)
                yield
                mm(psH[:, sl], GT[:, q, :], hB[:, q, :], [GT, (hB, q)], psH, start=False, stop=True,
                   inc=(q == QB - 1))
                yield
            tmp = scr.get()
            tmpv = tmp[:, 0:QB * 64].rearrange("p (q t) -> p q t", t=64)
            tt("dve", tmpv, hA[:, 0:QB, :], gam[:].rearrange("p (q o) -> p q o", o=1).to_broadcast([128, QB, 64]),
               ALU.mult, [(hA, q) for q in range(QB)] + [gam], [tmp])
            yield
            tt("dve", hO[:, 0:QB, :], tmpv, q64(psH), ALU.add, [tmp, psH], [hO])
            yield
            scr.put(tmp)
            hBr = [(hB, q) for q in range(QB)]
        ck("k6")
        psY = getps()
        for q in range(QB):
            sl = slice(q * 64, (q + 1) * 64)
            mm(psY[:, sl], Mrb[:, q, :], XU[:, q, 0:64], [Mrb, XU], psY, start=True, stop=False, inc=False)
            yield
            mm(psY[:, sl], Mrk[:, q, :], V_[:, q, 0:64], [Mrk, V_], psY, start=False, stop=False, inc=False)
            yield
            mm(psY[:, sl], Rh[:, q, :], hB[:, q, :], [Rh, (hB, q)], psY, start=False, stop=True, inc=(q == QB - 1))
            yield
        wk.put(Mrb, Mrk, XA, XU, Bg, Kg, V_, Rh, GT, rTc)
        if not isinstance(gam, Off):
            gam_pool.put(gam)
        ck("k7")
        ysb = scr.get()
        ysbv = ysb[:, 0:QB * 64].rearrange("p (q t) -> p q t", t=64)
        S.op("act", lambda E: E.copy(out=ysbv, in_=q64(psY)), [psY], [ysb])
        yield ("TAIL", (c, seg.idx) if kind == "p" else None)
        stq = lambda i: stat[:, i, :]
        ysq = scr.get()
        ysqv = ysq[:, 0:QB * 64].rearrange("p (q t) -> p q t", t=64)
        for q in range(QB):
            S.op("dve", lambda E, q=q: E.bn_stats(out=bst4[:, q, :], in_=ysb[:, q * 64:(q + 1) * 64]), [ysb], [(bst4, q)])
        for q in range(QB):
            S.op("dve", lambda E, q=q: E.bn_aggr(out=mvq[:, q, :], in_=bst4[:, q, :]), [(bst4, q)], [(mvq, q)])
        yield
        mvr = [(mvq, q) for q in range(QB)]
        S.op("pool", lambda E: E.tensor_scalar(out=stq(4), in0=mvq[:, :, 1], scalar1=1.0, scalar2=GN_EPS,
                                               op0=ALU.mult, op1=ALU.add), mvr, [(stat, 4)])
        S.op("pool", lambda E: E.tensor_tensor(out=stq(5), in0=stq(4), in1=negh[:], op=ALU.pow),
             [(stat, 4), negh], [(stat, 5)])
        yield
        tt("dve", ysqv, ysbv, mvq[:, :, 0:1].to_broadcast([128, QB, 64]), ALU.subtract, [ysb] + mvr, [ysq])
        scr.put(ysb)
        yield
        for h in range(2):
            P = slice(h * 64, (h + 1) * 64)
            tt(ew_eng(), ybd[P, :, h, :], ysqv[P], stat[P, 5, :].rearrange("p (q o) -> p q o", o=1).to_broadcast([64, QB, 64]),
               ALU.mult, [ysq, (stat, 5)], [(ybd, h)])
            yield
        scr.put(ysq)
        ck("k8")
        psT = getps()
        ybv = ybd[:].rearrange("p q h t -> p q (h t)")
        for q in range(QB):
            mm(psT[:, q * 64:(q + 1) * 64], ybv[:, q, :], II_f[:], [(ybd, 0), (ybd, 1), II_f], psT, inc=(q == QB - 1))
            yield
        ck("k9")
        T_, nb = seg.T, seg.nb
        nt = nb * T_

        def dv(t_):
            return t_[:, 0:nt].rearrange("p (b t) -> p b t", t=T_)

        if kind == "p":
            ysrc = psT[:, 0:QB * 64].rearrange("p (b t) -> p b t", b=1)
        else:
            ysrc = q64(psT)[:, :, 0:DEC]
        yo = scr.get()
        act(dv(yo), ysrc, AF.Identity, [psT, vecs], [yo], bias=vcol(V_LNXB, c), scale=vcol(V_LNXG, c))
        yield
        tt("dve", dv(yo), dv(yo), seg.tokv(s1["bonus"][:, 0:nco]), ALU.add, [yo, s1["bonus"]], [yo])
        yield
        tt("pool", dv(yo), dv(yo), seg.tokv(s1["g"][:, 0:nco]), ALU.mult, [yo, s1["g"]], [yo])
        yield
        S.op("dve", lambda E: E.scalar_tensor_tensor(out=dv(yo), in0=dv(s1["gA"]), scalar=1.0, in1=dv(yo),
                                                     op0=ALU.add, op1=ALU.mult), [yo, s1["gA"]], [yo])
        yield
        mv = merged[:, c, seg.m0:seg.m0 + seg.mcols].rearrange("p (b t) -> p b t", t=T_)
        S.op("dve", lambda E: E.scalar_tensor_tensor(out=mv, in0=dv(s1["m2"]), scalar=0.25, in1=dv(yo),
                                                     op0=ALU.mult, op1=ALU.add), [yo, s1["m2"]], [(merged, c)])
        yield
        scr.put(yo)
        if not isinstance(s1["bonus"], Off):
            scr.put(s1["bonus"], s1["g"], s1["gA"], s1["m2"])

    def state_out(c, hsrc_ap, hres, dst_ap):
        for h in range(2):
            P = slice(h * 64, (h + 1) * 64)
            S.op(ew_eng(), lambda E: E.tensor_copy(out=hbd[P, h, :], in_=hsrc_ap[P, :]), [hres], [(hbd, h)])
        ps = getps()
        mm(ps[:, 0:64], hbd[:].rearrange("p h t -> p (h t)"), II_f[:], [(hbd, 0), (hbd, 1), II_f], ps)
        so = scr.get()
        copy_op(ev_eng(), so[:, 0:64], ps[:, 0:64], [ps], [so])
        S.dma(dst_ap, so[:, 0:64], reads=[so], q="act")
        scr.put(so)

    for g in range(3):
        S.op("pool", lambda E, g=g: E.memset(zbuf[g][:], 0.0), writes=[zbuf[g]])

    def s2full(c, seg, s1, par):
        if seg.kind == "p":
            hA, hB = H32[par], Hbf[par]

            def chain_init():
                if seg.idx == 0:
                    S.op("pool", lambda E: E.memset(hA[:, 0, :], 0.0), writes=[(hA, 0)])
                    S.op("pool", lambda E: E.memset(hB[:, 0, :], 0.0), writes=[(hB, 0)])
                else:
                    pA, pB = H32[1 - par], Hbf[1 - par]
                    S.op("pool", lambda E: E.tensor_copy(out=hA[:, 0, :], in_=pA[:, QB, :]), [(pA, QB)], [(hA, 0)])
                    S.op("pool", lambda E: E.tensor_copy(out=hB[:, 0, :], in_=pB[:, QB, :]), [(pB, QB)], [(hB, 0)])

            yield from stage2(c, seg, s1, dict(hA=hA, hB=hB, init=chain_init))
            if seg.idx == NSEGP - 1:
                state_out(c, hA[:, QB, :], (hA, QB),
                          o_wkv_p[2 * c:2 * c + 2, :, :].rearrange("h i j -> (h i) j"))
                yield
        else:
            sp_ = seg.idx % 2
            hA, hB, hO = H32s[sp_][0], Hbfs[sp_], H32s[sp_][1]
            b0 = seg.idx * QB
            for q in range(QB):
                sin = sin_tiles[sin_ctr["i"] % 3]
                sin_ctr["i"] += 1
                S.dma(sin[:, 0:64], s0[b0 + q, 2 * c:2 * c + 2, :, :].rearrange("h i j -> (h i) j"), writes=[sin])
                for h in range(2):
                    P = slice(h * 64, (h + 1) * 64)
                    S.op(ew_eng(), lambda E: E.tensor_copy(out=sbd[P, h, :], in_=sin[P, 0:64]), [sin], [(sbd, h)])
                ps = getps()
                mm(ps[:, 0:64], sbd[:].rearrange("p h t -> p (h t)"), II_f[:], [(sbd, 0), (sbd, 1), II_f], ps)
                S.op("act", lambda E: E.copy(out=hA[:, q, :], in_=ps[:, 0:64]), [ps], [(hA, q)])
                S.op("dve", lambda E: E.tensor_copy(out=hB[:, q, :], in_=ps[:, 0:64]), [ps], [(hB, q)])
                yield
            yield from stage2(c, seg, s1, dict(hA=hA, hB=hB, hO=hO))
            for q in range(QB):
                state_out(c, hO[:, q, :], hO,
                          o_wkv_s[b0 + q, 2 * c:2 * c + 2, :, :].rearrange("h i j -> (h i) j"))
                yield

    NSTREAM = 2
    mains = []
    tails = []
    SID = ("A", "B")

    chain_done = set()
    bd_free = set()

    def step_main(m):
        if m[2] == "TAILWAIT":
            if tails:
                return True
            mains.remove(m)
            tails.append(m)
            return False
        if m[2] is not None:
            if m[2][1] >= 0 and m[2] not in chain_done:
                return True
            m[2] = None
        cur_alloc["who"] = "s2" + SID[m[1]]
        try:
            v = next(m[0])
        except StopIteration:
            mains.remove(m)
            return False
        if isinstance(v, tuple) and v[0] == "BDFREE":
            bd_free.add(v[1])
            return True
        if isinstance(v, tuple) and v[0] == "CHAIN":
            m[2] = v[1]
            return True
        if isinstance(v, tuple) and v[0] == "TAIL":
            if v[1] is not None:
                chain_done.add(v[1])
            if tails:
                m[2] = "TAILWAIT"
                return True
            mains.remove(m)
            tails.append(m)
            return False
        return True

    def step_tails():
        for m in list(tails):
            cur_alloc["who"] = "tail" + SID[m[1]]
            try:
                next(m[0])
            except StopIteration:
                tails.remove(m)

    def step_bg():
        for m in list(mains):
            step_main(m)
        step_tails()
        step_wgen()

    def run_stage1(g1):
        acc = 0.0
        while True:
            step_bg()
            acc += RSCALE if (mains or tails) else 8.0
            while acc >= 1.0:
                acc -= 1.0
                cur_alloc["who"] = "s1"
                try:
                    next(g1)
                except StopIteration as e_:
                    cur_alloc["who"] = None
                    return e_.value

    def start_main(gen):
        while len(mains) >= NSTREAM:
            step_bg()
        used = {m[1] for m in mains} | {m[1] for m in tails}
        while len(used) >= 2:
            step_bg()
            used = {m[1] for m in mains} | {m[1] for m in tails}
        sid = 0 if 0 not in used else 1
        mains.append([gen, sid, None])

    def drain_all():
        while mains or tails:
            step_bg()
        cur_alloc["who"] = None

    def load_weights(c):
        for g in range(7):
            stg = stage[g % 3]
            S.dma(stg[:], w1[c, :, g, :, :], writes=[stg])
            S.op("dve", lambda E, g=g, stg=stg: E.tensor_copy(out=wcb[:, g, :, :], in_=stg[:]), [stg], [(wcb, g)])

    def load_weights_gen(c, gap=7):
        for g in range(7):
            stg = stage[g % 3]
            S.dma(stg[:], w1[c, :, g, :, :], writes=[stg])
            for _ in range(gap):
                yield
            S.op("dve", lambda E, g=g, stg=stg: E.tensor_copy(out=wcb[:, g, :, :], in_=stg[:]), [stg], [(wcb, g)])
            yield

    wgen = {"g": None}

    def step_wgen():
        if wgen["g"] is not None:
            try:
                next(wgen["g"])
            except StopIteration:
                wgen["g"] = None

    def finish_wgen():
        while wgen["g"] is not None:
            step_wgen()

    def finalize_after(gen, fn):
        yield from gen
        fn()

    try:
        load_weights(0)
        for c in range(NCH):
            par = 0
            for segi, seg in enumerate(segs):
                if stop_after == "seg%d" % segi:
                    raise StopBuild()
                cur["segi"] = segi
                if segi == 0:
                    finish_wgen()
                s1 = run_stage1(stage1(c, seg))
                if seg.kind == "p":
                    start_main(s2full(c, seg, s1, par))
                    par = 1 - par
                else:
                    if c + 1 < NCH and stop_after != "c0":
                        wgen["g"] = load_weights_gen(c + 1)
                    nbatch = NSB // QB
                    for b in range(nbatch):
                        sub = SubSeg(b)
                        while b >= 2 and (c, "s", b - 2) not in bd_free:
                            step_bg()
                        cur_alloc["who"] = "s1"
                        for _ in bd_write(sub, bd["s"][b % 2], s1["keep"], b * QB * 5):
                            pass
                        s1b = dict(bonus=Off(s1["bonus"], b * QB * 5, QB * 5), g=Off(s1["g"], b * QB * 5, QB * 5),
                                   gA=Off(s1["gA"], b * QB * DEC, QB * DEC), m2=Off(s1["m2"], b * QB * DEC, QB * DEC),
                                   gam=Off(s1["gam"], b * QB, QB))
                        gen = s2full(c, sub, s1b, 0)
                        if b == nbatch - 1:
                            K_ = s1["keep"]
                            scr.put(K_["kk"], K_["E3"], K_["tb"], K_["E2"], K_["E4"], K_["kf"], K_["zr"], K_["E1"], K_["zv"])

                            def fin_(s1=s1):
                                scr.put(s1["bonus"], s1["g"], s1["gA"], s1["m2"])
                                gam_pool.put(s1["gam"])

                            gen = finalize_after(gen, fin_)
                        start_main(gen)
            if stop_after == "c0":
                break
        drain_all()
    except StopBuild:
        pass
    dump("merged", merged[:, 0, :], [(merged, 0)])
    dump("merged7", merged[:, 7, :], [(merged, 7)])
    dump("fin", fin[:, 0, :], [(fin, 0)])

    print("nops", S.nops, "cnt", {k: v for k, v in S.cnt.items() if not k.startswith("d")})
    if stop_after is not None:
        S.finish()
        p1.close()
        es.close()
        return nc
    psF = [getps(), getps()]
    for c in range(NCH):
        hf, k4 = divmod(c, 4)
        S.op("pe", lambda E: E.transpose(out=psF[hf][0:NFIN, k4 * 128:(k4 + 1) * 128], in_=fin[:, c, :],
                                         identity=ident_f[:]), [(fin, c), ident_f], [psF[hf]])
    S.barrier()
    p1.close()
    p2 = ExitStack()
    lnvs = sb("lnvs", [128, 2, D], F32, p2)
    h1T = sb("h1T", [128, 8, MCOLS], BF16, p2)
    stg2 = [sb("stg2_%d" % i, [128, D], F32, p2) for i in range(3)]
    big = [sb("big%d" % i, [128, D], F32, p2) for i in range(4)]
    bst = sb("bst", [128, 2, 6], F32, p2)
    mv_ = sb("mv_", [128, 8], F32, p2)
    aI = sb("aI", [128, 2, 128], BF16, p2)
    finT = sb("finT", [NFIN, D], F32, p2)
    p2a = ExitStack()
    wo_b = sb("wo_b", [128, 8, D], BF16, p2a)
    a_hi = float(np.float32(ALPHA).astype(ml_bf16).astype(np.float32))
    a_lo = float(np.float32(ALPHA - a_hi).astype(ml_bf16).astype(np.float32))
    S.op("pool", lambda E: E.tensor_scalar(out=aI[:, 0, :], in0=ident_f[:], scalar1=a_hi, scalar2=0.0,
                                           op0=ALU.mult, op1=ALU.add), [ident_f], [(aI, 0)])
    S.op("pool", lambda E: E.tensor_scalar(out=aI[:, 1, :], in0=ident_f[:], scalar1=a_lo, scalar2=0.0,
                                           op0=ALU.mult, op1=ALU.add), [ident_f], [(aI, 1)])
    for hf in range(2):
        copy_op(ev_eng(), finT[:, hf * 512:(hf + 1) * 512], psF[hf][0:NFIN, :], [psF[hf]], [finT])
    S.dma(o_fin[:, :], finT[:, :], reads=[finT])
    S.dma(o_shift[0:1, :], xp[SEQ - 1:SEQ, :])
    S.dma(o_shift[1:1 + NSB, :], xst.rearrange("(b t) d -> b t d", t=DEC)[:, DEC - 1, :])
    S.dma(lnvs[:], lnv[:, 0:2, :], writes=[lnvs])
    for kc in range(8):
        stg = stg2[kc % 3]
        S.dma(stg[:], wo[:, kc, :], writes=[stg])
        S.op("dve", lambda E, kc=kc, stg=stg: E.tensor_copy(out=wo_b[:, kc, :], in_=stg[:]), [stg], [(wo_b, kc)])

    def layer_norm(src, dst, gi, rows):
        R = slice(0, rows)
        for j in range(2):
            S.op("dve", lambda E, j=j: E.bn_stats(out=bst[R, j, :], in_=src[R, j * 512:(j + 1) * 512]), [src], [bst])
        S.op("dve", lambda E: E.bn_aggr(out=mv_[R, 0:2], in_=bst[R, :, :]), [bst], [(mv_, 0)])
        S.op("dve", lambda E: E.tensor_scalar(out=mv_[R, 2:3], in0=mv_[R, 1:2], scalar1=LN_EPS, scalar2=None,
                                              op0=ALU.add), [(mv_, 0)], [(mv_, 2)])
        act(mv_[R, 3:4], mv_[R, 2:3], AF.Sqrt, [(mv_, 2)], [(mv_, 3)])
        S.op("dve", lambda E: E.reciprocal(out=mv_[R, 4:5], in_=mv_[R, 3:4]), [(mv_, 3)], [(mv_, 4)])
        S.op("dve", lambda E: E.scalar_tensor_tensor(out=mv_[R, 5:6], in0=mv_[R, 0:1], scalar=-1.0, in1=mv_[R, 4:5],
                                                     op0=ALU.mult, op1=ALU.mult), [(mv_, 0), (mv_, 4)], [(mv_, 5)])
        act(dst[R, :], src[R, :], AF.Identity, [src, (mv_, 4), (mv_, 5)], [dst], bias=mv_[R, 5:6], scale=mv_[R, 4:5])
        tt("pool", dst[R, :], dst[R, :], lnvs[R, 0, :], ALU.mult, [dst, lnvs], [dst])
        tt("dve", dst[R, :], dst[R, :], lnvs[R, 1, :], ALU.add, [dst, lnvs], [dst])

    ttiles = [(t * 128, 128, yp[t * 128:(t + 1) * 128, :], xp[t * 128:(t + 1) * 128, :]) for t in range(SEQ // 128)]
    ttiles.append((SEQ, NSB * DEC, ys[:, :], xst[:, :]))

    for ti, (m0, rows, _, xrows) in enumerate(ttiles):
        R = slice(0, rows)
        xtm = big[ti % 2]
        s1t = big[2 + ti % 2]
        S.dma(xtm[R, :], xrows, writes=[xtm])
        pss = [getps(), getps()]
        for hf in range(2):
            for kc in range(8):
                mm(pss[hf][R, :], merged[:, kc, m0:m0 + rows], wo_b[:, kc, hf * 512:(hf + 1) * 512],
                   [(merged, kc), (wo_b, kc)], pss[hf], start=(kc == 0), stop=(kc == 7), inc=(kc == 7))
        for hf in range(2):
            S.op("dve", lambda E, hf=hf: E.scalar_tensor_tensor(
                out=s1t[R, hf * 512:(hf + 1) * 512], in0=xtm[R, hf * 512:(hf + 1) * 512], scalar=ALPHA,
                in1=pss[hf][R, :], op0=ALU.mult, op1=ALU.add), [xtm, pss[hf]], [s1t])
        layer_norm(s1t, xtm, 0, rows)
        pst = [getps(), getps()]
        for kc in range(8):
            hf, k4 = divmod(kc, 4)
            S.op("pe", lambda E, hf=hf, k4=k4, kc=kc: E.transpose(
                out=pst[hf][:, k4 * 128:k4 * 128 + rows], in_=xtm[R, kc * 128:(kc + 1) * 128],
                identity=ident_f[R, R]), [xtm, ident_f], [pst[hf]], inc=(k4 == 3))
        for hf in range(2):
            copy_op(ev_eng(), h1T[:, hf * 4:(hf + 1) * 4, m0:m0 + rows],
                    pst[hf][:].rearrange("p (k t) -> p k t", t=128)[:, :, 0:rows], [pst[hf]],
                    [(h1T, hf * 4 + k) for k in range(4)])
    if "h1T" in dbg_out:
        S.dma(dbg_out["h1T"], h1T[:, 0, :], reads=[(h1T, 0)])

    S.barrier()
    p2a.close()
    S.dma(lnvs[:], lnv[:, 2:4, :], writes=[lnvs])
    wd_b = sb("wd_b", [128, NFC, D], BF16, p2)
    NTH = 1088
    NFA = 15
    uTa = T(merged[:].rearrange("p a b -> p (a b)")[:, 0:NFA * NTH].rearrange("p (f t) -> p f t", t=NTH), "uTa")
    uTb = sb("uTb", [128, NFC - NFA, NTH], BF16, p2)

    class _UT:
        def __getitem__(self, k):
            p_, fc_, cols_ = k
            if fc_ < NFA:
                return uTa.h[p_, fc_, cols_]
            return uTb.h[p_, fc_ - NFA, cols_]

    uT = _UT()
    wgub = [sb("wgub%d" % i, [128, 2, 8, 128], BF16, p2) for i in range(2)]
    sgt = [sb("sgt%d" % i, [128, 512], F32, p2) for i in range(2)]
    for fc in range(NFC):
        stg = stg2[fc % 3]
        S.dma(stg[:], wd[fc, :, :], writes=[stg])
        S.op("dve", lambda E, fc=fc, stg=stg: E.tensor_copy(out=wd_b[:, fc, :], in_=stg[:]), [stg], [(wd_b, fc)])
    halves = [(0, 1024, ttiles[0:8]), (1024, 1088, ttiles[8:17])]
    for (c0, ncols, tls) in halves:
        blocks = [(b0_, min(512, ncols - b0_)) for b0_ in range(0, ncols, 512)]
        def load_fc(fc):
            wb_ = wgub[fc % 2]
            for j in range(2):
                stg = stg2[(2 * fc + j) % 3]
                S.dma(stg[:].rearrange("p (a b) -> p a b", b=128), wgu[fc, :, j, :, :], writes=[stg])
                S.op("dve", lambda E, j=j, stg=stg: E.tensor_copy(
                    out=wb_[:, j, :, :], in_=stg[:].rearrange("p (a b) -> p a b", b=128)), [stg], [(wb_, j)])

        if c0 == 0:
            load_fc(0)
        for fc in range(NFC):
            wb = wgub[fc % 2]
            if fc + 1 < NFC:
                load_fc(fc + 1)
            for bi, (b0_, bn) in enumerate(blocks):
                psg = getps()
                psu = getps()
                for j, ps_ in ((0, psg), (1, psu)):
                    for kc in range(8):
                        mm(ps_[:, 0:bn], wb[:, j, kc, :], h1T[:, kc, c0 + b0_:c0 + b0_ + bn], [(wb, j), (h1T, kc)], ps_,
                           start=(kc == 0), stop=(kc == 7), inc=(kc == 7))
                sg_ = sgt[(fc * 3 + bi) % 2]
                act(sg_[:, 0:bn], psg[:, 0:bn], AF.Silu, [psg], [sg_])
                tt("dve", uT[:, fc, b0_:b0_ + bn], psu[:, 0:bn], sg_[:, 0:bn], ALU.mult, [psu, sg_], [(uT, fc)])
        if c0 == 0:
            load_fc(0)
        for ti, (m0, rows, yout, _) in enumerate(tls):
            R = slice(0, rows)
            l0 = m0 - c0
            pss = [getps(), getps()]
            for hf in range(2):
                for fc in range(NFC):
                    mm(pss[hf][R, :], uT[:, fc, l0:l0 + rows], wd_b[:, fc, hf * 512:(hf + 1) * 512],
                       [(uT, fc), (wd_b, fc)], pss[hf], start=(fc == 0), stop=False, inc=False)
                for k4 in range(4):
                    kc = hf * 4 + k4
                    for a in range(2):
                        last = (k4 == 3 and a == 1)
                        mm(pss[hf][R, k4 * 128:(k4 + 1) * 128], h1T[:, kc, m0:m0 + rows], aI[:, a, :],
                           [(h1T, kc), (aI, a)], pss[hf], start=False, stop=last, inc=last)
            s2t = big[ti % 2]
            yt = big[2 + ti % 2]
            for hf in range(2):
                copy_op(ev_eng(), s2t[R, hf * 512:(hf + 1) * 512], pss[hf][R, :], [pss[hf]], [s2t])
            layer_norm(s2t, yt, 2, rows)
            S.dma(yout, yt[R, :], reads=[yt])

    print("nops", S.nops)
    S.finish()
    p2.close()
    es.close()
    return nc


def kernel(**inputs):
    maps = prep_inputs(inputs)
    nc = build_program()
    res = run_bass_kernel_spmd(nc, maps, core_ids=list(range(NCORES)))
    R = res.results
    f32 = np.float32
    y_p = np.stack([R[i]["yp"] for i in range(NCORES)]).astype(f32)
    y_s = np.concatenate([R[i]["ys"].reshape(NSB, DEC, D) for i in range(NCORES)]).astype(f32)
    fin = [R[i]["o_fin"] for i in range(NCORES)]
    shf = [R[i]["o_shift"] for i in range(NCORES)]
    new_shift_p = np.stack([s[0] for s in shf])[None].astype(f32)
    new_shift_s = np.concatenate([s[1:] for s in shf])[None].astype(f32)
    new_wkv_p = np.stack([R[i]["o_wkv_p"] for i in range(NCORES)])[None].astype(f32)
    new_wkv_s = np.concatenate([R[i]["o_wkv_s"] for i in range(NCORES)])[None].astype(f32)
    new_conv_p = np.stack([f[0:3] for f in fin])[None].astype(f32)
    new_lru_p = np.stack([f[3] for f in fin])[None].astype(f32)
    new_conv_s = np.concatenate([f[4:52].reshape(NSB, 3, D) for f in fin])[None].astype(f32)
    new_lru_s = np.concatenate([f[52:68] for f in fin])[None].astype(f32)
    return (y_p, y_s, new_shift_p, new_wkv_p, new_conv_p, new_lru_p,
            new_shift_s, new_wkv_s, new_conv_s, new_lru_s)


def prep_inputs(inp):
    f = lambda a: np.ascontiguousarray(np.asarray(a, dtype=np.float32))
    w_in = f(inp["w_in"])[0]
    bases = [0, 1024, 2048, 3328, 4352, 5376, 6400]
    w1 = np.stack([w_in[:, b:b + 1024].reshape(8, 128, 8, 128).transpose(2, 1, 0, 3) for b in bases], axis=2)
    wl = w_in[:, 3072:3328].reshape(8, 128, 2, 128).transpose(1, 2, 0, 3)
    wo = f(inp["w_o"])[0].reshape(8, 128, D).transpose(1, 0, 2)
    wg = f(inp["w_ffn_gate"])[0].reshape(8, 128, NFC, 128).transpose(2, 1, 0, 3)
    wu = f(inp["w_ffn_up"])[0].reshape(8, 128, NFC, 128).transpose(2, 1, 0, 3)
    wgu = np.stack([wg, wu], axis=2)
    wd = f(inp["w_ffn_down"])[0].reshape(NFC, 128, D)
    vec = np.zeros((NV, D), np.float32)
    mu = f(inp["tmix_mu"])[0]
    vec[V_MUR] = mu[0:1024]; vec[V_MUK] = mu[1024:2048]; vec[V_MUV] = mu[2048:3072]
    vec[V_MUL, 0:256] = mu[3072:3328]
    vec[V_W0] = f(inp["w0"])[0]; vec[V_A0] = f(inp["a0"])[0]
    vec[V_KK] = f(inp["k_k"])[0]; vec[V_KA] = f(inp["k_a"])[0]
    vec[V_RK] = f(inp["r_k"])[0].reshape(-1)
    vec[V_LNXG] = f(inp["lnx_g"])[0]; vec[V_LNXB] = f(inp["lnx_b"])[0]
    cw = f(inp["conv_w"])[0]
    for j in range(4):
        vec[V_CW0 + j] = cw[j]
    vec[V_CB] = f(inp["conv_b"])[0]
    vec[V_BA] = f(inp["lru_ba"])[0].reshape(-1); vec[V_BI] = f(inp["lru_bi"])[0].reshape(-1)
    vec[V_LAM] = f(inp["lru_lambda"])[0]
    vecp = np.ascontiguousarray(vec.reshape(NV, 8, 128).transpose(2, 0, 1))
    wab = np.zeros((128, 2, 8, 128), np.float32)
    for j, nm in enumerate(("lru_wa", "lru_wi")):
        w = f(inp[nm])[0]
        for c in range(8):
            for bl in range(2):
                wab[bl * 64:(bl + 1) * 64, j, c, bl * 64:(bl + 1) * 64] = w[2 * c + bl]
    lnv = np.stack([np.broadcast_to(f(inp[n])[0], (128, D)) for n in ("ln1_g", "ln1_b", "ln2_g", "ln2_b")], axis=1)
    shared = dict(w1=w1, wl=wl, wo=wo, wgu=wgu, wd=wd, vec=vecp, w2d=f(inp["w2_decay"])[0],
                  a2d=f(inp["a2_iclr"])[0], g2d=f(inp["g2_gate"])[0], wab=wab, lnv=lnv)
    shared = {k: np.ascontiguousarray(v, dtype=np.float32) for k, v in shared.items()}
    x_prompt = f(inp["x_prompt"]); x_sample = f(inp["x_sample"])
    sh = f(inp["state_shift"])[0]; swkv = f(inp["state_wkv"])[0]
    sconv = f(inp["state_conv"])[0]; slru = f(inp["state_lru"])[0]
    maps = []
    for i in range(NCORES):
        b0 = i * NSB
        xs = np.concatenate([sh[b0:b0 + NSB, None, :], x_sample[b0:b0 + NSB]], axis=1).reshape(NSB * 5, D)
        stt = np.concatenate([sconv[b0:b0 + NSB].reshape(NSB * 3, D), slru[b0:b0 + NSB]], axis=0)
        m = dict(xp=x_prompt[i], xs=xs, xst=x_sample[b0:b0 + NSB].reshape(NSB * DEC, D), st=stt,
                 s0=swkv[b0:b0 + NSB])
        m = {k: np.ascontiguousarray(v, dtype=np.float32) for k, v in m.items()}
        m.update(shared)
        maps.append(m)
    return maps
```
